# Optimizing a Trainium2 kernel written in Bass

```python
import math
import jax, jax.numpy as jnp
from jax import lax
import numpy as np

D_MODEL = 2048
BATCH = 1
SEQ = 16384
DEPTH = 2

CHUNK = 64
Q_BLOCK = 128
N_MIXERS = 2
EPS = 1e-6

A_HEADS = 16
A_NOPE = 128
A_ROPE = 64
A_QK = A_NOPE + A_ROPE
A_V = 128
A_Q_LORA = 512
A_KV_LORA = 512
A_WIDTH = A_HEADS * A_V
A_IN = A_Q_LORA + A_KV_LORA + A_ROPE + A_WIDTH
ROPE_THETA = 10000.0

B_HEADS = 8
B_HD = 128
B_V = 2 * B_HD
B_WIDTH = B_HEADS * B_V
B_QK = B_HEADS * 2 * B_HD
B_IN = 2 * B_QK + B_WIDTH + B_WIDTH

N_A = (DEPTH + 1) // 2
N_B = DEPTH // 2

kernel_name = "hybrid_mla_diffattn_chunk_causal"


def rms_norm(x, gain):
    xf = x.astype(jnp.float32)
    y = xf * lax.rsqrt(jnp.mean(xf * xf, axis=-1, keepdims=True) + EPS)
    return (y * gain.astype(jnp.float32)).astype(x.dtype)


def rope(x, pos):
    r = x.shape[-1]
    inv_freq = ROPE_THETA ** (-jnp.arange(0, r, 2, dtype=jnp.float32) / r)
    ang = pos.astype(jnp.float32)[..., None] * inv_freq
    cos = jnp.cos(ang)[:, :, None, :]
    sin = jnp.sin(ang)[:, :, None, :]
    xf = x.astype(jnp.float32)
    x1, x2 = xf[..., : r // 2], xf[..., r // 2:]
    return jnp.concatenate([x1 * cos - x2 * sin, x1 * sin + x2 * cos], axis=-1).astype(x.dtype)


def chunk_allowed(pos_q, pos_k):
    return (pos_k // CHUNK)[:, None, :] <= (pos_q // CHUNK)[:, :, None]


def mla_attend(q, k, v, pos):
    b, s, h, d = q.shape
    nblk = s // Q_BLOCK
    scale = 1.0 / math.sqrt(d)
    k_t = k.transpose(0, 2, 1, 3)
    v_t = v.transpose(0, 2, 1, 3)
    q_blk = q.reshape(b, nblk, Q_BLOCK, h, d).transpose(1, 0, 3, 2, 4)
    p_blk = pos.reshape(b, nblk, Q_BLOCK).transpose(1, 0, 2)

    def one_block(args):
        qi, pi = args
        sc = jnp.einsum('bhqd,bhkd->bhqk', qi, k_t).astype(jnp.float32) * scale
        sc = jnp.where(chunk_allowed(pi, pos)[:, None], sc, -jnp.inf)
        p = jax.nn.softmax(sc, axis=-1)
        return jnp.einsum('bhqk,bhkd->bhqd', p.astype(v_t.dtype), v_t)

    o = lax.map(one_block, (q_blk, p_blk))
    return o.transpose(1, 0, 3, 2, 4).reshape(b, s, h, v.shape[-1])


def diff_attend(q, k, v, pos, lam):
    b, s, h, _, d = q.shape
    nblk = s // Q_BLOCK
    scale = 1.0 / math.sqrt(d)
    slopes = 2.0 ** (-8.0 * jnp.arange(1, h + 1, dtype=jnp.float32) / h)
    k_t = k.transpose(0, 2, 3, 1, 4)
    v_t = v.transpose(0, 2, 1, 3)
    q_blk = q.reshape(b, nblk, Q_BLOCK, h, 2, d).transpose(1, 0, 3, 4, 2, 5)
    p_blk = pos.reshape(b, nblk, Q_BLOCK).transpose(1, 0, 2)

    def one_block(args):
        qi, pi = args
        sc = jnp.einsum('bhcqd,bhckd->bhcqk', qi, k_t).astype(jnp.float32) * scale
        dist = jnp.abs(pi[:, :, None] - pos[:, None, :]).astype(jnp.float32)
        bias = -slopes[None, :, None, None] * dist[:, None]
        sc = sc + bias[:, :, None]
        sc = jnp.where(chunk_allowed(pi, pos)[:, None, None], sc, -jnp.inf)
        p = jax.nn.softmax(sc, axis=-1)
        p_diff = p[:, :, 0] - lam * p[:, :, 1]
        return jnp.einsum('bhqk,bhkd->bhqd', p_diff.astype(v_t.dtype), v_t)

    o = lax.map(one_block, (q_blk, p_blk))
    return o.transpose(1, 0, 3, 2, 4).reshape(b, s, h, v.shape[-1])


def mla_layer(x, pos, norm_g, w_in, q_norm_g, w_q_up, kv_norm_g, w_kv_up, q_gain, k_gain, w_out):
    b, s, _ = x.shape
    z = rms_norm(x, norm_g) @ w_in
    cq, ckv, k_pe, gate = jnp.split(
        z, [A_Q_LORA, A_Q_LORA + A_KV_LORA, A_Q_LORA + A_KV_LORA + A_ROPE], axis=-1)
    q = (rms_norm(cq, q_norm_g) @ w_q_up).reshape(b, s, A_HEADS, A_QK)
    kv = (rms_norm(ckv, kv_norm_g) @ w_kv_up).reshape(b, s, A_HEADS, A_NOPE + A_V)
    k_nope, v = kv[..., :A_NOPE], kv[..., A_NOPE:]
    k = jnp.concatenate(
        [k_nope, jnp.broadcast_to(k_pe[:, :, None, :], (b, s, A_HEADS, A_ROPE))], axis=-1)
    q = rms_norm(q, q_gain)
    k = rms_norm(k, k_gain)
    q = jnp.concatenate([q[..., :A_NOPE], rope(q[..., A_NOPE:], pos)], axis=-1)
    k = jnp.concatenate([k[..., :A_NOPE], rope(k[..., A_NOPE:], pos)], axis=-1)
    o = mla_attend(q, k, v, pos).reshape(b, s, A_WIDTH)
    return x + (o * jax.nn.silu(gate)) @ w_out


def diff_layer(x, pos, layer_idx, norm_g, w_in, q_gain, k_gain, lq1, lk1, lq2, lk2, subln_g, w_out):
    b, s, _ = x.shape
    z = rms_norm(x, norm_g) @ w_in
    q, k, v, gate = jnp.split(z, [B_QK, 2 * B_QK, 2 * B_QK + B_WIDTH], axis=-1)
    q = rms_norm(q.reshape(b, s, B_HEADS, 2, B_HD), q_gain)
    k = rms_norm(k.reshape(b, s, B_HEADS, 2, B_HD), k_gain)
    v = v.reshape(b, s, B_HEADS, B_V)
    lam_init = 0.8 - 0.6 * math.exp(-0.3 * layer_idx)
    lam = (jnp.exp(jnp.sum(lq1.astype(jnp.float32) * lk1.astype(jnp.float32)))
           - jnp.exp(jnp.sum(lq2.astype(jnp.float32) * lk2.astype(jnp.float32))) + lam_init)
    o = diff_attend(q, k, v, pos, lam)
    o = rms_norm(o, subln_g) * (1.0 - lam_init)
    return x + (o.reshape(b, s, B_WIDTH) * jax.nn.silu(gate)) @ w_out


def setup_inputs(seed: int = 0) -> dict:
    key = jax.random.key(seed)
    ks = jax.random.split(key, 24)
    f32 = jnp.float32

    def nrm(k, shape, scale):
        return jax.random.normal(k, shape, f32) * scale

    def gain(k, shape):
        return 1.0 + 0.02 * jax.random.normal(k, shape, f32)

    x = jax.random.normal(ks[0], (BATCH, SEQ, D_MODEL), f32)
    positions = jnp.broadcast_to(jnp.arange(SEQ, dtype=jnp.int32)[None], (BATCH, SEQ))
    return {
        "x": x,
        "positions": positions,
        "a_norm": gain(ks[1], (N_A, D_MODEL)),
        "a_w_in": nrm(ks[2], (N_A, D_MODEL, A_IN), D_MODEL ** -0.5),
        "a_q_norm": gain(ks[3], (N_A, A_Q_LORA)),
        "a_w_q_up": nrm(ks[4], (N_A, A_Q_LORA, A_HEADS * A_QK), A_Q_LORA ** -0.5),
        "a_kv_norm": gain(ks[5], (N_A, A_KV_LORA)),
        "a_w_kv_up": nrm(ks[6], (N_A, A_KV_LORA, A_HEADS * (A_NOPE + A_V)), A_KV_LORA ** -0.5),
        "a_q_gain": gain(ks[7], (N_A, A_QK)),
        "a_k_gain": gain(ks[8], (N_A, A_QK)),
        "a_w_out": nrm(ks[9], (N_A, A_WIDTH, D_MODEL), A_WIDTH ** -0.5),
        "b_norm": gain(ks[10], (N_B, D_MODEL)),
        "b_w_in": nrm(ks[11], (N_B, D_MODEL, B_IN), D_MODEL ** -0.5),
        "b_q_gain": gain(ks[12], (N_B, B_HD)),
        "b_k_gain": gain(ks[13], (N_B, B_HD)),
        "b_lambda_q1": nrm(ks[14], (N_B, B_HD), 0.1),
        "b_lambda_k1": nrm(ks[15], (N_B, B_HD), 0.1),
        "b_lambda_q2": nrm(ks[16], (N_B, B_HD), 0.1),
        "b_lambda_k2": nrm(ks[17], (N_B, B_HD), 0.1),
        "b_subln": gain(ks[18], (N_B, B_V)),
        "b_w_out": nrm(ks[19], (N_B, B_WIDTH, D_MODEL), B_WIDTH ** -0.5),
    }


def reference(x, positions, a_norm, a_w_in, a_q_norm, a_w_q_up, a_kv_norm, a_w_kv_up,
              a_q_gain, a_k_gain, a_w_out, b_norm, b_w_in, b_q_gain, b_k_gain,
              b_lambda_q1, b_lambda_k1, b_lambda_q2, b_lambda_k2, b_subln, b_w_out):
    for i in range(DEPTH):
        j = i // N_MIXERS
        if i % N_MIXERS == 0:
            x = mla_layer(x, positions, a_norm[j], a_w_in[j], a_q_norm[j], a_w_q_up[j],
                          a_kv_norm[j], a_w_kv_up[j], a_q_gain[j], a_k_gain[j], a_w_out[j])
        else:
            x = diff_layer(x, positions, i, b_norm[j], b_w_in[j], b_q_gain[j], b_k_gain[j],
                           b_lambda_q1[j], b_lambda_k1[j], b_lambda_q2[j], b_lambda_k2[j],
                           b_subln[j], b_w_out[j])
    return x
```

```python
import math
from contextlib import ExitStack

import numpy as np
import concourse.bass as bass
import concourse.mybir as mybir
from concourse.bass_utils import run_bass_kernel_spmd

F32 = mybir.dt.float32
BF16 = mybir.dt.bfloat16
I32 = mybir.dt.int32
AF = mybir.ActivationFunctionType
ALU = mybir.AluOpType
AX = mybir.AxisListType

NCORES = 8
S = 16384
D = 2048
TPC = S // NCORES
NBLK = S // 128
EPS = 1e-6
CHUNK = 64
A_H, A_NOPE, A_ROPE, A_QK, A_V, A_LORA = 16, 128, 64, 192, 128, 512
B_H, B_HD, B_V = 8, 128, 256
TWO_PI = 2.0 * math.pi
NEG_BIG = -1.0e30


class Eng:
    def __init__(self, name, sem, serialize=False):
        self.name = name
        self.sem = sem
        self.count = 0
        self.waited = {}
        self.thunks = []
        self.serialize = serialize

    def wait(self, *toks):
        for tok in toks:
            if tok is None:
                continue
            if isinstance(tok, (list, tuple)) and (len(tok) == 0 or isinstance(tok[0], (list, tuple)) or tok[0] is None):
                self.wait(*tok)
                continue
            sem, val = tok
            if self.waited.get(sem, 0) >= val:
                continue
            self.waited[sem] = val
            self.thunks.append(lambda e, sem=sem, val=val: e.wait_ge(sem, val))

    def op(self, fn, pub=True, indep=False):
        if self.serialize and not indep and self.count > 0:
            self.wait((self.sem, self.count))
        self.count += 1
        c = self.count
        sem = self.sem
        self.thunks.append(lambda e: fn(e).then_inc(sem, 1))
        return (sem, c)


class DmaSem:
    def __init__(self, sem):
        self.sem = sem
        self.n = 0

    def issue(self, eng, fn):
        self.n += 16
        sem = self.sem
        eng.thunks.append(lambda e: fn(e).then_inc(sem, 16))
        return (sem, self.n)


class Prog:
    def __init__(self):
        self.nc = bass.Bass("TRN2", target_bir_lowering=False)
        self.es = ExitStack()
        self._n = 0
        self.SP = Eng("sync", self.sem("s_sp"))
        self.ACT = Eng("scalar", self.sem("s_act"), serialize=True)
        self.DVE = Eng("vector", self.sem("s_dve"), serialize=True)
        self.POOL = Eng("gpsimd", self.sem("s_pool"), serialize=True)
        self.PE = Eng("tensor", self.sem("s_pe"))
        self.engs = [self.SP, self.ACT, self.DVE, self.POOL, self.PE]
        self.psum = self.es.enter_context(self.nc.psum_tensor("psum", [128, 8, 512], F32))
        self.out_toks = []

    def uid(self, p):
        self._n += 1
        return "%s%d" % (p, self._n)

    def sem(self, name=None):
        return self.es.enter_context(self.nc.semaphore(name or self.uid("sem")))

    def dsem(self, name=None):
        return DmaSem(self.sem(name))

    def sb(self, name, shape, dt):
        return self.es.enter_context(self.nc.sbuf_tensor(name, list(shape), dt))

    def dram(self, name, shape, dt, kind):
        return self.nc.dram_tensor(name, list(shape), dt, kind=kind).ap()

    def bank(self, b):
        return self.psum[:, b, :]

    def bank_bf(self, b):
        return self.psum[:, b, :].bitcast(BF16)

    def finish(self):
        self.SP.wait(*self.out_toks)
        with self.nc.Block() as block:
            for eng in self.engs:
                if not eng.thunks:
                    continue

                def body(e, eng=eng):
                    for th in eng.thunks:
                        th(e)

                getattr(block, eng.name)(body)
        self.es.close()
        return self.nc

    def make_ident(self):
        idf = self.sb("ident_f", [128, 128], F32)
        idb = self.sb("ident_b", [128, 128], BF16)
        t0 = self.POOL.op(lambda e: e.memset(idf[:], 1.0))
        self.POOL.wait(t0)
        t1 = self.POOL.op(lambda e: e.affine_select(out=idf[:], in_=idf[:], pattern=[[-1, 128]],
                                                    compare_op=ALU.is_equal, fill=0.0, base=0,
                                                    channel_multiplier=1))
        self.DVE.wait(t1)
        t2 = self.DVE.op(lambda e: e.tensor_copy(out=idb[:], in_=idf[:]))
        self.ident = idb
        self.ident_tok = t2
        return idb

    def const_tile(self, val, name=None):
        t = self.sb(name or self.uid("c"), [128, 1], F32)
        tok = self.POOL.op(lambda e: e.memset(t[:], float(val)))
        return t, tok

    def load_weight_bf16(self, dst, src, nkc, ncols, gcol, gcol_tok, stage, stage_sems, state):
        W = stage[0].shape[1]
        t_cv = None
        for kc in range(nkc):
            for c0 in range(0, ncols, W):
                c1 = min(ncols, c0 + W)
                i = state["i"] % len(stage)
                st = stage[i]
                self.SP.wait(state["free"][i])
                t_ld = stage_sems[i].issue(self.SP, lambda e, st=st, kc=kc, c0=c0, c1=c1: e.dma_start(
                    out=st[:, 0:c1 - c0], in_=src[kc * 128:(kc + 1) * 128, c0:c1]))
                self.ACT.wait(t_ld, gcol_tok)
                if gcol is not None:
                    t_cv = self.ACT.op(lambda e, st=st, kc=kc, c0=c0, c1=c1: e.activation(
                        out=dst[:, kc, c0:c1], in_=st[:, 0:c1 - c0], func=AF.Copy, scale=gcol[:, kc:kc + 1]), indep=True)
                else:
                    t_cv = self.ACT.op(lambda e, st=st, kc=kc, c0=c0, c1=c1: e.activation(
                        out=dst[:, kc, c0:c1], in_=st[:, 0:c1 - c0], func=AF.Copy), indep=True)
                state["free"][i] = t_cv
                state["i"] += 1
        return t_cv


def _stage(P, ncols, n=3):
    stage = [P.sb(P.uid("wst"), [128, ncols], F32) for _ in range(n)]
    sems = [P.dsem() for _ in range(n)]
    state = {"i": 0, "free": [None] * n}
    return stage, sems, state


def emit_rstd(P, ss, n, eps_t, eps_tok, out, after):
    P.ACT.wait(after, eps_tok)
    t = P.ACT.op(lambda e: e.activation(out=out[:], in_=ss[:], func=AF.Sqrt, bias=eps_t[:], scale=1.0 / n))
    P.DVE.wait(t)
    return P.DVE.op(lambda e: e.reciprocal(out=out[:], in_=out[:]))


def build_A1():
    P = Prog()
    nc = P.nc
    NL = 2 * A_LORA + A_ROPE
    x = P.dram("x", [TPC, D], F32, "ExternalInput")
    w = P.dram("w_lat", [D, NL], F32, "ExternalInput")
    gn = P.dram("g_norm", [128, D // 128], F32, "ExternalInput")
    latT = P.dram("latT", [2 * A_LORA, TPC], BF16, "ExternalOutput")
    kpe = P.dram("kpe", [TPC, A_ROPE], F32, "ExternalOutput")
    KC = D // 128
    P.make_ident()
    eps_t, eps_tok = P.const_tile(EPS, "eps")
    gcol = P.sb("gcol", [128, KC], F32)
    ds0 = P.dsem()
    t_g = ds0.issue(P.SP, lambda e: e.dma_start(out=gcol[:], in_=gn))
    wb = P.sb("wb", [128, KC, NL], BF16)
    stage, ssems, sstate = _stage(P, NL, n=2)
    t_w = P.load_weight_bf16(wb, w, KC, NL, gcol, t_g, stage, ssems, sstate)

    NB = TPC // 128
    xt = [P.sb("xt%d" % i, [128, D], F32) for i in range(2)]
    xsem = [P.dsem() for _ in range(2)]
    xfree = [None, None]
    junk = P.sb("junk", [128, D], BF16)
    xb = P.sb("xb", [128, D], BF16)
    xnT = P.sb("xnT", [128, KC, 128], BF16)
    ss = P.sb("ss", [128, 4], F32)
    rs = P.sb("rs", [128, 4], F32)
    latb = P.sb("latb", [128, 2 * A_LORA], BF16)
    kpt = P.sb("kpt", [128, A_ROPE], F32)
    latTt = P.sb("latTt", [128, 8, 128], BF16)
    osem1 = P.dsem()
    osem2 = P.dsem()
    t_prev_z = None
    t_prev_lt = None
    t_out1 = None
    t_out2 = None
    t_xb_free = None
    t_evac_lt = None
    t_z_free = None
    for b in range(NB):
        s = b % 2
        P.SP.wait(xfree[s])
        t_x = xsem[s].issue(P.SP, lambda e, s=s, b=b: e.dma_start(out=xt[s][:], in_=x[b * 128:(b + 1) * 128, :]))
        P.ACT.wait(t_x)
        t_ss = P.ACT.op(lambda e, s=s: e.activation(out=junk[:], in_=xt[s][:], func=AF.Square, accum_out=ss[:, 0:1]))
        t_r = emit_rstd(P, ss[:, 0:1], D, eps_t, eps_tok, rs[:, 0:1], t_ss)
        P.DVE.wait(t_r, t_xb_free)
        t_xb = P.DVE.op(lambda e, s=s: e.tensor_scalar(out=xb[:], in0=xt[s][:], scalar1=rs[:, 0:1], scalar2=None, op0=ALU.mult))
        xfree[s] = t_xb
        P.PE.wait(t_xb, P.ident_tok, t_prev_z)
        for kc in range(KC):
            bk = kc // 8
            o = (kc % 8) * 128
            tt = P.PE.op(lambda e, kc=kc, bk=bk, o=o: e.transpose(out=P.bank_bf(bk)[:, o:o + 128], in_=xb[:, kc * 128:(kc + 1) * 128], identity=P.ident[:]),
                         pub=(kc == KC - 1))
        t_xb_free = tt
        P.DVE.wait(tt)
        P.DVE.op(lambda e: e.tensor_copy(out=xnT[:, 0:8, :], in_=P.bank_bf(0)[:, 0:1024]), pub=False)
        t_ev = P.DVE.op(lambda e: e.tensor_copy(out=xnT[:, 8:16, :], in_=P.bank_bf(1)[:, 0:1024]))
        P.PE.wait(t_ev, t_w, t_z_free)
        for gi, (c0, c1) in enumerate([(0, 512), (512, 1024), (1024, NL)]):
            for kc in range(KC):
                tz = P.PE.op(lambda e, gi=gi, c0=c0, c1=c1, kc=kc: e.matmul(
                    P.bank(2 + gi)[:, 0:c1 - c0], lhsT=xnT[:, kc, :], rhs=wb[:, kc, c0:c1],
                    start=(kc == 0), stop=(kc == KC - 1)), pub=(gi == 2 and kc == KC - 1))
        t_prev_z = tz
        P.ACT.wait(tz)
        t_s1 = P.ACT.op(lambda e: e.activation(out=junk[:, 0:512], in_=P.bank(2)[:, :], func=AF.Square, accum_out=ss[:, 1:2]))
        t_s2 = P.ACT.op(lambda e: e.activation(out=junk[:, 512:1024], in_=P.bank(3)[:, :], func=AF.Square, accum_out=ss[:, 2:3]))
        t_r1 = emit_rstd(P, ss[:, 1:3], A_LORA, eps_t, eps_tok, rs[:, 1:3], t_s2)
        P.ACT.wait(t_r1, t_prev_lt)
        P.ACT.op(lambda e: e.activation(out=latb[:, 0:512], in_=P.bank(2)[:, :], func=AF.Copy, scale=rs[:, 1:2]), pub=False)
        t_lb = P.ACT.op(lambda e: e.activation(out=latb[:, 512:1024], in_=P.bank(3)[:, :], func=AF.Copy, scale=rs[:, 2:3]))
        P.DVE.wait(tz, t_out2)
        t_kp = P.DVE.op(lambda e: e.tensor_copy(out=kpt[:], in_=P.bank(4)[:, 0:A_ROPE]))
        t_z_free = [t_lb, t_kp]
        P.SP.wait(t_kp)
        t_out2 = osem2.issue(P.SP, lambda e, b=b: e.dma_start(out=kpe[b * 128:(b + 1) * 128, :], in_=kpt[:]))
        P.PE.wait(t_lb, t_evac_lt)
        for j in range(8):
            tl = P.PE.op(lambda e, j=j: e.transpose(out=P.bank_bf(5)[:, j * 128:(j + 1) * 128], in_=latb[:, j * 128:(j + 1) * 128], identity=P.ident[:]),
                         pub=(j == 7))
        t_prev_lt = tl
        P.DVE.wait(tl, t_out1)
        t_evac_lt = P.DVE.op(lambda e: e.tensor_copy(out=latTt[:].rearrange("p j t -> p (j t)"), in_=P.bank_bf(5)[:, 0:1024]))
        P.SP.wait(t_evac_lt)
        t_out1 = osem1.issue(P.SP, lambda e, b=b: e.dma_start(
            out=latT.rearrange("(j p) t -> p j t", p=128)[:, :, b * 128:(b + 1) * 128], in_=latTt[:]))
    P.out_toks += [t_out1, t_out2]
    return P.finish()


def emit_rope_tables(P, pos_i, pos_tok, cs):
    R2 = A_ROPE // 2
    invf = (np.float32(10000.0) ** (-(np.arange(0, A_ROPE, 2, dtype=np.float32)) / np.float32(A_ROPE))).astype(np.float32)
    posf = P.sb("posf", [128, NBLK], F32)
    CB = 16
    u = P.sb("rt_u", [128, CB, R2], F32)
    tt = P.sb("rt_t", [128, CB, R2], F32)
    ki = P.sb("rt_ki", [128, CB, R2], I32)
    kf = P.sb("rt_kf", [128, CB, R2], F32)
    r = P.sb("rt_r", [128, CB, R2], F32)
    V = P.DVE
    V.wait(pos_tok)
    V.op(lambda e: e.tensor_copy(out=posf[:], in_=pos_i[:]), pub=False)
    C1 = 6.28125
    C2 = float(np.float32(TWO_PI - C1))
    last = None
    for ch in range(NBLK // CB):
        b0 = ch * CB
        for which in range(2):
            for i in range(R2):
                V.op(lambda e, i=i, b0=b0: e.tensor_scalar(out=u[:, :, i], in0=posf[:, b0:b0 + CB], scalar1=float(invf[i]), scalar2=None, op0=ALU.mult), pub=False)
            if which == 0:
                V.op(lambda e: e.tensor_scalar(out=u[:], in0=u[:], scalar1=float(math.pi / 2), scalar2=None, op0=ALU.add), pub=False)
            V.op(lambda e: e.tensor_scalar(out=tt[:], in0=u[:], scalar1=float(1.0 / TWO_PI), scalar2=None, op0=ALU.mult), pub=False)
            V.op(lambda e: e.tensor_copy(out=ki[:], in_=tt[:]), pub=False)
            V.op(lambda e: e.tensor_copy(out=kf[:], in_=ki[:]), pub=False)
            V.op(lambda e: e.scalar_tensor_tensor(out=r[:], in0=kf[:], scalar=-C1, in1=u[:], op0=ALU.mult, op1=ALU.add), pub=False)
            V.op(lambda e: e.scalar_tensor_tensor(out=r[:], in0=kf[:], scalar=-C2, in1=r[:], op0=ALU.mult, op1=ALU.add), pub=False)
            V.op(lambda e: e.tensor_scalar(out=tt[:], in0=r[:], scalar1=float(math.pi), scalar2=-TWO_PI, op0=ALU.is_gt, op1=ALU.mult), pub=False)
            V.op(lambda e: e.tensor_tensor(out=r[:], in0=r[:], in1=tt[:], op=ALU.add), pub=False)
            V.op(lambda e: e.tensor_scalar(out=tt[:], in0=r[:], scalar1=float(-math.pi), scalar2=TWO_PI, op0=ALU.is_lt, op1=ALU.mult), pub=False)
            V.op(lambda e: e.tensor_tensor(out=r[:], in0=r[:], in1=tt[:], op=ALU.add), pub=False)
            tr = V.op(lambda e: e.tensor_scalar(out=r[:], in0=r[:], scalar1=float(-math.pi), scalar2=float(math.pi), op0=ALU.max, op1=ALU.min))
            P.ACT.wait(tr)
            ta = P.ACT.op(lambda e, b0=b0, which=which: e.activation(out=cs[:, b0:b0 + CB, which * R2:(which + 1) * R2], in_=r[:], func=AF.Sin))
            V.wait(ta)
            last = ta
    return last


def emit_rope(P, src, dst, cs, blk, tmp):
    V = P.DVE
    h = A_ROPE // 2
    cos = cs[:, blk, 0:h]
    sin = cs[:, blk, h:2 * h]
    V.op(lambda e: e.tensor_tensor(out=tmp[:, 0:h], in0=src[:, 0:h], in1=cos, op=ALU.mult), pub=False)
    V.op(lambda e: e.tensor_tensor(out=tmp[:, h:2 * h], in0=src[:, h:2 * h], in1=sin, op=ALU.mult), pub=False)
    V.op(lambda e: e.tensor_tensor(out=tmp[:, 2 * h:3 * h], in0=src[:, 0:h], in1=sin, op=ALU.mult), pub=False)
    V.op(lambda e: e.tensor_tensor(out=tmp[:, 3 * h:4 * h], in0=src[:, h:2 * h], in1=cos, op=ALU.mult), pub=False)
    V.op(lambda e: e.tensor_tensor(out=dst[:, 0:h], in0=tmp[:, 0:h], in1=tmp[:, h:2 * h], op=ALU.subtract), pub=False)
    return V.op(lambda e: e.tensor_tensor(out=dst[:, h:2 * h], in0=tmp[:, 2 * h:3 * h], in1=tmp[:, 3 * h:4 * h], op=ALU.add))


def emit_absmax_bcast(P, src_dram, n, out, dsem, tmp):
    t = dsem.issue(P.SP, lambda e: e.dma_start(out=tmp[:, 0:n], in_=src_dram.partition_broadcast(128)))
    P.DVE.wait(t)
    return P.DVE.op(lambda e: e.tensor_reduce(out=out, in_=tmp[:, 0:n], axis=AX.X, op=ALU.max, apply_absolute_value=True))


class AttnState:
    pass


class _Stop(Exception):
    pass


DBG_STEP = [0]
DBG_SKIP = [0]
DBG_VAR = [0]


def _chk(n):
    if DBG_STEP[0] == n:
        if DBG_SKIP[0] > 0:
            DBG_SKIP[0] -= 1
            return
        raise _Stop()


def emit_attention_group(P, st, g, qk_parts, vb, dvp, nacc_banks, acc_bank0, s_banks, exp_scale, bias_fn,
                         alibi=None, mask_pool=True):
    PE, ACT, DVE, POOL = P.PE, P.ACT, P.DVE, P.POOL
    per_bank = 4 // nacc_banks
    nk = 4 * g + 4

    def acc(qb):
        bk = acc_bank0 + qb // per_bank
        o = (qb % per_bank) * dvp
        return P.bank(bk)[:, o:o + dvp]

    first_in_bank = [True] * nacc_banks
    PE.wait(st.q_ready, st.acc_free)
    pend = []
    t_last = None

    def emit_pv(j, slot, r, t_p):
        nonlocal t_last
        PE.wait(t_p)
        for qb in range(r, 4):
            bk = qb // per_bank
            stt = first_in_bank[bk]
            first_in_bank[bk] = False
            last = (qb == 3)
            tk = PE.op(lambda e, qb=qb, j=j, slot=slot, stt=stt: e.matmul(
                acc(qb), lhsT=st.pT[slot][:, qb * 128:(qb + 1) * 128], rhs=vb[:, j, 0:dvp],
                start=stt, stop=(j == nk - 1), skip_group_check=True), pub=last)
        st.p_free[slot] = tk
        t_last = tk

    for j in range(nk):
        r = max(0, j - 4 * g)
        c0 = r * 128
        slot = st.it % 2
        st.it += 1
        PE.wait(st.s_free[slot])
        for pi, (ktf, qt) in enumerate(qk_parts):
            ts = PE.op(lambda e, ktf=ktf, qt=qt, j=j, slot=slot, c0=c0, pi=pi: e.matmul(
                P.bank(s_banks[slot])[:, c0:512], lhsT=ktf(j), rhs=qt[:, c0:512],
                start=(pi == 0), stop=(pi == len(qk_parts) - 1)), pub=(pi == len(qk_parts) - 1))
        if pend:
            emit_pv(*pend.pop(0))
        src = P.bank(s_banks[slot])
        if alibi is not None:
            sbt = alibi["sbuf"][slot]
            DVE.wait(ts, st.sb_free[slot])
            if r < 4 and j >= 4 * g:
                DVE.op(lambda e, c0=c0, src=src, sbt=sbt: e.tensor_tensor(out=sbt[:, c0:c0 + 128], in0=src[:, c0:c0 + 128], in1=alibi["Tdiag"][:], op=ALU.add), indep=True)
                if c0 + 128 < 512:
                    td = DVE.op(lambda e, c0=c0, src=src, sbt=sbt: e.tensor_tensor(out=sbt[:, c0 + 128:512], in0=src[:, c0 + 128:512], in1=alibi["T2"][:, c0 + 128:512], op=ALU.add), indep=True)
                else:
                    td = (DVE.sem, DVE.count)
            else:
                td = DVE.op(lambda e, src=src, sbt=sbt: e.tensor_tensor(out=sbt[:], in0=src[:], in1=alibi["T2"][:], op=ALU.add), indep=True)
            st.s_free[slot] = td
            ACT.wait(td, st.p_free[slot])
            if j >= 4 * g:
                ACT.op(lambda e, c0=c0, sbt=sbt, slot=slot: e.activation(out=st.pT[slot][:, c0:c0 + 128], in_=sbt[:, c0:c0 + 128], func=AF.Exp, bias=alibi["negM"][:], scale=exp_scale), indep=True)
                if c0 + 128 < 512:
                    tp = ACT.op(lambda e, c0=c0, sbt=sbt, slot=slot, j=j: e.activation(out=st.pT[slot][:, c0 + 128:512], in_=sbt[:, c0 + 128:512], func=AF.Exp, bias=bias_fn(j), scale=exp_scale), indep=True)
                else:
                    tp = (ACT.sem, ACT.count)
            else:
                tp = ACT.op(lambda e, sbt=sbt, slot=slot, j=j: e.activation(out=st.pT[slot][:], in_=sbt[:], func=AF.Exp, bias=bias_fn(j), scale=exp_scale), indep=True)
            st.sb_free[slot] = tp
        else:
            ACT.wait(ts, st.p_free[slot])
            tp = ACT.op(lambda e, c0=c0, src=src, slot=slot, j=j: e.activation(out=st.pT[slot][:, c0:512], in_=src[:, c0:512], func=AF.Exp, bias=bias_fn(j), scale=exp_scale), indep=True)
            st.s_free[slot] = tp
            if j >= 4 * g:
                POOL.wait(tp)
                tp = POOL.op(lambda e, c0=c0, slot=slot: e.memset(st.pT[slot][64:128, c0:c0 + 64], 0.0), indep=True)
        pend.append((j, slot, r, tp))
    while pend:
        emit_pv(*pend.pop(0))
    return t_last


def build_A2(NG=S // 512, NH=2, dbg=0):
    P = Prog()
    latT = P.dram("latT", [2 * A_LORA, S], BF16, "ExternalInput")
    kpe = P.dram("kpe", [S, A_ROPE], F32, "ExternalInput")
    pos = P.dram("pos", [128, NBLK], I32, "ExternalInput")
    wq = P.dram("wq", [A_LORA, 2 * A_QK], F32, "ExternalInput")
    wkv = P.dram("wkv", [A_LORA, 2 * (A_NOPE + A_V)], F32, "ExternalInput")
    gq = P.dram("gq", [128, 4], F32, "ExternalInput")
    gkv = P.dram("gkv", [128, 4], F32, "ExternalInput")
    qgain = P.dram("qgain", [1, A_QK], F32, "ExternalInput")
    kgain = P.dram("kgain", [1, A_QK], F32, "ExternalInput")
    o = P.dram("o", [S, 2 * A_V], F32, "ExternalOutput")
    PE, ACT, DVE, POOL, SP = P.PE, P.ACT, P.DVE, P.POOL, P.SP
    P.make_ident()
    eps_t, eps_tok = P.const_tile(EPS, "eps")
    mhalf, mhalf_tok = P.const_tile(-0.5, "mhalf")
    ds = P.dsem()
    gq_t = P.sb("gq_t", [128, 4], F32)
    gkv_t = P.sb("gkv_t", [128, 4], F32)
    SP_tok = ds.issue(SP, lambda e: e.dma_start(out=gq_t[:], in_=gq))
    SP_tok = ds.issue(SP, lambda e: e.dma_start(out=gkv_t[:], in_=gkv))
    qg_t = P.sb("qg_t", [128, A_QK], F32)
    kg_t = P.sb("kg_t", [128, A_QK], F32)
    ds.issue(SP, lambda e: e.dma_start(out=qg_t[:], in_=qgain.partition_broadcast(128)))
    t_small = ds.issue(SP, lambda e: e.dma_start(out=kg_t[:], in_=kgain.partition_broadcast(128)))
    pos_i = P.sb("pos_i", [128, NBLK], I32)
    t_pos = ds.issue(SP, lambda e: e.dma_start(out=pos_i[:], in_=pos))
    t_small = t_pos
    wq_b = P.sb("wq_b", [128, 4, 2 * A_QK], BF16)
    wkv_b = P.sb("wkv_b", [128, 4, 512], BF16)
    stage, ssems, sstate = _stage(P, 512, n=2)
    t_wq = P.load_weight_bf16(wq_b, wq, 4, 2 * A_QK, gq_t, t_pos, stage, ssems, sstate)
    t_wkv = P.load_weight_bf16(wkv_b, wkv, 4, 512, gkv_t, t_pos, stage, ssems, sstate)
    mq = P.sb("mq", [128, 2], F32)
    negM = P.sb("negM", [128, 1], F32)
    DVE.wait(t_small)
    DVE.op(lambda e: e.tensor_reduce(out=mq[:, 0:1], in_=qg_t[:], axis=AX.X, op=ALU.max, apply_absolute_value=True), pub=False)
    t_mk = DVE.op(lambda e: e.tensor_reduce(out=mq[:, 1:2], in_=kg_t[:], axis=AX.X, op=ALU.max, apply_absolute_value=True))
    DVE.wait(t_mk)
    t_negM = DVE.op(lambda e: e.scalar_tensor_tensor(out=negM[:], in0=mq[:, 0:1], scalar=-math.sqrt(A_QK), in1=mq[:, 1:2], op0=ALU.mult, op1=ALU.mult))
    cs = P.sb("cs", [128, NBLK, A_ROPE], F32)
    t_cs = emit_rope_tables(P, pos_i, t_pos, cs)

    KTn = P.sb("KTn", [128, S], BF16)
    KTr = P.sb("KTr", [128, S], BF16)
    dvp = A_V + 16
    vb = P.sb("vb", [128, NBLK, dvp], BF16)
    t_ones = POOL.op(lambda e: e.memset(vb[:, :, A_V:dvp], 1.0))
    latg = [P.sb("latg%d" % i, [128, 4, 512], BF16) for i in range(2)]
    latsem = [P.dsem() for _ in range(2)]
    latfree = [None, None]
    kpg = [P.sb("kpg%d" % i, [128, 4, A_ROPE], F32) for i in range(2)]
    kpsem = [P.dsem() for _ in range(2)]
    kpfree = [None, None]
    full = P.sb("full", [128, A_QK], F32)
    fn = P.sb("fn", [128, A_QK], F32)
    junk = P.sb("junk", [128, A_QK], F32)
    rtmp = P.sb("rtmp", [128, 128], F32)
    nb = P.sb("nb", [128, A_QK], BF16)
    ss = P.sb("ss", [128, 2], F32)
    QTn = [P.sb("QTn%d" % i, [128, 512], BF16) for i in range(2)]
    QTr = [P.sb("QTr%d" % i, [128, 512], BF16) for i in range(2)]
    st = AttnState()
    st.pT = [P.sb("pT%d" % i, [128, 512], BF16) for i in range(2)]
    st.p_free = [None, None]
    st.s_free = [None, None]
    st.it = 0
    st.acc_free = None
    st.q_ready = None
    osb = [P.sb("osb%d" % i, [128, 4, A_V], F32) for i in range(2)]
    osem = [P.dsem() for _ in range(2)]
    rec = P.sb("rec", [128, 4], F32)
    BK_PROJ, BK_TR = 4, 5
    full_free = None
    nb_free = None
    proj_free = None
    tr_free = None
    lat_it = 0

    def proj_block(latt, tb, wcols, ncol, kpe_ap, gain_t, blk, is_k, dstT_n, dstT_r, tcol, v_dst):
        nonlocal full_free, nb_free, proj_free, tr_free
        PE.wait(proj_free)
        for kc in range(4):
            tp = PE.op(lambda e, kc=kc: e.matmul(P.bank(BK_PROJ)[:, 0:ncol], lhsT=latt[:, kc, tb * 128:(tb + 1) * 128],
                                                 rhs=wcols(kc), start=(kc == 0), stop=(kc == 3)), pub=(kc == 3))
        _chk(1)
        DVE.wait(tp, full_free)
        if is_k:
            DVE.op(lambda e: e.tensor_copy(out=full[:, 0:A_NOPE], in_=P.bank(BK_PROJ)[:, 0:A_NOPE]), pub=False)
            if not (DBG_VAR[0] == 2 and blk >= 1):
                DVE.op(lambda e: e.tensor_copy(out=full[:, A_NOPE:A_QK], in_=kpe_ap), pub=False)
            tv = DVE.op(lambda e: e.tensor_copy(out=v_dst, in_=P.bank(BK_PROJ)[:, A_NOPE:A_NOPE + A_V]))
        else:
            DVE.op(lambda e: e.tensor_copy(out=full[:], in_=P.bank(BK_PROJ)[:, 0:A_QK]), pub=False)
            tv = None
        _chk(2)
        DVE.op(lambda e: e.tensor_tensor(out=junk[:], in0=full[:], in1=full[:], op=ALU.mult))
        t_ss = DVE.op(lambda e: e.tensor_reduce(out=ss[:, 0:1], in_=junk[:], axis=AX.X, op=ALU.add))
        _chk(3)
        proj_free = [t_ss, tv]
        ACT.wait(t_ss, eps_tok)
        ACT.op(lambda e: e.activation(out=ss[:, 1:2], in_=ss[:, 0:1], func=AF.Ln, bias=eps_t[:], scale=1.0 / A_QK), pub=False)
        if DBG_STEP[0] == 44:
            raise _Stop()
        t_rs = ACT.op(lambda e: e.activation(out=ss[:, 1:2], in_=ss[:, 1:2], func=AF.Exp, scale=-0.5))
        _chk(4)
        DVE.wait(t_rs, nb_free)
        DVE.op(lambda e: e.scalar_tensor_tensor(out=fn[:], in0=full[:], scalar=ss[:, 1:2], in1=gain_t[:], op0=ALU.mult, op1=ALU.mult), pub=False)
        DVE.op(lambda e: e.tensor_copy(out=nb[:, 0:A_NOPE], in_=fn[:, 0:A_NOPE]), pub=False)
        _chk(5)
        t_nb = emit_rope(P, fn[:, A_NOPE:A_QK], nb[:, A_NOPE:A_QK], cs, blk, rtmp)
        _chk(6)
        full_free = t_nb
        PE.wait(t_nb, tr_free)
        PE.op(lambda e: e.transpose(out=P.bank_bf(BK_TR)[:, 0:128], in_=nb[:, 0:A_NOPE], identity=P.ident[:]), pub=False)
        t_tr = PE.op(lambda e: e.transpose(out=P.bank_bf(BK_TR)[0:A_ROPE, 128:256], in_=nb[:, A_NOPE:A_QK], identity=P.ident[:]))
        nb_free = t_tr
        _chk(7)
        DVE.wait(t_tr)
        DVE.op(lambda e: e.tensor_copy(out=dstT_n[:, tcol:tcol + 128], in_=P.bank_bf(BK_TR)[:, 0:128]), pub=False)
        if DBG_STEP[0] == 88:
            raise _Stop()
        t_e = DVE.op(lambda e: e.tensor_copy(out=dstT_r[0:A_ROPE, tcol:tcol + 128], in_=P.bank_bf(BK_TR)[0:A_ROPE, 128:256]))
        tr_free = t_e
        _chk(8)
        return t_e

    DVE.wait(t_cs, t_negM)
    PE.wait(t_wq, t_wkv, P.ident_tok)
    POOL.wait(t_ones)
    if dbg == 1:
        SP.wait((DVE.sem, DVE.count), (ACT.sem, ACT.count), (POOL.sem, POOL.count))
        return P.finish()
    out_tok = [None, None]
    oi = 0
    attn_done_prev_head = None
    try:
      for hh in range(NH):
          t_kv_last = None
          for g in range(NG):
              s = lat_it % 2
              lat_it += 1
              SP.wait(latfree[s], kpfree[s])
              t_l = latsem[s].issue(SP, lambda e, s=s, g=g: e.dma_start(
                  out=latg[s][:], in_=latT[A_LORA:2 * A_LORA, g * 512:(g + 1) * 512].rearrange("(kc p) t -> p kc t", p=128)))
              t_k = kpsem[s].issue(SP, lambda e, s=s, g=g: e.dma_start(
                  out=kpg[s][:], in_=kpe[g * 512:(g + 1) * 512, :].rearrange("(tb p) d -> p tb d", p=128)))
              PE.wait(t_l)
              DVE.wait(t_k)
              if g == 0:
                  DVE.wait(attn_done_prev_head)
                  ACT.wait(attn_done_prev_head)
              for tb in range(4):
                  blk = g * 4 + tb
                  t_kv_last = proj_block(latg[s], tb, lambda kc, hh=hh: wkv_b[:, kc, hh * 256:(hh + 1) * 256], 256,
                                         kpg[s][:, tb, :], kg_t, blk, True, KTn, KTr, blk * 128, vb[:, blk, 0:A_V])
              latfree[s] = (PE.sem, PE.count)
              kpfree[s] = (DVE.sem, DVE.count)
          kv_ready = [t_kv_last, (ACT.sem, ACT.count)]
          if dbg == 2:
              SP.wait(kv_ready)
              return P.finish()
          q_tok = {}

          def q_proj(g):
              nonlocal lat_it
              s = lat_it % 2
              lat_it += 1
              qs = g % 2
              SP.wait(latfree[s])
              t_l = latsem[s].issue(SP, lambda e, s=s, g=g: e.dma_start(
                  out=latg[s][:], in_=latT[0:A_LORA, g * 512:(g + 1) * 512].rearrange("(kc p) t -> p kc t", p=128)))
              PE.wait(t_l)
              DVE.wait(q_tok.get(("free", qs)))
              tq = None
              for tb in range(4):
                  blk = g * 4 + tb
                  tq = proj_block(latg[s], tb, lambda kc, hh=hh: wq_b[:, kc, hh * A_QK:(hh + 1) * A_QK], A_QK,
                                  None, qg_t, blk, False, QTn[qs], QTr[qs], tb * 128, None)
              latfree[s] = (PE.sem, PE.count)
              q_tok[g] = tq

          q_proj(0)
          for g in range(NG):
              qs = g % 2
              st.q_ready = [q_tok[g], kv_ready]
              parts = [(lambda j: KTn[:, j * 128:(j + 1) * 128], QTn[qs]),
                       (lambda j: KTr[0:A_ROPE, j * 128:(j + 1) * 128], QTr[qs][0:A_ROPE, :])]
              if g + 1 < NG:
                  q_proj(g + 1)
              if dbg == 3:
                  SP.wait(q_tok[g])
                  continue
              t_acc = emit_attention_group(P, st, g, parts, vb, dvp, 2, 2, [0, 1], 1.0 / math.sqrt(A_QK),
                                           lambda j: negM[:])
              q_tok[("free", qs)] = t_acc
              ob = osb[oi % 2]
              DVE.wait(t_acc, out_tok[oi % 2])
              for qb in range(4):
                  bk = 2 + qb // 2
                  off = (qb % 2) * dvp
                  DVE.op(lambda e, qb=qb, bk=bk, off=off: e.reciprocal(out=rec[:, qb:qb + 1], in_=P.bank(bk)[:, off + A_V:off + A_V + 1]), pub=False)
              t_rc = (DVE.sem, DVE.count)
              for qb in range(4):
                  bk = 2 + qb // 2
                  off = (qb % 2) * dvp
                  t_on = DVE.op(lambda e, qb=qb, bk=bk, off=off, ob=ob: e.tensor_scalar(out=ob[:, qb, :], in0=P.bank(bk)[:, off:off + A_V], scalar1=rec[:, qb:qb + 1], scalar2=None, op0=ALU.mult), pub=(qb == 3))
              st.acc_free = t_on
              SP.wait(t_on)
              out_tok[oi % 2] = osem[oi % 2].issue(SP, lambda e, g=g, hh=hh, ob=ob: e.dma_start(
                  out=o[g * 512:(g + 1) * 512, hh * A_V:(hh + 1) * A_V].rearrange("(qb p) d -> p qb d", p=128), in_=ob[:]))
              oi += 1
              attn_done_prev_head = t_acc
    except _Stop:
        SP.wait((DVE.sem, DVE.count), (ACT.sem, ACT.count), (POOL.sem, POOL.count), (PE.sem, PE.count))
    P.out_toks += [t for t in out_tok if t is not None]
    return P.finish()


def build_B2(NG=S // 512, dbg=0):
    P = Prog()
    xnT = P.dram("xnT", [D, S], BF16, "ExternalInput")
    pos = P.dram("pos", [128, NBLK], I32, "ExternalInput")
    posrow = P.dram("posrow", [1, 512], I32, "ExternalInput")
    posg = P.dram("posg", [1, S // 512], I32, "ExternalInput")
    wq = P.dram("wq", [D, 2 * B_HD], F32, "ExternalInput")
    wkv = P.dram("wkv", [D, 2 * B_HD + B_V], F32, "ExternalInput")
    gn = P.dram("g_norm", [128, D // 128], F32, "ExternalInput")
    qgain = P.dram("qgain", [1, B_HD], F32, "ExternalInput")
    kgain = P.dram("kgain", [1, B_HD], F32, "ExternalInput")
    lam4 = P.dram("lam4", [4, B_HD], F32, "ExternalInput")
    subln = P.dram("subln", [1, B_V], F32, "ExternalInput")
    slope = P.dram("slope", [1, 1], F32, "ExternalInput")
    o = P.dram("o", [S, B_V], F32, "ExternalOutput")
    PE, ACT, DVE, POOL, SP = P.PE, P.ACT, P.DVE, P.POOL, P.SP
    KC = D // 128
    LAM_INIT = 0.8 - 0.6 * math.exp(-0.3 * 1)
    SQ = math.sqrt(B_HD)
    P.make_ident()
    eps_t, eps_tok = P.const_tile(EPS, "eps")
    ds = P.dsem()
    gcol = P.sb("gcol", [128, KC], F32)
    qg_t = P.sb("qg_t", [128, B_HD], F32)
    kg_t = P.sb("kg_t", [128, B_HD], F32)
    lam_t = P.sb("lam_t", [128, 4, B_HD], F32)
    sub_t = P.sb("sub_t", [128, B_V], F32)
    slope_t = P.sb("slope_t", [128, 1], F32)
    pos_i = P.sb("pos_i", [128, NBLK], I32)
    posg_i = P.sb("posg_i", [128, S // 512], I32)
    ds.issue(SP, lambda e: e.dma_start(out=gcol[:], in_=gn))
    ds.issue(SP, lambda e: e.dma_start(out=qg_t[:], in_=qgain.partition_broadcast(128)))
    ds.issue(SP, lambda e: e.dma_start(out=kg_t[:], in_=kgain.partition_broadcast(128)))
    for i in range(4):
        ds.issue(SP, lambda e, i=i: e.dma_start(out=lam_t[:, i, :], in_=lam4[i:i + 1, :].partition_broadcast(128)))
    ds.issue(SP, lambda e: e.dma_start(out=sub_t[:], in_=subln.partition_broadcast(128)))
    ds.issue(SP, lambda e: e.dma_start(out=slope_t[:], in_=slope.partition_broadcast(128)))
    ds.issue(SP, lambda e: e.dma_start(out=posg_i[:], in_=posg.partition_broadcast(128)))
    t_set = ds.issue(SP, lambda e: e.dma_start(out=pos_i[:], in_=pos))
    sbufs = [P.sb("sbias%d" % i, [128, 512], F32) for i in range(2)]
    prow_i = sbufs[0][:].bitcast(I32)
    prow_f = sbufs[1]
    ds2 = P.dsem()
    t_prow = ds2.issue(SP, lambda e: e.dma_start(out=prow_i, in_=posrow.partition_broadcast(128)))
    small = P.sb("small", [128, 16], F32)
    posf = P.sb("posf", [128, NBLK], F32)
    posgf = P.sb("posgf", [128, S // 512], F32)
    T2 = P.sb("T2", [128, 512], F32)
    Tdiag = P.sb("Tdiag", [128, 128], F32)
    tmpd = P.sb("tmpd", [128, 128], F32)
    negM = small[:, 0:1]
    nslope_s = small[:, 1:2]
    neglam = small[:, 2:3]
    DVE.wait(t_set, t_prow)
    DVE.op(lambda e: e.tensor_reduce(out=small[:, 3:4], in_=qg_t[:], axis=AX.X, op=ALU.max, apply_absolute_value=True))
    DVE.op(lambda e: e.tensor_reduce(out=small[:, 4:5], in_=kg_t[:], axis=AX.X, op=ALU.max, apply_absolute_value=True))
    DVE.op(lambda e: e.scalar_tensor_tensor(out=negM, in0=small[:, 3:4], scalar=-SQ, in1=small[:, 4:5], op0=ALU.mult, op1=ALU.mult))
    DVE.op(lambda e: e.tensor_scalar(out=nslope_s, in0=slope_t[:], scalar1=-SQ, scalar2=None, op0=ALU.mult))
    DVE.op(lambda e: e.tensor_tensor(out=tmpd[:], in0=lam_t[:, 0, :], in1=lam_t[:, 1, :], op=ALU.mult))
    DVE.op(lambda e: e.tensor_reduce(out=small[:, 5:6], in_=tmpd[:], axis=AX.X, op=ALU.add))
    DVE.op(lambda e: e.tensor_tensor(out=tmpd[:], in0=lam_t[:, 2, :], in1=lam_t[:, 3, :], op=ALU.mult))
    t_l = DVE.op(lambda e: e.tensor_reduce(out=small[:, 6:7], in_=tmpd[:], axis=AX.X, op=ALU.add))
    ACT.wait(t_l)
    t_e = ACT.op(lambda e: e.activation(out=small[:, 7:9], in_=small[:, 5:7], func=AF.Exp))
    DVE.wait(t_e)
    DVE.op(lambda e: e.tensor_tensor(out=neglam, in0=small[:, 8:9], in1=small[:, 7:8], op=ALU.subtract))
    DVE.op(lambda e: e.tensor_scalar(out=neglam, in0=neglam, scalar1=-LAM_INIT, scalar2=None, op0=ALU.add))
    DVE.op(lambda e: e.tensor_scalar(out=sub_t[:], in0=sub_t[:], scalar1=1.0 - LAM_INIT, scalar2=None, op0=ALU.mult))
    DVE.op(lambda e: e.tensor_copy(out=posf[:], in_=pos_i[:]))
    DVE.op(lambda e: e.tensor_copy(out=posgf[:], in_=posg_i[:]))
    DVE.op(lambda e: e.tensor_copy(out=prow_f[:], in_=prow_i))
    DVE.op(lambda e: e.tensor_scalar(out=T2[:], in0=prow_f[:], scalar1=prow_f[:, 0:1], scalar2=nslope_s, op0=ALU.subtract, op1=ALU.mult))
    DVE.op(lambda e: e.tensor_scalar(out=Tdiag[:], in0=prow_f[:, 0:128], scalar1=posf[:, 0:1], scalar2=None, op0=ALU.subtract))
    DVE.op(lambda e: e.tensor_scalar(out=tmpd[:], in0=Tdiag[:], scalar1=-1.0, scalar2=None, op0=ALU.mult))
    DVE.op(lambda e: e.tensor_tensor(out=Tdiag[:], in0=Tdiag[:], in1=tmpd[:], op=ALU.max))
    DVE.op(lambda e: e.tensor_scalar(out=Tdiag[:], in0=Tdiag[:], scalar1=nslope_s, scalar2=None, op0=ALU.mult))
    t_setup = DVE.op(lambda e: e.tensor_scalar(out=Tdiag[64:128, 0:64], in0=Tdiag[64:128, 0:64], scalar1=NEG_BIG, scalar2=None, op0=ALU.add))
    ACT.wait(t_setup)

    w_b = P.sb("w_b", [128, KC, 512], BF16)
    stage, ssems, sstate = _stage(P, 512, n=2)
    t_w = P.load_weight_bf16(w_b, wkv, KC, 512, gcol, t_set, stage, ssems, sstate)

    KT = [P.sb("KT%d" % c, [128, S], BF16) for c in range(2)]
    dvp = B_V + 2
    vb = P.sb("vb", [128, NBLK, dvp], BF16)
    POOL.op(lambda e: e.memset(vb[:, :, B_V:dvp], 1.0))
    GT = 256
    xg = [P.sb("xg%d" % i, [128, KC, GT], BF16) for i in range(2)]
    xsem = [P.dsem() for _ in range(2)]
    xfree = [None, None]
    ff = P.sb("ff", [128, 2 * B_HD], F32)
    junk = P.sb("junk", [128, 2 * B_HD], F32)
    nb = P.sb("nb", [128, 2 * B_HD], BF16)
    ss = P.sb("ss", [128, 8], F32)
    QT = [[P.sb("QT%d_%d" % (c, i), [128, 512], BF16) for i in range(2)] for c in range(2)]
    st = AttnState()
    st.pT = [P.sb("pT%d" % i, [128, 512], BF16) for i in range(2)]
    st.p_free = [None, None]
    st.s_free = [None, None]
    st.sb_free = [t_setup, t_setup]
    st.it = 0
    st.acc_free = None
    st.q_ready = None
    kbias = [P.sb("kbias%d" % i, [128, NBLK], F32) for i in range(2)]
    kb_free = [None, None]
    oc = [P.sb("oc%d" % c, [128, 4, B_V], F32) for c in range(2)]
    osem = P.dsem()
    rec = P.sb("rec", [128, 4], F32)
    BK_PROJ, BK_TR = 6, 7
    tok = {"proj_free": None, "ff_free": None, "nb_free": None, "tr_free": None}
    x_it = [0]

    def proj_block(xt, tb, ncol, gain_t, is_k, blk, dstT, tcol):
        PE.wait(tok["proj_free"])
        for kc in range(KC):
            tp = PE.op(lambda e, kc=kc: e.matmul(P.bank(BK_PROJ)[:, 0:ncol], lhsT=xt[:, kc, tb * 128:(tb + 1) * 128],
                                                 rhs=w_b[:, kc, 0:ncol], start=(kc == 0), stop=(kc == KC - 1)))
        DVE.wait(tp, tok["ff_free"])
        DVE.op(lambda e: e.tensor_copy(out=ff[:], in_=P.bank(BK_PROJ)[:, 0:2 * B_HD]))
        if is_k:
            DVE.op(lambda e: e.tensor_copy(out=vb[:, blk, 0:B_V], in_=P.bank(BK_PROJ)[:, 2 * B_HD:2 * B_HD + B_V]))
        DVE.op(lambda e: e.tensor_tensor(out=junk[:], in0=ff[:], in1=ff[:], op=ALU.mult))
        t_ss = DVE.op(lambda e: e.tensor_reduce(out=ss[:, 0:2], in_=junk[:].rearrange("p (c d) -> p c d", c=2), axis=AX.X, op=ALU.add))
        tok["proj_free"] = t_ss
        ACT.wait(t_ss, eps_tok)
        ACT.op(lambda e: e.activation(out=ss[:, 2:4], in_=ss[:, 0:2], func=AF.Ln, bias=eps_t[:], scale=1.0 / B_HD))
        t_rs = ACT.op(lambda e: e.activation(out=ss[:, 4:6], in_=ss[:, 2:4], func=AF.Exp, scale=-0.5))
        DVE.wait(t_rs, tok["nb_free"])
        for c in range(2):
            t_nb = DVE.op(lambda e, c=c: e.scalar_tensor_tensor(out=nb[:, c * B_HD:(c + 1) * B_HD], in0=ff[:, c * B_HD:(c + 1) * B_HD],
                                                               scalar=ss[:, 4 + c:5 + c], in1=gain_t[:], op0=ALU.mult, op1=ALU.mult))
        tok["ff_free"] = t_nb
        PE.wait(t_nb, tok["tr_free"])
        for c in range(2):
            t_tr = PE.op(lambda e, c=c: e.transpose(out=P.bank_bf(BK_TR)[:, c * 128:(c + 1) * 128], in_=nb[:, c * B_HD:(c + 1) * B_HD], identity=P.ident[:]))
        tok["nb_free"] = t_tr
        DVE.wait(t_tr)
        for c in range(2):
            t_e = DVE.op(lambda e, c=c: e.tensor_copy(out=dstT[c][:, tcol:tcol + 128], in_=P.bank_bf(BK_TR)[:, c * 128:(c + 1) * 128]))
        tok["tr_free"] = t_e
        return t_e

    def load_x(t0):
        s = x_it[0] % 2
        x_it[0] += 1
        SP.wait(xfree[s])
        t = xsem[s].issue(SP, lambda e, s=s, t0=t0: e.dma_start(
            out=xg[s][:], in_=xnT.rearrange("(kc p) t -> p kc t", p=128)[:, :, t0:t0 + GT]))
        return s, t

    PE.wait(t_w, P.ident_tok)
    t_kv = None
    for gg in range(NG * 512 // GT):
        s, t = load_x(gg * GT)
        PE.wait(t)
        for tb in range(GT // 128):
            blk = gg * (GT // 128) + tb
            t_kv = proj_block(xg[s], tb, 512, kg_t, True, blk, KT, blk * 128)
        xfree[s] = (PE.sem, PE.count)
    kv_ready = t_kv
    ACT.wait((PE.sem, PE.count))
    t_wq = P.load_weight_bf16(w_b, wq, KC, 2 * B_HD, gcol, t_set, stage, ssems, sstate)
    PE.wait(t_wq)
    q_tok = {}

    def q_proj(g):
        qs = g % 2
        DVE.wait(q_tok.get(("free", qs)))
        tq = None
        for hh in range(512 // GT):
            s, t = load_x(g * 512 + hh * GT)
            PE.wait(t)
            for tb in range(GT // 128):
                tq = proj_block(xg[s], tb, 2 * B_HD, qg_t, False, None, [QT[0][qs], QT[1][qs]], (hh * (GT // 128) + tb) * 128)
            xfree[s] = (PE.sem, PE.count)
        q_tok[g] = tq

    out_tok = None
    q_proj(0)
    for g in range(NG):
        qs = g % 2
        if g + 1 < NG:
            q_proj(g + 1)
        kb = kbias[g % 2]
        DVE.wait(kb_free[g % 2])
        DVE.op(lambda e, kb=kb, g=g: e.tensor_scalar(out=kb[:], in0=posf[:], scalar1=posgf[:, g:g + 1], scalar2=slope_t[:], op0=ALU.subtract, op1=ALU.mult))
        t_kb = DVE.op(lambda e, kb=kb: e.tensor_scalar(out=kb[:], in0=kb[:], scalar1=negM, scalar2=None, op0=ALU.add))
        ACT.wait(t_kb)
        for c in range(2):
            st.q_ready = [q_tok[g], kv_ready]
            parts = [(lambda j, c=c: KT[c][:, j * 128:(j + 1) * 128], QT[c][qs])]
            t_acc = emit_attention_group(P, st, g, parts, vb, dvp, 4, 2, [0, 1], 1.0 / SQ,
                                         lambda j, kb=kb: kb[:, j:j + 1],
                                         alibi=dict(T2=T2, Tdiag=Tdiag, negM=negM, sbuf=sbufs))
            DVE.wait(t_acc, out_tok if c == 0 else None)
            for qb in range(4):
                DVE.op(lambda e, qb=qb: e.reciprocal(out=rec[:, qb:qb + 1], in_=P.bank(2 + qb)[:, B_V:B_V + 1]))
            for qb in range(4):
                t_on = DVE.op(lambda e, qb=qb, c=c: e.tensor_scalar(out=oc[c][:, qb, :], in0=P.bank(2 + qb)[:, 0:B_V], scalar1=rec[:, qb:qb + 1], scalar2=None, op0=ALU.mult))
            st.acc_free = t_on
        q_tok[("free", qs)] = t_acc
        kb_free[g % 2] = t_acc
        o0f = oc[0][:].rearrange("p a d -> p (a d)")
        o1f = oc[1][:].rearrange("p a d -> p (a d)")
        DVE.op(lambda e: e.scalar_tensor_tensor(out=o0f, in0=o1f, scalar=neglam, in1=o0f, op0=ALU.mult, op1=ALU.add))
        DVE.op(lambda e: e.tensor_tensor(out=o1f, in0=o0f, in1=o0f, op=ALU.mult))
        t_s4 = DVE.op(lambda e: e.tensor_reduce(out=ss[:, 6:8], in_=oc[1][:, 0:2, :], axis=AX.X, op=ALU.add))
        t_s4 = DVE.op(lambda e: e.tensor_reduce(out=rec[:, 0:2], in_=oc[1][:, 2:4, :], axis=AX.X, op=ALU.add))
        ACT.wait(t_s4, eps_tok)
        ACT.op(lambda e: e.activation(out=ss[:, 6:8], in_=ss[:, 6:8], func=AF.Ln, bias=eps_t[:], scale=1.0 / B_V))
        ACT.op(lambda e: e.activation(out=rec[:, 0:2], in_=rec[:, 0:2], func=AF.Ln, bias=eps_t[:], scale=1.0 / B_V))
        ACT.op(lambda e: e.activation(out=ss[:, 6:8], in_=ss[:, 6:8], func=AF.Exp, scale=-0.5))
        t_r4 = ACT.op(lambda e: e.activation(out=rec[:, 0:2], in_=rec[:, 0:2], func=AF.Exp, scale=-0.5))
        DVE.wait(t_r4)
        for qb in range(4):
            rcol = ss[:, 6 + qb:7 + qb] if qb < 2 else rec[:, qb - 2:qb - 1]
            t_fin = DVE.op(lambda e, qb=qb, rcol=rcol: e.scalar_tensor_tensor(out=oc[1][:, qb, :], in0=oc[0][:, qb, :], scalar=rcol, in1=sub_t[:], op0=ALU.mult, op1=ALU.mult))
        SP.wait(t_fin)
        out_tok = osem.issue(SP, lambda e, g=g: e.dma_start(
            out=o[g * 512:(g + 1) * 512, :].rearrange("(qb p) d -> p qb d", p=128), in_=oc[1][:]))
    P.out_toks.append(out_tok)
    return P.finish()


def build_MIX(with_norm_out):
    P = Prog()
    xres = P.dram("xres", [TPC, D], F32, "ExternalInput")
    oin = P.dram("oin", [TPC, D], F32, "ExternalInput")
    wg = P.dram("wg", [D, D], F32, "ExternalInput")
    wo = P.dram("wo", [D, D], F32, "ExternalInput")
    gn = P.dram("g_norm", [128, D // 128], F32, "ExternalInput")
    xnew = P.dram("xnew", [TPC, D], F32, "ExternalOutput")
    if with_norm_out:
        xnT_out = P.dram("xnT", [D, TPC], BF16, "ExternalOutput")
    PE, ACT, DVE, POOL, SP = P.PE, P.ACT, P.DVE, P.POOL, P.SP
    KC = D // 128
    P.make_ident()
    eps_t, eps_tok = P.const_tile(EPS, "eps")
    gcol = P.sb("gcol", [128, KC], F32)
    ds0 = P.dsem()
    t_g = ds0.issue(SP, lambda e: e.dma_start(out=gcol[:], in_=gn))
    wgb = P.sb("wgb", [128, KC, D], BF16)
    wob = P.sb("wob", [128, KC, D], BF16)
    stage, ssems, sstate = _stage(P, 1024, n=2)
    t_wg = P.load_weight_bf16(wgb, wg, KC, D, gcol, t_g, stage, ssems, sstate)
    t_wo = P.load_weight_bf16(wob, wo, KC, D, None, None, stage, ssems, sstate)
    NB = TPC // 128
    xt = P.sb("xt", [128, D], F32)
    ot = P.sb("ot", [128, D], F32)
    xsem = P.dsem()
    osem_in = P.dsem()
    junk = P.sb("junk", [128, D], BF16)
    xb = P.sb("xb", [128, D], BF16)
    xnT = P.sb("xnT_s", [128, KC, 128], BF16)
    hb = P.sb("hb", [128, D], BF16)
    hT = P.sb("hT", [128, KC, 128], BF16)
    xo = P.sb("xo", [128, D], F32)
    sg = xo
    ss = P.sb("ss", [128, 2], F32)
    rs = P.sb("rs", [128, 2], F32)
    osem = P.dsem()
    osem2 = P.dsem()
    if with_norm_out:
        x2b = P.sb("x2b", [128, D], BF16)
        x2T = P.sb("x2T", [128, KC, 128], BF16)
    xt_free = None
    ot_free = None
    t_out = None
    t_out2 = None
    for b in range(NB):
        SP.wait(xt_free)
        t_x = xsem.issue(SP, lambda e, b=b: e.dma_start(out=xt[:], in_=xres[b * 128:(b + 1) * 128, :]))
        SP.wait(ot_free)
        t_o = osem_in.issue(SP, lambda e, b=b: e.dma_start(out=ot[:], in_=oin[b * 128:(b + 1) * 128, :]))
        ACT.wait(t_x)
        t_ss = ACT.op(lambda e: e.activation(out=junk[:], in_=xt[:], func=AF.Square, accum_out=ss[:, 0:1]))
        t_r = emit_rstd(P, ss[:, 0:1], D, eps_t, eps_tok, rs[:, 0:1], t_ss)
        DVE.wait(t_r)
        t_xb = DVE.op(lambda e: e.tensor_scalar(out=xb[:], in0=xt[:], scalar1=rs[:, 0:1], scalar2=None, op0=ALU.mult))

        def transpose16(src, dst, after):
            PE.wait(after, P.ident_tok)
            for kc in range(KC):
                bk = kc // 8
                oo = (kc % 8) * 128
                tt = PE.op(lambda e, kc=kc, bk=bk, oo=oo: e.transpose(out=P.bank_bf(bk)[:, oo:oo + 128], in_=src[:, kc * 128:(kc + 1) * 128], identity=P.ident[:]),
                           pub=(kc == KC - 1))
            DVE.wait(tt)
            DVE.op(lambda e: e.tensor_copy(out=dst[:, 0:8, :], in_=P.bank_bf(0)[:, 0:1024]), pub=False)
            return DVE.op(lambda e: e.tensor_copy(out=dst[:, 8:16, :], in_=P.bank_bf(1)[:, 0:1024]))

        t_ev = transpose16(xb, xnT, t_xb)
        PE.wait(t_ev, t_wg)
        for gi in range(4):
            for kc in range(KC):
                tz = PE.op(lambda e, gi=gi, kc=kc: e.matmul(P.bank(2 + gi)[:, :], lhsT=xnT[:, kc, :], rhs=wgb[:, kc, gi * 512:(gi + 1) * 512],
                                                           start=(kc == 0), stop=(kc == KC - 1)), pub=(kc == KC - 1))
        ACT.wait(tz, t_out)
        for gi in range(4):
            t_sg = ACT.op(lambda e, gi=gi: e.activation(out=sg[:, gi * 512:(gi + 1) * 512], in_=P.bank(2 + gi)[:, :], func=AF.Silu), pub=(gi == 3))
        DVE.wait(t_sg, t_o)
        t_hb = DVE.op(lambda e: e.tensor_tensor(out=hb[:], in0=sg[:], in1=ot[:], op=ALU.mult))
        ot_free = t_hb
        t_ev2 = transpose16(hb, hT, t_hb)
        PE.wait(t_ev2, t_wo)
        for gi in range(4):
            for kc in range(KC):
                tz2 = PE.op(lambda e, gi=gi, kc=kc: e.matmul(P.bank(2 + gi)[:, :], lhsT=hT[:, kc, :], rhs=wob[:, kc, gi * 512:(gi + 1) * 512],
                                                            start=(kc == 0), stop=(kc == KC - 1)), pub=(kc == KC - 1))
        DVE.wait(tz2, t_out)
        for gi in range(4):
            t_xo = DVE.op(lambda e, gi=gi: e.tensor_tensor(out=xo[:, gi * 512:(gi + 1) * 512], in0=P.bank(2 + gi)[:, :], in1=xt[:, gi * 512:(gi + 1) * 512], op=ALU.add), pub=(gi == 3))
        xt_free = t_xo
        SP.wait(t_xo)
        t_out = osem.issue(SP, lambda e, b=b: e.dma_start(out=xnew[b * 128:(b + 1) * 128, :], in_=xo[:]))
        if with_norm_out:
            ACT.wait(t_xo)
            t_ss2 = ACT.op(lambda e: e.activation(out=junk[:], in_=xo[:], func=AF.Square, accum_out=ss[:, 1:2]))
            t_r2 = emit_rstd(P, ss[:, 1:2], D, eps_t, eps_tok, rs[:, 1:2], t_ss2)
            DVE.wait(t_r2)
            t_x2b = DVE.op(lambda e: e.tensor_scalar(out=x2b[:], in0=xo[:], scalar1=rs[:, 1:2], scalar2=None, op0=ALU.mult))
            DVE.wait(t_out2)
            t_ev3 = transpose16(x2b, x2T, t_x2b)
            SP.wait(t_ev3)
            t_out2 = osem2.issue(SP, lambda e, b=b: e.dma_start(
                out=xnT_out.rearrange("(kc p) t -> p kc t", p=128)[:, :, b * 128:(b + 1) * 128], in_=x2T[:]))
    P.out_toks += [t_out] + ([t_out2] if with_norm_out else [])
    return P.finish()


_CACHE = {}


def _get(name, fn):
    if name not in _CACHE:
        _CACHE[name] = fn()
    return _CACHE[name]


def _col(v):
    v = np.asarray(v, dtype=np.float32)
    return np.ascontiguousarray(v.reshape(-1, 128).T)


def run(nc, in_maps):
    res = run_bass_kernel_spmd(nc, in_maps, core_ids=list(range(NCORES)))
    return res.results


def _posl(pos):
    return np.ascontiguousarray(np.asarray(pos, dtype=np.int32).reshape(NBLK, 128).T)


def stage_A1(x, inp):
    nc = _get("A1", build_A1)
    w_lat = np.ascontiguousarray(inp["a_w_in"][0][:, :2 * A_LORA + A_ROPE])
    g = _col(inp["a_norm"][0])
    res = run(nc, [{"x": np.ascontiguousarray(x[c * TPC:(c + 1) * TPC]), "w_lat": w_lat, "g_norm": g} for c in range(NCORES)])
    latT = np.concatenate([r["latT"] for r in res], axis=1)
    kpe = np.concatenate([r["kpe"] for r in res], axis=0)
    return latT, kpe


def stage_A2(latT, kpe, inp):
    nc = _get("A2", build_A2)
    pos = _posl(inp["positions"][0])
    wq = inp["a_w_q_up"][0]
    wkv = inp["a_w_kv_up"][0]
    ims = []
    for c in range(NCORES):
        ims.append({"latT": latT, "kpe": kpe, "pos": pos,
                    "wq": np.ascontiguousarray(wq[:, 2 * c * A_QK:(2 * c + 2) * A_QK]),
                    "wkv": np.ascontiguousarray(wkv[:, 2 * c * 256:(2 * c + 2) * 256]),
                    "gq": _col(inp["a_q_norm"][0]), "gkv": _col(inp["a_kv_norm"][0]),
                    "qgain": np.ascontiguousarray(inp["a_q_gain"][0][None, :]),
                    "kgain": np.ascontiguousarray(inp["a_k_gain"][0][None, :])})
    res = run(nc, ims)
    return np.concatenate([r["o"] for r in res], axis=1)


def stage_MIX(xres, o, w_gate, w_out, g_norm, with_norm_out):
    nc = _get("MIX%d" % int(with_norm_out), lambda: build_MIX(with_norm_out))
    w_gate = np.ascontiguousarray(w_gate)
    w_out = np.ascontiguousarray(w_out)
    g = _col(g_norm)
    res = run(nc, [{"xres": np.ascontiguousarray(xres[c * TPC:(c + 1) * TPC]),
                    "oin": np.ascontiguousarray(o[c * TPC:(c + 1) * TPC]),
                    "wg": w_gate, "wo": w_out, "g_norm": g} for c in range(NCORES)])
    xnew = np.concatenate([r["xnew"] for r in res], axis=0)
    xnT = np.concatenate([r["xnT"] for r in res], axis=1) if with_norm_out else None
    return xnew, xnT


def stage_B2(xnT, inp):
    nc = _get("B2", build_B2)
    posv = np.asarray(inp["positions"][0], dtype=np.int32)
    pos = _posl(posv)
    w = inp["b_w_in"][0]
    QK = B_H * 2 * B_HD
    ims = []
    for c in range(NCORES):
        wq = np.ascontiguousarray(w[:, c * 256:(c + 1) * 256])
        wkv = np.ascontiguousarray(np.concatenate([w[:, QK + c * 256:QK + (c + 1) * 256],
                                                   w[:, 2 * QK + c * B_V:2 * QK + (c + 1) * B_V]], axis=1))
        ims.append({"xnT": xnT, "pos": pos, "posrow": np.ascontiguousarray(posv[None, 0:512]),
                    "posg": np.ascontiguousarray(posv[None, ::512]), "wq": wq, "wkv": wkv,
                    "g_norm": _col(inp["b_norm"][0]),
                    "qgain": np.ascontiguousarray(inp["b_q_gain"][0][None, :]),
                    "kgain": np.ascontiguousarray(inp["b_k_gain"][0][None, :]),
                    "lam4": np.ascontiguousarray(np.stack([inp["b_lambda_q1"][0], inp["b_lambda_k1"][0],
                                                           inp["b_lambda_q2"][0], inp["b_lambda_k2"][0]])),
                    "subln": np.ascontiguousarray(inp["b_subln"][0][None, :]),
                    "slope": np.full((1, 1), 2.0 ** (-8.0 * (c + 1) / B_H), dtype=np.float32)})
    res = run(nc, ims)
    return np.concatenate([r["o"] for r in res], axis=1)


def kernel(**inputs):
    inp = {k: np.asarray(v) for k, v in inputs.items()}
    x = np.ascontiguousarray(inp["x"][0])
    latT, kpe = stage_A1(x, inp)
    oA = stage_A2(latT, kpe, inp)
    x1, xn1T = stage_MIX(x, oA, inp["a_w_in"][0][:, 2 * A_LORA + A_ROPE:], inp["a_w_out"][0], inp["a_norm"][0], True)
    QK = B_H * 2 * B_HD
    oB = stage_B2(xn1T, inp)
    x2, _ = stage_MIX(x1, oB, inp["b_w_in"][0][:, 2 * QK + B_H * B_V:], inp["b_w_out"][0], inp["b_norm"][0], False)
    return x2[None].astype(np.float32)
```

```python
import math
from contextlib import ExitStack

import numpy as np
import concourse.bass as bass
import concourse.mybir as mybir
from concourse.bass_utils import run_bass_kernel_spmd

F32 = mybir.dt.float32
BF16 = mybir.dt.bfloat16
I32 = mybir.dt.int32
AF = mybir.ActivationFunctionType
ALU = mybir.AluOpType
AX = mybir.AxisListType

NCORES = 8
S = 16384
D = 2048
TPC = S // NCORES
NBLK = S // 128
EPS = 1e-6
CHUNK = 64
A_H, A_NOPE, A_ROPE, A_QK, A_V, A_LORA = 16, 128, 64, 192, 128, 512
B_H, B_HD, B_V = 8, 128, 256
TWO_PI = 2.0 * math.pi
NEG_BIG = -1.0e30


class Eng:
    def __init__(self, name, sem, serialize=False):
        self.name = name
        self.sem = sem
        self.count = 0
        self.waited = {}
        self.thunks = []
        self.serialize = serialize

    def wait(self, *toks):
        for tok in toks:
            if tok is None:
                continue
            if isinstance(tok, (list, tuple)) and (len(tok) == 0 or isinstance(tok[0], (list, tuple)) or tok[0] is None):
                self.wait(*tok)
                continue
            sem, val = tok
            if self.waited.get(sem, 0) >= val:
                continue
            self.waited[sem] = val
            self.thunks.append(lambda e, sem=sem, val=val: e.wait_ge(sem, val))

    def op(self, fn, pub=True, indep=False):
        if self.serialize and not indep and self.count > 0:
            self.wait((self.sem, self.count))
        self.count += 1
        c = self.count
        sem = self.sem
        self.thunks.append(lambda e: fn(e).then_inc(sem, 1))
        return (sem, c)


class DmaSem:
    def __init__(self, sem):
        self.sem = sem
        self.n = 0

    def issue(self, eng, fn):
        self.n += 16
        sem = self.sem
        eng.thunks.append(lambda e: fn(e).then_inc(sem, 16))
        return (sem, self.n)


class Prog:
    def __init__(self):
        self.nc = bass.Bass("TRN2", target_bir_lowering=False)
        self.es = ExitStack()
        self._n = 0
        self.SP = Eng("sync", self.sem("s_sp"))
        self.ACT = Eng("scalar", self.sem("s_act"), serialize=True)
        self.DVE = Eng("vector", self.sem("s_dve"), serialize=True)
        self.POOL = Eng("gpsimd", self.sem("s_pool"), serialize=True)
        self.PE = Eng("tensor", self.sem("s_pe"))
        self.engs = [self.SP, self.ACT, self.DVE, self.POOL, self.PE]
        self.psum = self.es.enter_context(self.nc.psum_tensor("psum", [128, 8, 512], F32))
        self.out_toks = []

    def uid(self, p):
        self._n += 1
        return "%s%d" % (p, self._n)

    def sem(self, name=None):
        return self.es.enter_context(self.nc.semaphore(name or self.uid("sem")))

    def dsem(self, name=None):
        return DmaSem(self.sem(name))

    def sb(self, name, shape, dt):
        return self.es.enter_context(self.nc.sbuf_tensor(name, list(shape), dt))

    def dram(self, name, shape, dt, kind):
        return self.nc.dram_tensor(name, list(shape), dt, kind=kind).ap()

    def bank(self, b):
        return self.psum[:, b, :]

    def bank_bf(self, b):
        return self.psum[:, b, :].bitcast(BF16)

    def finish(self):
        self.SP.wait(*self.out_toks)
        with self.nc.Block() as block:
            for eng in self.engs:
                if not eng.thunks:
                    continue

                def body(e, eng=eng):
                    for th in eng.thunks:
                        th(e)

                getattr(block, eng.name)(body)
        self.es.close()
        return self.nc

    def make_ident(self):
        idf = self.sb("ident_f", [128, 128], F32)
        idb = self.sb("ident_b", [128, 128], BF16)
        t0 = self.POOL.op(lambda e: e.memset(idf[:], 1.0))
        self.POOL.wait(t0)
        t1 = self.POOL.op(lambda e: e.affine_select(out=idf[:], in_=idf[:], pattern=[[-1, 128]],
                                                    compare_op=ALU.is_equal, fill=0.0, base=0,
                                                    channel_multiplier=1))
        self.DVE.wait(t1)
        t2 = self.DVE.op(lambda e: e.tensor_copy(out=idb[:], in_=idf[:]))
        self.ident = idb
        self.ident_tok = t2
        return idb

    def const_tile(self, val, name=None):
        t = self.sb(name or self.uid("c"), [128, 1], F32)
        tok = self.POOL.op(lambda e: e.memset(t[:], float(val)))
        return t, tok

    def load_weight_bf16(self, dst, src, nkc, ncols, gcol, gcol_tok, stage, stage_sems, state):
        W = stage[0].shape[1]
        t_cv = None
        for kc in range(nkc):
            for c0 in range(0, ncols, W):
                c1 = min(ncols, c0 + W)
                i = state["i"] % len(stage)
                st = stage[i]
                self.SP.wait(state["free"][i])
                t_ld = stage_sems[i].issue(self.SP, lambda e, st=st, kc=kc, c0=c0, c1=c1: e.dma_start(
                    out=st[:, 0:c1 - c0], in_=src[kc * 128:(kc + 1) * 128, c0:c1]))
                self.ACT.wait(t_ld, gcol_tok)
                if gcol is not None:
                    t_cv = self.ACT.op(lambda e, st=st, kc=kc, c0=c0, c1=c1: e.activation(
                        out=dst[:, kc, c0:c1], in_=st[:, 0:c1 - c0], func=AF.Copy, scale=gcol[:, kc:kc + 1]), indep=True)
                else:
                    t_cv = self.ACT.op(lambda e, st=st, kc=kc, c0=c0, c1=c1: e.activation(
                        out=dst[:, kc, c0:c1], in_=st[:, 0:c1 - c0], func=AF.Copy), indep=True)
                state["free"][i] = t_cv
                state["i"] += 1
        return t_cv


def _stage(P, ncols, n=3):
    stage = [P.sb(P.uid("wst"), [128, ncols], F32) for _ in range(n)]
    sems = [P.dsem() for _ in range(n)]
    state = {"i": 0, "free": [None] * n}
    return stage, sems, state


def emit_rstd(P, ss, n, eps_t, eps_tok, out, after):
    P.ACT.wait(after, eps_tok)
    t = P.ACT.op(lambda e: e.activation(out=out[:], in_=ss[:], func=AF.Sqrt, bias=eps_t[:], scale=1.0 / n))
    P.DVE.wait(t)
    return P.DVE.op(lambda e: e.reciprocal(out=out[:], in_=out[:]))


def build_A1():
    P = Prog()
    nc = P.nc
    NL = 2 * A_LORA + A_ROPE
    x = P.dram("x", [TPC, D], F32, "ExternalInput")
    w = P.dram("w_lat", [D, NL], F32, "ExternalInput")
    gn = P.dram("g_norm", [128, D // 128], F32, "ExternalInput")
    latT = P.dram("latT", [2 * A_LORA, TPC], BF16, "ExternalOutput")
    kpe = P.dram("kpe", [TPC, A_ROPE], F32, "ExternalOutput")
    KC = D // 128
    P.make_ident()
    eps_t, eps_tok = P.const_tile(EPS, "eps")
    gcol = P.sb("gcol", [128, KC], F32)
    ds0 = P.dsem()
    t_g = ds0.issue(P.SP, lambda e: e.dma_start(out=gcol[:], in_=gn))
    wb = P.sb("wb", [128, KC, NL], BF16)
    stage, ssems, sstate = _stage(P, NL, n=2)
    t_w = P.load_weight_bf16(wb, w, KC, NL, gcol, t_g, stage, ssems, sstate)

    NB = TPC // 128
    xt = [P.sb("xt%d" % i, [128, D], F32) for i in range(2)]
    xsem = [P.dsem() for _ in range(2)]
    xfree = [None, None]
    junk = P.sb("junk", [128, D], BF16)
    xb = P.sb("xb", [128, D], BF16)
    xnT = P.sb("xnT", [128, KC, 128], BF16)
    ss = P.sb("ss", [128, 4], F32)
    rs = P.sb("rs", [128, 4], F32)
    latb = P.sb("latb", [128, 2 * A_LORA], BF16)
    kpt = P.sb("kpt", [128, A_ROPE], F32)
    latTt = P.sb("latTt", [128, 8, 128], BF16)
    osem1 = P.dsem()
    osem2 = P.dsem()
    t_prev_z = None
    t_prev_lt = None
    t_out1 = None
    t_out2 = None
    t_xb_free = None
    t_evac_lt = None
    t_z_free = None
    for b in range(NB):
        s = b % 2
        P.SP.wait(xfree[s])
        t_x = xsem[s].issue(P.SP, lambda e, s=s, b=b: e.dma_start(out=xt[s][:], in_=x[b * 128:(b + 1) * 128, :]))
        P.ACT.wait(t_x)
        t_ss = P.ACT.op(lambda e, s=s: e.activation(out=junk[:], in_=xt[s][:], func=AF.Square, accum_out=ss[:, 0:1]))
        t_r = emit_rstd(P, ss[:, 0:1], D, eps_t, eps_tok, rs[:, 0:1], t_ss)
        P.DVE.wait(t_r, t_xb_free)
        t_xb = P.DVE.op(lambda e, s=s: e.tensor_scalar(out=xb[:], in0=xt[s][:], scalar1=rs[:, 0:1], scalar2=None, op0=ALU.mult))
        xfree[s] = t_xb
        P.PE.wait(t_xb, P.ident_tok, t_prev_z)
        for kc in range(KC):
            bk = kc // 8
            o = (kc % 8) * 128
            tt = P.PE.op(lambda e, kc=kc, bk=bk, o=o: e.transpose(out=P.bank_bf(bk)[:, o:o + 128], in_=xb[:, kc * 128:(kc + 1) * 128], identity=P.ident[:]),
                         pub=(kc == KC - 1))
        t_xb_free = tt
        P.DVE.wait(tt)
        P.DVE.op(lambda e: e.tensor_copy(out=xnT[:, 0:8, :], in_=P.bank_bf(0)[:, 0:1024]), pub=False)
        t_ev = P.DVE.op(lambda e: e.tensor_copy(out=xnT[:, 8:16, :], in_=P.bank_bf(1)[:, 0:1024]))
        P.PE.wait(t_ev, t_w, t_z_free)
        for gi, (c0, c1) in enumerate([(0, 512), (512, 1024), (1024, NL)]):
            for kc in range(KC):
                tz = P.PE.op(lambda e, gi=gi, c0=c0, c1=c1, kc=kc: e.matmul(
                    P.bank(2 + gi)[:, 0:c1 - c0], lhsT=xnT[:, kc, :], rhs=wb[:, kc, c0:c1],
                    start=(kc == 0), stop=(kc == KC - 1)), pub=(gi == 2 and kc == KC - 1))
        t_prev_z = tz
        P.ACT.wait(tz)
        t_s1 = P.ACT.op(lambda e: e.activation(out=junk[:, 0:512], in_=P.bank(2)[:, :], func=AF.Square, accum_out=ss[:, 1:2]))
        t_s2 = P.ACT.op(lambda e: e.activation(out=junk[:, 512:1024], in_=P.bank(3)[:, :], func=AF.Square, accum_out=ss[:, 2:3]))
        t_r1 = emit_rstd(P, ss[:, 1:3], A_LORA, eps_t, eps_tok, rs[:, 1:3], t_s2)
        P.ACT.wait(t_r1, t_prev_lt)
        P.ACT.op(lambda e: e.activation(out=latb[:, 0:512], in_=P.bank(2)[:, :], func=AF.Copy, scale=rs[:, 1:2]), pub=False)
        t_lb = P.ACT.op(lambda e: e.activation(out=latb[:, 512:1024], in_=P.bank(3)[:, :], func=AF.Copy, scale=rs[:, 2:3]))
        P.DVE.wait(tz, t_out2)
        t_kp = P.DVE.op(lambda e: e.tensor_copy(out=kpt[:], in_=P.bank(4)[:, 0:A_ROPE]))
        t_z_free = [t_lb, t_kp]
        P.SP.wait(t_kp)
        t_out2 = osem2.issue(P.SP, lambda e, b=b: e.dma_start(out=kpe[b * 128:(b + 1) * 128, :], in_=kpt[:]))
        P.PE.wait(t_lb, t_evac_lt)
        for j in range(8):
            tl = P.PE.op(lambda e, j=j: e.transpose(out=P.bank_bf(5)[:, j * 128:(j + 1) * 128], in_=latb[:, j * 128:(j + 1) * 128], identity=P.ident[:]),
                         pub=(j == 7))
        t_prev_lt = tl
        P.DVE.wait(tl, t_out1)
        t_evac_lt = P.DVE.op(lambda e: e.tensor_copy(out=latTt[:].rearrange("p j t -> p (j t)"), in_=P.bank_bf(5)[:, 0:1024]))
        P.SP.wait(t_evac_lt)
        t_out1 = osem1.issue(P.SP, lambda e, b=b: e.dma_start(
            out=latT.rearrange("(j p) t -> p j t", p=128)[:, :, b * 128:(b + 1) * 128], in_=latTt[:]))
    P.out_toks += [t_out1, t_out2]
    return P.finish()


def emit_rope_tables(P, pos_i, pos_tok, cs):
    R2 = A_ROPE // 2
    invf = (np.float32(10000.0) ** (-(np.arange(0, A_ROPE, 2, dtype=np.float32)) / np.float32(A_ROPE))).astype(np.float32)
    posf = P.sb("posf", [128, NBLK], F32)
    CB = 16
    u = P.sb("rt_u", [128, CB, R2], F32)
    tt = P.sb("rt_t", [128, CB, R2], F32)
    ki = P.sb("rt_ki", [128, CB, R2], I32)
    kf = P.sb("rt_kf", [128, CB, R2], F32)
    r = P.sb("rt_r", [128, CB, R2], F32)
    V = P.DVE
    V.wait(pos_tok)
    V.op(lambda e: e.tensor_copy(out=posf[:], in_=pos_i[:]), pub=False)
    C1 = 6.28125
    C2 = float(np.float32(TWO_PI - C1))
    last = None
    for ch in range(NBLK // CB):
        b0 = ch * CB
        for which in range(2):
            for i in range(R2):
                V.op(lambda e, i=i, b0=b0: e.tensor_scalar(out=u[:, :, i], in0=posf[:, b0:b0 + CB], scalar1=float(invf[i]), scalar2=None, op0=ALU.mult), pub=False)
            if which == 0:
                V.op(lambda e: e.tensor_scalar(out=u[:], in0=u[:], scalar1=float(math.pi / 2), scalar2=None, op0=ALU.add), pub=False)
            V.op(lambda e: e.tensor_scalar(out=tt[:], in0=u[:], scalar1=float(1.0 / TWO_PI), scalar2=None, op0=ALU.mult), pub=False)
            V.op(lambda e: e.tensor_copy(out=ki[:], in_=tt[:]), pub=False)
            V.op(lambda e: e.tensor_copy(out=kf[:], in_=ki[:]), pub=False)
            V.op(lambda e: e.scalar_tensor_tensor(out=r[:], in0=kf[:], scalar=-C1, in1=u[:], op0=ALU.mult, op1=ALU.add), pub=False)
            V.op(lambda e: e.scalar_tensor_tensor(out=r[:], in0=kf[:], scalar=-C2, in1=r[:], op0=ALU.mult, op1=ALU.add), pub=False)
            V.op(lambda e: e.tensor_scalar(out=tt[:], in0=r[:], scalar1=float(math.pi), scalar2=-TWO_PI, op0=ALU.is_gt, op1=ALU.mult), pub=False)
            V.op(lambda e: e.tensor_tensor(out=r[:], in0=r[:], in1=tt[:], op=ALU.add), pub=False)
            V.op(lambda e: e.tensor_scalar(out=tt[:], in0=r[:], scalar1=float(-math.pi), scalar2=TWO_PI, op0=ALU.is_lt, op1=ALU.mult), pub=False)
            V.op(lambda e: e.tensor_tensor(out=r[:], in0=r[:], in1=tt[:], op=ALU.add), pub=False)
            tr = V.op(lambda e: e.tensor_scalar(out=r[:], in0=r[:], scalar1=float(-math.pi), scalar2=float(math.pi), op0=ALU.max, op1=ALU.min))
            P.ACT.wait(tr)
            ta = P.ACT.op(lambda e, b0=b0, which=which: e.activation(out=cs[:, b0:b0 + CB, which * R2:(which + 1) * R2], in_=r[:], func=AF.Sin))
            V.wait(ta)
            last = ta
    return last


def emit_rope(P, src, dst, cs, blk, tmp):
    V = P.DVE
    h = A_ROPE // 2
    cos = cs[:, blk, 0:h]
    sin = cs[:, blk, h:2 * h]
    V.op(lambda e: e.tensor_tensor(out=tmp[:, 0:h], in0=src[:, 0:h], in1=cos, op=ALU.mult), pub=False)
    V.op(lambda e: e.tensor_tensor(out=tmp[:, h:2 * h], in0=src[:, h:2 * h], in1=sin, op=ALU.mult), pub=False)
    V.op(lambda e: e.tensor_tensor(out=tmp[:, 2 * h:3 * h], in0=src[:, 0:h], in1=sin, op=ALU.mult), pub=False)
    V.op(lambda e: e.tensor_tensor(out=tmp[:, 3 * h:4 * h], in0=src[:, h:2 * h], in1=cos, op=ALU.mult), pub=False)
    V.op(lambda e: e.tensor_tensor(out=dst[:, 0:h], in0=tmp[:, 0:h], in1=tmp[:, h:2 * h], op=ALU.subtract), pub=False)
    return V.op(lambda e: e.tensor_tensor(out=dst[:, h:2 * h], in0=tmp[:, 2 * h:3 * h], in1=tmp[:, 3 * h:4 * h], op=ALU.add))


def emit_absmax_bcast(P, src_dram, n, out, dsem, tmp):
    t = dsem.issue(P.SP, lambda e: e.dma_start(out=tmp[:, 0:n], in_=src_dram.partition_broadcast(128)))
    P.DVE.wait(t)
    return P.DVE.op(lambda e: e.tensor_reduce(out=out, in_=tmp[:, 0:n], axis=AX.X, op=ALU.max, apply_absolute_value=True))


class AttnState:
    pass


class _Stop(Exception):
    pass


DBG_STEP = [0]
DBG_SKIP = [0]
DBG_VAR = [0]


def _chk(n):
    if DBG_STEP[0] == n:
        if DBG_SKIP[0] > 0:
            DBG_SKIP[0] -= 1
            return
        raise _Stop()


def emit_attention_group(P, st, g, qk_parts, vb, dvp, nacc_banks, acc_bank0, s_banks, exp_scale, bias_fn,
                         alibi=None, mask_pool=True, hooks=None):
    PE, ACT, DVE, POOL = P.PE, P.ACT, P.DVE, P.POOL
    per_bank = 4 // nacc_banks
    nk = 4 * g + 4
    nslots = len(s_banks)

    def acc(qb):
        bk = acc_bank0 + qb // per_bank
        o = (qb % per_bank) * dvp
        return P.bank(bk)[:, o:o + dvp]

    first_in_bank = [True] * nacc_banks
    PE.wait(st.q_ready, st.acc_free)
    pend = []
    t_last = None

    def emit_pv(j, slot, r, t_p):
        nonlocal t_last
        PE.wait(t_p)
        for qb in range(r, 4):
            bk = qb // per_bank
            stt = first_in_bank[bk]
            first_in_bank[bk] = False
            last = (qb == 3)
            tk = PE.op(lambda e, qb=qb, j=j, slot=slot, stt=stt: e.matmul(
                acc(qb), lhsT=st.pT[slot][:, qb * 128:(qb + 1) * 128], rhs=vb[:, j, 0:dvp],
                start=stt, stop=(j == nk - 1), skip_group_check=True), pub=last)
        st.p_free[slot] = tk
        t_last = tk

    for j in range(nk):
        r = max(0, j - 4 * g)
        c0 = r * 128
        slot = st.it % nslots
        st.it += 1
        PE.wait(st.s_free[slot])
        for pi, (ktf, qt) in enumerate(qk_parts):
            ts = PE.op(lambda e, ktf=ktf, qt=qt, j=j, slot=slot, c0=c0, pi=pi: e.matmul(
                P.bank(s_banks[slot])[:, c0:512], lhsT=ktf(j), rhs=qt[:, c0:512],
                start=(pi == 0), stop=(pi == len(qk_parts) - 1)), pub=(pi == len(qk_parts) - 1))
        if len(pend) >= nslots - 1:
            emit_pv(*pend.pop(0))
        if hooks and j in hooks:
            for hk in hooks[j]:
                hk()
        src = P.bank(s_banks[slot])
        if alibi is not None:
            sbt = alibi["sbuf"][slot]
            DVE.wait(ts, st.sb_free[slot])
            if r < 4 and j >= 4 * g:
                DVE.op(lambda e, c0=c0, src=src, sbt=sbt: e.tensor_tensor(out=sbt[:, c0:c0 + 128], in0=src[:, c0:c0 + 128], in1=alibi["Tdiag"][:], op=ALU.add), indep=True)
                if c0 + 128 < 512:
                    td = DVE.op(lambda e, c0=c0, src=src, sbt=sbt: e.tensor_tensor(out=sbt[:, c0 + 128:512], in0=src[:, c0 + 128:512], in1=alibi["T2"][:, c0 + 128:512], op=ALU.add), indep=True)
                else:
                    td = (DVE.sem, DVE.count)
            else:
                td = DVE.op(lambda e, src=src, sbt=sbt: e.tensor_tensor(out=sbt[:], in0=src[:], in1=alibi["T2"][:], op=ALU.add), indep=True)
            st.s_free[slot] = td
            ACT.wait(td, st.p_free[slot])
            if j >= 4 * g:
                ACT.op(lambda e, c0=c0, sbt=sbt, slot=slot: e.activation(out=st.pT[slot][:, c0:c0 + 128], in_=sbt[:, c0:c0 + 128], func=AF.Exp, bias=alibi["negM"][:], scale=exp_scale), indep=True)
                if c0 + 128 < 512:
                    tp = ACT.op(lambda e, c0=c0, sbt=sbt, slot=slot, j=j: e.activation(out=st.pT[slot][:, c0 + 128:512], in_=sbt[:, c0 + 128:512], func=AF.Exp, bias=bias_fn(j), scale=exp_scale), indep=True)
                else:
                    tp = (ACT.sem, ACT.count)
            else:
                tp = ACT.op(lambda e, sbt=sbt, slot=slot, j=j: e.activation(out=st.pT[slot][:], in_=sbt[:], func=AF.Exp, bias=bias_fn(j), scale=exp_scale), indep=True)
            st.sb_free[slot] = tp
        else:
            ACT.wait(ts, st.p_free[slot])
            tp = ACT.op(lambda e, c0=c0, src=src, slot=slot, j=j: e.activation(out=st.pT[slot][:, c0:512], in_=src[:, c0:512], func=AF.Exp, bias=bias_fn(j), scale=exp_scale), indep=True)
            st.s_free[slot] = tp
            if j >= 4 * g:
                POOL.wait(tp)
                tp = POOL.op(lambda e, c0=c0, slot=slot: e.memset(st.pT[slot][64:128, c0:c0 + 64], 0.0), indep=True)
        pend.append((j, slot, r, tp))
    while pend:
        emit_pv(*pend.pop(0))
    return t_last


def build_A2(NG=S // 512, NH=2, dbg=0, G0=0):
    P = Prog()
    latT = P.dram("latT", [2 * A_LORA, S], BF16, "ExternalInput")
    kpe = P.dram("kpe", [S, A_ROPE], F32, "ExternalInput")
    pos = P.dram("pos", [128, NBLK], I32, "ExternalInput")
    wq = P.dram("wq", [A_LORA, 2 * A_QK], F32, "ExternalInput")
    wkv = P.dram("wkv", [A_LORA, 2 * (A_NOPE + A_V)], F32, "ExternalInput")
    gq = P.dram("gq", [128, 4], F32, "ExternalInput")
    gkv = P.dram("gkv", [128, 4], F32, "ExternalInput")
    qgain = P.dram("qgain", [1, A_QK], F32, "ExternalInput")
    kgain = P.dram("kgain", [1, A_QK], F32, "ExternalInput")
    o = P.dram("o", [S, 2 * A_V], F32, "ExternalOutput")
    PE, ACT, DVE, POOL, SP = P.PE, P.ACT, P.DVE, P.POOL, P.SP
    P.make_ident()
    eps_t, eps_tok = P.const_tile(EPS, "eps")
    mhalf, mhalf_tok = P.const_tile(-0.5, "mhalf")
    ds = P.dsem()
    gq_t = P.sb("gq_t", [128, 4], F32)
    gkv_t = P.sb("gkv_t", [128, 4], F32)
    SP_tok = ds.issue(SP, lambda e: e.dma_start(out=gq_t[:], in_=gq))
    SP_tok = ds.issue(SP, lambda e: e.dma_start(out=gkv_t[:], in_=gkv))
    qg_t = P.sb("qg_t", [128, A_QK], F32)
    kg_t = P.sb("kg_t", [128, A_QK], F32)
    ds.issue(SP, lambda e: e.dma_start(out=qg_t[:], in_=qgain.partition_broadcast(128)))
    t_small = ds.issue(SP, lambda e: e.dma_start(out=kg_t[:], in_=kgain.partition_broadcast(128)))
    pos_i = P.sb("pos_i", [128, NBLK], I32)
    t_pos = ds.issue(SP, lambda e: e.dma_start(out=pos_i[:], in_=pos))
    t_small = t_pos
    wq_b = P.sb("wq_b", [128, 4, 2 * A_QK], BF16)
    wkv_b = P.sb("wkv_b", [128, 4, 512], BF16)
    stage, ssems, sstate = _stage(P, 512, n=2)
    t_wq = P.load_weight_bf16(wq_b, wq, 4, 2 * A_QK, gq_t, t_pos, stage, ssems, sstate)
    t_wkv = P.load_weight_bf16(wkv_b, wkv, 4, 512, gkv_t, t_pos, stage, ssems, sstate)
    mq = P.sb("mq", [128, 2], F32)
    negM = P.sb("negM", [128, 1], F32)
    DVE.wait(t_small)
    DVE.op(lambda e: e.tensor_reduce(out=mq[:, 0:1], in_=qg_t[:], axis=AX.X, op=ALU.max, apply_absolute_value=True), pub=False)
    t_mk = DVE.op(lambda e: e.tensor_reduce(out=mq[:, 1:2], in_=kg_t[:], axis=AX.X, op=ALU.max, apply_absolute_value=True))
    DVE.wait(t_mk)
    t_negM = DVE.op(lambda e: e.scalar_tensor_tensor(out=negM[:], in0=mq[:, 0:1], scalar=-math.sqrt(A_QK), in1=mq[:, 1:2], op0=ALU.mult, op1=ALU.mult))
    cs = P.sb("cs", [128, NBLK, A_ROPE], F32)
    t_cs = emit_rope_tables(P, pos_i, t_pos, cs)

    KTn = P.sb("KTn", [128, S], BF16)
    KTr = P.sb("KTr", [128, S], BF16)
    dvp = A_V + 16
    vb = P.sb("vb", [128, NBLK, dvp], BF16)
    t_ones = POOL.op(lambda e: e.memset(vb[:, :, A_V:dvp], 1.0))
    latg = [P.sb("latg%d" % i, [128, 4, 512], BF16) for i in range(2)]
    latsem = [P.dsem() for _ in range(2)]
    latfree = [None, None]
    kpg = [P.sb("kpg%d" % i, [128, 4, A_ROPE], F32) for i in range(2)]
    kpsem = [P.dsem() for _ in range(2)]
    kpfree = [None, None]
    full4 = P.sb("full4", [128, 4, A_QK], F32)
    fn4 = P.sb("fn4", [128, 4, A_QK], F32)
    junk4 = P.sb("junk4", [128, 4, A_QK], F32)
    rt4 = P.sb("rt4", [128, 4, 128], F32)
    nb4 = P.sb("nb4", [128, 4, A_QK], BF16)
    ss = P.sb("ss", [128, 16], F32)
    QTn = [P.sb("QTn%d" % i, [128, 512], BF16) for i in range(2)]
    QTr = [P.sb("QTr%d" % i, [128, 512], BF16) for i in range(2)]
    st = AttnState()
    st.pT = [P.sb("pT%d" % i, [128, 512], BF16) for i in range(3)]
    st.p_free = [None, None, None]
    st.s_free = [None, None, None]
    st.it = 0
    st.acc_free = None
    st.q_ready = None
    osb = [P.sb("osb%d" % i, [128, 4, A_V], F32) for i in range(2)]
    osem = [P.dsem() for _ in range(2)]
    rec = P.sb("rec", [128, 4], F32)
    BK_P0, BK_TN = 4, 6
    TR_OFF = 512
    tok = {"p_free": None, "full_free": None, "nb_free": None, "tr_free": None}
    H2 = A_ROPE // 2

    def proj_part1(latt, wsel, ncol, is_k, kp, gain_t, blk0):
        PE.wait(tok["p_free"])
        for b in range(4):
            bk = BK_P0 + b // 2
            off = (b % 2) * ncol
            for kc in range(4):
                tp = PE.op(lambda e, b=b, bk=bk, off=off, kc=kc: e.matmul(
                    P.bank(bk)[:, off:off + ncol], lhsT=latt[:, kc, b * 128:(b + 1) * 128], rhs=wsel(kc),
                    start=(kc == 0), stop=(kc == 3), skip_group_check=True))
        DVE.wait(tp, tok["full_free"])
        for h2 in range(2):
            src = P.bank(BK_P0 + h2)[:, 0:2 * ncol].rearrange("p (b c) -> p b c", b=2)
            if is_k:
                DVE.op(lambda e, h2=h2, src=src: e.tensor_copy(out=full4[:, 2 * h2:2 * h2 + 2, 0:A_NOPE], in_=src[:, :, 0:A_NOPE]))
                DVE.op(lambda e, h2=h2, src=src: e.tensor_copy(out=vb[:, blk0 + 2 * h2:blk0 + 2 * h2 + 2, 0:A_V], in_=src[:, :, A_NOPE:A_NOPE + A_V]))
            else:
                DVE.op(lambda e, h2=h2, src=src: e.tensor_copy(out=full4[:, 2 * h2:2 * h2 + 2, :], in_=src[:, :, 0:A_QK]))
        tok["p_free"] = (DVE.sem, DVE.count)
        if is_k:
            DVE.op(lambda e: e.tensor_copy(out=full4[:, :, A_NOPE:A_QK], in_=kp[:]))
        DVE.op(lambda e: e.tensor_tensor(out=junk4[:], in0=full4[:], in1=full4[:], op=ALU.mult))
        t_ss = DVE.op(lambda e: e.tensor_reduce(out=ss[:, 0:4], in_=junk4[:], axis=AX.X, op=ALU.add))
        ACT.wait(t_ss, eps_tok)
        ACT.op(lambda e: e.activation(out=ss[:, 4:8], in_=ss[:, 0:4], func=AF.Ln, bias=eps_t[:], scale=1.0 / A_QK))
        t_rs = ACT.op(lambda e: e.activation(out=ss[:, 8:12], in_=ss[:, 4:8], func=AF.Exp, scale=-0.5))
        DVE.wait(t_rs, tok["nb_free"])
        for b in range(4):
            DVE.op(lambda e, b=b: e.scalar_tensor_tensor(out=fn4[:, b, :], in0=full4[:, b, :], scalar=ss[:, 8 + b:9 + b], in1=gain_t[:], op0=ALU.mult, op1=ALU.mult))
        DVE.op(lambda e: e.tensor_copy(out=nb4[:, :, 0:A_NOPE], in_=fn4[:, :, 0:A_NOPE]))
        cos = cs[:, blk0:blk0 + 4, 0:H2]
        sin = cs[:, blk0:blk0 + 4, H2:2 * H2]
        x1 = fn4[:, :, A_NOPE:A_NOPE + H2]
        x2 = fn4[:, :, A_NOPE + H2:A_QK]
        DVE.op(lambda e: e.tensor_tensor(out=rt4[:, :, 0:H2], in0=x1, in1=cos, op=ALU.mult))
        DVE.op(lambda e: e.tensor_tensor(out=rt4[:, :, H2:2 * H2], in0=x2, in1=sin, op=ALU.mult))
        DVE.op(lambda e: e.tensor_tensor(out=rt4[:, :, 2 * H2:3 * H2], in0=x1, in1=sin, op=ALU.mult))
        DVE.op(lambda e: e.tensor_tensor(out=rt4[:, :, 3 * H2:4 * H2], in0=x2, in1=cos, op=ALU.mult))
        DVE.op(lambda e: e.tensor_tensor(out=nb4[:, :, A_NOPE:A_NOPE + H2], in0=rt4[:, :, 0:H2], in1=rt4[:, :, H2:2 * H2], op=ALU.subtract))
        t_nb = DVE.op(lambda e: e.tensor_tensor(out=nb4[:, :, A_NOPE + H2:A_QK], in0=rt4[:, :, 2 * H2:3 * H2], in1=rt4[:, :, 3 * H2:4 * H2], op=ALU.add))
        tok["full_free"] = t_nb
        return t_nb

    def proj_part2(t_nb, dstT_n, dstT_r, tcol0, extra_wait=None):
        PE.wait(t_nb, tok["tr_free"])
        for b in range(4):
            PE.op(lambda e, b=b: e.transpose(out=P.bank_bf(BK_TN)[:, b * 128:(b + 1) * 128], in_=nb4[:, b, 0:A_NOPE], identity=P.ident[:]))
        for b in range(4):
            t_tr = PE.op(lambda e, b=b: e.transpose(out=P.bank_bf(BK_TN)[0:A_ROPE, TR_OFF + b * 128:TR_OFF + (b + 1) * 128], in_=nb4[:, b, A_NOPE:A_QK], identity=P.ident[:]))
        tok["nb_free"] = t_tr
        DVE.wait(t_tr, extra_wait)
        DVE.op(lambda e: e.tensor_copy(out=dstT_n[:, tcol0:tcol0 + 512], in_=P.bank_bf(BK_TN)[:, 0:512]))
        t_e = DVE.op(lambda e: e.tensor_copy(out=dstT_r[0:A_ROPE, tcol0:tcol0 + 512], in_=P.bank_bf(BK_TN)[0:A_ROPE, TR_OFF:TR_OFF + 512]))
        tok["tr_free"] = t_e
        return t_e

    DVE.wait(t_cs, t_negM)
    PE.wait(t_wq, t_wkv, P.ident_tok)
    POOL.wait(t_ones)
    if dbg == 1:
        SP.wait((DVE.sem, DVE.count), (ACT.sem, ACT.count), (POOL.sem, POOL.count))
        return P.finish()
    out_tok = [None, None]
    oi = 0
    attn_done_prev_head = None
    lat_it = [0]

    def load_lat(row0, g, with_kpe):
        s = lat_it[0] % 2
        lat_it[0] += 1
        SP.wait(latfree[s], kpfree[s] if with_kpe else None)
        t_l = latsem[s].issue(SP, lambda e, s=s, g=g: e.dma_start(
            out=latg[s][:], in_=latT[row0:row0 + A_LORA, g * 512:(g + 1) * 512].rearrange("(kc p) t -> p kc t", p=128)))
        t_k = None
        if with_kpe:
            t_k = kpsem[s].issue(SP, lambda e, s=s, g=g: e.dma_start(
                out=kpg[s][:], in_=kpe[g * 512:(g + 1) * 512, :].rearrange("(tb p) d -> p tb d", p=128)))
        return s, t_l, t_k

    for hh in range(NH):
        t_kv_last = None
        for g in range(NG):
            s, t_l, t_k = load_lat(A_LORA, g, True)
            PE.wait(t_l)
            DVE.wait(t_k)
            if g == 0:
                DVE.wait(attn_done_prev_head)
            t_nb = proj_part1(latg[s], lambda kc, hh=hh: wkv_b[:, kc, hh * 256:(hh + 1) * 256], 256, True, kpg[s], kg_t, g * 4)
            latfree[s] = (PE.sem, PE.count)
            kpfree[s] = t_nb
            t_kv_last = proj_part2(t_nb, KTn, KTr, g * 512)
        kv_ready = t_kv_last
        if dbg == 2:
            SP.wait(kv_ready)
            return P.finish()
        q_tok = {}
        q_pend = {}
        lat_pre = {}

        def q_prefetch(g):
            if g < NG and g not in lat_pre:
                lat_pre[g] = load_lat(0, g, False)

        def q_part1(g):
            q_prefetch(g)
            s, t_l, _ = lat_pre[g]
            PE.wait(t_l)
            q_pend[g] = proj_part1(latg[s], lambda kc, hh=hh: wq_b[:, kc, hh * A_QK:(hh + 1) * A_QK], A_QK, False, None, qg_t, g * 4)
            latfree[s] = (PE.sem, PE.count)
            q_prefetch(g + 1)

        def q_part2(g):
            qs = g % 2
            q_tok[g] = proj_part2(q_pend[g], QTn[qs], QTr[qs], 0, q_tok.get(("free", qs)))

        q_part1(G0)
        q_part2(G0)
        for g in range(G0, NG):
            qs = g % 2
            st.q_ready = [q_tok[g], kv_ready]
            parts = [(lambda j: KTn[:, j * 128:(j + 1) * 128], QTn[qs]),
                     (lambda j: KTr[0:A_ROPE, j * 128:(j + 1) * 128], QTr[qs][0:A_ROPE, :])]
            hooks = {}
            if g + 1 < NG:
                nk = 4 * g + 4
                hooks[0] = [lambda g=g: q_part1(g + 1)]
                hooks.setdefault(min(nk - 1, 8), []).append(lambda g=g: q_part2(g + 1))
            t_acc = emit_attention_group(P, st, g, parts, vb, dvp, 2, 2, [0, 1, 7], 1.0 / math.sqrt(A_QK),
                                         lambda j: negM[:], hooks=hooks)
            q_tok[("free", qs)] = t_acc
            ob = osb[oi % 2]
            DVE.wait(t_acc, out_tok[oi % 2])
            for qb in range(4):
                bk = 2 + qb // 2
                off = (qb % 2) * dvp
                DVE.op(lambda e, qb=qb, bk=bk, off=off: e.reciprocal(out=rec[:, qb:qb + 1], in_=P.bank(bk)[:, off + A_V:off + A_V + 1]))
            for qb in range(4):
                bk = 2 + qb // 2
                off = (qb % 2) * dvp
                t_on = DVE.op(lambda e, qb=qb, bk=bk, off=off, ob=ob: e.tensor_scalar(out=ob[:, qb, :], in0=P.bank(bk)[:, off:off + A_V], scalar1=rec[:, qb:qb + 1], scalar2=None, op0=ALU.mult))
            st.acc_free = t_on
            SP.wait(t_on)
            out_tok[oi % 2] = osem[oi % 2].issue(SP, lambda e, g=g, hh=hh, ob=ob: e.dma_start(
                out=o[g * 512:(g + 1) * 512, hh * A_V:(hh + 1) * A_V].rearrange("(qb p) d -> p qb d", p=128), in_=ob[:]))
            oi += 1
            attn_done_prev_head = t_acc
    P.out_toks += [t for t in out_tok if t is not None]
    return P.finish()


def build_B2(NG=S // 512, dbg=0):
    P = Prog()
    xnT = P.dram("xnT", [D, S], BF16, "ExternalInput")
    pos = P.dram("pos", [128, NBLK], I32, "ExternalInput")
    posrow = P.dram("posrow", [1, 512], I32, "ExternalInput")
    posg = P.dram("posg", [1, S // 512], I32, "ExternalInput")
    wq = P.dram("wq", [D, 2 * B_HD], F32, "ExternalInput")
    wkv = P.dram("wkv", [D, 2 * B_HD + B_V], F32, "ExternalInput")
    gn = P.dram("g_norm", [128, D // 128], F32, "ExternalInput")
    qgain = P.dram("qgain", [1, B_HD], F32, "ExternalInput")
    kgain = P.dram("kgain", [1, B_HD], F32, "ExternalInput")
    lam4 = P.dram("lam4", [4, B_HD], F32, "ExternalInput")
    subln = P.dram("subln", [1, B_V], F32, "ExternalInput")
    slope = P.dram("slope", [1, 1], F32, "ExternalInput")
    o = P.dram("o", [S, B_V], F32, "ExternalOutput")
    PE, ACT, DVE, POOL, SP = P.PE, P.ACT, P.DVE, P.POOL, P.SP
    KC = D // 128
    LAM_INIT = 0.8 - 0.6 * math.exp(-0.3 * 1)
    SQ = math.sqrt(B_HD)
    P.make_ident()
    eps_t, eps_tok = P.const_tile(EPS, "eps")
    ds = P.dsem()
    gcol = P.sb("gcol", [128, KC], F32)
    qg_t = P.sb("qg_t", [128, B_HD], F32)
    kg_t = P.sb("kg_t", [128, B_HD], F32)
    lam_t = P.sb("lam_t", [128, 4, B_HD], F32)
    sub_t = P.sb("sub_t", [128, B_V], F32)
    slope_t = P.sb("slope_t", [128, 1], F32)
    pos_i = P.sb("pos_i", [128, NBLK], I32)
    posg_i = P.sb("posg_i", [128, S // 512], I32)
    ds.issue(SP, lambda e: e.dma_start(out=gcol[:], in_=gn))
    ds.issue(SP, lambda e: e.dma_start(out=qg_t[:], in_=qgain.partition_broadcast(128)))
    ds.issue(SP, lambda e: e.dma_start(out=kg_t[:], in_=kgain.partition_broadcast(128)))
    for i in range(4):
        ds.issue(SP, lambda e, i=i: e.dma_start(out=lam_t[:, i, :], in_=lam4[i:i + 1, :].partition_broadcast(128)))
    ds.issue(SP, lambda e: e.dma_start(out=sub_t[:], in_=subln.partition_broadcast(128)))
    ds.issue(SP, lambda e: e.dma_start(out=slope_t[:], in_=slope.partition_broadcast(128)))
    ds.issue(SP, lambda e: e.dma_start(out=posg_i[:], in_=posg.partition_broadcast(128)))
    t_set = ds.issue(SP, lambda e: e.dma_start(out=pos_i[:], in_=pos))
    sbufs = [P.sb("sbias%d" % i, [128, 512], F32) for i in range(3)]
    prow_i = sbufs[0][:].bitcast(I32)
    prow_f = sbufs[1]
    ds2 = P.dsem()
    t_prow = ds2.issue(SP, lambda e: e.dma_start(out=prow_i, in_=posrow.partition_broadcast(128)))
    small = P.sb("small", [128, 16], F32)
    posf = P.sb("posf", [128, NBLK], F32)
    posgf = P.sb("posgf", [128, S // 512], F32)
    T2 = P.sb("T2", [128, 512], F32)
    Tdiag = P.sb("Tdiag", [128, 128], F32)
    tmpd = P.sb("tmpd", [128, 128], F32)
    negM = small[:, 0:1]
    nslope_s = small[:, 1:2]
    neglam = small[:, 2:3]
    DVE.wait(t_set, t_prow)
    DVE.op(lambda e: e.tensor_reduce(out=small[:, 3:4], in_=qg_t[:], axis=AX.X, op=ALU.max, apply_absolute_value=True))
    DVE.op(lambda e: e.tensor_reduce(out=small[:, 4:5], in_=kg_t[:], axis=AX.X, op=ALU.max, apply_absolute_value=True))
    DVE.op(lambda e: e.scalar_tensor_tensor(out=negM, in0=small[:, 3:4], scalar=-SQ, in1=small[:, 4:5], op0=ALU.mult, op1=ALU.mult))
    DVE.op(lambda e: e.tensor_scalar(out=nslope_s, in0=slope_t[:], scalar1=-SQ, scalar2=None, op0=ALU.mult))
    DVE.op(lambda e: e.tensor_tensor(out=tmpd[:], in0=lam_t[:, 0, :], in1=lam_t[:, 1, :], op=ALU.mult))
    DVE.op(lambda e: e.tensor_reduce(out=small[:, 5:6], in_=tmpd[:], axis=AX.X, op=ALU.add))
    DVE.op(lambda e: e.tensor_tensor(out=tmpd[:], in0=lam_t[:, 2, :], in1=lam_t[:, 3, :], op=ALU.mult))
    t_l = DVE.op(lambda e: e.tensor_reduce(out=small[:, 6:7], in_=tmpd[:], axis=AX.X, op=ALU.add))
    ACT.wait(t_l)
    t_e = ACT.op(lambda e: e.activation(out=small[:, 7:9], in_=small[:, 5:7], func=AF.Exp))
    DVE.wait(t_e)
    DVE.op(lambda e: e.tensor_tensor(out=neglam, in0=small[:, 8:9], in1=small[:, 7:8], op=ALU.subtract))
    DVE.op(lambda e: e.tensor_scalar(out=neglam, in0=neglam, scalar1=-LAM_INIT, scalar2=None, op0=ALU.add))
    DVE.op(lambda e: e.tensor_scalar(out=sub_t[:], in0=sub_t[:], scalar1=1.0 - LAM_INIT, scalar2=None, op0=ALU.mult))
    DVE.op(lambda e: e.tensor_copy(out=posf[:], in_=pos_i[:]))
    DVE.op(lambda e: e.tensor_copy(out=posgf[:], in_=posg_i[:]))
    DVE.op(lambda e: e.tensor_copy(out=prow_f[:], in_=prow_i))
    DVE.op(lambda e: e.tensor_scalar(out=T2[:], in0=prow_f[:], scalar1=prow_f[:, 0:1], scalar2=nslope_s, op0=ALU.subtract, op1=ALU.mult))
    DVE.op(lambda e: e.tensor_scalar(out=Tdiag[:], in0=prow_f[:, 0:128], scalar1=posf[:, 0:1], scalar2=None, op0=ALU.subtract))
    DVE.op(lambda e: e.tensor_scalar(out=tmpd[:], in0=Tdiag[:], scalar1=-1.0, scalar2=None, op0=ALU.mult))
    DVE.op(lambda e: e.tensor_tensor(out=Tdiag[:], in0=Tdiag[:], in1=tmpd[:], op=ALU.max))
    DVE.op(lambda e: e.tensor_scalar(out=Tdiag[:], in0=Tdiag[:], scalar1=nslope_s, scalar2=None, op0=ALU.mult))
    t_setup = DVE.op(lambda e: e.tensor_scalar(out=Tdiag[64:128, 0:64], in0=Tdiag[64:128, 0:64], scalar1=NEG_BIG, scalar2=None, op0=ALU.add))
    ACT.wait(t_setup)

    w_b = P.sb("w_b", [128, KC, 512], BF16)
    stage, ssems, sstate = _stage(P, 512, n=2)
    t_w = P.load_weight_bf16(w_b, wkv, KC, 512, gcol, t_set, stage, ssems, sstate)

    KT = [P.sb("KT%d" % c, [128, S], BF16) for c in range(2)]
    dvp = B_V + 2
    vb = P.sb("vb", [128, NBLK, dvp], BF16)
    POOL.op(lambda e: e.memset(vb[:, :, B_V:dvp], 1.0))
    GT = 256
    xg = [P.sb("xg%d" % i, [128, KC, GT], BF16) for i in range(2)]
    xsem = [P.dsem() for _ in range(2)]
    xfree = [None, None]
    ff4 = P.sb("ff4", [128, 4, 2 * B_HD], F32)
    junk4 = P.sb("junk4", [128, 4, 2 * B_HD], F32)
    nb4 = P.sb("nb4", [128, 4, 2 * B_HD], BF16)
    ss = P.sb("ss", [128, 32], F32)
    QT = [[P.sb("QT%d_%d" % (c, i), [128, 512], BF16) for i in range(2)] for c in range(2)]
    st = AttnState()
    st.pT = [P.sb("pT%d" % i, [128, 512], BF16) for i in range(3)]
    st.p_free = [None, None, None]
    st.s_free = [None, None, None]
    st.sb_free = [t_setup, t_setup, t_setup]
    st.it = 0
    st.acc_free = None
    st.q_ready = None
    kbias = [P.sb("kbias%d" % i, [128, NBLK], F32) for i in range(2)]
    kb_free = [None, None]
    oc = [P.sb("oc%d" % c, [128, 4, B_V], F32) for c in range(2)]
    osem = P.dsem()
    rec = P.sb("rec", [128, 4], F32)
    BK_PROJ, BK_TR = 7, 7
    tok = {"p_free": None, "ff_free": None, "nb_free": None, "tr_free": None}
    x_it = [0]

    def load_x(t0):
        s = x_it[0] % 2
        x_it[0] += 1
        SP.wait(xfree[s])
        t = xsem[s].issue(SP, lambda e, s=s, t0=t0: e.dma_start(
            out=xg[s][:], in_=xnT.rearrange("(kc p) t -> p kc t", p=128)[:, :, t0:t0 + GT]))
        return s, t

    def proj_part1(loads, ncol, gain_t, is_k, blk0):
        first = True
        for li, (s, t) in enumerate(loads):
            PE.wait(t)
            for tb in range(GT // 128):
                b = li * (GT // 128) + tb
                PE.wait(tok["p_free"], tok["tr_free"])
                for kc in range(KC):
                    tp = PE.op(lambda e, kc=kc, s=s, tb=tb: e.matmul(P.bank(BK_PROJ)[:, 0:ncol], lhsT=xg[s][:, kc, tb * 128:(tb + 1) * 128],
                                                                   rhs=w_b[:, kc, 0:ncol], start=(kc == 0), stop=(kc == KC - 1)))
                DVE.wait(tp, tok["ff_free"] if first else None)
                first = False
                DVE.op(lambda e, b=b: e.tensor_copy(out=ff4[:, b, :], in_=P.bank(BK_PROJ)[:, 0:2 * B_HD]))
                if is_k:
                    DVE.op(lambda e, b=b: e.tensor_copy(out=vb[:, blk0 + b, 0:B_V], in_=P.bank(BK_PROJ)[:, 2 * B_HD:2 * B_HD + B_V]))
                tok["p_free"] = (DVE.sem, DVE.count)
            xfree[s] = (PE.sem, PE.count)
        DVE.op(lambda e: e.tensor_tensor(out=junk4[:], in0=ff4[:], in1=ff4[:], op=ALU.mult))
        t_ss = DVE.op(lambda e: e.tensor_reduce(out=ss[:, 0:8], in_=junk4[:].rearrange("p b (c d) -> p (b c) d", c=2), axis=AX.X, op=ALU.add))
        ACT.wait(t_ss, eps_tok)
        ACT.op(lambda e: e.activation(out=ss[:, 8:16], in_=ss[:, 0:8], func=AF.Ln, bias=eps_t[:], scale=1.0 / B_HD))
        t_rs = ACT.op(lambda e: e.activation(out=ss[:, 16:24], in_=ss[:, 8:16], func=AF.Exp, scale=-0.5))
        DVE.wait(t_rs, tok["nb_free"])
        for b in range(4):
            for c in range(2):
                t_nb = DVE.op(lambda e, b=b, c=c: e.scalar_tensor_tensor(
                    out=nb4[:, b, c * B_HD:(c + 1) * B_HD], in0=ff4[:, b, c * B_HD:(c + 1) * B_HD],
                    scalar=ss[:, 16 + 2 * b + c:17 + 2 * b + c], in1=gain_t[:], op0=ALU.mult, op1=ALU.mult))
        tok["ff_free"] = t_nb
        return t_nb

    def proj_part2(t_nb, dstT, tcol0, extra_wait=None):
        PE.wait(t_nb, tok["tr_free"])
        for b in range(4):
            for c in range(2):
                t_tr = PE.op(lambda e, b=b, c=c: e.transpose(out=P.bank_bf(BK_TR)[:, (2 * b + c) * 128:(2 * b + c + 1) * 128],
                                                            in_=nb4[:, b, c * B_HD:(c + 1) * B_HD], identity=P.ident[:]))
        tok["nb_free"] = t_tr
        DVE.wait(t_tr, extra_wait)
        trv = P.bank_bf(BK_TR)[:, 0:1024].rearrange("p (b c d) -> p b c d", b=4, c=2)
        for c in range(2):
            t_e = DVE.op(lambda e, c=c: e.tensor_copy(out=dstT[c][:, tcol0:tcol0 + 512].rearrange("p (b d) -> p b d", b=4), in_=trv[:, :, c, :]))
        tok["tr_free"] = t_e
        return t_e

    PE.wait(t_w, P.ident_tok)
    t_kv = None
    for g in range(NG):
        loads = [load_x(g * 512 + i * GT) for i in range(512 // GT)]
        t_nb = proj_part1(loads, 512, kg_t, True, g * 4)
        t_kv = proj_part2(t_nb, KT, g * 512)
    kv_ready = t_kv
    ACT.wait((PE.sem, PE.count))
    t_wq = P.load_weight_bf16(w_b, wq, KC, 2 * B_HD, gcol, t_set, stage, ssems, sstate)
    PE.wait(t_wq)
    q_tok = {}
    q_pend = {}
    x_pre = {}

    def q_prefetch(g):
        if g < NG and g not in x_pre:
            x_pre[g] = [load_x(g * 512 + i * GT) for i in range(512 // GT)]

    def q_part1(g):
        q_prefetch(g)
        q_pend[g] = proj_part1(x_pre[g], 2 * B_HD, qg_t, False, None)

    def q_part2(g):
        qs = g % 2
        q_tok[g] = proj_part2(q_pend[g], [QT[0][qs], QT[1][qs]], 0, q_tok.get(("free", qs)))

    out_tok = None
    q_part1(0)
    q_part2(0)
    for g in range(NG):
        qs = g % 2
        kb = kbias[g % 2]
        DVE.wait(kb_free[g % 2])
        DVE.op(lambda e, kb=kb, g=g: e.tensor_scalar(out=kb[:], in0=posf[:], scalar1=posgf[:, g:g + 1], scalar2=slope_t[:], op0=ALU.subtract, op1=ALU.mult))
        t_kb = DVE.op(lambda e, kb=kb: e.tensor_scalar(out=kb[:], in0=kb[:], scalar1=negM, scalar2=None, op0=ALU.add))
        ACT.wait(t_kb)
        for c in range(2):
            st.q_ready = [q_tok[g], kv_ready]
            parts = [(lambda j, c=c: KT[c][:, j * 128:(j + 1) * 128], QT[c][qs])]
            hooks = {}
            if c == 0 and g + 1 < NG:
                nk = 4 * g + 4
                hooks[0] = [lambda g=g: q_part1(g + 1)]
                hooks.setdefault(min(nk - 1, 10), []).append(lambda g=g: q_part2(g + 1))
            t_acc = emit_attention_group(P, st, g, parts, vb, dvp, 4, 2, [0, 1, 6], 1.0 / SQ,
                                         lambda j, kb=kb: kb[:, j:j + 1],
                                         alibi=dict(T2=T2, Tdiag=Tdiag, negM=negM, sbuf=sbufs), hooks=hooks)
            DVE.wait(t_acc, out_tok if c == 0 else None)
            for qb in range(4):
                DVE.op(lambda e, qb=qb: e.reciprocal(out=rec[:, qb:qb + 1], in_=P.bank(2 + qb)[:, B_V:B_V + 1]))
            for qb in range(4):
                t_on = DVE.op(lambda e, qb=qb, c=c: e.tensor_scalar(out=oc[c][:, qb, :], in0=P.bank(2 + qb)[:, 0:B_V], scalar1=rec[:, qb:qb + 1], scalar2=None, op0=ALU.mult))
            st.acc_free = t_on
        q_tok[("free", qs)] = t_acc
        kb_free[g % 2] = t_acc
        o0f = oc[0][:].rearrange("p a d -> p (a d)")
        o1f = oc[1][:].rearrange("p a d -> p (a d)")
        DVE.op(lambda e: e.scalar_tensor_tensor(out=o0f, in0=o1f, scalar=neglam, in1=o0f, op0=ALU.mult, op1=ALU.add))
        DVE.op(lambda e: e.tensor_tensor(out=o1f, in0=o0f, in1=o0f, op=ALU.mult))
        t_s4 = DVE.op(lambda e: e.tensor_reduce(out=ss[:, 24:28], in_=oc[1][:], axis=AX.X, op=ALU.add))
        ACT.wait(t_s4, eps_tok)
        ACT.op(lambda e: e.activation(out=ss[:, 24:28], in_=ss[:, 24:28], func=AF.Ln, bias=eps_t[:], scale=1.0 / B_V))
        t_r4 = ACT.op(lambda e: e.activation(out=ss[:, 28:32], in_=ss[:, 24:28], func=AF.Exp, scale=-0.5))
        DVE.wait(t_r4)
        for qb in range(4):
            t_fin = DVE.op(lambda e, qb=qb: e.scalar_tensor_tensor(out=oc[1][:, qb, :], in0=oc[0][:, qb, :], scalar=ss[:, 28 + qb:29 + qb], in1=sub_t[:], op0=ALU.mult, op1=ALU.mult))
        SP.wait(t_fin)
        out_tok = osem.issue(SP, lambda e, g=g: e.dma_start(
            out=o[g * 512:(g + 1) * 512, :].rearrange("(qb p) d -> p qb d", p=128), in_=oc[1][:]))
    P.out_toks.append(out_tok)
    return P.finish()


def build_MIX(with_norm_out):
    P = Prog()
    xres = P.dram("xres", [TPC, D], F32, "ExternalInput")
    oin = P.dram("oin", [TPC, D], F32, "ExternalInput")
    wg = P.dram("wg", [D, D], F32, "ExternalInput")
    wo = P.dram("wo", [D, D], F32, "ExternalInput")
    gn = P.dram("g_norm", [128, D // 128], F32, "ExternalInput")
    xnew = P.dram("xnew", [TPC, D], F32, "ExternalOutput")
    if with_norm_out:
        xnT_out = P.dram("xnT", [D, TPC], BF16, "ExternalOutput")
    PE, ACT, DVE, POOL, SP = P.PE, P.ACT, P.DVE, P.POOL, P.SP
    KC = D // 128
    P.make_ident()
    eps_t, eps_tok = P.const_tile(EPS, "eps")
    gcol = P.sb("gcol", [128, KC], F32)
    ds0 = P.dsem()
    t_g = ds0.issue(SP, lambda e: e.dma_start(out=gcol[:], in_=gn))
    wgb = P.sb("wgb", [128, KC, D], BF16)
    wob = P.sb("wob", [128, KC, D], BF16)
    stage, ssems, sstate = _stage(P, 1024, n=2)
    t_wg = P.load_weight_bf16(wgb, wg, KC, D, gcol, t_g, stage, ssems, sstate)
    t_wo = P.load_weight_bf16(wob, wo, KC, D, None, None, stage, ssems, sstate)
    NB = TPC // 128
    xt = P.sb("xt", [128, D], F32)
    ot = P.sb("ot", [128, D], F32)
    xsem = P.dsem()
    osem_in = P.dsem()
    junk = P.sb("junk", [128, D], BF16)
    xb = P.sb("xb", [128, D], BF16)
    xnT = P.sb("xnT_s", [128, KC, 128], BF16)
    hb = P.sb("hb", [128, D], BF16)
    hT = P.sb("hT", [128, KC, 128], BF16)
    xo = P.sb("xo", [128, D], F32)
    sg = xo
    ss = P.sb("ss", [128, 2], F32)
    rs = P.sb("rs", [128, 2], F32)
    osem = P.dsem()
    osem2 = P.dsem()
    if with_norm_out:
        x2b = P.sb("x2b", [128, D], BF16)
        x2T = P.sb("x2T", [128, KC, 128], BF16)
    xt_free = None
    ot_free = None
    t_out = None
    t_out2 = None
    for b in range(NB):
        SP.wait(xt_free)
        t_x = xsem.issue(SP, lambda e, b=b: e.dma_start(out=xt[:], in_=xres[b * 128:(b + 1) * 128, :]))
        SP.wait(ot_free)
        t_o = osem_in.issue(SP, lambda e, b=b: e.dma_start(out=ot[:], in_=oin[b * 128:(b + 1) * 128, :]))
        ACT.wait(t_x)
        t_ss = ACT.op(lambda e: e.activation(out=junk[:], in_=xt[:], func=AF.Square, accum_out=ss[:, 0:1]))
        t_r = emit_rstd(P, ss[:, 0:1], D, eps_t, eps_tok, rs[:, 0:1], t_ss)
        DVE.wait(t_r)
        t_xb = DVE.op(lambda e: e.tensor_scalar(out=xb[:], in0=xt[:], scalar1=rs[:, 0:1], scalar2=None, op0=ALU.mult))

        def transpose16(src, dst, after):
            PE.wait(after, P.ident_tok)
            for kc in range(KC):
                bk = kc // 8
                oo = (kc % 8) * 128
                tt = PE.op(lambda e, kc=kc, bk=bk, oo=oo: e.transpose(out=P.bank_bf(bk)[:, oo:oo + 128], in_=src[:, kc * 128:(kc + 1) * 128], identity=P.ident[:]),
                           pub=(kc == KC - 1))
            DVE.wait(tt)
            DVE.op(lambda e: e.tensor_copy(out=dst[:, 0:8, :], in_=P.bank_bf(0)[:, 0:1024]), pub=False)
            return DVE.op(lambda e: e.tensor_copy(out=dst[:, 8:16, :], in_=P.bank_bf(1)[:, 0:1024]))

        t_ev = transpose16(xb, xnT, t_xb)
        PE.wait(t_ev, t_wg)
        for gi in range(4):
            for kc in range(KC):
                tz = PE.op(lambda e, gi=gi, kc=kc: e.matmul(P.bank(2 + gi)[:, :], lhsT=xnT[:, kc, :], rhs=wgb[:, kc, gi * 512:(gi + 1) * 512],
                                                           start=(kc == 0), stop=(kc == KC - 1)), pub=(kc == KC - 1))
        ACT.wait(tz, t_out)
        for gi in range(4):
            t_sg = ACT.op(lambda e, gi=gi: e.activation(out=sg[:, gi * 512:(gi + 1) * 512], in_=P.bank(2 + gi)[:, :], func=AF.Silu), pub=(gi == 3))
        DVE.wait(t_sg, t_o)
        t_hb = DVE.op(lambda e: e.tensor_tensor(out=hb[:], in0=sg[:], in1=ot[:], op=ALU.mult))
        ot_free = t_hb
        t_ev2 = transpose16(hb, hT, t_hb)
        PE.wait(t_ev2, t_wo)
        for gi in range(4):
            for kc in range(KC):
                tz2 = PE.op(lambda e, gi=gi, kc=kc: e.matmul(P.bank(2 + gi)[:, :], lhsT=hT[:, kc, :], rhs=wob[:, kc, gi * 512:(gi + 1) * 512],
                                                            start=(kc == 0), stop=(kc == KC - 1)), pub=(kc == KC - 1))
        DVE.wait(tz2, t_out)
        for gi in range(4):
            t_xo = DVE.op(lambda e, gi=gi: e.tensor_tensor(out=xo[:, gi * 512:(gi + 1) * 512], in0=P.bank(2 + gi)[:, :], in1=xt[:, gi * 512:(gi + 1) * 512], op=ALU.add), pub=(gi == 3))
        xt_free = t_xo
        SP.wait(t_xo)
        t_out = osem.issue(SP, lambda e, b=b: e.dma_start(out=xnew[b * 128:(b + 1) * 128, :], in_=xo[:]))
        if with_norm_out:
            ACT.wait(t_xo)
            t_ss2 = ACT.op(lambda e: e.activation(out=junk[:], in_=xo[:], func=AF.Square, accum_out=ss[:, 1:2]))
            t_r2 = emit_rstd(P, ss[:, 1:2], D, eps_t, eps_tok, rs[:, 1:2], t_ss2)
            DVE.wait(t_r2)
            t_x2b = DVE.op(lambda e: e.tensor_scalar(out=x2b[:], in0=xo[:], scalar1=rs[:, 1:2], scalar2=None, op0=ALU.mult))
            DVE.wait(t_out2)
            t_ev3 = transpose16(x2b, x2T, t_x2b)
            SP.wait(t_ev3)
            t_out2 = osem2.issue(SP, lambda e, b=b: e.dma_start(
                out=xnT_out.rearrange("(kc p) t -> p kc t", p=128)[:, :, b * 128:(b + 1) * 128], in_=x2T[:]))
    P.out_toks += [t_out] + ([t_out2] if with_norm_out else [])
    return P.finish()


_CACHE = {}


def _get(name, fn):
    if name not in _CACHE:
        _CACHE[name] = fn()
    return _CACHE[name]


def _col(v):
    v = np.asarray(v, dtype=np.float32)
    return np.ascontiguousarray(v.reshape(-1, 128).T)


def run(nc, in_maps):
    res = run_bass_kernel_spmd(nc, in_maps, core_ids=list(range(NCORES)))
    return res.results


def _posl(pos):
    return np.ascontiguousarray(np.asarray(pos, dtype=np.int32).reshape(NBLK, 128).T)


def stage_A1(x, inp):
    nc = _get("A1", build_A1)
    w_lat = np.ascontiguousarray(inp["a_w_in"][0][:, :2 * A_LORA + A_ROPE])
    g = _col(inp["a_norm"][0])
    res = run(nc, [{"x": np.ascontiguousarray(x[c * TPC:(c + 1) * TPC]), "w_lat": w_lat, "g_norm": g} for c in range(NCORES)])
    latT = np.concatenate([r["latT"] for r in res], axis=1)
    kpe = np.concatenate([r["kpe"] for r in res], axis=0)
    return latT, kpe


def stage_A2(latT, kpe, inp):
    nc = _get("A2", build_A2)
    pos = _posl(inp["positions"][0])
    wq = inp["a_w_q_up"][0]
    wkv = inp["a_w_kv_up"][0]
    ims = []
    for c in range(NCORES):
        ims.append({"latT": latT, "kpe": kpe, "pos": pos,
                    "wq": np.ascontiguousarray(wq[:, 2 * c * A_QK:(2 * c + 2) * A_QK]),
                    "wkv": np.ascontiguousarray(wkv[:, 2 * c * 256:(2 * c + 2) * 256]),
                    "gq": _col(inp["a_q_norm"][0]), "gkv": _col(inp["a_kv_norm"][0]),
                    "qgain": np.ascontiguousarray(inp["a_q_gain"][0][None, :]),
                    "kgain": np.ascontiguousarray(inp["a_k_gain"][0][None, :])})
    res = run(nc, ims)
    return np.concatenate([r["o"] for r in res], axis=1)


def stage_MIX(xres, o, w_gate, w_out, g_norm, with_norm_out):
    nc = _get("MIX%d" % int(with_norm_out), lambda: build_MIX(with_norm_out))
    w_gate = np.ascontiguousarray(w_gate)
    w_out = np.ascontiguousarray(w_out)
    g = _col(g_norm)
    res = run(nc, [{"xres": np.ascontiguousarray(xres[c * TPC:(c + 1) * TPC]),
                    "oin": np.ascontiguousarray(o[c * TPC:(c + 1) * TPC]),
                    "wg": w_gate, "wo": w_out, "g_norm": g} for c in range(NCORES)])
    xnew = np.concatenate([r["xnew"] for r in res], axis=0)
    xnT = np.concatenate([r["xnT"] for r in res], axis=1) if with_norm_out else None
    return xnew, xnT


def stage_B2(xnT, inp):
    nc = _get("B2", build_B2)
    posv = np.asarray(inp["positions"][0], dtype=np.int32)
    pos = _posl(posv)
    w = inp["b_w_in"][0]
    QK = B_H * 2 * B_HD
    ims = []
    for c in range(NCORES):
        wq = np.ascontiguousarray(w[:, c * 256:(c + 1) * 256])
        wkv = np.ascontiguousarray(np.concatenate([w[:, QK + c * 256:QK + (c + 1) * 256],
                                                   w[:, 2 * QK + c * B_V:2 * QK + (c + 1) * B_V]], axis=1))
        ims.append({"xnT": xnT, "pos": pos, "posrow": np.ascontiguousarray(posv[None, 0:512]),
                    "posg": np.ascontiguousarray(posv[None, ::512]), "wq": wq, "wkv": wkv,
                    "g_norm": _col(inp["b_norm"][0]),
                    "qgain": np.ascontiguousarray(inp["b_q_gain"][0][None, :]),
                    "kgain": np.ascontiguousarray(inp["b_k_gain"][0][None, :]),
                    "lam4": np.ascontiguousarray(np.stack([inp["b_lambda_q1"][0], inp["b_lambda_k1"][0],
                                                           inp["b_lambda_q2"][0], inp["b_lambda_k2"][0]])),
                    "subln": np.ascontiguousarray(inp["b_subln"][0][None, :]),
                    "slope": np.full((1, 1), 2.0 ** (-8.0 * (c + 1) / B_H), dtype=np.float32)})
    res = run(nc, ims)
    return np.concatenate([r["o"] for r in res], axis=1)


def kernel(**inputs):
    inp = {k: np.asarray(v) for k, v in inputs.items()}
    x = np.ascontiguousarray(inp["x"][0])
    latT, kpe = stage_A1(x, inp)
    oA = stage_A2(latT, kpe, inp)
    x1, xn1T = stage_MIX(x, oA, inp["a_w_in"][0][:, 2 * A_LORA + A_ROPE:], inp["a_w_out"][0], inp["a_norm"][0], True)
    QK = B_H * 2 * B_HD
    oB = stage_B2(xn1T, inp)
    x2, _ = stage_MIX(x1, oB, inp["b_w_in"][0][:, 2 * QK + B_H * B_V:], inp["b_w_out"][0], inp["b_norm"][0], False)
    return x2[None].astype(np.float32)
```

```python
import math
from contextlib import ExitStack

import numpy as np
import concourse.bass as bass
import concourse.mybir as mybir
from concourse.bass_utils import run_bass_kernel_spmd

F32 = mybir.dt.float32
BF16 = mybir.dt.bfloat16
I32 = mybir.dt.int32
AF = mybir.ActivationFunctionType
ALU = mybir.AluOpType
AX = mybir.AxisListType

NCORES = 8
S = 16384
D = 2048
TPC = S // NCORES
NBLK = S // 128
EPS = 1e-6
CHUNK = 64
A_H, A_NOPE, A_ROPE, A_QK, A_V, A_LORA = 16, 128, 64, 192, 128, 512
B_H, B_HD, B_V = 8, 128, 256
TWO_PI = 2.0 * math.pi
NEG_BIG = -1.0e30


class Eng:
    def __init__(self, name, sem, serialize=False):
        self.name = name
        self.sem = sem
        self.count = 0
        self.waited = {}
        self.thunks = []
        self.serialize = serialize

    def wait(self, *toks):
        for tok in toks:
            if tok is None:
                continue
            if isinstance(tok, (list, tuple)) and (len(tok) == 0 or isinstance(tok[0], (list, tuple)) or tok[0] is None):
                self.wait(*tok)
                continue
            sem, val = tok
            if self.waited.get(sem, 0) >= val:
                continue
            self.waited[sem] = val
            self.thunks.append(lambda e, sem=sem, val=val: e.wait_ge(sem, val))

    def op(self, fn, pub=True, indep=False):
        if self.serialize and not indep and self.count > 0:
            self.wait((self.sem, self.count))
        self.count += 1
        c = self.count
        sem = self.sem
        self.thunks.append(lambda e: fn(e).then_inc(sem, 1))
        return (sem, c)


class DmaSem:
    def __init__(self, sem):
        self.sem = sem
        self.n = 0

    def issue(self, eng, fn):
        self.n += 16
        sem = self.sem
        eng.thunks.append(lambda e: fn(e).then_inc(sem, 16))
        return (sem, self.n)


class Prog:
    def __init__(self):
        self.nc = bass.Bass("TRN2", target_bir_lowering=False)
        self.es = ExitStack()
        self._n = 0
        self.SP = Eng("sync", self.sem("s_sp"))
        self.ACT = Eng("scalar", self.sem("s_act"), serialize=True)
        self.DVE = Eng("vector", self.sem("s_dve"), serialize=True)
        self.POOL = Eng("gpsimd", self.sem("s_pool"), serialize=True)
        self.PE = Eng("tensor", self.sem("s_pe"))
        self.engs = [self.SP, self.ACT, self.DVE, self.POOL, self.PE]
        self.psum = self.es.enter_context(self.nc.psum_tensor("psum", [128, 8, 512], F32))
        self.out_toks = []

    def uid(self, p):
        self._n += 1
        return "%s%d" % (p, self._n)

    def sem(self, name=None):
        return self.es.enter_context(self.nc.semaphore(name or self.uid("sem")))

    def dsem(self, name=None):
        return DmaSem(self.sem(name))

    def sb(self, name, shape, dt):
        return self.es.enter_context(self.nc.sbuf_tensor(name, list(shape), dt))

    def dram(self, name, shape, dt, kind):
        return self.nc.dram_tensor(name, list(shape), dt, kind=kind).ap()

    def bank(self, b):
        return self.psum[:, b, :]

    def bank_bf(self, b):
        return self.psum[:, b, :].bitcast(BF16)

    def finish(self):
        self.SP.wait(*self.out_toks)
        with self.nc.Block() as block:
            for eng in self.engs:
                if not eng.thunks:
                    continue

                def body(e, eng=eng):
                    for th in eng.thunks:
                        th(e)

                getattr(block, eng.name)(body)
        self.es.close()
        return self.nc

    def make_ident(self):
        idf = self.sb("ident_f", [128, 128], F32)
        idb = self.sb("ident_b", [128, 128], BF16)
        t0 = self.POOL.op(lambda e: e.memset(idf[:], 1.0))
        self.POOL.wait(t0)
        t1 = self.POOL.op(lambda e: e.affine_select(out=idf[:], in_=idf[:], pattern=[[-1, 128]],
                                                    compare_op=ALU.is_equal, fill=0.0, base=0,
                                                    channel_multiplier=1))
        self.DVE.wait(t1)
        t2 = self.DVE.op(lambda e: e.tensor_copy(out=idb[:], in_=idf[:]))
        self.ident = idb
        self.ident_tok = t2
        return idb

    def const_tile(self, val, name=None):
        t = self.sb(name or self.uid("c"), [128, 1], F32)
        tok = self.POOL.op(lambda e: e.memset(t[:], float(val)))
        return t, tok

    def load_weight_bf16(self, dst, src, nkc, ncols, gcol, gcol_tok, stage, stage_sems, state):
        W = stage[0].shape[-1]
        t_cv = None
        for kc in range(nkc):
            for c0 in range(0, ncols, W):
                c1 = min(ncols, c0 + W)
                i = state["i"] % len(stage)
                st = stage[i]
                self.SP.wait(state["free"][i])
                t_ld = stage_sems[i].issue(self.SP, lambda e, st=st, kc=kc, c0=c0, c1=c1: e.dma_start(
                    out=st[:, 0:c1 - c0], in_=src[kc * 128:(kc + 1) * 128, c0:c1]))
                self.ACT.wait(t_ld, gcol_tok)
                if gcol is not None:
                    t_cv = self.ACT.op(lambda e, st=st, kc=kc, c0=c0, c1=c1: e.activation(
                        out=dst[:, kc, c0:c1], in_=st[:, 0:c1 - c0], func=AF.Copy, scale=gcol[:, kc:kc + 1]), indep=True)
                else:
                    t_cv = self.ACT.op(lambda e, st=st, kc=kc, c0=c0, c1=c1: e.activation(
                        out=dst[:, kc, c0:c1], in_=st[:, 0:c1 - c0], func=AF.Copy), indep=True)
                state["free"][i] = t_cv
                state["i"] += 1
        return t_cv


def _stage(P, ncols, n=3):
    stage = [P.sb(P.uid("wst"), [128, ncols], F32) for _ in range(n)]
    sems = [P.dsem() for _ in range(n)]
    state = {"i": 0, "free": [None] * n}
    return stage, sems, state


def emit_rstd(P, ss, n, eps_t, eps_tok, out, after):
    P.ACT.wait(after, eps_tok)
    t = P.ACT.op(lambda e: e.activation(out=out[:], in_=ss[:], func=AF.Sqrt, bias=eps_t[:], scale=1.0 / n))
    P.DVE.wait(t)
    return P.DVE.op(lambda e: e.reciprocal(out=out[:], in_=out[:]))


def build_A1():
    P = Prog()
    nc = P.nc
    NL = 2 * A_LORA + A_ROPE
    x = P.dram("x", [TPC, D], F32, "ExternalInput")
    w = P.dram("w_lat", [D, NL], F32, "ExternalInput")
    gn = P.dram("g_norm", [128, D // 128], F32, "ExternalInput")
    latT = P.dram("latT", [2 * A_LORA, TPC], BF16, "ExternalOutput")
    kpe = P.dram("kpe", [TPC, A_ROPE], F32, "ExternalOutput")
    KC = D // 128
    P.make_ident()
    eps_t, eps_tok = P.const_tile(EPS, "eps")
    gcol = P.sb("gcol", [128, KC], F32)
    ds0 = P.dsem()
    t_g = ds0.issue(P.SP, lambda e: e.dma_start(out=gcol[:], in_=gn))
    wb = P.sb("wb", [128, KC, NL], BF16)
    stage, ssems, sstate = _stage(P, NL, n=2)
    t_w = P.load_weight_bf16(wb, w, KC, NL, gcol, t_g, stage, ssems, sstate)

    NB = TPC // 128
    xt = [P.sb("xt%d" % i, [128, D], F32) for i in range(2)]
    xsem = [P.dsem() for _ in range(2)]
    xfree = [None, None]
    junk = P.sb("junk", [128, D], BF16)
    xb = P.sb("xb", [128, D], BF16)
    xnT = P.sb("xnT", [128, KC, 128], BF16)
    ss = P.sb("ss", [128, 4], F32)
    rs = P.sb("rs", [128, 4], F32)
    latb = P.sb("latb", [128, 2 * A_LORA], BF16)
    kpt = P.sb("kpt", [128, A_ROPE], F32)
    latTt = P.sb("latTt", [128, 8, 128], BF16)
    osem1 = P.dsem()
    osem2 = P.dsem()
    t_prev_z = None
    t_prev_lt = None
    t_out1 = None
    t_out2 = None
    t_xb_free = None
    t_evac_lt = None
    t_z_free = None
    for b in range(NB):
        s = b % 2
        P.SP.wait(xfree[s])
        t_x = xsem[s].issue(P.SP, lambda e, s=s, b=b: e.dma_start(out=xt[s][:], in_=x[b * 128:(b + 1) * 128, :]))
        P.ACT.wait(t_x)
        t_ss = P.ACT.op(lambda e, s=s: e.activation(out=junk[:], in_=xt[s][:], func=AF.Square, accum_out=ss[:, 0:1]))
        t_r = emit_rstd(P, ss[:, 0:1], D, eps_t, eps_tok, rs[:, 0:1], t_ss)
        P.DVE.wait(t_r, t_xb_free)
        t_xb = P.DVE.op(lambda e, s=s: e.tensor_scalar(out=xb[:], in0=xt[s][:], scalar1=rs[:, 0:1], scalar2=None, op0=ALU.mult))
        xfree[s] = t_xb
        P.PE.wait(t_xb, P.ident_tok, t_prev_z)
        for kc in range(KC):
            bk = kc // 8
            o = (kc % 8) * 128
            tt = P.PE.op(lambda e, kc=kc, bk=bk, o=o: e.transpose(out=P.bank_bf(bk)[:, o:o + 128], in_=xb[:, kc * 128:(kc + 1) * 128], identity=P.ident[:]),
                         pub=(kc == KC - 1))
        t_xb_free = tt
        P.DVE.wait(tt)
        P.DVE.op(lambda e: e.tensor_copy(out=xnT[:, 0:8, :], in_=P.bank_bf(0)[:, 0:1024]), pub=False)
        t_ev = P.DVE.op(lambda e: e.tensor_copy(out=xnT[:, 8:16, :], in_=P.bank_bf(1)[:, 0:1024]))
        P.PE.wait(t_ev, t_w, t_z_free)
        for gi, (c0, c1) in enumerate([(0, 512), (512, 1024), (1024, NL)]):
            for kc in range(KC):
                tz = P.PE.op(lambda e, gi=gi, c0=c0, c1=c1, kc=kc: e.matmul(
                    P.bank(2 + gi)[:, 0:c1 - c0], lhsT=xnT[:, kc, :], rhs=wb[:, kc, c0:c1],
                    start=(kc == 0), stop=(kc == KC - 1)), pub=(gi == 2 and kc == KC - 1))
        t_prev_z = tz
        P.ACT.wait(tz)
        t_s1 = P.ACT.op(lambda e: e.activation(out=junk[:, 0:512], in_=P.bank(2)[:, :], func=AF.Square, accum_out=ss[:, 1:2]))
        t_s2 = P.ACT.op(lambda e: e.activation(out=junk[:, 512:1024], in_=P.bank(3)[:, :], func=AF.Square, accum_out=ss[:, 2:3]))
        t_r1 = emit_rstd(P, ss[:, 1:3], A_LORA, eps_t, eps_tok, rs[:, 1:3], t_s2)
        P.ACT.wait(t_r1, t_prev_lt)
        P.ACT.op(lambda e: e.activation(out=latb[:, 0:512], in_=P.bank(2)[:, :], func=AF.Copy, scale=rs[:, 1:2]), pub=False)
        t_lb = P.ACT.op(lambda e: e.activation(out=latb[:, 512:1024], in_=P.bank(3)[:, :], func=AF.Copy, scale=rs[:, 2:3]))
        P.DVE.wait(tz, t_out2)
        t_kp = P.DVE.op(lambda e: e.tensor_copy(out=kpt[:], in_=P.bank(4)[:, 0:A_ROPE]))
        t_z_free = [t_lb, t_kp]
        P.SP.wait(t_kp)
        t_out2 = osem2.issue(P.SP, lambda e, b=b: e.dma_start(out=kpe[b * 128:(b + 1) * 128, :], in_=kpt[:]))
        P.PE.wait(t_lb, t_evac_lt)
        for j in range(8):
            tl = P.PE.op(lambda e, j=j: e.transpose(out=P.bank_bf(5)[:, j * 128:(j + 1) * 128], in_=latb[:, j * 128:(j + 1) * 128], identity=P.ident[:]),
                         pub=(j == 7))
        t_prev_lt = tl
        P.DVE.wait(tl, t_out1)
        t_evac_lt = P.DVE.op(lambda e: e.tensor_copy(out=latTt[:].rearrange("p j t -> p (j t)"), in_=P.bank_bf(5)[:, 0:1024]))
        P.SP.wait(t_evac_lt)
        t_out1 = osem1.issue(P.SP, lambda e, b=b: e.dma_start(
            out=latT.rearrange("(j p) t -> p j t", p=128)[:, :, b * 128:(b + 1) * 128], in_=latTt[:]))
    P.out_toks += [t_out1, t_out2]
    return P.finish()


def emit_rope_tables(P, pos_i, pos_tok, cs):
    R2 = A_ROPE // 2
    invf = (np.float32(10000.0) ** (-(np.arange(0, A_ROPE, 2, dtype=np.float32)) / np.float32(A_ROPE))).astype(np.float32)
    posf = P.sb("posf", [128, NBLK], F32)
    CB = 16
    u = P.sb("rt_u", [128, CB, R2], F32)
    tt = P.sb("rt_t", [128, CB, R2], F32)
    ki = P.sb("rt_ki", [128, CB, R2], I32)
    kf = P.sb("rt_kf", [128, CB, R2], F32)
    r = P.sb("rt_r", [128, CB, R2], F32)
    V = P.DVE
    V.wait(pos_tok)
    V.op(lambda e: e.tensor_copy(out=posf[:], in_=pos_i[:]), pub=False)
    C1 = 6.28125
    C2 = float(np.float32(TWO_PI - C1))
    last = None
    for ch in range(NBLK // CB):
        b0 = ch * CB
        for which in range(2):
            for i in range(R2):
                V.op(lambda e, i=i, b0=b0: e.tensor_scalar(out=u[:, :, i], in0=posf[:, b0:b0 + CB], scalar1=float(invf[i]), scalar2=None, op0=ALU.mult), pub=False)
            if which == 0:
                V.op(lambda e: e.tensor_scalar(out=u[:], in0=u[:], scalar1=float(math.pi / 2), scalar2=None, op0=ALU.add), pub=False)
            V.op(lambda e: e.tensor_scalar(out=tt[:], in0=u[:], scalar1=float(1.0 / TWO_PI), scalar2=None, op0=ALU.mult), pub=False)
            V.op(lambda e: e.tensor_copy(out=ki[:], in_=tt[:]), pub=False)
            V.op(lambda e: e.tensor_copy(out=kf[:], in_=ki[:]), pub=False)
            V.op(lambda e: e.scalar_tensor_tensor(out=r[:], in0=kf[:], scalar=-C1, in1=u[:], op0=ALU.mult, op1=ALU.add), pub=False)
            V.op(lambda e: e.scalar_tensor_tensor(out=r[:], in0=kf[:], scalar=-C2, in1=r[:], op0=ALU.mult, op1=ALU.add), pub=False)
            V.op(lambda e: e.tensor_scalar(out=tt[:], in0=r[:], scalar1=float(math.pi), scalar2=-TWO_PI, op0=ALU.is_gt, op1=ALU.mult), pub=False)
            V.op(lambda e: e.tensor_tensor(out=r[:], in0=r[:], in1=tt[:], op=ALU.add), pub=False)
            V.op(lambda e: e.tensor_scalar(out=tt[:], in0=r[:], scalar1=float(-math.pi), scalar2=TWO_PI, op0=ALU.is_lt, op1=ALU.mult), pub=False)
            V.op(lambda e: e.tensor_tensor(out=r[:], in0=r[:], in1=tt[:], op=ALU.add), pub=False)
            tr = V.op(lambda e: e.tensor_scalar(out=r[:], in0=r[:], scalar1=float(-math.pi), scalar2=float(math.pi), op0=ALU.max, op1=ALU.min))
            P.ACT.wait(tr)
            ta = P.ACT.op(lambda e, b0=b0, which=which: e.activation(out=cs[:, b0:b0 + CB, which * R2:(which + 1) * R2], in_=r[:], func=AF.Sin))
            V.wait(ta)
            last = ta
    return last


def emit_rope(P, src, dst, cs, blk, tmp):
    V = P.DVE
    h = A_ROPE // 2
    cos = cs[:, blk, 0:h]
    sin = cs[:, blk, h:2 * h]
    V.op(lambda e: e.tensor_tensor(out=tmp[:, 0:h], in0=src[:, 0:h], in1=cos, op=ALU.mult), pub=False)
    V.op(lambda e: e.tensor_tensor(out=tmp[:, h:2 * h], in0=src[:, h:2 * h], in1=sin, op=ALU.mult), pub=False)
    V.op(lambda e: e.tensor_tensor(out=tmp[:, 2 * h:3 * h], in0=src[:, 0:h], in1=sin, op=ALU.mult), pub=False)
    V.op(lambda e: e.tensor_tensor(out=tmp[:, 3 * h:4 * h], in0=src[:, h:2 * h], in1=cos, op=ALU.mult), pub=False)
    V.op(lambda e: e.tensor_tensor(out=dst[:, 0:h], in0=tmp[:, 0:h], in1=tmp[:, h:2 * h], op=ALU.subtract), pub=False)
    return V.op(lambda e: e.tensor_tensor(out=dst[:, h:2 * h], in0=tmp[:, 2 * h:3 * h], in1=tmp[:, 3 * h:4 * h], op=ALU.add))


def emit_absmax_bcast(P, src_dram, n, out, dsem, tmp):
    t = dsem.issue(P.SP, lambda e: e.dma_start(out=tmp[:, 0:n], in_=src_dram.partition_broadcast(128)))
    P.DVE.wait(t)
    return P.DVE.op(lambda e: e.tensor_reduce(out=out, in_=tmp[:, 0:n], axis=AX.X, op=ALU.max, apply_absolute_value=True))


class AttnState:
    pass


class _Stop(Exception):
    pass


DBG_STEP = [0]
DBG_SKIP = [0]
DBG_VAR = [0]


def _chk(n):
    if DBG_STEP[0] == n:
        if DBG_SKIP[0] > 0:
            DBG_SKIP[0] -= 1
            return
        raise _Stop()


def emit_attention_group(P, st, g, qk_parts, vb, dvp, nacc_banks, acc_bank0, s_banks, exp_scale, bias_fn,
                         alibi=None, mask_pool=True, hooks=None):
    PE, ACT, DVE, POOL = P.PE, P.ACT, P.DVE, P.POOL
    per_bank = 4 // nacc_banks
    nk = 4 * g + 4
    nslots = len(s_banks)

    def acc(qb):
        bk = acc_bank0 + qb // per_bank
        o = (qb % per_bank) * dvp
        return P.bank(bk)[:, o:o + dvp]

    first_in_bank = [True] * nacc_banks
    PE.wait(st.q_ready, st.acc_free)
    pend = []
    t_last = None

    def emit_pv(j, slot, r, t_p):
        nonlocal t_last
        PE.wait(t_p)
        for qb in range(r, 4):
            bk = qb // per_bank
            stt = first_in_bank[bk]
            first_in_bank[bk] = False
            last = (qb == 3)
            tk = PE.op(lambda e, qb=qb, j=j, slot=slot, stt=stt: e.matmul(
                acc(qb), lhsT=st.pT[slot][:, qb * 128:(qb + 1) * 128], rhs=vb[:, j, 0:dvp],
                start=stt, stop=(j == nk - 1), skip_group_check=True), pub=last)
        st.p_free[slot] = tk
        t_last = tk

    for j in range(nk):
        r = max(0, j - 4 * g)
        c0 = r * 128
        slot = st.it % nslots
        st.it += 1
        PE.wait(st.s_free[slot])
        for pi, (ktf, qt) in enumerate(qk_parts):
            ts = PE.op(lambda e, ktf=ktf, qt=qt, j=j, slot=slot, c0=c0, pi=pi: e.matmul(
                P.bank(s_banks[slot])[:, c0:512], lhsT=ktf(j), rhs=qt[:, c0:512],
                start=(pi == 0), stop=(pi == len(qk_parts) - 1)), pub=(pi == len(qk_parts) - 1))
        if len(pend) >= nslots - 1:
            emit_pv(*pend.pop(0))
        if hooks and j in hooks:
            for hk in hooks[j]:
                hk()
        src = P.bank(s_banks[slot])
        if alibi is not None:
            sbt = alibi["sbuf"][slot]
            DVE.wait(ts, st.sb_free[slot])
            if r < 4 and j >= 4 * g:
                DVE.op(lambda e, c0=c0, src=src, sbt=sbt: e.tensor_tensor(out=sbt[:, c0:c0 + 128], in0=src[:, c0:c0 + 128], in1=alibi["Tdiag"][:], op=ALU.add), indep=True)
                if c0 + 128 < 512:
                    td = DVE.op(lambda e, c0=c0, src=src, sbt=sbt: e.tensor_tensor(out=sbt[:, c0 + 128:512], in0=src[:, c0 + 128:512], in1=alibi["T2"][:, c0 + 128:512], op=ALU.add), indep=True)
                else:
                    td = (DVE.sem, DVE.count)
            else:
                td = DVE.op(lambda e, src=src, sbt=sbt: e.tensor_tensor(out=sbt[:], in0=src[:], in1=alibi["T2"][:], op=ALU.add), indep=True)
            st.s_free[slot] = td
            ACT.wait(td, st.p_free[slot])
            if j >= 4 * g:
                ACT.op(lambda e, c0=c0, sbt=sbt, slot=slot: e.activation(out=st.pT[slot][:, c0:c0 + 128], in_=sbt[:, c0:c0 + 128], func=AF.Exp, bias=alibi["negM"][:], scale=exp_scale), indep=True)
                if c0 + 128 < 512:
                    tp = ACT.op(lambda e, c0=c0, sbt=sbt, slot=slot, j=j: e.activation(out=st.pT[slot][:, c0 + 128:512], in_=sbt[:, c0 + 128:512], func=AF.Exp, bias=bias_fn(j), scale=exp_scale), indep=True)
                else:
                    tp = (ACT.sem, ACT.count)
            else:
                tp = ACT.op(lambda e, sbt=sbt, slot=slot, j=j: e.activation(out=st.pT[slot][:], in_=sbt[:], func=AF.Exp, bias=bias_fn(j), scale=exp_scale), indep=True)
            st.sb_free[slot] = tp
        else:
            ACT.wait(ts, st.p_free[slot])
            tp = ACT.op(lambda e, c0=c0, src=src, slot=slot, j=j: e.activation(out=st.pT[slot][:, c0:512], in_=src[:, c0:512], func=AF.Exp, bias=bias_fn(j), scale=exp_scale), indep=True)
            st.s_free[slot] = tp
            if j >= 4 * g:
                POOL.wait(tp)
                tp = POOL.op(lambda e, c0=c0, slot=slot: e.memset(st.pT[slot][64:128, c0:c0 + 64], 0.0), indep=True)
        pend.append((j, slot, r, tp))
    while pend:
        emit_pv(*pend.pop(0))
    return t_last


def build_A2(NG=S // 512, NH=2, dbg=0, G0=0):
    P = Prog()
    latT = P.dram("latT", [2 * A_LORA, S], BF16, "ExternalInput")
    kpe = P.dram("kpe", [S, A_ROPE], F32, "ExternalInput")
    pos = P.dram("pos", [128, NBLK], I32, "ExternalInput")
    wq = P.dram("wq", [A_LORA, 2 * A_QK], F32, "ExternalInput")
    wkv = P.dram("wkv", [A_LORA, 2 * (A_NOPE + A_V)], F32, "ExternalInput")
    gq = P.dram("gq", [128, 4], F32, "ExternalInput")
    gkv = P.dram("gkv", [128, 4], F32, "ExternalInput")
    qgain = P.dram("qgain", [1, A_QK], F32, "ExternalInput")
    kgain = P.dram("kgain", [1, A_QK], F32, "ExternalInput")
    o = P.dram("o", [S, 2 * A_V], F32, "ExternalOutput")
    PE, ACT, DVE, POOL, SP = P.PE, P.ACT, P.DVE, P.POOL, P.SP
    P.make_ident()
    eps_t, eps_tok = P.const_tile(EPS, "eps")
    mhalf, mhalf_tok = P.const_tile(-0.5, "mhalf")
    ds = P.dsem()
    gq_t = P.sb("gq_t", [128, 4], F32)
    gkv_t = P.sb("gkv_t", [128, 4], F32)
    SP_tok = ds.issue(SP, lambda e: e.dma_start(out=gq_t[:], in_=gq))
    SP_tok = ds.issue(SP, lambda e: e.dma_start(out=gkv_t[:], in_=gkv))
    qg_t = P.sb("qg_t", [128, A_QK], F32)
    kg_t = P.sb("kg_t", [128, A_QK], F32)
    ds.issue(SP, lambda e: e.dma_start(out=qg_t[:], in_=qgain.partition_broadcast(128)))
    t_small = ds.issue(SP, lambda e: e.dma_start(out=kg_t[:], in_=kgain.partition_broadcast(128)))
    pos_i = P.sb("pos_i", [128, NBLK], I32)
    t_pos = ds.issue(SP, lambda e: e.dma_start(out=pos_i[:], in_=pos))
    t_small = t_pos
    wq_b = P.sb("wq_b", [128, 4, 2 * A_QK], BF16)
    wkv_b = P.sb("wkv_b", [128, 4, 512], BF16)
    stage, ssems, sstate = _stage(P, 512, n=2)
    t_wq = P.load_weight_bf16(wq_b, wq, 4, 2 * A_QK, gq_t, t_pos, stage, ssems, sstate)
    t_wkv = P.load_weight_bf16(wkv_b, wkv, 4, 512, gkv_t, t_pos, stage, ssems, sstate)
    mq = P.sb("mq", [128, 2], F32)
    negM = P.sb("negM", [128, 1], F32)
    DVE.wait(t_small)
    DVE.op(lambda e: e.tensor_reduce(out=mq[:, 0:1], in_=qg_t[:], axis=AX.X, op=ALU.max, apply_absolute_value=True), pub=False)
    t_mk = DVE.op(lambda e: e.tensor_reduce(out=mq[:, 1:2], in_=kg_t[:], axis=AX.X, op=ALU.max, apply_absolute_value=True))
    DVE.wait(t_mk)
    t_negM = DVE.op(lambda e: e.scalar_tensor_tensor(out=negM[:], in0=mq[:, 0:1], scalar=-math.sqrt(A_QK), in1=mq[:, 1:2], op0=ALU.mult, op1=ALU.mult))
    cs = P.sb("cs", [128, NBLK, A_ROPE], F32)
    t_cs = emit_rope_tables(P, pos_i, t_pos, cs)

    KTn = P.sb("KTn", [128, S], BF16)
    KTr = P.sb("KTr", [128, S], BF16)
    dvp = A_V + 16
    vb = P.sb("vb", [128, NBLK, dvp], BF16)
    t_ones = POOL.op(lambda e: e.memset(vb[:, :, A_V:dvp], 1.0))
    latg = [P.sb("latg%d" % i, [128, 4, 512], BF16) for i in range(2)]
    latsem = [P.dsem() for _ in range(2)]
    latfree = [None, None]
    kpg = [P.sb("kpg%d" % i, [128, 4, A_ROPE], F32) for i in range(2)]
    kpsem = [P.dsem() for _ in range(2)]
    kpfree = [None, None]
    full4 = P.sb("full4", [128, 4, A_QK], F32)
    fn4 = P.sb("fn4", [128, 4, A_QK], F32)
    junk4 = P.sb("junk4", [128, 4, A_QK], F32)
    rt4 = P.sb("rt4", [128, 4, 128], F32)
    nb4 = P.sb("nb4", [128, 4, 256], BF16)
    t_nbz = POOL.op(lambda e: e.memset(nb4[:, :, A_QK:256], 0.0))
    ss = P.sb("ss", [128, 16], F32)
    QTn = [P.sb("QTn%d" % i, [128, 512], BF16) for i in range(2)]
    QTr = [P.sb("QTr%d" % i, [128, 512], BF16) for i in range(2)]
    st = AttnState()
    st.pT = [P.sb("pT%d" % i, [128, 512], BF16) for i in range(3)]
    st.p_free = [None, None, None]
    st.s_free = [None, None, None]
    st.it = 0
    st.acc_free = None
    st.q_ready = None
    osb = [P.sb("osb%d" % i, [128, 4, A_V], F32) for i in range(2)]
    osem = [P.dsem() for _ in range(2)]
    rec = P.sb("rec", [128, 4], F32)
    BK_P0, BK_TN = 4, 6
    TR_OFF = 512
    tok = {"p_free": None, "full_free": None, "nb_free": None, "tr_free": None}
    H2 = A_ROPE // 2

    def proj_part1(latt, wsel, ncol, is_k, kp, gain_t, blk0):
        PE.wait(tok["p_free"])
        for b in range(4):
            bk = BK_P0 + b // 2
            off = (b % 2) * ncol
            for kc in range(4):
                tp = PE.op(lambda e, b=b, bk=bk, off=off, kc=kc: e.matmul(
                    P.bank(bk)[:, off:off + ncol], lhsT=latt[:, kc, b * 128:(b + 1) * 128], rhs=wsel(kc),
                    start=(kc == 0), stop=(kc == 3), skip_group_check=True))
        DVE.wait(tp, tok["full_free"])
        for h2 in range(2):
            src = P.bank(BK_P0 + h2)[:, 0:2 * ncol].rearrange("p (b c) -> p b c", b=2)
            if is_k:
                DVE.op(lambda e, h2=h2, src=src: e.tensor_copy(out=full4[:, 2 * h2:2 * h2 + 2, 0:A_NOPE], in_=src[:, :, 0:A_NOPE]))
                DVE.op(lambda e, h2=h2, src=src: e.tensor_copy(out=vb[:, blk0 + 2 * h2:blk0 + 2 * h2 + 2, 0:A_V], in_=src[:, :, A_NOPE:A_NOPE + A_V]))
            else:
                DVE.op(lambda e, h2=h2, src=src: e.tensor_copy(out=full4[:, 2 * h2:2 * h2 + 2, :], in_=src[:, :, 0:A_QK]))
        tok["p_free"] = (DVE.sem, DVE.count)
        if is_k:
            DVE.op(lambda e: e.tensor_copy(out=full4[:, :, A_NOPE:A_QK], in_=kp[:]))
        DVE.op(lambda e: e.tensor_tensor(out=junk4[:], in0=full4[:], in1=full4[:], op=ALU.mult))
        t_ss = DVE.op(lambda e: e.tensor_reduce(out=ss[:, 0:4], in_=junk4[:], axis=AX.X, op=ALU.add))
        ACT.wait(t_ss, eps_tok)
        ACT.op(lambda e: e.activation(out=ss[:, 4:8], in_=ss[:, 0:4], func=AF.Ln, bias=eps_t[:], scale=1.0 / A_QK))
        t_rs = ACT.op(lambda e: e.activation(out=ss[:, 8:12], in_=ss[:, 4:8], func=AF.Exp, scale=-0.5))
        DVE.wait(t_rs, tok["nb_free"])
        for b in range(4):
            DVE.op(lambda e, b=b: e.scalar_tensor_tensor(out=fn4[:, b, :], in0=full4[:, b, :], scalar=ss[:, 8 + b:9 + b], in1=gain_t[:], op0=ALU.mult, op1=ALU.mult))
        DVE.op(lambda e: e.tensor_copy(out=nb4[:, :, 0:A_NOPE], in_=fn4[:, :, 0:A_NOPE]))
        cos = cs[:, blk0:blk0 + 4, 0:H2]
        sin = cs[:, blk0:blk0 + 4, H2:2 * H2]
        x1 = fn4[:, :, A_NOPE:A_NOPE + H2]
        x2 = fn4[:, :, A_NOPE + H2:A_QK]
        DVE.op(lambda e: e.tensor_tensor(out=rt4[:, :, 0:H2], in0=x1, in1=cos, op=ALU.mult))
        DVE.op(lambda e: e.tensor_tensor(out=rt4[:, :, H2:2 * H2], in0=x2, in1=sin, op=ALU.mult))
        DVE.op(lambda e: e.tensor_tensor(out=rt4[:, :, 2 * H2:3 * H2], in0=x1, in1=sin, op=ALU.mult))
        DVE.op(lambda e: e.tensor_tensor(out=rt4[:, :, 3 * H2:4 * H2], in0=x2, in1=cos, op=ALU.mult))
        DVE.op(lambda e: e.tensor_tensor(out=nb4[:, :, A_NOPE:A_NOPE + H2], in0=rt4[:, :, 0:H2], in1=rt4[:, :, H2:2 * H2], op=ALU.subtract))
        t_nb = DVE.op(lambda e: e.tensor_tensor(out=nb4[:, :, A_NOPE + H2:A_QK], in0=rt4[:, :, 2 * H2:3 * H2], in1=rt4[:, :, 3 * H2:4 * H2], op=ALU.add))
        tok["full_free"] = t_nb
        return t_nb

    def proj_part2(t_nb, dstT_n, dstT_r, tcol0, extra_wait=None):
        PE.wait(t_nb, tok["tr_free"])
        for b in range(4):
            PE.op(lambda e, b=b: e.transpose(out=P.bank_bf(BK_TN)[:, b * 128:(b + 1) * 128], in_=nb4[:, b, 0:A_NOPE], identity=P.ident[:]))
        for b in range(4):
            t_tr = PE.op(lambda e, b=b: e.transpose(out=P.bank_bf(BK_TN)[:, TR_OFF + b * 128:TR_OFF + (b + 1) * 128], in_=nb4[:, b, A_NOPE:256], identity=P.ident[:]))
        tok["nb_free"] = t_tr
        DVE.wait(t_tr, extra_wait)
        DVE.op(lambda e: e.tensor_copy(out=dstT_n[:, tcol0:tcol0 + 512], in_=P.bank_bf(BK_TN)[:, 0:512]))
        t_e = DVE.op(lambda e: e.tensor_copy(out=dstT_r[:, tcol0:tcol0 + 512], in_=P.bank_bf(BK_TN)[:, TR_OFF:TR_OFF + 512]))
        tok["tr_free"] = t_e
        return t_e

    DVE.wait(t_cs, t_negM)
    PE.wait(t_wq, t_wkv, P.ident_tok)
    POOL.wait(t_ones)
    PE.wait(t_nbz)
    if dbg == 1:
        SP.wait((DVE.sem, DVE.count), (ACT.sem, ACT.count), (POOL.sem, POOL.count))
        return P.finish()
    out_tok = [None, None]
    oi = 0
    attn_done_prev_head = None
    lat_it = [0]

    def load_lat(row0, g, with_kpe):
        s = lat_it[0] % 2
        lat_it[0] += 1
        SP.wait(latfree[s], kpfree[s] if with_kpe else None)
        t_l = latsem[s].issue(SP, lambda e, s=s, g=g: e.dma_start(
            out=latg[s][:], in_=latT[row0:row0 + A_LORA, g * 512:(g + 1) * 512].rearrange("(kc p) t -> p kc t", p=128)))
        t_k = None
        if with_kpe:
            t_k = kpsem[s].issue(SP, lambda e, s=s, g=g: e.dma_start(
                out=kpg[s][:], in_=kpe[g * 512:(g + 1) * 512, :].rearrange("(tb p) d -> p tb d", p=128)))
        return s, t_l, t_k

    for hh in range(NH):
        t_kv_last = None
        for g in range(NG):
            s, t_l, t_k = load_lat(A_LORA, g, True)
            PE.wait(t_l)
            DVE.wait(t_k)
            if g == 0:
                DVE.wait(attn_done_prev_head)
            t_nb = proj_part1(latg[s], lambda kc, hh=hh: wkv_b[:, kc, hh * 256:(hh + 1) * 256], 256, True, kpg[s], kg_t, g * 4)
            latfree[s] = (PE.sem, PE.count)
            kpfree[s] = t_nb
            t_kv_last = proj_part2(t_nb, KTn, KTr, g * 512)
        kv_ready = t_kv_last
        if dbg == 2:
            SP.wait(kv_ready)
            return P.finish()
        q_tok = {}
        q_pend = {}
        lat_pre = {}

        def q_prefetch(g):
            if g < NG and g not in lat_pre:
                lat_pre[g] = load_lat(0, g, False)

        def q_part1(g):
            q_prefetch(g)
            s, t_l, _ = lat_pre[g]
            PE.wait(t_l)
            q_pend[g] = proj_part1(latg[s], lambda kc, hh=hh: wq_b[:, kc, hh * A_QK:(hh + 1) * A_QK], A_QK, False, None, qg_t, g * 4)
            latfree[s] = (PE.sem, PE.count)
            q_prefetch(g + 1)

        def q_part2(g):
            qs = g % 2
            q_tok[g] = proj_part2(q_pend[g], QTn[qs], QTr[qs], 0, q_tok.get(("free", qs)))

        q_part1(G0)
        q_part2(G0)
        for g in range(G0, NG):
            qs = g % 2
            st.q_ready = [q_tok[g], kv_ready]
            parts = [(lambda j: KTn[:, j * 128:(j + 1) * 128], QTn[qs]),
                     (lambda j: KTr[:, j * 128:(j + 1) * 128], QTr[qs])]
            hooks = {}
            if g + 1 < NG:
                nk = 4 * g + 4
                hooks[0] = [lambda g=g: q_part1(g + 1)]
                hooks.setdefault(min(nk - 1, 8), []).append(lambda g=g: q_part2(g + 1))
            t_acc = emit_attention_group(P, st, g, parts, vb, dvp, 2, 2, [0, 1, 7], 1.0 / math.sqrt(A_QK),
                                         lambda j: negM[:], hooks=hooks)
            q_tok[("free", qs)] = t_acc
            ob = osb[oi % 2]
            DVE.wait(t_acc, out_tok[oi % 2])
            for qb in range(4):
                bk = 2 + qb // 2
                off = (qb % 2) * dvp
                DVE.op(lambda e, qb=qb, bk=bk, off=off: e.reciprocal(out=rec[:, qb:qb + 1], in_=P.bank(bk)[:, off + A_V:off + A_V + 1]))
            for qb in range(4):
                bk = 2 + qb // 2
                off = (qb % 2) * dvp
                t_on = DVE.op(lambda e, qb=qb, bk=bk, off=off, ob=ob: e.tensor_scalar(out=ob[:, qb, :], in0=P.bank(bk)[:, off:off + A_V], scalar1=rec[:, qb:qb + 1], scalar2=None, op0=ALU.mult))
            st.acc_free = t_on
            SP.wait(t_on)
            out_tok[oi % 2] = osem[oi % 2].issue(SP, lambda e, g=g, hh=hh, ob=ob: e.dma_start(
                out=o[g * 512:(g + 1) * 512, hh * A_V:(hh + 1) * A_V].rearrange("(qb p) d -> p qb d", p=128), in_=ob[:]))
            oi += 1
            attn_done_prev_head = t_acc
    P.out_toks += [t for t in out_tok if t is not None]
    return P.finish()


def build_B2(NG=S // 512, dbg=0):
    P = Prog()
    xnT = P.dram("xnT", [D, S], BF16, "ExternalInput")
    pos = P.dram("pos", [128, NBLK], I32, "ExternalInput")
    posrow = P.dram("posrow", [1, 512], I32, "ExternalInput")
    posg = P.dram("posg", [1, S // 512], I32, "ExternalInput")
    wq = P.dram("wq", [D, 2 * B_HD], F32, "ExternalInput")
    wkv = P.dram("wkv", [D, 2 * B_HD + B_V], F32, "ExternalInput")
    gn = P.dram("g_norm", [128, D // 128], F32, "ExternalInput")
    qgain = P.dram("qgain", [1, B_HD], F32, "ExternalInput")
    kgain = P.dram("kgain", [1, B_HD], F32, "ExternalInput")
    lam4 = P.dram("lam4", [4, B_HD], F32, "ExternalInput")
    subln = P.dram("subln", [1, B_V], F32, "ExternalInput")
    slope = P.dram("slope", [1, 1], F32, "ExternalInput")
    o = P.dram("o", [S, B_V], F32, "ExternalOutput")
    PE, ACT, DVE, POOL, SP = P.PE, P.ACT, P.DVE, P.POOL, P.SP
    KC = D // 128
    LAM_INIT = 0.8 - 0.6 * math.exp(-0.3 * 1)
    SQ = math.sqrt(B_HD)
    P.make_ident()
    eps_t, eps_tok = P.const_tile(EPS, "eps")
    ds = P.dsem()
    gcol = P.sb("gcol", [128, KC], F32)
    qg_t = P.sb("qg_t", [128, B_HD], F32)
    kg_t = P.sb("kg_t", [128, B_HD], F32)
    lam_t = P.sb("lam_t", [128, 4, B_HD], F32)
    sub_t = P.sb("sub_t", [128, B_V], F32)
    slope_t = P.sb("slope_t", [128, 1], F32)
    pos_i = P.sb("pos_i", [128, NBLK], I32)
    posg_i = P.sb("posg_i", [128, S // 512], I32)
    ds.issue(SP, lambda e: e.dma_start(out=gcol[:], in_=gn))
    ds.issue(SP, lambda e: e.dma_start(out=qg_t[:], in_=qgain.partition_broadcast(128)))
    ds.issue(SP, lambda e: e.dma_start(out=kg_t[:], in_=kgain.partition_broadcast(128)))
    for i in range(4):
        ds.issue(SP, lambda e, i=i: e.dma_start(out=lam_t[:, i, :], in_=lam4[i:i + 1, :].partition_broadcast(128)))
    ds.issue(SP, lambda e: e.dma_start(out=sub_t[:], in_=subln.partition_broadcast(128)))
    ds.issue(SP, lambda e: e.dma_start(out=slope_t[:], in_=slope.partition_broadcast(128)))
    ds.issue(SP, lambda e: e.dma_start(out=posg_i[:], in_=posg.partition_broadcast(128)))
    t_set = ds.issue(SP, lambda e: e.dma_start(out=pos_i[:], in_=pos))
    sbufs = [P.sb("sbias%d" % i, [128, 512], F32) for i in range(4)]
    prow_i = sbufs[0][:].bitcast(I32)
    prow_f = sbufs[1]
    ds2 = P.dsem()
    t_prow = ds2.issue(SP, lambda e: e.dma_start(out=prow_i, in_=posrow.partition_broadcast(128)))
    small = P.sb("small", [128, 16], F32)
    posf = P.sb("posf", [128, NBLK], F32)
    posgf = P.sb("posgf", [128, S // 512], F32)
    T2 = P.sb("T2", [128, 512], F32)
    Tdiag = P.sb("Tdiag", [128, 128], F32)
    tmpd = P.sb("tmpd", [128, 128], F32)
    negM = small[:, 0:1]
    nslope_s = small[:, 1:2]
    neglam = small[:, 2:3]
    DVE.wait(t_set, t_prow)
    DVE.op(lambda e: e.tensor_reduce(out=small[:, 3:4], in_=qg_t[:], axis=AX.X, op=ALU.max, apply_absolute_value=True))
    DVE.op(lambda e: e.tensor_reduce(out=small[:, 4:5], in_=kg_t[:], axis=AX.X, op=ALU.max, apply_absolute_value=True))
    DVE.op(lambda e: e.scalar_tensor_tensor(out=negM, in0=small[:, 3:4], scalar=-SQ, in1=small[:, 4:5], op0=ALU.mult, op1=ALU.mult))
    DVE.op(lambda e: e.tensor_scalar(out=nslope_s, in0=slope_t[:], scalar1=-SQ, scalar2=None, op0=ALU.mult))
    DVE.op(lambda e: e.tensor_tensor(out=tmpd[:], in0=lam_t[:, 0, :], in1=lam_t[:, 1, :], op=ALU.mult))
    DVE.op(lambda e: e.tensor_reduce(out=small[:, 5:6], in_=tmpd[:], axis=AX.X, op=ALU.add))
    DVE.op(lambda e: e.tensor_tensor(out=tmpd[:], in0=lam_t[:, 2, :], in1=lam_t[:, 3, :], op=ALU.mult))
    t_l = DVE.op(lambda e: e.tensor_reduce(out=small[:, 6:7], in_=tmpd[:], axis=AX.X, op=ALU.add))
    ACT.wait(t_l)
    t_e = ACT.op(lambda e: e.activation(out=small[:, 7:9], in_=small[:, 5:7], func=AF.Exp))
    DVE.wait(t_e)
    DVE.op(lambda e: e.tensor_tensor(out=neglam, in0=small[:, 8:9], in1=small[:, 7:8], op=ALU.subtract))
    DVE.op(lambda e: e.tensor_scalar(out=neglam, in0=neglam, scalar1=-LAM_INIT, scalar2=None, op0=ALU.add))
    DVE.op(lambda e: e.tensor_scalar(out=sub_t[:], in0=sub_t[:], scalar1=1.0 - LAM_INIT, scalar2=None, op0=ALU.mult))
    DVE.op(lambda e: e.tensor_copy(out=posf[:], in_=pos_i[:]))
    DVE.op(lambda e: e.tensor_copy(out=posgf[:], in_=posg_i[:]))
    DVE.op(lambda e: e.tensor_copy(out=prow_f[:], in_=prow_i))
    DVE.op(lambda e: e.tensor_scalar(out=T2[:], in0=prow_f[:], scalar1=prow_f[:, 0:1], scalar2=nslope_s, op0=ALU.subtract, op1=ALU.mult))
    DVE.op(lambda e: e.tensor_scalar(out=Tdiag[:], in0=prow_f[:, 0:128], scalar1=posf[:, 0:1], scalar2=None, op0=ALU.subtract))
    DVE.op(lambda e: e.tensor_scalar(out=tmpd[:], in0=Tdiag[:], scalar1=-1.0, scalar2=None, op0=ALU.mult))
    DVE.op(lambda e: e.tensor_tensor(out=Tdiag[:], in0=Tdiag[:], in1=tmpd[:], op=ALU.max))
    DVE.op(lambda e: e.tensor_scalar(out=Tdiag[:], in0=Tdiag[:], scalar1=nslope_s, scalar2=None, op0=ALU.mult))
    t_setup = DVE.op(lambda e: e.tensor_scalar(out=Tdiag[64:128, 0:64], in0=Tdiag[64:128, 0:64], scalar1=NEG_BIG, scalar2=None, op0=ALU.add))
    ACT.wait(t_setup)

    w_b = P.sb("w_b", [128, KC, 512], BF16)
    stage_all = P.sb("stage_all", [128, 1024], F32)
    stage = [stage_all[:, 0:512], stage_all[:, 512:1024]]
    ssems = [P.dsem() for _ in range(2)]
    sstate = {"i": 0, "free": [None, None]}
    t_w = P.load_weight_bf16(w_b, wkv, KC, 512, gcol, t_set, stage, ssems, sstate)
    DVE.wait(t_w)

    KT = [P.sb("KT%d" % c, [128, S], BF16) for c in range(2)]
    dvp = B_V + 2
    vb = P.sb("vb", [128, NBLK, dvp], BF16)
    POOL.op(lambda e: e.memset(vb[:, :, B_V:dvp], 1.0))
    GT = 256
    xg = [P.sb("xg%d" % i, [128, KC, GT], BF16) for i in range(2)]
    xsem = [P.dsem() for _ in range(2)]
    xfree = [None, None]
    ff4 = P.sb("ff4", [128, 4, 2 * B_HD], F32)
    junk4 = stage_all[:, :].rearrange("p (b d) -> p b d", b=4)
    nb4 = P.sb("nb4", [128, 4, 2 * B_HD], BF16)
    ss = P.sb("ss", [128, 32], F32)
    QT = [[P.sb("QT%d_%d" % (c, i), [128, 512], BF16) for i in range(2)] for c in range(2)]
    st = AttnState()
    st.pT = [P.sb("pT%d" % i, [128, 512], BF16) for i in range(4)]
    st.p_free = [None] * 4
    st.s_free = [None] * 4
    st.sb_free = [t_setup] * 4
    st.it = 0
    st.acc_free = None
    st.q_ready = None
    kbias = [P.sb("kbias%d" % i, [128, NBLK], F32) for i in range(2)]
    kb_free = [None, None]
    oc = [P.sb("oc%d" % c, [128, 4, B_V], F32) for c in range(2)]
    osem = P.dsem()
    rec = P.sb("rec", [128, 4], F32)
    S_BANKS = [0, 1, 6, 7]

    def take_slot():
        slot = st.it % len(S_BANKS)
        st.it += 1
        PE.wait(st.s_free[slot])
        return slot, S_BANKS[slot]
    tok = {"p_free": None, "ff_free": None, "nb_free": None, "tr_free": None}
    x_it = [0]

    def load_x(t0):
        s = x_it[0] % 2
        x_it[0] += 1
        SP.wait(xfree[s])
        t = xsem[s].issue(SP, lambda e, s=s, t0=t0: e.dma_start(
            out=xg[s][:], in_=xnT.rearrange("(kc p) t -> p kc t", p=128)[:, :, t0:t0 + GT]))
        return s, t

    def proj_part1(loads, ncol, gain_t, is_k, blk0):
        first = True
        pslot, BK_PROJ = take_slot()
        for li, (s, t) in enumerate(loads):
            PE.wait(t)
            for tb in range(GT // 128):
                b = li * (GT // 128) + tb
                PE.wait(tok["p_free"])
                for kc in range(KC):
                    tp = PE.op(lambda e, kc=kc, s=s, tb=tb: e.matmul(P.bank(BK_PROJ)[:, 0:ncol], lhsT=xg[s][:, kc, tb * 128:(tb + 1) * 128],
                                                                   rhs=w_b[:, kc, 0:ncol], start=(kc == 0), stop=(kc == KC - 1)))
                DVE.wait(tp, tok["ff_free"] if first else None)
                first = False
                DVE.op(lambda e, b=b: e.tensor_copy(out=ff4[:, b, :], in_=P.bank(BK_PROJ)[:, 0:2 * B_HD]))
                if is_k:
                    DVE.op(lambda e, b=b: e.tensor_copy(out=vb[:, blk0 + b, 0:B_V], in_=P.bank(BK_PROJ)[:, 2 * B_HD:2 * B_HD + B_V]))
                tok["p_free"] = (DVE.sem, DVE.count)
            xfree[s] = (PE.sem, PE.count)
        st.s_free[pslot] = tok["p_free"]
        DVE.op(lambda e: e.tensor_tensor(out=junk4, in0=ff4[:], in1=ff4[:], op=ALU.mult))
        t_ss = DVE.op(lambda e: e.tensor_reduce(out=ss[:, 0:8], in_=stage_all[:, :].rearrange("p (b d) -> p b d", b=8), axis=AX.X, op=ALU.add))
        ACT.wait(t_ss, eps_tok)
        ACT.op(lambda e: e.activation(out=ss[:, 8:16], in_=ss[:, 0:8], func=AF.Ln, bias=eps_t[:], scale=1.0 / B_HD))
        t_rs = ACT.op(lambda e: e.activation(out=ss[:, 16:24], in_=ss[:, 8:16], func=AF.Exp, scale=-0.5))
        DVE.wait(t_rs, tok["nb_free"])
        for b in range(4):
            for c in range(2):
                t_nb = DVE.op(lambda e, b=b, c=c: e.scalar_tensor_tensor(
                    out=nb4[:, b, c * B_HD:(c + 1) * B_HD], in0=ff4[:, b, c * B_HD:(c + 1) * B_HD],
                    scalar=ss[:, 16 + 2 * b + c:17 + 2 * b + c], in1=gain_t[:], op0=ALU.mult, op1=ALU.mult))
        tok["ff_free"] = t_nb
        return t_nb

    def proj_part2(t_nb, dstT, tcol0, extra_wait=None):
        tslot, BK_TR = take_slot()
        PE.wait(t_nb)
        for b in range(4):
            for c in range(2):
                t_tr = PE.op(lambda e, b=b, c=c, BK_TR=BK_TR: e.transpose(out=P.bank_bf(BK_TR)[:, (2 * b + c) * 128:(2 * b + c + 1) * 128],
                                                            in_=nb4[:, b, c * B_HD:(c + 1) * B_HD], identity=P.ident[:]))
        tok["nb_free"] = t_tr
        DVE.wait(t_tr, extra_wait)
        trv = P.bank_bf(BK_TR)[:, 0:1024].rearrange("p (b c d) -> p b c d", b=4, c=2)
        for c in range(2):
            t_e = DVE.op(lambda e, c=c: e.tensor_copy(out=dstT[c][:, tcol0:tcol0 + 512].rearrange("p (b d) -> p b d", b=4), in_=trv[:, :, c, :]))
        tok["tr_free"] = t_e
        st.s_free[tslot] = t_e
        return t_e

    PE.wait(t_w, P.ident_tok)
    t_kv = None
    for g in range(NG):
        loads = [load_x(g * 512 + i * GT) for i in range(512 // GT)]
        t_nb = proj_part1(loads, 512, kg_t, True, g * 4)
        t_kv = proj_part2(t_nb, KT, g * 512)
    kv_ready = t_kv
    ACT.wait((PE.sem, PE.count))
    SP.wait((DVE.sem, DVE.count))
    t_wq = P.load_weight_bf16(w_b, wq, KC, 2 * B_HD, gcol, t_set, stage, ssems, sstate)
    PE.wait(t_wq)
    DVE.wait(t_wq)
    q_tok = {}
    q_pend = {}
    x_pre = {}

    def q_prefetch(g):
        if g < NG and g not in x_pre:
            x_pre[g] = [load_x(g * 512 + i * GT) for i in range(512 // GT)]

    def q_part1(g):
        q_prefetch(g)
        q_pend[g] = proj_part1(x_pre[g], 2 * B_HD, qg_t, False, None)

    def q_part2(g):
        qs = g % 2
        q_tok[g] = proj_part2(q_pend[g], [QT[0][qs], QT[1][qs]], 0, q_tok.get(("free", qs)))

    out_tok = None
    q_part1(0)
    q_part2(0)
    for g in range(NG):
        qs = g % 2
        kb = kbias[g % 2]
        DVE.wait(kb_free[g % 2])
        DVE.op(lambda e, kb=kb, g=g: e.tensor_scalar(out=kb[:], in0=posf[:], scalar1=posgf[:, g:g + 1], scalar2=slope_t[:], op0=ALU.subtract, op1=ALU.mult))
        t_kb = DVE.op(lambda e, kb=kb: e.tensor_scalar(out=kb[:], in0=kb[:], scalar1=negM, scalar2=None, op0=ALU.add))
        ACT.wait(t_kb)
        for c in range(2):
            st.q_ready = [q_tok[g], kv_ready]
            parts = [(lambda j, c=c: KT[c][:, j * 128:(j + 1) * 128], QT[c][qs])]
            hooks = {}
            if c == 0 and g + 1 < NG:
                nk = 4 * g + 4
                hooks[0] = [lambda g=g: q_part1(g + 1)]
                hooks.setdefault(min(nk - 1, 10), []).append(lambda g=g: q_part2(g + 1))
            t_acc = emit_attention_group(P, st, g, parts, vb, dvp, 4, 2, S_BANKS, 1.0 / SQ,
                                         lambda j, kb=kb: kb[:, j:j + 1],
                                         alibi=dict(T2=T2, Tdiag=Tdiag, negM=negM, sbuf=sbufs), hooks=hooks)
            DVE.wait(t_acc, out_tok if c == 0 else None)
            for qb in range(4):
                DVE.op(lambda e, qb=qb: e.reciprocal(out=rec[:, qb:qb + 1], in_=P.bank(2 + qb)[:, B_V:B_V + 1]))
            for qb in range(4):
                t_on = DVE.op(lambda e, qb=qb, c=c: e.tensor_scalar(out=oc[c][:, qb, :], in0=P.bank(2 + qb)[:, 0:B_V], scalar1=rec[:, qb:qb + 1], scalar2=None, op0=ALU.mult))
            st.acc_free = t_on
        q_tok[("free", qs)] = t_acc
        kb_free[g % 2] = t_acc
        o0f = oc[0][:].rearrange("p a d -> p (a d)")
        o1f = oc[1][:].rearrange("p a d -> p (a d)")
        DVE.op(lambda e: e.scalar_tensor_tensor(out=o0f, in0=o1f, scalar=neglam, in1=o0f, op0=ALU.mult, op1=ALU.add))
        DVE.op(lambda e: e.tensor_tensor(out=o1f, in0=o0f, in1=o0f, op=ALU.mult))
        t_s4 = DVE.op(lambda e: e.tensor_reduce(out=ss[:, 24:28], in_=oc[1][:], axis=AX.X, op=ALU.add))
        ACT.wait(t_s4, eps_tok)
        ACT.op(lambda e: e.activation(out=ss[:, 24:28], in_=ss[:, 24:28], func=AF.Ln, bias=eps_t[:], scale=1.0 / B_V))
        t_r4 = ACT.op(lambda e: e.activation(out=ss[:, 28:32], in_=ss[:, 24:28], func=AF.Exp, scale=-0.5))
        DVE.wait(t_r4)
        for qb in range(4):
            t_fin = DVE.op(lambda e, qb=qb: e.scalar_tensor_tensor(out=oc[1][:, qb, :], in0=oc[0][:, qb, :], scalar=ss[:, 28 + qb:29 + qb], in1=sub_t[:], op0=ALU.mult, op1=ALU.mult))
        SP.wait(t_fin)
        out_tok = osem.issue(SP, lambda e, g=g: e.dma_start(
            out=o[g * 512:(g + 1) * 512, :].rearrange("(qb p) d -> p qb d", p=128), in_=oc[1][:]))
    P.out_toks.append(out_tok)
    return P.finish()


def build_MIX(with_norm_out):
    P = Prog()
    xres = P.dram("xres", [TPC, D], F32, "ExternalInput")
    oin = P.dram("oin", [TPC, D], F32, "ExternalInput")
    wg = P.dram("wg", [D, D], F32, "ExternalInput")
    wo = P.dram("wo", [D, D], F32, "ExternalInput")
    gn = P.dram("g_norm", [128, D // 128], F32, "ExternalInput")
    xnew = P.dram("xnew", [TPC, D], F32, "ExternalOutput")
    if with_norm_out:
        xnT_out = P.dram("xnT", [D, TPC], BF16, "ExternalOutput")
    PE, ACT, DVE, POOL, SP = P.PE, P.ACT, P.DVE, P.POOL, P.SP
    KC = D // 128
    P.make_ident()
    eps_t, eps_tok = P.const_tile(EPS, "eps")
    gcol = P.sb("gcol", [128, KC], F32)
    ds0 = P.dsem()
    t_g = ds0.issue(SP, lambda e: e.dma_start(out=gcol[:], in_=gn))
    wgb = P.sb("wgb", [128, KC, D], BF16)
    wob = P.sb("wob", [128, KC, D], BF16)
    stage, ssems, sstate = _stage(P, 1024, n=2)
    t_wg = P.load_weight_bf16(wgb, wg, KC, D, gcol, t_g, stage, ssems, sstate)
    t_wo = P.load_weight_bf16(wob, wo, KC, D, None, None, stage, ssems, sstate)
    NB = TPC // 128
    xt = P.sb("xt", [128, D], F32)
    ot = P.sb("ot", [128, D], F32)
    xsem = P.dsem()
    osem_in = P.dsem()
    junk = P.sb("junk", [128, D], BF16)
    xb = P.sb("xb", [128, D], BF16)
    xnT = P.sb("xnT_s", [128, KC, 128], BF16)
    hb = P.sb("hb", [128, D], BF16)
    hT = P.sb("hT", [128, KC, 128], BF16)
    xo = P.sb("xo", [128, D], F32)
    sg = xo
    ss = P.sb("ss", [128, 2], F32)
    rs = P.sb("rs", [128, 2], F32)
    osem = P.dsem()
    osem2 = P.dsem()
    if with_norm_out:
        x2b = P.sb("x2b", [128, D], BF16)
        x2T = P.sb("x2T", [128, KC, 128], BF16)
    xt_free = None
    ot_free = None
    t_out = None
    t_out2 = None
    for b in range(NB):
        SP.wait(xt_free)
        t_x = xsem.issue(SP, lambda e, b=b: e.dma_start(out=xt[:], in_=xres[b * 128:(b + 1) * 128, :]))
        SP.wait(ot_free)
        t_o = osem_in.issue(SP, lambda e, b=b: e.dma_start(out=ot[:], in_=oin[b * 128:(b + 1) * 128, :]))
        ACT.wait(t_x)
        t_ss = ACT.op(lambda e: e.activation(out=junk[:], in_=xt[:], func=AF.Square, accum_out=ss[:, 0:1]))
        t_r = emit_rstd(P, ss[:, 0:1], D, eps_t, eps_tok, rs[:, 0:1], t_ss)
        DVE.wait(t_r)
        t_xb = DVE.op(lambda e: e.tensor_scalar(out=xb[:], in0=xt[:], scalar1=rs[:, 0:1], scalar2=None, op0=ALU.mult))

        def transpose16(src, dst, after):
            PE.wait(after, P.ident_tok)
            for kc in range(KC):
                bk = kc // 8
                oo = (kc % 8) * 128
                tt = PE.op(lambda e, kc=kc, bk=bk, oo=oo: e.transpose(out=P.bank_bf(bk)[:, oo:oo + 128], in_=src[:, kc * 128:(kc + 1) * 128], identity=P.ident[:]),
                           pub=(kc == KC - 1))
            DVE.wait(tt)
            DVE.op(lambda e: e.tensor_copy(out=dst[:, 0:8, :], in_=P.bank_bf(0)[:, 0:1024]), pub=False)
            return DVE.op(lambda e: e.tensor_copy(out=dst[:, 8:16, :], in_=P.bank_bf(1)[:, 0:1024]))

        t_ev = transpose16(xb, xnT, t_xb)
        PE.wait(t_ev, t_wg)
        for gi in range(4):
            for kc in range(KC):
                tz = PE.op(lambda e, gi=gi, kc=kc: e.matmul(P.bank(2 + gi)[:, :], lhsT=xnT[:, kc, :], rhs=wgb[:, kc, gi * 512:(gi + 1) * 512],
                                                           start=(kc == 0), stop=(kc == KC - 1)), pub=(kc == KC - 1))
        ACT.wait(tz, t_out)
        for gi in range(4):
            t_sg = ACT.op(lambda e, gi=gi: e.activation(out=sg[:, gi * 512:(gi + 1) * 512], in_=P.bank(2 + gi)[:, :], func=AF.Silu), pub=(gi == 3))
        DVE.wait(t_sg, t_o)
        t_hb = DVE.op(lambda e: e.tensor_tensor(out=hb[:], in0=sg[:], in1=ot[:], op=ALU.mult))
        ot_free = t_hb
        t_ev2 = transpose16(hb, hT, t_hb)
        PE.wait(t_ev2, t_wo)
        for gi in range(4):
            for kc in range(KC):
                tz2 = PE.op(lambda e, gi=gi, kc=kc: e.matmul(P.bank(2 + gi)[:, :], lhsT=hT[:, kc, :], rhs=wob[:, kc, gi * 512:(gi + 1) * 512],
                                                            start=(kc == 0), stop=(kc == KC - 1)), pub=(kc == KC - 1))
        DVE.wait(tz2, t_out)
        for gi in range(4):
            t_xo = DVE.op(lambda e, gi=gi: e.tensor_tensor(out=xo[:, gi * 512:(gi + 1) * 512], in0=P.bank(2 + gi)[:, :], in1=xt[:, gi * 512:(gi + 1) * 512], op=ALU.add), pub=(gi == 3))
        xt_free = t_xo
        SP.wait(t_xo)
        t_out = osem.issue(SP, lambda e, b=b: e.dma_start(out=xnew[b * 128:(b + 1) * 128, :], in_=xo[:]))
        if with_norm_out:
            ACT.wait(t_xo)
            t_ss2 = ACT.op(lambda e: e.activation(out=junk[:], in_=xo[:], func=AF.Square, accum_out=ss[:, 1:2]))
            t_r2 = emit_rstd(P, ss[:, 1:2], D, eps_t, eps_tok, rs[:, 1:2], t_ss2)
            DVE.wait(t_r2)
            t_x2b = DVE.op(lambda e: e.tensor_scalar(out=x2b[:], in0=xo[:], scalar1=rs[:, 1:2], scalar2=None, op0=ALU.mult))
            DVE.wait(t_out2)
            t_ev3 = transpose16(x2b, x2T, t_x2b)
            SP.wait(t_ev3)
            t_out2 = osem2.issue(SP, lambda e, b=b: e.dma_start(
                out=xnT_out.rearrange("(kc p) t -> p kc t", p=128)[:, :, b * 128:(b + 1) * 128], in_=x2T[:]))
    P.out_toks += [t_out] + ([t_out2] if with_norm_out else [])
    return P.finish()


_CACHE = {}


def _get(name, fn):
    if name not in _CACHE:
        _CACHE[name] = fn()
    return _CACHE[name]


def _col(v):
    v = np.asarray(v, dtype=np.float32)
    return np.ascontiguousarray(v.reshape(-1, 128).T)


def run(nc, in_maps):
    res = run_bass_kernel_spmd(nc, in_maps, core_ids=list(range(NCORES)))
    return res.results


def _posl(pos):
    return np.ascontiguousarray(np.asarray(pos, dtype=np.int32).reshape(NBLK, 128).T)


def stage_A1(x, inp):
    nc = _get("A1", build_A1)
    w_lat = np.ascontiguousarray(inp["a_w_in"][0][:, :2 * A_LORA + A_ROPE])
    g = _col(inp["a_norm"][0])
    res = run(nc, [{"x": np.ascontiguousarray(x[c * TPC:(c + 1) * TPC]), "w_lat": w_lat, "g_norm": g} for c in range(NCORES)])
    latT = np.concatenate([r["latT"] for r in res], axis=1)
    kpe = np.concatenate([r["kpe"] for r in res], axis=0)
    return latT, kpe


def stage_A2(latT, kpe, inp):
    nc = _get("A2", build_A2)
    pos = _posl(inp["positions"][0])
    wq = inp["a_w_q_up"][0]
    wkv = inp["a_w_kv_up"][0]
    ims = []
    for c in range(NCORES):
        ims.append({"latT": latT, "kpe": kpe, "pos": pos,
                    "wq": np.ascontiguousarray(wq[:, 2 * c * A_QK:(2 * c + 2) * A_QK]),
                    "wkv": np.ascontiguousarray(wkv[:, 2 * c * 256:(2 * c + 2) * 256]),
                    "gq": _col(inp["a_q_norm"][0]), "gkv": _col(inp["a_kv_norm"][0]),
                    "qgain": np.ascontiguousarray(inp["a_q_gain"][0][None, :]),
                    "kgain": np.ascontiguousarray(inp["a_k_gain"][0][None, :])})
    res = run(nc, ims)
    return np.concatenate([r["o"] for r in res], axis=1)


def stage_MIX(xres, o, w_gate, w_out, g_norm, with_norm_out):
    nc = _get("MIX%d" % int(with_norm_out), lambda: build_MIX(with_norm_out))
    w_gate = np.ascontiguousarray(w_gate)
    w_out = np.ascontiguousarray(w_out)
    g = _col(g_norm)
    res = run(nc, [{"xres": np.ascontiguousarray(xres[c * TPC:(c + 1) * TPC]),
                    "oin": np.ascontiguousarray(o[c * TPC:(c + 1) * TPC]),
                    "wg": w_gate, "wo": w_out, "g_norm": g} for c in range(NCORES)])
    xnew = np.concatenate([r["xnew"] for r in res], axis=0)
    xnT = np.concatenate([r["xnT"] for r in res], axis=1) if with_norm_out else None
    return xnew, xnT


def stage_B2(xnT, inp):
    nc = _get("B2", build_B2)
    posv = np.asarray(inp["positions"][0], dtype=np.int32)
    pos = _posl(posv)
    w = inp["b_w_in"][0]
    QK = B_H * 2 * B_HD
    ims = []
    for c in range(NCORES):
        wq = np.ascontiguousarray(w[:, c * 256:(c + 1) * 256])
        wkv = np.ascontiguousarray(np.concatenate([w[:, QK + c * 256:QK + (c + 1) * 256],
                                                   w[:, 2 * QK + c * B_V:2 * QK + (c + 1) * B_V]], axis=1))
        ims.append({"xnT": xnT, "pos": pos, "posrow": np.ascontiguousarray(posv[None, 0:512]),
                    "posg": np.ascontiguousarray(posv[None, ::512]), "wq": wq, "wkv": wkv,
                    "g_norm": _col(inp["b_norm"][0]),
                    "qgain": np.ascontiguousarray(inp["b_q_gain"][0][None, :]),
                    "kgain": np.ascontiguousarray(inp["b_k_gain"][0][None, :]),
                    "lam4": np.ascontiguousarray(np.stack([inp["b_lambda_q1"][0], inp["b_lambda_k1"][0],
                                                           inp["b_lambda_q2"][0], inp["b_lambda_k2"][0]])),
                    "subln": np.ascontiguousarray(inp["b_subln"][0][None, :]),
                    "slope": np.full((1, 1), 2.0 ** (-8.0 * (c + 1) / B_H), dtype=np.float32)})
    res = run(nc, ims)
    return np.concatenate([r["o"] for r in res], axis=1)


def kernel(**inputs):
    inp = {k: np.asarray(v) for k, v in inputs.items()}
    x = np.ascontiguousarray(inp["x"][0])
    latT, kpe = stage_A1(x, inp)
    oA = stage_A2(latT, kpe, inp)
    x1, xn1T = stage_MIX(x, oA, inp["a_w_in"][0][:, 2 * A_LORA + A_ROPE:], inp["a_w_out"][0], inp["a_norm"][0], True)
    QK = B_H * 2 * B_HD
    oB = stage_B2(xn1T, inp)
    x2, _ = stage_MIX(x1, oB, inp["b_w_in"][0][:, 2 * QK + B_H * B_V:], inp["b_w_out"][0], inp["b_norm"][0], False)
    return x2[None].astype(np.float32)
```

```python
import math
from contextlib import ExitStack

import numpy as np
import concourse.bass as bass
import concourse.mybir as mybir
from concourse.bass_utils import run_bass_kernel_spmd

F32 = mybir.dt.float32
BF16 = mybir.dt.bfloat16
I32 = mybir.dt.int32
AF = mybir.ActivationFunctionType
ALU = mybir.AluOpType
AX = mybir.AxisListType

NCORES = 8
S = 16384
D = 2048
TPC = S // NCORES
NBLK = S // 128
EPS = 1e-6
CHUNK = 64
A_H, A_NOPE, A_ROPE, A_QK, A_V, A_LORA = 16, 128, 64, 192, 128, 512
B_H, B_HD, B_V = 8, 128, 256
TWO_PI = 2.0 * math.pi
NEG_BIG = -1.0e30


class Eng:
    def __init__(self, name, sem, serialize=False):
        self.name = name
        self.sem = sem
        self.count = 0
        self.waited = {}
        self.thunks = []
        self.serialize = serialize

    def wait(self, *toks):
        for tok in toks:
            if tok is None:
                continue
            if isinstance(tok, (list, tuple)) and (len(tok) == 0 or isinstance(tok[0], (list, tuple)) or tok[0] is None):
                self.wait(*tok)
                continue
            sem, val = tok
            if self.waited.get(sem, 0) >= val:
                continue
            self.waited[sem] = val
            self.thunks.append(lambda e, sem=sem, val=val: e.wait_ge(sem, val))

    def op(self, fn, pub=True, indep=False):
        if self.serialize and not indep and self.count > 0:
            self.wait((self.sem, self.count))
        self.count += 1
        c = self.count
        sem = self.sem
        self.thunks.append(lambda e: fn(e).then_inc(sem, 1))
        return (sem, c)


class DmaSem:
    def __init__(self, sem):
        self.sem = sem
        self.n = 0

    def issue(self, eng, fn):
        self.n += 16
        sem = self.sem
        eng.thunks.append(lambda e: fn(e).then_inc(sem, 16))
        return (sem, self.n)


class Prog:
    def __init__(self):
        self.nc = bass.Bass("TRN2", target_bir_lowering=False)
        self.es = ExitStack()
        self._n = 0
        self.SP = Eng("sync", self.sem("s_sp"))
        self.ACT = Eng("scalar", self.sem("s_act"), serialize=True)
        self.DVE = Eng("vector", self.sem("s_dve"), serialize=True)
        self.POOL = Eng("gpsimd", self.sem("s_pool"), serialize=True)
        self.PE = Eng("tensor", self.sem("s_pe"))
        self.engs = [self.SP, self.ACT, self.DVE, self.POOL, self.PE]
        self.psum = self.es.enter_context(self.nc.psum_tensor("psum", [128, 8, 512], F32))
        self.out_toks = []

    def uid(self, p):
        self._n += 1
        return "%s%d" % (p, self._n)

    def sem(self, name=None):
        return self.es.enter_context(self.nc.semaphore(name or self.uid("sem")))

    def dsem(self, name=None):
        return DmaSem(self.sem(name))

    def sb(self, name, shape, dt):
        return self.es.enter_context(self.nc.sbuf_tensor(name, list(shape), dt))

    def dram(self, name, shape, dt, kind):
        return self.nc.dram_tensor(name, list(shape), dt, kind=kind).ap()

    def bank(self, b):
        return self.psum[:, b, :]

    def bank_bf(self, b):
        return self.psum[:, b, :].bitcast(BF16)

    def finish(self):
        self.SP.wait(*self.out_toks)
        with self.nc.Block() as block:
            for eng in self.engs:
                if not eng.thunks:
                    continue

                def body(e, eng=eng):
                    for th in eng.thunks:
                        th(e)

                getattr(block, eng.name)(body)
        self.es.close()
        return self.nc

    def make_ident(self):
        idf = self.sb("ident_f", [128, 128], F32)
        idb = self.sb("ident_b", [128, 128], BF16)
        t0 = self.POOL.op(lambda e: e.memset(idf[:], 1.0))
        self.POOL.wait(t0)
        t1 = self.POOL.op(lambda e: e.affine_select(out=idf[:], in_=idf[:], pattern=[[-1, 128]],
                                                    compare_op=ALU.is_equal, fill=0.0, base=0,
                                                    channel_multiplier=1))
        self.DVE.wait(t1)
        t2 = self.DVE.op(lambda e: e.tensor_copy(out=idb[:], in_=idf[:]))
        self.ident = idb
        self.ident_tok = t2
        return idb

    def const_tile(self, val, name=None):
        t = self.sb(name or self.uid("c"), [128, 1], F32)
        tok = self.POOL.op(lambda e: e.memset(t[:], float(val)))
        return t, tok

    def load_weight_bf16(self, dst, src, nkc, ncols, gcol, gcol_tok, stage, stage_sems, state):
        W = stage[0].shape[-1]
        t_cv = None
        for kc in range(nkc):
            for c0 in range(0, ncols, W):
                c1 = min(ncols, c0 + W)
                i = state["i"] % len(stage)
                st = stage[i]
                self.SP.wait(state["free"][i])
                t_ld = stage_sems[i].issue(self.SP, lambda e, st=st, kc=kc, c0=c0, c1=c1: e.dma_start(
                    out=st[:, 0:c1 - c0], in_=src[kc * 128:(kc + 1) * 128, c0:c1]))
                self.ACT.wait(t_ld, gcol_tok)
                if gcol is not None:
                    t_cv = self.ACT.op(lambda e, st=st, kc=kc, c0=c0, c1=c1: e.activation(
                        out=dst[:, kc, c0:c1], in_=st[:, 0:c1 - c0], func=AF.Copy, scale=gcol[:, kc:kc + 1]), indep=True)
                else:
                    t_cv = self.ACT.op(lambda e, st=st, kc=kc, c0=c0, c1=c1: e.activation(
                        out=dst[:, kc, c0:c1], in_=st[:, 0:c1 - c0], func=AF.Copy), indep=True)
                state["free"][i] = t_cv
                state["i"] += 1
        return t_cv


def load_weight_cast(P, dst, src, nkc, ncols):
    dsm = P.dsem()
    tok = None
    for kc in range(nkc):
        tok = dsm.issue(P.POOL, lambda e, kc=kc: e.dma_start(out=dst[:, kc, :], in_=src[kc * 128:(kc + 1) * 128, :],
                                                            max_dma_last_dim=4096))
    return tok


def _stage(P, ncols, n=3):
    stage = [P.sb(P.uid("wst"), [128, ncols], F32) for _ in range(n)]
    sems = [P.dsem() for _ in range(n)]
    state = {"i": 0, "free": [None] * n}
    return stage, sems, state


def emit_rstd(P, ss, n, eps_t, eps_tok, out, after):
    P.ACT.wait(after, eps_tok)
    t = P.ACT.op(lambda e: e.activation(out=out[:], in_=ss[:], func=AF.Sqrt, bias=eps_t[:], scale=1.0 / n))
    P.DVE.wait(t)
    return P.DVE.op(lambda e: e.reciprocal(out=out[:], in_=out[:]))


def build_A1():
    P = Prog()
    nc = P.nc
    NL = 2 * A_LORA + A_ROPE
    x = P.dram("x", [TPC, D], F32, "ExternalInput")
    w = P.dram("w_lat", [D, NL], F32, "ExternalInput")
    gn = P.dram("g_norm", [1, D], F32, "ExternalInput")
    latT = P.dram("latT", [2 * A_LORA, TPC], BF16, "ExternalOutput")
    kpe = P.dram("kpe", [TPC, A_ROPE], F32, "ExternalOutput")
    KC = D // 128
    P.make_ident()
    eps_t, eps_tok = P.const_tile(EPS, "eps")
    gain = P.sb("gain", [128, D], F32)
    ds0 = P.dsem()
    t_g = ds0.issue(P.SP, lambda e: e.dma_start(out=gain[:], in_=gn.partition_broadcast(128)))
    wb = P.sb("wb", [128, KC, NL], BF16)
    t_w = load_weight_cast(P, wb, w, KC, NL)

    NB = TPC // 128
    xt = [P.sb("xt%d" % i, [128, D], F32) for i in range(2)]
    xsem = [P.dsem() for _ in range(2)]
    xfree = [None, None]
    junk = P.sb("junk", [128, D], BF16)
    xb = P.sb("xb", [128, D], BF16)
    xnT = P.sb("xnT", [128, KC, 128], BF16)
    ss = P.sb("ss", [128, 4], F32)
    rs = P.sb("rs", [128, 4], F32)
    latb = P.sb("latb", [128, 2 * A_LORA], BF16)
    kpt = P.sb("kpt", [128, A_ROPE], F32)
    latTt = P.sb("latTt", [128, 8, 128], BF16)
    osem1 = P.dsem()
    osem2 = P.dsem()
    t_prev_z = None
    t_prev_lt = None
    t_out1 = None
    t_out2 = None
    t_xb_free = None
    t_evac_lt = None
    t_z_free = None
    for b in range(NB):
        s = b % 2
        P.SP.wait(xfree[s])
        t_x = xsem[s].issue(P.SP, lambda e, s=s, b=b: e.dma_start(out=xt[s][:], in_=x[b * 128:(b + 1) * 128, :]))
        P.ACT.wait(t_x)
        t_ss = P.ACT.op(lambda e, s=s: e.activation(out=junk[:], in_=xt[s][:], func=AF.Square, accum_out=ss[:, 0:1]))
        t_r = emit_rstd(P, ss[:, 0:1], D, eps_t, eps_tok, rs[:, 0:1], t_ss)
        P.DVE.wait(t_r, t_xb_free, t_g)
        t_xb = P.DVE.op(lambda e, s=s: e.scalar_tensor_tensor(out=xb[:], in0=xt[s][:], scalar=rs[:, 0:1], in1=gain[:], op0=ALU.mult, op1=ALU.mult))
        xfree[s] = t_xb
        P.PE.wait(t_xb, P.ident_tok, t_prev_z)
        for kc in range(KC):
            bk = kc // 8
            o = (kc % 8) * 128
            tt = P.PE.op(lambda e, kc=kc, bk=bk, o=o: e.transpose(out=P.bank_bf(bk)[:, o:o + 128], in_=xb[:, kc * 128:(kc + 1) * 128], identity=P.ident[:]),
                         pub=(kc == KC - 1))
        t_xb_free = tt
        P.DVE.wait(tt)
        P.DVE.op(lambda e: e.tensor_copy(out=xnT[:, 0:8, :], in_=P.bank_bf(0)[:, 0:1024]), pub=False)
        t_ev = P.DVE.op(lambda e: e.tensor_copy(out=xnT[:, 8:16, :], in_=P.bank_bf(1)[:, 0:1024]))
        P.PE.wait(t_ev, t_w, t_z_free)
        for gi, (c0, c1) in enumerate([(0, 512), (512, 1024), (1024, NL)]):
            for kc in range(KC):
                tz = P.PE.op(lambda e, gi=gi, c0=c0, c1=c1, kc=kc: e.matmul(
                    P.bank(2 + gi)[:, 0:c1 - c0], lhsT=xnT[:, kc, :], rhs=wb[:, kc, c0:c1],
                    start=(kc == 0), stop=(kc == KC - 1)), pub=(gi == 2 and kc == KC - 1))
        t_prev_z = tz
        P.ACT.wait(tz)
        t_s1 = P.ACT.op(lambda e: e.activation(out=junk[:, 0:512], in_=P.bank(2)[:, :], func=AF.Square, accum_out=ss[:, 1:2]))
        t_s2 = P.ACT.op(lambda e: e.activation(out=junk[:, 512:1024], in_=P.bank(3)[:, :], func=AF.Square, accum_out=ss[:, 2:3]))
        t_r1 = emit_rstd(P, ss[:, 1:3], A_LORA, eps_t, eps_tok, rs[:, 1:3], t_s2)
        P.ACT.wait(t_r1, t_prev_lt)
        P.ACT.op(lambda e: e.activation(out=latb[:, 0:512], in_=P.bank(2)[:, :], func=AF.Copy, scale=rs[:, 1:2]), pub=False)
        t_lb = P.ACT.op(lambda e: e.activation(out=latb[:, 512:1024], in_=P.bank(3)[:, :], func=AF.Copy, scale=rs[:, 2:3]))
        P.DVE.wait(tz, t_out2)
        t_kp = P.DVE.op(lambda e: e.tensor_copy(out=kpt[:], in_=P.bank(4)[:, 0:A_ROPE]))
        t_z_free = [t_lb, t_kp]
        P.SP.wait(t_kp)
        t_out2 = osem2.issue(P.SP, lambda e, b=b: e.dma_start(out=kpe[b * 128:(b + 1) * 128, :], in_=kpt[:]))
        P.PE.wait(t_lb, t_evac_lt)
        for j in range(8):
            tl = P.PE.op(lambda e, j=j: e.transpose(out=P.bank_bf(5)[:, j * 128:(j + 1) * 128], in_=latb[:, j * 128:(j + 1) * 128], identity=P.ident[:]),
                         pub=(j == 7))
        t_prev_lt = tl
        P.DVE.wait(tl, t_out1)
        t_evac_lt = P.DVE.op(lambda e: e.tensor_copy(out=latTt[:].rearrange("p j t -> p (j t)"), in_=P.bank_bf(5)[:, 0:1024]))
        P.SP.wait(t_evac_lt)
        t_out1 = osem1.issue(P.SP, lambda e, b=b: e.dma_start(
            out=latT.rearrange("(j p) t -> p j t", p=128)[:, :, b * 128:(b + 1) * 128], in_=latTt[:]))
    P.out_toks += [t_out1, t_out2]
    return P.finish()


def emit_rope_tables(P, pos_i, pos_tok, cs):
    R2 = A_ROPE // 2
    invf = (np.float32(10000.0) ** (-(np.arange(0, A_ROPE, 2, dtype=np.float32)) / np.float32(A_ROPE))).astype(np.float32)
    posf = P.sb("posf", [128, NBLK], F32)
    CB = 16
    u = P.sb("rt_u", [128, CB, R2], F32)
    tt = P.sb("rt_t", [128, CB, R2], F32)
    ki = P.sb("rt_ki", [128, CB, R2], I32)
    kf = P.sb("rt_kf", [128, CB, R2], F32)
    r = P.sb("rt_r", [128, CB, R2], F32)
    V = P.DVE
    V.wait(pos_tok)
    V.op(lambda e: e.tensor_copy(out=posf[:], in_=pos_i[:]), pub=False)
    C1 = 6.28125
    C2 = float(np.float32(TWO_PI - C1))
    last = None
    for ch in range(NBLK // CB):
        b0 = ch * CB
        for which in range(2):
            for i in range(R2):
                V.op(lambda e, i=i, b0=b0: e.tensor_scalar(out=u[:, :, i], in0=posf[:, b0:b0 + CB], scalar1=float(invf[i]), scalar2=None, op0=ALU.mult), pub=False)
            if which == 0:
                V.op(lambda e: e.tensor_scalar(out=u[:], in0=u[:], scalar1=float(math.pi / 2), scalar2=None, op0=ALU.add), pub=False)
            V.op(lambda e: e.tensor_scalar(out=tt[:], in0=u[:], scalar1=float(1.0 / TWO_PI), scalar2=None, op0=ALU.mult), pub=False)
            V.op(lambda e: e.tensor_copy(out=ki[:], in_=tt[:]), pub=False)
            V.op(lambda e: e.tensor_copy(out=kf[:], in_=ki[:]), pub=False)
            V.op(lambda e: e.scalar_tensor_tensor(out=r[:], in0=kf[:], scalar=-C1, in1=u[:], op0=ALU.mult, op1=ALU.add), pub=False)
            V.op(lambda e: e.scalar_tensor_tensor(out=r[:], in0=kf[:], scalar=-C2, in1=r[:], op0=ALU.mult, op1=ALU.add), pub=False)
            V.op(lambda e: e.tensor_scalar(out=tt[:], in0=r[:], scalar1=float(math.pi), scalar2=-TWO_PI, op0=ALU.is_gt, op1=ALU.mult), pub=False)
            V.op(lambda e: e.tensor_tensor(out=r[:], in0=r[:], in1=tt[:], op=ALU.add), pub=False)
            V.op(lambda e: e.tensor_scalar(out=tt[:], in0=r[:], scalar1=float(-math.pi), scalar2=TWO_PI, op0=ALU.is_lt, op1=ALU.mult), pub=False)
            V.op(lambda e: e.tensor_tensor(out=r[:], in0=r[:], in1=tt[:], op=ALU.add), pub=False)
            tr = V.op(lambda e: e.tensor_scalar(out=r[:], in0=r[:], scalar1=float(-math.pi), scalar2=float(math.pi), op0=ALU.max, op1=ALU.min))
            P.ACT.wait(tr)
            ta = P.ACT.op(lambda e, b0=b0, which=which: e.activation(out=cs[:, b0:b0 + CB, which * R2:(which + 1) * R2], in_=r[:], func=AF.Sin))
            V.wait(ta)
            last = ta
    return last


def emit_rope(P, src, dst, cs, blk, tmp):
    V = P.DVE
    h = A_ROPE // 2
    cos = cs[:, blk, 0:h]
    sin = cs[:, blk, h:2 * h]
    V.op(lambda e: e.tensor_tensor(out=tmp[:, 0:h], in0=src[:, 0:h], in1=cos, op=ALU.mult), pub=False)
    V.op(lambda e: e.tensor_tensor(out=tmp[:, h:2 * h], in0=src[:, h:2 * h], in1=sin, op=ALU.mult), pub=False)
    V.op(lambda e: e.tensor_tensor(out=tmp[:, 2 * h:3 * h], in0=src[:, 0:h], in1=sin, op=ALU.mult), pub=False)
    V.op(lambda e: e.tensor_tensor(out=tmp[:, 3 * h:4 * h], in0=src[:, h:2 * h], in1=cos, op=ALU.mult), pub=False)
    V.op(lambda e: e.tensor_tensor(out=dst[:, 0:h], in0=tmp[:, 0:h], in1=tmp[:, h:2 * h], op=ALU.subtract), pub=False)
    return V.op(lambda e: e.tensor_tensor(out=dst[:, h:2 * h], in0=tmp[:, 2 * h:3 * h], in1=tmp[:, 3 * h:4 * h], op=ALU.add))


def emit_absmax_bcast(P, src_dram, n, out, dsem, tmp):
    t = dsem.issue(P.SP, lambda e: e.dma_start(out=tmp[:, 0:n], in_=src_dram.partition_broadcast(128)))
    P.DVE.wait(t)
    return P.DVE.op(lambda e: e.tensor_reduce(out=out, in_=tmp[:, 0:n], axis=AX.X, op=ALU.max, apply_absolute_value=True))


class AttnState:
    pass


class _Stop(Exception):
    pass


DBG_STEP = [0]
DBG_SKIP = [0]
DBG_VAR = [0]


def _chk(n):
    if DBG_STEP[0] == n:
        if DBG_SKIP[0] > 0:
            DBG_SKIP[0] -= 1
            return
        raise _Stop()


def emit_attention_group(P, st, g, qk_parts, vb, dvp, nacc_banks, acc_bank0, s_banks, exp_scale, bias_fn,
                         alibi=None, mask_pool=True, hooks=None):
    PE, ACT, DVE, POOL = P.PE, P.ACT, P.DVE, P.POOL
    per_bank = 4 // nacc_banks
    nk = 4 * g + 4
    nslots = len(s_banks)

    def acc(qb):
        bk = acc_bank0 + qb // per_bank
        o = (qb % per_bank) * dvp
        return P.bank(bk)[:, o:o + dvp]

    first_in_bank = [True] * nacc_banks
    PE.wait(st.q_ready, st.acc_free)
    pend = []
    t_last = None

    def emit_pv(j, slot, r, t_p):
        nonlocal t_last
        PE.wait(t_p)
        for qb in range(r, 4):
            bk = qb // per_bank
            stt = first_in_bank[bk]
            first_in_bank[bk] = False
            last = (qb == 3)
            tk = PE.op(lambda e, qb=qb, j=j, slot=slot, stt=stt: e.matmul(
                acc(qb), lhsT=st.pT[slot][:, qb * 128:(qb + 1) * 128], rhs=vb[:, j, 0:dvp],
                start=stt, stop=(j == nk - 1), skip_group_check=True), pub=last)
        st.p_free[slot] = tk
        t_last = tk

    for j in range(nk):
        r = max(0, j - 4 * g)
        c0 = r * 128
        slot = st.it % nslots
        st.it += 1
        PE.wait(st.s_free[slot])
        for pi, (ktf, qt) in enumerate(qk_parts):
            ts = PE.op(lambda e, ktf=ktf, qt=qt, j=j, slot=slot, c0=c0, pi=pi: e.matmul(
                P.bank(s_banks[slot])[:, c0:512], lhsT=ktf(j), rhs=qt[:, c0:512],
                start=(pi == 0), stop=(pi == len(qk_parts) - 1)), pub=(pi == len(qk_parts) - 1))
        if len(pend) >= nslots - 1:
            emit_pv(*pend.pop(0))
        if hooks and j in hooks:
            for hk in hooks[j]:
                hk()
        src = P.bank(s_banks[slot])
        if alibi is not None:
            sbt = alibi["sbuf"][slot]
            DVE.wait(ts, st.sb_free[slot])
            if r < 4 and j >= 4 * g:
                DVE.op(lambda e, c0=c0, src=src, sbt=sbt: e.tensor_tensor(out=sbt[:, c0:c0 + 128], in0=src[:, c0:c0 + 128], in1=alibi["Tdiag"][:], op=ALU.add), indep=True)
                if c0 + 128 < 512:
                    td = DVE.op(lambda e, c0=c0, src=src, sbt=sbt: e.tensor_tensor(out=sbt[:, c0 + 128:512], in0=src[:, c0 + 128:512], in1=alibi["T2"][:, c0 + 128:512], op=ALU.add), indep=True)
                else:
                    td = (DVE.sem, DVE.count)
            else:
                td = DVE.op(lambda e, src=src, sbt=sbt: e.tensor_tensor(out=sbt[:], in0=src[:], in1=alibi["T2"][:], op=ALU.add), indep=True)
            st.s_free[slot] = td
            ACT.wait(td, st.p_free[slot])
            if j >= 4 * g:
                ACT.op(lambda e, c0=c0, sbt=sbt, slot=slot: e.activation(out=st.pT[slot][:, c0:c0 + 128], in_=sbt[:, c0:c0 + 128], func=AF.Exp, bias=alibi["negM"][:], scale=exp_scale), indep=True)
                if c0 + 128 < 512:
                    tp = ACT.op(lambda e, c0=c0, sbt=sbt, slot=slot, j=j: e.activation(out=st.pT[slot][:, c0 + 128:512], in_=sbt[:, c0 + 128:512], func=AF.Exp, bias=bias_fn(j), scale=exp_scale), indep=True)
                else:
                    tp = (ACT.sem, ACT.count)
            else:
                tp = ACT.op(lambda e, sbt=sbt, slot=slot, j=j: e.activation(out=st.pT[slot][:], in_=sbt[:], func=AF.Exp, bias=bias_fn(j), scale=exp_scale), indep=True)
            st.sb_free[slot] = tp
        else:
            ACT.wait(ts, st.p_free[slot])
            tp = ACT.op(lambda e, c0=c0, src=src, slot=slot, j=j: e.activation(out=st.pT[slot][:, c0:512], in_=src[:, c0:512], func=AF.Exp, bias=bias_fn(j), scale=exp_scale), indep=True)
            st.s_free[slot] = tp
            if j >= 4 * g:
                POOL.wait(tp)
                tp = POOL.op(lambda e, c0=c0, slot=slot: e.memset(st.pT[slot][64:128, c0:c0 + 64], 0.0), indep=True)
        pend.append((j, slot, r, tp))
    while pend:
        emit_pv(*pend.pop(0))
    return t_last


def build_A2(NG=S // 512, NH=2, dbg=0, G0=0):
    P = Prog()
    latT = P.dram("latT", [2 * A_LORA, S], BF16, "ExternalInput")
    kpe = P.dram("kpe", [S, A_ROPE], F32, "ExternalInput")
    pos = P.dram("pos", [128, NBLK], I32, "ExternalInput")
    wq = P.dram("wq", [A_LORA, 2 * A_QK], F32, "ExternalInput")
    wkv = P.dram("wkv", [A_LORA, 2 * (A_NOPE + A_V)], F32, "ExternalInput")
    gq = P.dram("gq", [128, 4], F32, "ExternalInput")
    gkv = P.dram("gkv", [128, 4], F32, "ExternalInput")
    qgain = P.dram("qgain", [1, A_QK], F32, "ExternalInput")
    kgain = P.dram("kgain", [1, A_QK], F32, "ExternalInput")
    o = P.dram("o", [S, 2 * A_V], F32, "ExternalOutput")
    PE, ACT, DVE, POOL, SP = P.PE, P.ACT, P.DVE, P.POOL, P.SP
    P.make_ident()
    eps_t, eps_tok = P.const_tile(EPS, "eps")
    mhalf, mhalf_tok = P.const_tile(-0.5, "mhalf")
    ds = P.dsem()
    gq_t = P.sb("gq_t", [128, 4], F32)
    gkv_t = P.sb("gkv_t", [128, 4], F32)
    SP_tok = ds.issue(SP, lambda e: e.dma_start(out=gq_t[:], in_=gq))
    SP_tok = ds.issue(SP, lambda e: e.dma_start(out=gkv_t[:], in_=gkv))
    qg_t = P.sb("qg_t", [128, A_QK], F32)
    kg_t = P.sb("kg_t", [128, A_QK], F32)
    ds.issue(SP, lambda e: e.dma_start(out=qg_t[:], in_=qgain.partition_broadcast(128)))
    t_small = ds.issue(SP, lambda e: e.dma_start(out=kg_t[:], in_=kgain.partition_broadcast(128)))
    pos_i = P.sb("pos_i", [128, NBLK], I32)
    t_pos = ds.issue(SP, lambda e: e.dma_start(out=pos_i[:], in_=pos))
    t_small = t_pos
    wq_b = P.sb("wq_b", [128, 4, 2 * A_QK], BF16)
    wkv_b = P.sb("wkv_b", [128, 4, 512], BF16)
    stage, ssems, sstate = _stage(P, 512, n=2)
    t_wq = P.load_weight_bf16(wq_b, wq, 4, 2 * A_QK, gq_t, t_pos, stage, ssems, sstate)
    t_wkv = P.load_weight_bf16(wkv_b, wkv, 4, 512, gkv_t, t_pos, stage, ssems, sstate)
    mq = P.sb("mq", [128, 2], F32)
    negM = P.sb("negM", [128, 1], F32)
    DVE.wait(t_small)
    DVE.op(lambda e: e.tensor_reduce(out=mq[:, 0:1], in_=qg_t[:], axis=AX.X, op=ALU.max, apply_absolute_value=True), pub=False)
    t_mk = DVE.op(lambda e: e.tensor_reduce(out=mq[:, 1:2], in_=kg_t[:], axis=AX.X, op=ALU.max, apply_absolute_value=True))
    DVE.wait(t_mk)
    t_negM = DVE.op(lambda e: e.scalar_tensor_tensor(out=negM[:], in0=mq[:, 0:1], scalar=-math.sqrt(A_QK), in1=mq[:, 1:2], op0=ALU.mult, op1=ALU.mult))
    cs = P.sb("cs", [128, NBLK, A_ROPE], F32)
    t_cs = emit_rope_tables(P, pos_i, t_pos, cs)

    KTn = P.sb("KTn", [128, S], BF16)
    KTr = P.sb("KTr", [128, S], BF16)
    dvp = A_V + 16
    vb = P.sb("vb", [128, NBLK, dvp], BF16)
    t_ones = POOL.op(lambda e: e.memset(vb[:, :, A_V:dvp], 1.0))
    latg = [P.sb("latg%d" % i, [128, 4, 512], BF16) for i in range(2)]
    latsem = [P.dsem() for _ in range(2)]
    latfree = [None, None]
    kpg = [P.sb("kpg%d" % i, [128, 4, A_ROPE], F32) for i in range(2)]
    kpsem = [P.dsem() for _ in range(2)]
    kpfree = [None, None]
    full4 = P.sb("full4", [128, 4, A_QK], F32)
    fn4 = P.sb("fn4", [128, 4, A_QK], F32)
    junk4 = P.sb("junk4", [128, 4, A_QK], F32)
    rt4 = P.sb("rt4", [128, 4, 128], F32)
    nb4 = P.sb("nb4", [128, 4, 256], BF16)
    t_nbz = POOL.op(lambda e: e.memset(nb4[:, :, A_QK:256], 0.0))
    ss = P.sb("ss", [128, 16], F32)
    QTn = [P.sb("QTn%d" % i, [128, 512], BF16) for i in range(2)]
    QTr = [P.sb("QTr%d" % i, [128, 512], BF16) for i in range(2)]
    st = AttnState()
    st.pT = [P.sb("pT%d" % i, [128, 512], BF16) for i in range(3)]
    st.p_free = [None, None, None]
    st.s_free = [None, None, None]
    st.it = 0
    st.acc_free = None
    st.q_ready = None
    osb = [P.sb("osb%d" % i, [128, 4, A_V], F32) for i in range(2)]
    osem = [P.dsem() for _ in range(2)]
    rec = P.sb("rec", [128, 4], F32)
    BK_P0, BK_TN = 4, 6
    TR_OFF = 512
    tok = {"p_free": None, "full_free": None, "nb_free": None, "tr_free": None}
    H2 = A_ROPE // 2

    def proj_part1(latt, wsel, ncol, is_k, kp, gain_t, blk0):
        PE.wait(tok["p_free"])
        for b in range(4):
            bk = BK_P0 + b // 2
            off = (b % 2) * ncol
            for kc in range(4):
                tp = PE.op(lambda e, b=b, bk=bk, off=off, kc=kc: e.matmul(
                    P.bank(bk)[:, off:off + ncol], lhsT=latt[:, kc, b * 128:(b + 1) * 128], rhs=wsel(kc),
                    start=(kc == 0), stop=(kc == 3), skip_group_check=True))
        DVE.wait(tp, tok["full_free"])
        for h2 in range(2):
            src = P.bank(BK_P0 + h2)[:, 0:2 * ncol].rearrange("p (b c) -> p b c", b=2)
            if is_k:
                DVE.op(lambda e, h2=h2, src=src: e.tensor_copy(out=full4[:, 2 * h2:2 * h2 + 2, 0:A_NOPE], in_=src[:, :, 0:A_NOPE]))
                DVE.op(lambda e, h2=h2, src=src: e.tensor_copy(out=vb[:, blk0 + 2 * h2:blk0 + 2 * h2 + 2, 0:A_V], in_=src[:, :, A_NOPE:A_NOPE + A_V]))
            else:
                DVE.op(lambda e, h2=h2, src=src: e.tensor_copy(out=full4[:, 2 * h2:2 * h2 + 2, :], in_=src[:, :, 0:A_QK]))
        tok["p_free"] = (DVE.sem, DVE.count)
        if is_k:
            DVE.op(lambda e: e.tensor_copy(out=full4[:, :, A_NOPE:A_QK], in_=kp[:]))
        DVE.op(lambda e: e.tensor_tensor(out=junk4[:], in0=full4[:], in1=full4[:], op=ALU.mult))
        t_ss = DVE.op(lambda e: e.tensor_reduce(out=ss[:, 0:4], in_=junk4[:], axis=AX.X, op=ALU.add))
        ACT.wait(t_ss, eps_tok)
        ACT.op(lambda e: e.activation(out=ss[:, 4:8], in_=ss[:, 0:4], func=AF.Ln, bias=eps_t[:], scale=1.0 / A_QK))
        t_rs = ACT.op(lambda e: e.activation(out=ss[:, 8:12], in_=ss[:, 4:8], func=AF.Exp, scale=-0.5))
        DVE.wait(t_rs, tok["nb_free"])
        for b in range(4):
            DVE.op(lambda e, b=b: e.scalar_tensor_tensor(out=fn4[:, b, :], in0=full4[:, b, :], scalar=ss[:, 8 + b:9 + b], in1=gain_t[:], op0=ALU.mult, op1=ALU.mult))
        DVE.op(lambda e: e.tensor_copy(out=nb4[:, :, 0:A_NOPE], in_=fn4[:, :, 0:A_NOPE]))
        cos = cs[:, blk0:blk0 + 4, 0:H2]
        sin = cs[:, blk0:blk0 + 4, H2:2 * H2]
        x1 = fn4[:, :, A_NOPE:A_NOPE + H2]
        x2 = fn4[:, :, A_NOPE + H2:A_QK]
        DVE.op(lambda e: e.tensor_tensor(out=rt4[:, :, 0:H2], in0=x1, in1=cos, op=ALU.mult))
        DVE.op(lambda e: e.tensor_tensor(out=rt4[:, :, H2:2 * H2], in0=x2, in1=sin, op=ALU.mult))
        DVE.op(lambda e: e.tensor_tensor(out=rt4[:, :, 2 * H2:3 * H2], in0=x1, in1=sin, op=ALU.mult))
        DVE.op(lambda e: e.tensor_tensor(out=rt4[:, :, 3 * H2:4 * H2], in0=x2, in1=cos, op=ALU.mult))
        DVE.op(lambda e: e.tensor_tensor(out=nb4[:, :, A_NOPE:A_NOPE + H2], in0=rt4[:, :, 0:H2], in1=rt4[:, :, H2:2 * H2], op=ALU.subtract))
        t_nb = DVE.op(lambda e: e.tensor_tensor(out=nb4[:, :, A_NOPE + H2:A_QK], in0=rt4[:, :, 2 * H2:3 * H2], in1=rt4[:, :, 3 * H2:4 * H2], op=ALU.add))
        tok["full_free"] = t_nb
        return t_nb

    def proj_part2(t_nb, dstT_n, dstT_r, tcol0, extra_wait=None):
        PE.wait(t_nb, tok["tr_free"])
        for b in range(4):
            PE.op(lambda e, b=b: e.transpose(out=P.bank_bf(BK_TN)[:, b * 128:(b + 1) * 128], in_=nb4[:, b, 0:A_NOPE], identity=P.ident[:]))
        for b in range(4):
            t_tr = PE.op(lambda e, b=b: e.transpose(out=P.bank_bf(BK_TN)[:, TR_OFF + b * 128:TR_OFF + (b + 1) * 128], in_=nb4[:, b, A_NOPE:256], identity=P.ident[:]))
        tok["nb_free"] = t_tr
        DVE.wait(t_tr, extra_wait)
        DVE.op(lambda e: e.tensor_copy(out=dstT_n[:, tcol0:tcol0 + 512], in_=P.bank_bf(BK_TN)[:, 0:512]))
        t_e = DVE.op(lambda e: e.tensor_copy(out=dstT_r[:, tcol0:tcol0 + 512], in_=P.bank_bf(BK_TN)[:, TR_OFF:TR_OFF + 512]))
        tok["tr_free"] = t_e
        return t_e

    DVE.wait(t_cs, t_negM)
    PE.wait(t_wq, t_wkv, P.ident_tok)
    POOL.wait(t_ones)
    PE.wait(t_nbz)
    if dbg == 1:
        SP.wait((DVE.sem, DVE.count), (ACT.sem, ACT.count), (POOL.sem, POOL.count))
        return P.finish()
    out_tok = [None, None]
    oi = 0
    attn_done_prev_head = None
    lat_it = [0]

    def load_lat(row0, g, with_kpe):
        s = lat_it[0] % 2
        lat_it[0] += 1
        SP.wait(latfree[s], kpfree[s] if with_kpe else None)
        t_l = latsem[s].issue(SP, lambda e, s=s, g=g: e.dma_start(
            out=latg[s][:], in_=latT[row0:row0 + A_LORA, g * 512:(g + 1) * 512].rearrange("(kc p) t -> p kc t", p=128)))
        t_k = None
        if with_kpe:
            t_k = kpsem[s].issue(SP, lambda e, s=s, g=g: e.dma_start(
                out=kpg[s][:], in_=kpe[g * 512:(g + 1) * 512, :].rearrange("(tb p) d -> p tb d", p=128)))
        return s, t_l, t_k

    for hh in range(NH):
        t_kv_last = None
        for g in range(NG):
            s, t_l, t_k = load_lat(A_LORA, g, True)
            PE.wait(t_l)
            DVE.wait(t_k)
            if g == 0:
                DVE.wait(attn_done_prev_head)
            t_nb = proj_part1(latg[s], lambda kc, hh=hh: wkv_b[:, kc, hh * 256:(hh + 1) * 256], 256, True, kpg[s], kg_t, g * 4)
            latfree[s] = (PE.sem, PE.count)
            kpfree[s] = t_nb
            t_kv_last = proj_part2(t_nb, KTn, KTr, g * 512)
        kv_ready = t_kv_last
        if dbg == 2:
            SP.wait(kv_ready)
            return P.finish()
        q_tok = {}
        q_pend = {}
        lat_pre = {}

        def q_prefetch(g):
            if g < NG and g not in lat_pre:
                lat_pre[g] = load_lat(0, g, False)

        def q_part1(g):
            q_prefetch(g)
            s, t_l, _ = lat_pre[g]
            PE.wait(t_l)
            q_pend[g] = proj_part1(latg[s], lambda kc, hh=hh: wq_b[:, kc, hh * A_QK:(hh + 1) * A_QK], A_QK, False, None, qg_t, g * 4)
            latfree[s] = (PE.sem, PE.count)
            q_prefetch(g + 1)

        def q_part2(g):
            qs = g % 2
            q_tok[g] = proj_part2(q_pend[g], QTn[qs], QTr[qs], 0, q_tok.get(("free", qs)))

        q_part1(G0)
        q_part2(G0)
        for g in range(G0, NG):
            qs = g % 2
            st.q_ready = [q_tok[g], kv_ready]
            parts = [(lambda j: KTn[:, j * 128:(j + 1) * 128], QTn[qs]),
                     (lambda j: KTr[:, j * 128:(j + 1) * 128], QTr[qs])]
            hooks = {}
            if g + 1 < NG:
                nk = 4 * g + 4
                hooks[0] = [lambda g=g: q_part1(g + 1)]
                hooks.setdefault(min(nk - 1, 8), []).append(lambda g=g: q_part2(g + 1))
            t_acc = emit_attention_group(P, st, g, parts, vb, dvp, 2, 2, [0, 1, 7], 1.0 / math.sqrt(A_QK),
                                         lambda j: negM[:], hooks=hooks)
            q_tok[("free", qs)] = t_acc
            ob = osb[oi % 2]
            DVE.wait(t_acc, out_tok[oi % 2])
            for qb in range(4):
                bk = 2 + qb // 2
                off = (qb % 2) * dvp
                DVE.op(lambda e, qb=qb, bk=bk, off=off: e.reciprocal(out=rec[:, qb:qb + 1], in_=P.bank(bk)[:, off + A_V:off + A_V + 1]))
            for qb in range(4):
                bk = 2 + qb // 2
                off = (qb % 2) * dvp
                t_on = DVE.op(lambda e, qb=qb, bk=bk, off=off, ob=ob: e.tensor_scalar(out=ob[:, qb, :], in0=P.bank(bk)[:, off:off + A_V], scalar1=rec[:, qb:qb + 1], scalar2=None, op0=ALU.mult))
            st.acc_free = t_on
            SP.wait(t_on)
            out_tok[oi % 2] = osem[oi % 2].issue(SP, lambda e, g=g, hh=hh, ob=ob: e.dma_start(
                out=o[g * 512:(g + 1) * 512, hh * A_V:(hh + 1) * A_V].rearrange("(qb p) d -> p qb d", p=128), in_=ob[:]))
            oi += 1
            attn_done_prev_head = t_acc
    P.out_toks += [t for t in out_tok if t is not None]
    return P.finish()


def build_B2(NG=S // 512, dbg=0):
    P = Prog()
    xnT = P.dram("xnT", [D, S], BF16, "ExternalInput")
    pos = P.dram("pos", [128, NBLK], I32, "ExternalInput")
    posrow = P.dram("posrow", [1, 512], I32, "ExternalInput")
    posg = P.dram("posg", [1, S // 512], I32, "ExternalInput")
    wq = P.dram("wq", [D, 2 * B_HD], F32, "ExternalInput")
    wkv = P.dram("wkv", [D, 2 * B_HD + B_V], F32, "ExternalInput")
    gn = P.dram("g_norm", [128, D // 128], F32, "ExternalInput")
    qgain = P.dram("qgain", [1, B_HD], F32, "ExternalInput")
    kgain = P.dram("kgain", [1, B_HD], F32, "ExternalInput")
    lam4 = P.dram("lam4", [4, B_HD], F32, "ExternalInput")
    subln = P.dram("subln", [1, B_V], F32, "ExternalInput")
    slope = P.dram("slope", [1, 1], F32, "ExternalInput")
    o = P.dram("o", [S, B_V], F32, "ExternalOutput")
    PE, ACT, DVE, POOL, SP = P.PE, P.ACT, P.DVE, P.POOL, P.SP
    KC = D // 128
    LAM_INIT = 0.8 - 0.6 * math.exp(-0.3 * 1)
    SQ = math.sqrt(B_HD)
    P.make_ident()
    eps_t, eps_tok = P.const_tile(EPS, "eps")
    ds = P.dsem()
    gcol = P.sb("gcol", [128, KC], F32)
    qg_t = P.sb("qg_t", [128, B_HD], F32)
    kg_t = P.sb("kg_t", [128, B_HD], F32)
    lam_t = P.sb("lam_t", [128, 4, B_HD], F32)
    sub_t = P.sb("sub_t", [128, B_V], F32)
    slope_t = P.sb("slope_t", [128, 1], F32)
    pos_i = P.sb("pos_i", [128, NBLK], I32)
    posg_i = P.sb("posg_i", [128, S // 512], I32)
    ds.issue(SP, lambda e: e.dma_start(out=gcol[:], in_=gn))
    ds.issue(SP, lambda e: e.dma_start(out=qg_t[:], in_=qgain.partition_broadcast(128)))
    ds.issue(SP, lambda e: e.dma_start(out=kg_t[:], in_=kgain.partition_broadcast(128)))
    for i in range(4):
        ds.issue(SP, lambda e, i=i: e.dma_start(out=lam_t[:, i, :], in_=lam4[i:i + 1, :].partition_broadcast(128)))
    ds.issue(SP, lambda e: e.dma_start(out=sub_t[:], in_=subln.partition_broadcast(128)))
    ds.issue(SP, lambda e: e.dma_start(out=slope_t[:], in_=slope.partition_broadcast(128)))
    ds.issue(SP, lambda e: e.dma_start(out=posg_i[:], in_=posg.partition_broadcast(128)))
    t_set = ds.issue(SP, lambda e: e.dma_start(out=pos_i[:], in_=pos))
    sbufs = [P.sb("sbias%d" % i, [128, 512], F32) for i in range(4)]
    prow_i = sbufs[0][:].bitcast(I32)
    prow_f = sbufs[1]
    ds2 = P.dsem()
    t_prow = ds2.issue(SP, lambda e: e.dma_start(out=prow_i, in_=posrow.partition_broadcast(128)))
    small = P.sb("small", [128, 16], F32)
    posf = P.sb("posf", [128, NBLK], F32)
    posgf = P.sb("posgf", [128, S // 512], F32)
    T2 = P.sb("T2", [128, 512], F32)
    Tdiag = P.sb("Tdiag", [128, 128], F32)
    tmpd = P.sb("tmpd", [128, 128], F32)
    negM = small[:, 0:1]
    nslope_s = small[:, 1:2]
    neglam = small[:, 2:3]
    DVE.wait(t_set, t_prow)
    DVE.op(lambda e: e.tensor_reduce(out=small[:, 3:4], in_=qg_t[:], axis=AX.X, op=ALU.max, apply_absolute_value=True))
    DVE.op(lambda e: e.tensor_reduce(out=small[:, 4:5], in_=kg_t[:], axis=AX.X, op=ALU.max, apply_absolute_value=True))
    DVE.op(lambda e: e.scalar_tensor_tensor(out=negM, in0=small[:, 3:4], scalar=-SQ, in1=small[:, 4:5], op0=ALU.mult, op1=ALU.mult))
    DVE.op(lambda e: e.tensor_scalar(out=nslope_s, in0=slope_t[:], scalar1=-SQ, scalar2=None, op0=ALU.mult))
    DVE.op(lambda e: e.tensor_tensor(out=tmpd[:], in0=lam_t[:, 0, :], in1=lam_t[:, 1, :], op=ALU.mult))
    DVE.op(lambda e: e.tensor_reduce(out=small[:, 5:6], in_=tmpd[:], axis=AX.X, op=ALU.add))
    DVE.op(lambda e: e.tensor_tensor(out=tmpd[:], in0=lam_t[:, 2, :], in1=lam_t[:, 3, :], op=ALU.mult))
    t_l = DVE.op(lambda e: e.tensor_reduce(out=small[:, 6:7], in_=tmpd[:], axis=AX.X, op=ALU.add))
    ACT.wait(t_l)
    t_e = ACT.op(lambda e: e.activation(out=small[:, 7:9], in_=small[:, 5:7], func=AF.Exp))
    DVE.wait(t_e)
    DVE.op(lambda e: e.tensor_tensor(out=neglam, in0=small[:, 8:9], in1=small[:, 7:8], op=ALU.subtract))
    DVE.op(lambda e: e.tensor_scalar(out=neglam, in0=neglam, scalar1=-LAM_INIT, scalar2=None, op0=ALU.add))
    DVE.op(lambda e: e.tensor_scalar(out=sub_t[:], in0=sub_t[:], scalar1=1.0 - LAM_INIT, scalar2=None, op0=ALU.mult))
    DVE.op(lambda e: e.tensor_copy(out=posf[:], in_=pos_i[:]))
    DVE.op(lambda e: e.tensor_copy(out=posgf[:], in_=posg_i[:]))
    DVE.op(lambda e: e.tensor_copy(out=prow_f[:], in_=prow_i))
    DVE.op(lambda e: e.tensor_scalar(out=T2[:], in0=prow_f[:], scalar1=prow_f[:, 0:1], scalar2=nslope_s, op0=ALU.subtract, op1=ALU.mult))
    DVE.op(lambda e: e.tensor_scalar(out=Tdiag[:], in0=prow_f[:, 0:128], scalar1=posf[:, 0:1], scalar2=None, op0=ALU.subtract))
    DVE.op(lambda e: e.tensor_scalar(out=tmpd[:], in0=Tdiag[:], scalar1=-1.0, scalar2=None, op0=ALU.mult))
    DVE.op(lambda e: e.tensor_tensor(out=Tdiag[:], in0=Tdiag[:], in1=tmpd[:], op=ALU.max))
    DVE.op(lambda e: e.tensor_scalar(out=Tdiag[:], in0=Tdiag[:], scalar1=nslope_s, scalar2=None, op0=ALU.mult))
    t_setup = DVE.op(lambda e: e.tensor_scalar(out=Tdiag[64:128, 0:64], in0=Tdiag[64:128, 0:64], scalar1=NEG_BIG, scalar2=None, op0=ALU.add))
    ACT.wait(t_setup)

    w_b = P.sb("w_b", [128, KC, 512], BF16)
    stage_all = P.sb("stage_all", [128, 1024], F32)
    stage = [stage_all[:, 0:512], stage_all[:, 512:1024]]
    ssems = [P.dsem() for _ in range(2)]
    sstate = {"i": 0, "free": [None, None]}
    t_w = P.load_weight_bf16(w_b, wkv, KC, 512, gcol, t_set, stage, ssems, sstate)
    DVE.wait(t_w)

    KT = [P.sb("KT%d" % c, [128, S], BF16) for c in range(2)]
    dvp = B_V + 2
    vb = P.sb("vb", [128, NBLK, dvp], BF16)
    POOL.op(lambda e: e.memset(vb[:, :, B_V:dvp], 1.0))
    GT = 256
    xg = [P.sb("xg%d" % i, [128, KC, GT], BF16) for i in range(2)]
    xsem = [P.dsem() for _ in range(2)]
    xfree = [None, None]
    ff4 = P.sb("ff4", [128, 4, 2 * B_HD], F32)
    junk4 = stage_all[:, :].rearrange("p (b d) -> p b d", b=4)
    nb4 = P.sb("nb4", [128, 4, 2 * B_HD], BF16)
    ss = P.sb("ss", [128, 32], F32)
    QT = [[P.sb("QT%d_%d" % (c, i), [128, 512], BF16) for i in range(2)] for c in range(2)]
    st = AttnState()
    st.pT = [P.sb("pT%d" % i, [128, 512], BF16) for i in range(4)]
    st.p_free = [None] * 4
    st.s_free = [None] * 4
    st.sb_free = [t_setup] * 4
    st.it = 0
    st.acc_free = None
    st.q_ready = None
    kbias = [P.sb("kbias%d" % i, [128, NBLK], F32) for i in range(2)]
    kb_free = [None, None]
    oc = [P.sb("oc%d" % c, [128, 4, B_V], F32) for c in range(2)]
    osem = P.dsem()
    rec = P.sb("rec", [128, 4], F32)
    S_BANKS = [0, 1, 6, 7]

    def take_slot():
        slot = st.it % len(S_BANKS)
        st.it += 1
        PE.wait(st.s_free[slot])
        return slot, S_BANKS[slot]
    tok = {"p_free": None, "ff_free": None, "nb_free": None, "tr_free": None}
    x_it = [0]

    def load_x(t0):
        s = x_it[0] % 2
        x_it[0] += 1
        SP.wait(xfree[s])
        t = xsem[s].issue(SP, lambda e, s=s, t0=t0: e.dma_start(
            out=xg[s][:], in_=xnT.rearrange("(kc p) t -> p kc t", p=128)[:, :, t0:t0 + GT]))
        return s, t

    def proj_part1(loads, ncol, gain_t, is_k, blk0):
        first = True
        pslot, BK_PROJ = take_slot()
        for li, (s, t) in enumerate(loads):
            PE.wait(t)
            for tb in range(GT // 128):
                b = li * (GT // 128) + tb
                PE.wait(tok["p_free"])
                for kc in range(KC):
                    tp = PE.op(lambda e, kc=kc, s=s, tb=tb: e.matmul(P.bank(BK_PROJ)[:, 0:ncol], lhsT=xg[s][:, kc, tb * 128:(tb + 1) * 128],
                                                                   rhs=w_b[:, kc, 0:ncol], start=(kc == 0), stop=(kc == KC - 1)))
                DVE.wait(tp, tok["ff_free"] if first else None)
                first = False
                DVE.op(lambda e, b=b: e.tensor_copy(out=ff4[:, b, :], in_=P.bank(BK_PROJ)[:, 0:2 * B_HD]))
                if is_k:
                    DVE.op(lambda e, b=b: e.tensor_copy(out=vb[:, blk0 + b, 0:B_V], in_=P.bank(BK_PROJ)[:, 2 * B_HD:2 * B_HD + B_V]))
                tok["p_free"] = (DVE.sem, DVE.count)
            xfree[s] = (PE.sem, PE.count)
        st.s_free[pslot] = tok["p_free"]
        DVE.op(lambda e: e.tensor_tensor(out=junk4, in0=ff4[:], in1=ff4[:], op=ALU.mult))
        t_ss = DVE.op(lambda e: e.tensor_reduce(out=ss[:, 0:8], in_=stage_all[:, :].rearrange("p (b d) -> p b d", b=8), axis=AX.X, op=ALU.add))
        ACT.wait(t_ss, eps_tok)
        ACT.op(lambda e: e.activation(out=ss[:, 8:16], in_=ss[:, 0:8], func=AF.Ln, bias=eps_t[:], scale=1.0 / B_HD))
        t_rs = ACT.op(lambda e: e.activation(out=ss[:, 16:24], in_=ss[:, 8:16], func=AF.Exp, scale=-0.5))
        DVE.wait(t_rs, tok["nb_free"])
        for b in range(4):
            for c in range(2):
                t_nb = DVE.op(lambda e, b=b, c=c: e.scalar_tensor_tensor(
                    out=nb4[:, b, c * B_HD:(c + 1) * B_HD], in0=ff4[:, b, c * B_HD:(c + 1) * B_HD],
                    scalar=ss[:, 16 + 2 * b + c:17 + 2 * b + c], in1=gain_t[:], op0=ALU.mult, op1=ALU.mult))
        tok["ff_free"] = t_nb
        return t_nb

    def proj_part2(t_nb, dstT, tcol0, extra_wait=None):
        tslot, BK_TR = take_slot()
        PE.wait(t_nb)
        for b in range(4):
            for c in range(2):
                t_tr = PE.op(lambda e, b=b, c=c, BK_TR=BK_TR: e.transpose(out=P.bank_bf(BK_TR)[:, (2 * b + c) * 128:(2 * b + c + 1) * 128],
                                                            in_=nb4[:, b, c * B_HD:(c + 1) * B_HD], identity=P.ident[:]))
        tok["nb_free"] = t_tr
        DVE.wait(t_tr, extra_wait)
        trv = P.bank_bf(BK_TR)[:, 0:1024].rearrange("p (b c d) -> p b c d", b=4, c=2)
        for c in range(2):
            t_e = DVE.op(lambda e, c=c: e.tensor_copy(out=dstT[c][:, tcol0:tcol0 + 512].rearrange("p (b d) -> p b d", b=4), in_=trv[:, :, c, :]))
        tok["tr_free"] = t_e
        st.s_free[tslot] = t_e
        return t_e

    PE.wait(t_w, P.ident_tok)
    t_kv = None
    for g in range(NG):
        loads = [load_x(g * 512 + i * GT) for i in range(512 // GT)]
        t_nb = proj_part1(loads, 512, kg_t, True, g * 4)
        t_kv = proj_part2(t_nb, KT, g * 512)
    kv_ready = t_kv
    ACT.wait((PE.sem, PE.count))
    SP.wait((DVE.sem, DVE.count))
    t_wq = P.load_weight_bf16(w_b, wq, KC, 2 * B_HD, gcol, t_set, stage, ssems, sstate)
    PE.wait(t_wq)
    DVE.wait(t_wq)
    q_tok = {}
    q_pend = {}
    x_pre = {}

    def q_prefetch(g):
        if g < NG and g not in x_pre:
            x_pre[g] = [load_x(g * 512 + i * GT) for i in range(512 // GT)]

    def q_part1(g):
        q_prefetch(g)
        q_pend[g] = proj_part1(x_pre[g], 2 * B_HD, qg_t, False, None)

    def q_part2(g):
        qs = g % 2
        q_tok[g] = proj_part2(q_pend[g], [QT[0][qs], QT[1][qs]], 0, q_tok.get(("free", qs)))

    out_tok = None
    q_part1(0)
    q_part2(0)
    for g in range(NG):
        qs = g % 2
        kb = kbias[g % 2]
        DVE.wait(kb_free[g % 2])
        DVE.op(lambda e, kb=kb, g=g: e.tensor_scalar(out=kb[:], in0=posf[:], scalar1=posgf[:, g:g + 1], scalar2=slope_t[:], op0=ALU.subtract, op1=ALU.mult))
        t_kb = DVE.op(lambda e, kb=kb: e.tensor_scalar(out=kb[:], in0=kb[:], scalar1=negM, scalar2=None, op0=ALU.add))
        ACT.wait(t_kb)
        for c in range(2):
            st.q_ready = [q_tok[g], kv_ready]
            parts = [(lambda j, c=c: KT[c][:, j * 128:(j + 1) * 128], QT[c][qs])]
            hooks = {}
            if c == 0 and g + 1 < NG:
                nk = 4 * g + 4
                hooks[0] = [lambda g=g: q_part1(g + 1)]
                hooks.setdefault(min(nk - 1, 10), []).append(lambda g=g: q_part2(g + 1))
            t_acc = emit_attention_group(P, st, g, parts, vb, dvp, 4, 2, S_BANKS, 1.0 / SQ,
                                         lambda j, kb=kb: kb[:, j:j + 1],
                                         alibi=dict(T2=T2, Tdiag=Tdiag, negM=negM, sbuf=sbufs), hooks=hooks)
            DVE.wait(t_acc, out_tok if c == 0 else None)
            for qb in range(4):
                DVE.op(lambda e, qb=qb: e.reciprocal(out=rec[:, qb:qb + 1], in_=P.bank(2 + qb)[:, B_V:B_V + 1]))
            for qb in range(4):
                t_on = DVE.op(lambda e, qb=qb, c=c: e.tensor_scalar(out=oc[c][:, qb, :], in0=P.bank(2 + qb)[:, 0:B_V], scalar1=rec[:, qb:qb + 1], scalar2=None, op0=ALU.mult))
            st.acc_free = t_on
        q_tok[("free", qs)] = t_acc
        kb_free[g % 2] = t_acc
        o0f = oc[0][:].rearrange("p a d -> p (a d)")
        o1f = oc[1][:].rearrange("p a d -> p (a d)")
        DVE.op(lambda e: e.scalar_tensor_tensor(out=o0f, in0=o1f, scalar=neglam, in1=o0f, op0=ALU.mult, op1=ALU.add))
        DVE.op(lambda e: e.tensor_tensor(out=o1f, in0=o0f, in1=o0f, op=ALU.mult))
        t_s4 = DVE.op(lambda e: e.tensor_reduce(out=ss[:, 24:28], in_=oc[1][:], axis=AX.X, op=ALU.add))
        ACT.wait(t_s4, eps_tok)
        ACT.op(lambda e: e.activation(out=ss[:, 24:28], in_=ss[:, 24:28], func=AF.Ln, bias=eps_t[:], scale=1.0 / B_V))
        t_r4 = ACT.op(lambda e: e.activation(out=ss[:, 28:32], in_=ss[:, 24:28], func=AF.Exp, scale=-0.5))
        DVE.wait(t_r4)
        for qb in range(4):
            t_fin = DVE.op(lambda e, qb=qb: e.scalar_tensor_tensor(out=oc[1][:, qb, :], in0=oc[0][:, qb, :], scalar=ss[:, 28 + qb:29 + qb], in1=sub_t[:], op0=ALU.mult, op1=ALU.mult))
        SP.wait(t_fin)
        out_tok = osem.issue(SP, lambda e, g=g: e.dma_start(
            out=o[g * 512:(g + 1) * 512, :].rearrange("(qb p) d -> p qb d", p=128), in_=oc[1][:]))
    P.out_toks.append(out_tok)
    return P.finish()


def build_MIX(with_norm_out):
    P = Prog()
    xres = P.dram("xres", [TPC, D], F32, "ExternalInput")
    oin = P.dram("oin", [TPC, D], F32, "ExternalInput")
    wg = P.dram("wg", [D, D], F32, "ExternalInput")
    wo = P.dram("wo", [D, D], F32, "ExternalInput")
    gn = P.dram("g_norm", [1, D], F32, "ExternalInput")
    xnew = P.dram("xnew", [TPC, D], F32, "ExternalOutput")
    if with_norm_out:
        xnT_out = P.dram("xnT", [D, TPC], BF16, "ExternalOutput")
    PE, ACT, DVE, POOL, SP = P.PE, P.ACT, P.DVE, P.POOL, P.SP
    KC = D // 128
    P.make_ident()
    eps_t, eps_tok = P.const_tile(EPS, "eps")
    gain = P.sb("gain", [128, D], F32)
    ds0 = P.dsem()
    t_g = ds0.issue(SP, lambda e: e.dma_start(out=gain[:], in_=gn.partition_broadcast(128)))
    wgb = P.sb("wgb", [128, KC, D], BF16)
    wob = P.sb("wob", [128, KC, D], BF16)
    t_wg = load_weight_cast(P, wgb, wg, KC, D)
    t_wo = load_weight_cast(P, wob, wo, KC, D)
    NB = TPC // 128
    xt = [P.sb("xt%d" % i, [128, D], F32) for i in range(2)]
    xsem = [P.dsem() for _ in range(2)]
    ot = P.sb("ot", [128, D], F32)
    osem_in = P.dsem()
    junk = P.sb("junk", [128, D], BF16)
    xb = P.sb("xb", [128, D], BF16)
    xnT = P.sb("xnT_s", [128, KC, 128], BF16)
    hb = P.sb("hb", [128, D], BF16)
    hT = P.sb("hT", [128, KC, 128], BF16)
    xo = P.sb("xo", [128, D], F32)
    ss = P.sb("ss", [128, 4], F32)
    rs = P.sb("rs", [128, 4], F32)
    osem = P.dsem()
    osem2 = P.dsem()
    if with_norm_out:
        x2b = P.sb("x2b", [128, D], BF16)
        x2T = P.sb("x2T", [128, KC, 128], BF16)
    T = {"xt_free": [None, None], "ot_free": None, "xb_free": None, "xnT_free": None, "tr_free": None,
         "g_free": None, "hb_free": None, "hT_free": None, "o_free": None, "xo_free": None, "x2T_free": None,
         "x2b_free": None}
    F = {}
    out_toks = {"o": None, "n": None}

    def transposes(src, after):
        PE.wait(after, P.ident_tok, T["tr_free"])
        for kc in range(KC):
            bk = kc // 8
            oo = (kc % 8) * 128
            tt = PE.op(lambda e, kc=kc, bk=bk, oo=oo: e.transpose(out=P.bank_bf(bk)[:, oo:oo + 128], in_=src[:, kc * 128:(kc + 1) * 128], identity=P.ident[:]))
        return tt

    def evac(dst, tt, dst_free):
        DVE.wait(tt, dst_free)
        DVE.op(lambda e: e.tensor_copy(out=dst[:, 0:8, :], in_=P.bank_bf(0)[:, 0:1024]))
        t = DVE.op(lambda e: e.tensor_copy(out=dst[:, 8:16, :], in_=P.bank_bf(1)[:, 0:1024]))
        T["tr_free"] = t
        return t

    def front(b):
        s = b % 2
        SP.wait(T["xt_free"][s])
        t_x = xsem[s].issue(SP, lambda e: e.dma_start(out=xt[s][:], in_=xres[b * 128:(b + 1) * 128, :]))
        SP.wait(T["ot_free"])
        t_o = osem_in.issue(SP, lambda e: e.dma_start(out=ot[:], in_=oin[b * 128:(b + 1) * 128, :]))
        ACT.wait(t_x)
        t_ss = ACT.op(lambda e: e.activation(out=junk[:], in_=xt[s][:], func=AF.Square, accum_out=ss[:, s:s + 1]))
        t_r = emit_rstd(P, ss[:, s:s + 1], D, eps_t, eps_tok, rs[:, s:s + 1], t_ss)
        DVE.wait(t_r, T["xb_free"], t_g)
        t_xb = DVE.op(lambda e: e.scalar_tensor_tensor(out=xb[:], in0=xt[s][:], scalar=rs[:, s:s + 1], in1=gain[:], op0=ALU.mult, op1=ALU.mult))
        tt = transposes(xb, t_xb)
        T["xb_free"] = tt
        t_ev = evac(xnT, tt, T["xnT_free"])
        PE.wait(t_ev, t_wg, T["g_free"])
        for gi in range(4):
            for kc in range(KC):
                tz = PE.op(lambda e, gi=gi, kc=kc: e.matmul(P.bank(2 + gi)[:, :], lhsT=xnT[:, kc, :], rhs=wgb[:, kc, gi * 512:(gi + 1) * 512],
                                                           start=(kc == 0), stop=(kc == KC - 1)))
        T["xnT_free"] = tz
        ACT.wait(tz)
        for gi in range(4):
            t_sg = ACT.op(lambda e, gi=gi: e.activation(out=P.bank(2 + gi)[:, :], in_=P.bank(2 + gi)[:, :], func=AF.Silu), indep=True)
        DVE.wait(t_sg, t_o, T["hb_free"])
        for gi in range(4):
            t_hb = DVE.op(lambda e, gi=gi: e.tensor_tensor(out=hb[:, gi * 512:(gi + 1) * 512], in0=P.bank(2 + gi)[:, :], in1=ot[:, gi * 512:(gi + 1) * 512], op=ALU.mult), indep=True)
        T["g_free"] = t_hb
        T["ot_free"] = t_hb
        F[b] = (t_hb, s)

    def back(b):
        t_hb, s = F.pop(b)
        tt2 = transposes(hb, t_hb)
        T["hb_free"] = tt2
        t_ev2 = evac(hT, tt2, T["hT_free"])
        t_e = None
        for half in range(2):
            PE.wait(t_ev2, t_wo, T["o_free"])
            for gi in range(2):
                c0 = half * 1024 + gi * 512
                for kc in range(KC):
                    tz2 = PE.op(lambda e, gi=gi, kc=kc, c0=c0: e.matmul(P.bank(6 + gi)[:, :], lhsT=hT[:, kc, :], rhs=wob[:, kc, c0:c0 + 512],
                                                                      start=(kc == 0), stop=(kc == KC - 1)))
            if half == 1:
                T["hT_free"] = tz2
            DVE.wait(tz2, T["xo_free"] if half == 0 else None)
            for gi in range(2):
                c0 = half * 1024 + gi * 512
                t_e = DVE.op(lambda e, gi=gi, c0=c0: e.tensor_tensor(out=xo[:, c0:c0 + 512], in0=P.bank(6 + gi)[:, :], in1=xt[s][:, c0:c0 + 512], op=ALU.add), indep=True)
            T["o_free"] = t_e
        T["xt_free"][s] = t_e
        SP.wait(t_e)
        t_out = osem.issue(SP, lambda e: e.dma_start(out=xnew[b * 128:(b + 1) * 128, :], in_=xo[:]))
        out_toks["o"] = t_out
        if with_norm_out:
            ACT.wait(t_e)
            t_ss2 = ACT.op(lambda e: e.activation(out=junk[:], in_=xo[:], func=AF.Square, accum_out=ss[:, 2:3]))
            t_r2 = emit_rstd(P, ss[:, 2:3], D, eps_t, eps_tok, rs[:, 2:3], t_ss2)
            DVE.wait(t_r2, T["x2b_free"])
            t_x2b = DVE.op(lambda e: e.tensor_scalar(out=x2b[:], in0=xo[:], scalar1=rs[:, 2:3], scalar2=None, op0=ALU.mult))
            T["xo_free"] = [t_out, t_x2b]
            tt3 = transposes(x2b, t_x2b)
            T["x2b_free"] = tt3
            t_ev3 = evac(x2T, tt3, T["x2T_free"])
            SP.wait(t_ev3)
            t_o2 = osem2.issue(SP, lambda e: e.dma_start(
                out=xnT_out.rearrange("(kc p) t -> p kc t", p=128)[:, :, b * 128:(b + 1) * 128], in_=x2T[:]))
            T["x2T_free"] = t_o2
            out_toks["n"] = t_o2
        else:
            T["xo_free"] = t_out

    front(0)
    for b in range(NB):
        if b + 1 < NB:
            front(b + 1)
        back(b)
    P.out_toks += [t for t in out_toks.values() if t is not None]
    return P.finish()


_CACHE = {}


def _get(name, fn):
    if name not in _CACHE:
        _CACHE[name] = fn()
    return _CACHE[name]


def _col(v):
    v = np.asarray(v, dtype=np.float32)
    return np.ascontiguousarray(v.reshape(-1, 128).T)


def run(nc, in_maps):
    res = run_bass_kernel_spmd(nc, in_maps, core_ids=list(range(NCORES)))
    return res.results


def _posl(pos):
    return np.ascontiguousarray(np.asarray(pos, dtype=np.int32).reshape(NBLK, 128).T)


def stage_A1(x, inp):
    nc = _get("A1", build_A1)
    w_lat = np.ascontiguousarray(inp["a_w_in"][0][:, :2 * A_LORA + A_ROPE])
    g = np.ascontiguousarray(np.asarray(inp["a_norm"][0], dtype=np.float32)[None, :])
    res = run(nc, [{"x": np.ascontiguousarray(x[c * TPC:(c + 1) * TPC]), "w_lat": w_lat, "g_norm": g} for c in range(NCORES)])
    latT = np.concatenate([r["latT"] for r in res], axis=1)
    kpe = np.concatenate([r["kpe"] for r in res], axis=0)
    return latT, kpe


def stage_A2(latT, kpe, inp):
    nc = _get("A2", build_A2)
    pos = _posl(inp["positions"][0])
    wq = inp["a_w_q_up"][0]
    wkv = inp["a_w_kv_up"][0]
    ims = []
    for c in range(NCORES):
        ims.append({"latT": latT, "kpe": kpe, "pos": pos,
                    "wq": np.ascontiguousarray(wq[:, 2 * c * A_QK:(2 * c + 2) * A_QK]),
                    "wkv": np.ascontiguousarray(wkv[:, 2 * c * 256:(2 * c + 2) * 256]),
                    "gq": _col(inp["a_q_norm"][0]), "gkv": _col(inp["a_kv_norm"][0]),
                    "qgain": np.ascontiguousarray(inp["a_q_gain"][0][None, :]),
                    "kgain": np.ascontiguousarray(inp["a_k_gain"][0][None, :])})
    res = run(nc, ims)
    return np.concatenate([r["o"] for r in res], axis=1)


def stage_MIX(xres, o, w_gate, w_out, g_norm, with_norm_out):
    nc = _get("MIX%d" % int(with_norm_out), lambda: build_MIX(with_norm_out))
    w_gate = np.ascontiguousarray(w_gate)
    w_out = np.ascontiguousarray(w_out)
    g = np.ascontiguousarray(np.asarray(g_norm, dtype=np.float32)[None, :])
    res = run(nc, [{"xres": np.ascontiguousarray(xres[c * TPC:(c + 1) * TPC]),
                    "oin": np.ascontiguousarray(o[c * TPC:(c + 1) * TPC]),
                    "wg": w_gate, "wo": w_out, "g_norm": g} for c in range(NCORES)])
    xnew = np.concatenate([r["xnew"] for r in res], axis=0)
    xnT = np.concatenate([r["xnT"] for r in res], axis=1) if with_norm_out else None
    return xnew, xnT


def stage_B2(xnT, inp):
    nc = _get("B2", build_B2)
    posv = np.asarray(inp["positions"][0], dtype=np.int32)
    pos = _posl(posv)
    w = inp["b_w_in"][0]
    QK = B_H * 2 * B_HD
    ims = []
    for c in range(NCORES):
        wq = np.ascontiguousarray(w[:, c * 256:(c + 1) * 256])
        wkv = np.ascontiguousarray(np.concatenate([w[:, QK + c * 256:QK + (c + 1) * 256],
                                                   w[:, 2 * QK + c * B_V:2 * QK + (c + 1) * B_V]], axis=1))
        ims.append({"xnT": xnT, "pos": pos, "posrow": np.ascontiguousarray(posv[None, 0:512]),
                    "posg": np.ascontiguousarray(posv[None, ::512]), "wq": wq, "wkv": wkv,
                    "g_norm": _col(inp["b_norm"][0]),
                    "qgain": np.ascontiguousarray(inp["b_q_gain"][0][None, :]),
                    "kgain": np.ascontiguousarray(inp["b_k_gain"][0][None, :]),
                    "lam4": np.ascontiguousarray(np.stack([inp["b_lambda_q1"][0], inp["b_lambda_k1"][0],
                                                           inp["b_lambda_q2"][0], inp["b_lambda_k2"][0]])),
                    "subln": np.ascontiguousarray(inp["b_subln"][0][None, :]),
                    "slope": np.full((1, 1), 2.0 ** (-8.0 * (c + 1) / B_H), dtype=np.float32)})
    res = run(nc, ims)
    return np.concatenate([r["o"] for r in res], axis=1)


def kernel(**inputs):
    inp = {k: np.asarray(v) for k, v in inputs.items()}
    x = np.ascontiguousarray(inp["x"][0])
    latT, kpe = stage_A1(x, inp)
    oA = stage_A2(latT, kpe, inp)
    x1, xn1T = stage_MIX(x, oA, inp["a_w_in"][0][:, 2 * A_LORA + A_ROPE:], inp["a_w_out"][0], inp["a_norm"][0], True)
    QK = B_H * 2 * B_HD
    oB = stage_B2(xn1T, inp)
    x2, _ = stage_MIX(x1, oB, inp["b_w_in"][0][:, 2 * QK + B_H * B_V:], inp["b_w_out"][0], inp["b_norm"][0], False)
    return x2[None].astype(np.float32)
```

```python
import math
from contextlib import ExitStack

import numpy as np
import concourse.bass as bass
import concourse.mybir as mybir
from concourse.bass_utils import run_bass_kernel_spmd

F32 = mybir.dt.float32
BF16 = mybir.dt.bfloat16
I32 = mybir.dt.int32
AF = mybir.ActivationFunctionType
ALU = mybir.AluOpType
AX = mybir.AxisListType

NCORES = 8
S = 16384
D = 2048
TPC = S // NCORES
NBLK = S // 128
EPS = 1e-6
CHUNK = 64
A_H, A_NOPE, A_ROPE, A_QK, A_V, A_LORA = 16, 128, 64, 192, 128, 512
B_H, B_HD, B_V = 8, 128, 256
TWO_PI = 2.0 * math.pi
NEG_BIG = -1.0e30


class Eng:
    def __init__(self, name, sem, serialize=False):
        self.name = name
        self.sem = sem
        self.count = 0
        self.waited = {}
        self.thunks = []
        self.serialize = serialize

    def wait(self, *toks):
        for tok in toks:
            if tok is None:
                continue
            if isinstance(tok, (list, tuple)) and (len(tok) == 0 or isinstance(tok[0], (list, tuple)) or tok[0] is None):
                self.wait(*tok)
                continue
            sem, val = tok
            if self.waited.get(sem, 0) >= val:
                continue
            self.waited[sem] = val
            self.thunks.append(lambda e, sem=sem, val=val: e.wait_ge(sem, val))

    def op(self, fn, pub=True, indep=False):
        if self.serialize and not indep and self.count > 0:
            self.wait((self.sem, self.count))
        self.count += 1
        c = self.count
        sem = self.sem
        self.thunks.append(lambda e: fn(e).then_inc(sem, 1))
        return (sem, c)


class DmaSem:
    def __init__(self, sem):
        self.sem = sem
        self.n = 0

    def issue(self, eng, fn):
        self.n += 16
        sem = self.sem
        eng.thunks.append(lambda e: fn(e).then_inc(sem, 16))
        return (sem, self.n)


class Prog:
    def __init__(self):
        self.nc = bass.Bass("TRN2", target_bir_lowering=False)
        self.es = ExitStack()
        self._n = 0
        self.SP = Eng("sync", self.sem("s_sp"))
        self.ACT = Eng("scalar", self.sem("s_act"), serialize=True)
        self.DVE = Eng("vector", self.sem("s_dve"), serialize=True)
        self.POOL = Eng("gpsimd", self.sem("s_pool"), serialize=True)
        self.PE = Eng("tensor", self.sem("s_pe"))
        self.engs = [self.SP, self.ACT, self.DVE, self.POOL, self.PE]
        self.psum = self.es.enter_context(self.nc.psum_tensor("psum", [128, 8, 512], F32))
        self.out_toks = []

    def uid(self, p):
        self._n += 1
        return "%s%d" % (p, self._n)

    def sem(self, name=None):
        return self.es.enter_context(self.nc.semaphore(name or self.uid("sem")))

    def dsem(self, name=None):
        return DmaSem(self.sem(name))

    def sb(self, name, shape, dt):
        return self.es.enter_context(self.nc.sbuf_tensor(name, list(shape), dt))

    def dram(self, name, shape, dt, kind):
        return self.nc.dram_tensor(name, list(shape), dt, kind=kind).ap()

    def bank(self, b):
        return self.psum[:, b, :]

    def bank_bf(self, b):
        return self.psum[:, b, :].bitcast(BF16)

    def finish(self):
        self.SP.wait(*self.out_toks)
        with self.nc.Block() as block:
            for eng in self.engs:
                if not eng.thunks:
                    continue

                def body(e, eng=eng):
                    for th in eng.thunks:
                        th(e)

                getattr(block, eng.name)(body)
        self.es.close()
        return self.nc

    def make_ident(self):
        idf = self.sb("ident_f", [128, 128], F32)
        idb = self.sb("ident_b", [128, 128], BF16)
        t0 = self.POOL.op(lambda e: e.memset(idf[:], 1.0))
        self.POOL.wait(t0)
        t1 = self.POOL.op(lambda e: e.affine_select(out=idf[:], in_=idf[:], pattern=[[-1, 128]],
                                                    compare_op=ALU.is_equal, fill=0.0, base=0,
                                                    channel_multiplier=1))
        self.DVE.wait(t1)
        t2 = self.DVE.op(lambda e: e.tensor_copy(out=idb[:], in_=idf[:]))
        self.ident = idb
        self.ident_tok = t2
        return idb

    def const_tile(self, val, name=None):
        t = self.sb(name or self.uid("c"), [128, 1], F32)
        tok = self.POOL.op(lambda e: e.memset(t[:], float(val)))
        return t, tok

    def load_weight_bf16(self, dst, src, nkc, ncols, gcol, gcol_tok, stage, stage_sems, state):
        W = stage[0].shape[-1]
        t_cv = None
        for kc in range(nkc):
            for c0 in range(0, ncols, W):
                c1 = min(ncols, c0 + W)
                i = state["i"] % len(stage)
                st = stage[i]
                self.SP.wait(state["free"][i])
                t_ld = stage_sems[i].issue(self.SP, lambda e, st=st, kc=kc, c0=c0, c1=c1: e.dma_start(
                    out=st[:, 0:c1 - c0], in_=src[kc * 128:(kc + 1) * 128, c0:c1]))
                self.ACT.wait(t_ld, gcol_tok)
                if gcol is not None:
                    t_cv = self.ACT.op(lambda e, st=st, kc=kc, c0=c0, c1=c1: e.activation(
                        out=dst[:, kc, c0:c1], in_=st[:, 0:c1 - c0], func=AF.Copy, scale=gcol[:, kc:kc + 1]), indep=True)
                else:
                    t_cv = self.ACT.op(lambda e, st=st, kc=kc, c0=c0, c1=c1: e.activation(
                        out=dst[:, kc, c0:c1], in_=st[:, 0:c1 - c0], func=AF.Copy), indep=True)
                state["free"][i] = t_cv
                state["i"] += 1
        return t_cv


def load_weight_cast(P, dst, src, nkc, ncols):
    dsm = P.dsem()
    tok = None
    for kc in range(nkc):
        tok = dsm.issue(P.POOL, lambda e, kc=kc: e.dma_start(out=dst[:, kc, :], in_=src[kc * 128:(kc + 1) * 128, :],
                                                            max_dma_last_dim=4096))
    return tok


def _stage(P, ncols, n=3):
    stage = [P.sb(P.uid("wst"), [128, ncols], F32) for _ in range(n)]
    sems = [P.dsem() for _ in range(n)]
    state = {"i": 0, "free": [None] * n}
    return stage, sems, state


def emit_rstd(P, ss, n, eps_t, eps_tok, out, after):
    P.ACT.wait(after, eps_tok)
    t = P.ACT.op(lambda e: e.activation(out=out[:], in_=ss[:], func=AF.Sqrt, bias=eps_t[:], scale=1.0 / n))
    P.DVE.wait(t)
    return P.DVE.op(lambda e: e.reciprocal(out=out[:], in_=out[:]))


def build_A1():
    P = Prog()
    nc = P.nc
    NL = 2 * A_LORA + A_ROPE
    x = P.dram("x", [TPC, D], F32, "ExternalInput")
    w = P.dram("w_lat", [D, NL], F32, "ExternalInput")
    gn = P.dram("g_norm", [1, D], F32, "ExternalInput")
    latT = P.dram("latT", [2 * A_LORA, TPC], BF16, "ExternalOutput")
    kpe = P.dram("kpe", [TPC, A_ROPE], F32, "ExternalOutput")
    KC = D // 128
    P.make_ident()
    eps_t, eps_tok = P.const_tile(EPS, "eps")
    gain = P.sb("gain", [128, D], F32)
    ds0 = P.dsem()
    t_g = ds0.issue(P.SP, lambda e: e.dma_start(out=gain[:], in_=gn.partition_broadcast(128)))
    wb = P.sb("wb", [128, KC, NL], BF16)
    t_w = load_weight_cast(P, wb, w, KC, NL)

    NB = TPC // 128
    xt = [P.sb("xt%d" % i, [128, D], F32) for i in range(2)]
    xsem = [P.dsem() for _ in range(2)]
    xfree = [None, None]
    junk = P.sb("junk", [128, D], BF16)
    xb = P.sb("xb", [128, D], BF16)
    xnT = P.sb("xnT", [128, KC, 128], BF16)
    ss = P.sb("ss", [128, 4], F32)
    rs = P.sb("rs", [128, 4], F32)
    latb = P.sb("latb", [128, 2 * A_LORA], BF16)
    kpt = P.sb("kpt", [128, A_ROPE], F32)
    latTt = P.sb("latTt", [128, 8, 128], BF16)
    osem1 = P.dsem()
    osem2 = P.dsem()
    t_prev_z = None
    t_prev_lt = None
    t_out1 = None
    t_out2 = None
    t_xb_free = None
    t_evac_lt = None
    t_z_free = None
    for b in range(NB):
        s = b % 2
        P.SP.wait(xfree[s])
        t_x = xsem[s].issue(P.SP, lambda e, s=s, b=b: e.dma_start(out=xt[s][:], in_=x[b * 128:(b + 1) * 128, :]))
        P.ACT.wait(t_x)
        t_ss = P.ACT.op(lambda e, s=s: e.activation(out=junk[:], in_=xt[s][:], func=AF.Square, accum_out=ss[:, 0:1]))
        t_r = emit_rstd(P, ss[:, 0:1], D, eps_t, eps_tok, rs[:, 0:1], t_ss)
        P.DVE.wait(t_r, t_xb_free, t_g)
        t_xb = P.DVE.op(lambda e, s=s: e.scalar_tensor_tensor(out=xb[:], in0=xt[s][:], scalar=rs[:, 0:1], in1=gain[:], op0=ALU.mult, op1=ALU.mult))
        xfree[s] = t_xb
        P.PE.wait(t_xb, P.ident_tok, t_prev_z)
        for kc in range(KC):
            bk = kc // 8
            o = (kc % 8) * 128
            tt = P.PE.op(lambda e, kc=kc, bk=bk, o=o: e.transpose(out=P.bank_bf(bk)[:, o:o + 128], in_=xb[:, kc * 128:(kc + 1) * 128], identity=P.ident[:]),
                         pub=(kc == KC - 1))
        t_xb_free = tt
        P.DVE.wait(tt)
        P.DVE.op(lambda e: e.tensor_copy(out=xnT[:, 0:8, :], in_=P.bank_bf(0)[:, 0:1024]), pub=False)
        t_ev = P.DVE.op(lambda e: e.tensor_copy(out=xnT[:, 8:16, :], in_=P.bank_bf(1)[:, 0:1024]))
        P.PE.wait(t_ev, t_w, t_z_free)
        for gi, (c0, c1) in enumerate([(0, 512), (512, 1024), (1024, NL)]):
            for kc in range(KC):
                tz = P.PE.op(lambda e, gi=gi, c0=c0, c1=c1, kc=kc: e.matmul(
                    P.bank(2 + gi)[:, 0:c1 - c0], lhsT=xnT[:, kc, :], rhs=wb[:, kc, c0:c1],
                    start=(kc == 0), stop=(kc == KC - 1)), pub=(gi == 2 and kc == KC - 1))
        t_prev_z = tz
        P.ACT.wait(tz)
        t_s1 = P.ACT.op(lambda e: e.activation(out=junk[:, 0:512], in_=P.bank(2)[:, :], func=AF.Square, accum_out=ss[:, 1:2]))
        t_s2 = P.ACT.op(lambda e: e.activation(out=junk[:, 512:1024], in_=P.bank(3)[:, :], func=AF.Square, accum_out=ss[:, 2:3]))
        t_r1 = emit_rstd(P, ss[:, 1:3], A_LORA, eps_t, eps_tok, rs[:, 1:3], t_s2)
        P.ACT.wait(t_r1, t_prev_lt)
        P.ACT.op(lambda e: e.activation(out=latb[:, 0:512], in_=P.bank(2)[:, :], func=AF.Copy, scale=rs[:, 1:2]), pub=False)
        t_lb = P.ACT.op(lambda e: e.activation(out=latb[:, 512:1024], in_=P.bank(3)[:, :], func=AF.Copy, scale=rs[:, 2:3]))
        P.DVE.wait(tz, t_out2)
        t_kp = P.DVE.op(lambda e: e.tensor_copy(out=kpt[:], in_=P.bank(4)[:, 0:A_ROPE]))
        t_z_free = [t_lb, t_kp]
        P.SP.wait(t_kp)
        t_out2 = osem2.issue(P.SP, lambda e, b=b: e.dma_start(out=kpe[b * 128:(b + 1) * 128, :], in_=kpt[:]))
        P.PE.wait(t_lb, t_evac_lt)
        for j in range(8):
            tl = P.PE.op(lambda e, j=j: e.transpose(out=P.bank_bf(5)[:, j * 128:(j + 1) * 128], in_=latb[:, j * 128:(j + 1) * 128], identity=P.ident[:]),
                         pub=(j == 7))
        t_prev_lt = tl
        P.DVE.wait(tl, t_out1)
        t_evac_lt = P.DVE.op(lambda e: e.tensor_copy(out=latTt[:].rearrange("p j t -> p (j t)"), in_=P.bank_bf(5)[:, 0:1024]))
        P.SP.wait(t_evac_lt)
        t_out1 = osem1.issue(P.SP, lambda e, b=b: e.dma_start(
            out=latT.rearrange("(j p) t -> p j t", p=128)[:, :, b * 128:(b + 1) * 128], in_=latTt[:]))
    P.out_toks += [t_out1, t_out2]
    return P.finish()


def emit_rope_tables(P, pos_i, pos_tok, cs):
    R2 = A_ROPE // 2
    invf = (np.float32(10000.0) ** (-(np.arange(0, A_ROPE, 2, dtype=np.float32)) / np.float32(A_ROPE))).astype(np.float32)
    posf = P.sb("posf", [128, NBLK], F32)
    CB = 16
    u = P.sb("rt_u", [128, CB, R2], F32)
    tt = P.sb("rt_t", [128, CB, R2], F32)
    ki = P.sb("rt_ki", [128, CB, R2], I32)
    kf = P.sb("rt_kf", [128, CB, R2], F32)
    r = P.sb("rt_r", [128, CB, R2], F32)
    V = P.DVE
    V.wait(pos_tok)
    V.op(lambda e: e.tensor_copy(out=posf[:], in_=pos_i[:]), pub=False)
    C1 = 6.28125
    C2 = float(np.float32(TWO_PI - C1))
    last = None
    for ch in range(NBLK // CB):
        b0 = ch * CB
        for which in range(2):
            for i in range(R2):
                V.op(lambda e, i=i, b0=b0: e.tensor_scalar(out=u[:, :, i], in0=posf[:, b0:b0 + CB], scalar1=float(invf[i]), scalar2=None, op0=ALU.mult), pub=False)
            if which == 0:
                V.op(lambda e: e.tensor_scalar(out=u[:], in0=u[:], scalar1=float(math.pi / 2), scalar2=None, op0=ALU.add), pub=False)
            V.op(lambda e: e.tensor_scalar(out=tt[:], in0=u[:], scalar1=float(1.0 / TWO_PI), scalar2=None, op0=ALU.mult), pub=False)
            V.op(lambda e: e.tensor_copy(out=ki[:], in_=tt[:]), pub=False)
            V.op(lambda e: e.tensor_copy(out=kf[:], in_=ki[:]), pub=False)
            V.op(lambda e: e.scalar_tensor_tensor(out=r[:], in0=kf[:], scalar=-C1, in1=u[:], op0=ALU.mult, op1=ALU.add), pub=False)
            V.op(lambda e: e.scalar_tensor_tensor(out=r[:], in0=kf[:], scalar=-C2, in1=r[:], op0=ALU.mult, op1=ALU.add), pub=False)
            V.op(lambda e: e.tensor_scalar(out=tt[:], in0=r[:], scalar1=float(math.pi), scalar2=-TWO_PI, op0=ALU.is_gt, op1=ALU.mult), pub=False)
            V.op(lambda e: e.tensor_tensor(out=r[:], in0=r[:], in1=tt[:], op=ALU.add), pub=False)
            V.op(lambda e: e.tensor_scalar(out=tt[:], in0=r[:], scalar1=float(-math.pi), scalar2=TWO_PI, op0=ALU.is_lt, op1=ALU.mult), pub=False)
            V.op(lambda e: e.tensor_tensor(out=r[:], in0=r[:], in1=tt[:], op=ALU.add), pub=False)
            tr = V.op(lambda e: e.tensor_scalar(out=r[:], in0=r[:], scalar1=float(-math.pi), scalar2=float(math.pi), op0=ALU.max, op1=ALU.min))
            P.ACT.wait(tr)
            ta = P.ACT.op(lambda e, b0=b0, which=which: e.activation(out=cs[:, b0:b0 + CB, which * R2:(which + 1) * R2], in_=r[:], func=AF.Sin))
            V.wait(ta)
            last = ta
    return last


def emit_rope(P, src, dst, cs, blk, tmp):
    V = P.DVE
    h = A_ROPE // 2
    cos = cs[:, blk, 0:h]
    sin = cs[:, blk, h:2 * h]
    V.op(lambda e: e.tensor_tensor(out=tmp[:, 0:h], in0=src[:, 0:h], in1=cos, op=ALU.mult), pub=False)
    V.op(lambda e: e.tensor_tensor(out=tmp[:, h:2 * h], in0=src[:, h:2 * h], in1=sin, op=ALU.mult), pub=False)
    V.op(lambda e: e.tensor_tensor(out=tmp[:, 2 * h:3 * h], in0=src[:, 0:h], in1=sin, op=ALU.mult), pub=False)
    V.op(lambda e: e.tensor_tensor(out=tmp[:, 3 * h:4 * h], in0=src[:, h:2 * h], in1=cos, op=ALU.mult), pub=False)
    V.op(lambda e: e.tensor_tensor(out=dst[:, 0:h], in0=tmp[:, 0:h], in1=tmp[:, h:2 * h], op=ALU.subtract), pub=False)
    return V.op(lambda e: e.tensor_tensor(out=dst[:, h:2 * h], in0=tmp[:, 2 * h:3 * h], in1=tmp[:, 3 * h:4 * h], op=ALU.add))


def emit_absmax_bcast(P, src_dram, n, out, dsem, tmp):
    t = dsem.issue(P.SP, lambda e: e.dma_start(out=tmp[:, 0:n], in_=src_dram.partition_broadcast(128)))
    P.DVE.wait(t)
    return P.DVE.op(lambda e: e.tensor_reduce(out=out, in_=tmp[:, 0:n], axis=AX.X, op=ALU.max, apply_absolute_value=True))


class AttnState:
    pass


class _Stop(Exception):
    pass


DBG_STEP = [0]
DBG_SKIP = [0]
DBG_VAR = [0]


def _chk(n):
    if DBG_STEP[0] == n:
        if DBG_SKIP[0] > 0:
            DBG_SKIP[0] -= 1
            return
        raise _Stop()


def emit_attention_group(P, st, g, qk_parts, vb, dvp, nacc_banks, acc_bank0, s_banks, exp_scale, bias_fn,
                         alibi=None, mask_pool=True, hooks=None):
    PE, ACT, DVE, POOL = P.PE, P.ACT, P.DVE, P.POOL
    per_bank = 4 // nacc_banks
    nk = 4 * g + 4
    nslots = len(s_banks)

    def acc(qb):
        bk = acc_bank0 + qb // per_bank
        o = (qb % per_bank) * dvp
        return P.bank(bk)[:, o:o + dvp]

    first_in_bank = [True] * nacc_banks
    PE.wait(st.q_ready, st.acc_free)
    pend = []
    t_last = None

    def emit_pv(j, slot, r, t_p):
        nonlocal t_last
        PE.wait(t_p)
        for qb in range(r, 4):
            bk = qb // per_bank
            stt = first_in_bank[bk]
            first_in_bank[bk] = False
            last = (qb == 3)
            tk = PE.op(lambda e, qb=qb, j=j, slot=slot, stt=stt: e.matmul(
                acc(qb), lhsT=st.pT[slot][:, qb * 128:(qb + 1) * 128], rhs=vb[:, j, 0:dvp],
                start=stt, stop=(j == nk - 1), skip_group_check=True), pub=last)
        st.p_free[slot] = tk
        t_last = tk

    for j in range(nk):
        r = max(0, j - 4 * g)
        c0 = r * 128
        slot = st.it % nslots
        st.it += 1
        PE.wait(st.s_free[slot])
        for pi, (ktf, qt) in enumerate(qk_parts):
            ts = PE.op(lambda e, ktf=ktf, qt=qt, j=j, slot=slot, c0=c0, pi=pi: e.matmul(
                P.bank(s_banks[slot])[:, c0:512], lhsT=ktf(j), rhs=qt[:, c0:512],
                start=(pi == 0), stop=(pi == len(qk_parts) - 1)), pub=(pi == len(qk_parts) - 1))
        if len(pend) >= nslots - 1:
            emit_pv(*pend.pop(0))
        if hooks and j in hooks:
            for hk in hooks[j]:
                hk()
        src = P.bank(s_banks[slot])
        if alibi is not None:
            sbt = alibi["sbuf"][slot]
            DVE.wait(ts, st.sb_free[slot])
            if r < 4 and j >= 4 * g:
                DVE.op(lambda e, c0=c0, src=src, sbt=sbt: e.tensor_tensor(out=sbt[:, c0:c0 + 128], in0=src[:, c0:c0 + 128], in1=alibi["Tdiag"][:], op=ALU.add), indep=True)
                if c0 + 128 < 512:
                    td = DVE.op(lambda e, c0=c0, src=src, sbt=sbt: e.tensor_tensor(out=sbt[:, c0 + 128:512], in0=src[:, c0 + 128:512], in1=alibi["T2"][:, c0 + 128:512], op=ALU.add), indep=True)
                else:
                    td = (DVE.sem, DVE.count)
            else:
                td = DVE.op(lambda e, src=src, sbt=sbt: e.tensor_tensor(out=sbt[:], in0=src[:], in1=alibi["T2"][:], op=ALU.add), indep=True)
            st.s_free[slot] = td
            ACT.wait(td, st.p_free[slot])
            if j >= 4 * g:
                ACT.op(lambda e, c0=c0, sbt=sbt, slot=slot: e.activation(out=st.pT[slot][:, c0:c0 + 128], in_=sbt[:, c0:c0 + 128], func=AF.Exp, bias=alibi["negM"][:], scale=exp_scale), indep=True)
                if c0 + 128 < 512:
                    tp = ACT.op(lambda e, c0=c0, sbt=sbt, slot=slot, j=j: e.activation(out=st.pT[slot][:, c0 + 128:512], in_=sbt[:, c0 + 128:512], func=AF.Exp, bias=bias_fn(j), scale=exp_scale), indep=True)
                else:
                    tp = (ACT.sem, ACT.count)
            else:
                tp = ACT.op(lambda e, sbt=sbt, slot=slot, j=j: e.activation(out=st.pT[slot][:], in_=sbt[:], func=AF.Exp, bias=bias_fn(j), scale=exp_scale), indep=True)
            st.sb_free[slot] = tp
        else:
            ACT.wait(ts, st.p_free[slot])
            tp = ACT.op(lambda e, c0=c0, src=src, slot=slot, j=j: e.activation(out=st.pT[slot][:, c0:512], in_=src[:, c0:512], func=AF.Exp, bias=bias_fn(j), scale=exp_scale), indep=True)
            st.s_free[slot] = tp
            if j >= 4 * g:
                POOL.wait(tp)
                tp = POOL.op(lambda e, c0=c0, slot=slot: e.memset(st.pT[slot][64:128, c0:c0 + 64], 0.0), indep=True)
        pend.append((j, slot, r, tp))
    while pend:
        emit_pv(*pend.pop(0))
    return t_last


def build_A2(NG=S // 512, NH=2, dbg=0, G0=0):
    P = Prog()
    latT = P.dram("latT", [2 * A_LORA, S], BF16, "ExternalInput")
    kpe = P.dram("kpe", [S, A_ROPE], F32, "ExternalInput")
    pos = P.dram("pos", [128, NBLK], I32, "ExternalInput")
    wq = P.dram("wq", [A_LORA, 2 * A_QK], F32, "ExternalInput")
    wkv = P.dram("wkv", [A_LORA, 2 * (A_NOPE + A_V)], F32, "ExternalInput")
    gq = P.dram("gq", [128, 4], F32, "ExternalInput")
    gkv = P.dram("gkv", [128, 4], F32, "ExternalInput")
    qgain = P.dram("qgain", [1, A_QK], F32, "ExternalInput")
    kgain = P.dram("kgain", [1, A_QK], F32, "ExternalInput")
    o = P.dram("o", [S, 2 * A_V], F32, "ExternalOutput")
    PE, ACT, DVE, POOL, SP = P.PE, P.ACT, P.DVE, P.POOL, P.SP
    P.make_ident()
    eps_t, eps_tok = P.const_tile(EPS, "eps")
    mhalf, mhalf_tok = P.const_tile(-0.5, "mhalf")
    ds = P.dsem()
    gq_t = P.sb("gq_t", [128, 4], F32)
    gkv_t = P.sb("gkv_t", [128, 4], F32)
    SP_tok = ds.issue(SP, lambda e: e.dma_start(out=gq_t[:], in_=gq))
    SP_tok = ds.issue(SP, lambda e: e.dma_start(out=gkv_t[:], in_=gkv))
    qg_t = P.sb("qg_t", [128, A_QK], F32)
    kg_t = P.sb("kg_t", [128, A_QK], F32)
    ds.issue(SP, lambda e: e.dma_start(out=qg_t[:], in_=qgain.partition_broadcast(128)))
    t_small = ds.issue(SP, lambda e: e.dma_start(out=kg_t[:], in_=kgain.partition_broadcast(128)))
    pos_i = P.sb("pos_i", [128, NBLK], I32)
    t_pos = ds.issue(SP, lambda e: e.dma_start(out=pos_i[:], in_=pos))
    t_small = t_pos
    wq_b = P.sb("wq_b", [128, 4, 2 * A_QK], BF16)
    wkv_b = P.sb("wkv_b", [128, 4, 512], BF16)
    stage, ssems, sstate = _stage(P, 512, n=2)
    t_wq = P.load_weight_bf16(wq_b, wq, 4, 2 * A_QK, gq_t, t_pos, stage, ssems, sstate)
    t_wkv = P.load_weight_bf16(wkv_b, wkv, 4, 512, gkv_t, t_pos, stage, ssems, sstate)
    mq = P.sb("mq", [128, 2], F32)
    negM = P.sb("negM", [128, 1], F32)
    DVE.wait(t_small)
    DVE.op(lambda e: e.tensor_reduce(out=mq[:, 0:1], in_=qg_t[:], axis=AX.X, op=ALU.max, apply_absolute_value=True), pub=False)
    t_mk = DVE.op(lambda e: e.tensor_reduce(out=mq[:, 1:2], in_=kg_t[:], axis=AX.X, op=ALU.max, apply_absolute_value=True))
    DVE.wait(t_mk)
    t_negM = DVE.op(lambda e: e.scalar_tensor_tensor(out=negM[:], in0=mq[:, 0:1], scalar=-math.sqrt(A_QK), in1=mq[:, 1:2], op0=ALU.mult, op1=ALU.mult))
    cs = P.sb("cs", [128, NBLK, A_ROPE], F32)
    t_cs = emit_rope_tables(P, pos_i, t_pos, cs)

    KTn = P.sb("KTn", [128, S], BF16)
    KTr = P.sb("KTr", [128, S], BF16)
    dvp = A_V + 16
    vb = P.sb("vb", [128, NBLK, dvp], BF16)
    t_ones = POOL.op(lambda e: e.memset(vb[:, :, A_V:dvp], 1.0))
    latg = [P.sb("latg%d" % i, [128, 4, 512], BF16) for i in range(2)]
    latsem = [P.dsem() for _ in range(2)]
    latfree = [None, None]
    kpg = [P.sb("kpg%d" % i, [128, 4, A_ROPE], F32) for i in range(2)]
    kpsem = [P.dsem() for _ in range(2)]
    kpfree = [None, None]
    full4 = P.sb("full4", [128, 4, A_QK], F32)
    fn4 = P.sb("fn4", [128, 4, A_QK], F32)
    junk4 = P.sb("junk4", [128, 4, A_QK], F32)
    rt4 = P.sb("rt4", [128, 4, 128], F32)
    nb4 = P.sb("nb4", [128, 4, 256], BF16)
    t_nbz = POOL.op(lambda e: e.memset(nb4[:, :, A_QK:256], 0.0))
    ss = P.sb("ss", [128, 16], F32)
    QTn = [P.sb("QTn%d" % i, [128, 512], BF16) for i in range(2)]
    QTr = [P.sb("QTr%d" % i, [128, 512], BF16) for i in range(2)]
    st = AttnState()
    st.pT = [P.sb("pT%d" % i, [128, 512], BF16) for i in range(4)]
    st.p_free = [None] * 4
    st.s_free = [None] * 4
    st.it = 0
    st.acc_free = None
    st.q_ready = None
    osb = [P.sb("osb%d" % i, [128, 4, A_V], F32) for i in range(2)]
    osem = [P.dsem() for _ in range(2)]
    rec = P.sb("rec", [128, 4], F32)
    BK_P0 = 4
    S_BANKS = [0, 1, 6, 7]

    def take_slot():
        slot = st.it % len(S_BANKS)
        st.it += 1
        PE.wait(st.s_free[slot])
        return slot, S_BANKS[slot]

    TR_OFF = 512
    tok = {"p_free": None, "full_free": None, "nb_free": None, "tr_free": None}
    H2 = A_ROPE // 2

    def proj_part1(latt, wsel, ncol, is_k, kp, gain_t, blk0):
        PE.wait(tok["p_free"])
        for b in range(4):
            bk = BK_P0 + b // 2
            off = (b % 2) * ncol
            for kc in range(4):
                tp = PE.op(lambda e, b=b, bk=bk, off=off, kc=kc: e.matmul(
                    P.bank(bk)[:, off:off + ncol], lhsT=latt[:, kc, b * 128:(b + 1) * 128], rhs=wsel(kc),
                    start=(kc == 0), stop=(kc == 3), skip_group_check=True))
        DVE.wait(tp, tok["full_free"])
        for h2 in range(2):
            src = P.bank(BK_P0 + h2)[:, 0:2 * ncol].rearrange("p (b c) -> p b c", b=2)
            if is_k:
                DVE.op(lambda e, h2=h2, src=src: e.tensor_copy(out=full4[:, 2 * h2:2 * h2 + 2, 0:A_NOPE], in_=src[:, :, 0:A_NOPE]))
                DVE.op(lambda e, h2=h2, src=src: e.tensor_copy(out=vb[:, blk0 + 2 * h2:blk0 + 2 * h2 + 2, 0:A_V], in_=src[:, :, A_NOPE:A_NOPE + A_V]))
            else:
                DVE.op(lambda e, h2=h2, src=src: e.tensor_copy(out=full4[:, 2 * h2:2 * h2 + 2, :], in_=src[:, :, 0:A_QK]))
        tok["p_free"] = (DVE.sem, DVE.count)
        if is_k:
            DVE.op(lambda e: e.tensor_copy(out=full4[:, :, A_NOPE:A_QK], in_=kp[:]))
        DVE.op(lambda e: e.tensor_tensor(out=junk4[:], in0=full4[:], in1=full4[:], op=ALU.mult))
        t_ss = DVE.op(lambda e: e.tensor_reduce(out=ss[:, 0:4], in_=junk4[:], axis=AX.X, op=ALU.add))
        ACT.wait(t_ss, eps_tok)
        ACT.op(lambda e: e.activation(out=ss[:, 4:8], in_=ss[:, 0:4], func=AF.Ln, bias=eps_t[:], scale=1.0 / A_QK))
        t_rs = ACT.op(lambda e: e.activation(out=ss[:, 8:12], in_=ss[:, 4:8], func=AF.Exp, scale=-0.5))
        DVE.wait(t_rs, tok["nb_free"])
        for b in range(4):
            DVE.op(lambda e, b=b: e.scalar_tensor_tensor(out=fn4[:, b, :], in0=full4[:, b, :], scalar=ss[:, 8 + b:9 + b], in1=gain_t[:], op0=ALU.mult, op1=ALU.mult))
        DVE.op(lambda e: e.tensor_copy(out=nb4[:, :, 0:A_NOPE], in_=fn4[:, :, 0:A_NOPE]))
        cos = cs[:, blk0:blk0 + 4, 0:H2]
        sin = cs[:, blk0:blk0 + 4, H2:2 * H2]
        x1 = fn4[:, :, A_NOPE:A_NOPE + H2]
        x2 = fn4[:, :, A_NOPE + H2:A_QK]
        DVE.op(lambda e: e.tensor_tensor(out=rt4[:, :, 0:H2], in0=x1, in1=cos, op=ALU.mult))
        DVE.op(lambda e: e.tensor_tensor(out=rt4[:, :, H2:2 * H2], in0=x2, in1=sin, op=ALU.mult))
        DVE.op(lambda e: e.tensor_tensor(out=rt4[:, :, 2 * H2:3 * H2], in0=x1, in1=sin, op=ALU.mult))
        DVE.op(lambda e: e.tensor_tensor(out=rt4[:, :, 3 * H2:4 * H2], in0=x2, in1=cos, op=ALU.mult))
        DVE.op(lambda e: e.tensor_tensor(out=nb4[:, :, A_NOPE:A_NOPE + H2], in0=rt4[:, :, 0:H2], in1=rt4[:, :, H2:2 * H2], op=ALU.subtract))
        t_nb = DVE.op(lambda e: e.tensor_tensor(out=nb4[:, :, A_NOPE + H2:A_QK], in0=rt4[:, :, 2 * H2:3 * H2], in1=rt4[:, :, 3 * H2:4 * H2], op=ALU.add))
        tok["full_free"] = t_nb
        return t_nb

    def proj_part2(t_nb, dstT_n, dstT_r, tcol0, extra_wait=None):
        tslot, BK_TN = take_slot()
        PE.wait(t_nb)
        for b in range(4):
            PE.op(lambda e, b=b: e.transpose(out=P.bank_bf(BK_TN)[:, b * 128:(b + 1) * 128], in_=nb4[:, b, 0:A_NOPE], identity=P.ident[:]))
        for b in range(4):
            t_tr = PE.op(lambda e, b=b: e.transpose(out=P.bank_bf(BK_TN)[:, TR_OFF + b * 128:TR_OFF + (b + 1) * 128], in_=nb4[:, b, A_NOPE:256], identity=P.ident[:]))
        tok["nb_free"] = t_tr
        DVE.wait(t_tr, extra_wait)
        DVE.op(lambda e: e.tensor_copy(out=dstT_n[:, tcol0:tcol0 + 512], in_=P.bank_bf(BK_TN)[:, 0:512]))
        t_e = DVE.op(lambda e: e.tensor_copy(out=dstT_r[:, tcol0:tcol0 + 512], in_=P.bank_bf(BK_TN)[:, TR_OFF:TR_OFF + 512]))
        tok["tr_free"] = t_e
        st.s_free[tslot] = t_e
        return t_e

    DVE.wait(t_cs, t_negM)
    PE.wait(t_wq, t_wkv, P.ident_tok)
    POOL.wait(t_ones)
    PE.wait(t_nbz)
    if dbg == 1:
        SP.wait((DVE.sem, DVE.count), (ACT.sem, ACT.count), (POOL.sem, POOL.count))
        return P.finish()
    out_tok = [None, None]
    oi = 0
    attn_done_prev_head = None
    lat_it = [0]

    def load_lat(row0, g, with_kpe):
        s = lat_it[0] % 2
        lat_it[0] += 1
        SP.wait(latfree[s], kpfree[s] if with_kpe else None)
        t_l = latsem[s].issue(SP, lambda e, s=s, g=g: e.dma_start(
            out=latg[s][:], in_=latT[row0:row0 + A_LORA, g * 512:(g + 1) * 512].rearrange("(kc p) t -> p kc t", p=128)))
        t_k = None
        if with_kpe:
            t_k = kpsem[s].issue(SP, lambda e, s=s, g=g: e.dma_start(
                out=kpg[s][:], in_=kpe[g * 512:(g + 1) * 512, :].rearrange("(tb p) d -> p tb d", p=128)))
        return s, t_l, t_k

    for hh in range(NH):
        t_kv_last = None
        for g in range(NG):
            s, t_l, t_k = load_lat(A_LORA, g, True)
            PE.wait(t_l)
            DVE.wait(t_k)
            if g == 0:
                DVE.wait(attn_done_prev_head)
            t_nb = proj_part1(latg[s], lambda kc, hh=hh: wkv_b[:, kc, hh * 256:(hh + 1) * 256], 256, True, kpg[s], kg_t, g * 4)
            latfree[s] = (PE.sem, PE.count)
            kpfree[s] = t_nb
            t_kv_last = proj_part2(t_nb, KTn, KTr, g * 512)
        kv_ready = t_kv_last
        if dbg == 2:
            SP.wait(kv_ready)
            return P.finish()
        q_tok = {}
        q_pend = {}
        lat_pre = {}

        def q_prefetch(g):
            if g < NG and g not in lat_pre:
                lat_pre[g] = load_lat(0, g, False)

        def q_part1(g):
            q_prefetch(g)
            s, t_l, _ = lat_pre[g]
            PE.wait(t_l)
            q_pend[g] = proj_part1(latg[s], lambda kc, hh=hh: wq_b[:, kc, hh * A_QK:(hh + 1) * A_QK], A_QK, False, None, qg_t, g * 4)
            latfree[s] = (PE.sem, PE.count)
            q_prefetch(g + 1)

        def q_part2(g):
            qs = g % 2
            q_tok[g] = proj_part2(q_pend[g], QTn[qs], QTr[qs], 0, q_tok.get(("free", qs)))

        q_part1(G0)
        q_part2(G0)
        for g in range(G0, NG):
            qs = g % 2
            st.q_ready = [q_tok[g], kv_ready]
            parts = [(lambda j: KTn[:, j * 128:(j + 1) * 128], QTn[qs]),
                     (lambda j: KTr[:, j * 128:(j + 1) * 128], QTr[qs])]
            hooks = {}
            if g + 1 < NG:
                nk = 4 * g + 4
                hooks[0] = [lambda g=g: q_part1(g + 1)]
                hooks.setdefault(min(nk - 1, 8), []).append(lambda g=g: q_part2(g + 1))
            t_acc = emit_attention_group(P, st, g, parts, vb, dvp, 2, 2, S_BANKS, 1.0 / math.sqrt(A_QK),
                                         lambda j: negM[:], hooks=hooks)
            q_tok[("free", qs)] = t_acc
            ob = osb[oi % 2]
            DVE.wait(t_acc, out_tok[oi % 2])
            for qb in range(4):
                bk = 2 + qb // 2
                off = (qb % 2) * dvp
                DVE.op(lambda e, qb=qb, bk=bk, off=off: e.reciprocal(out=rec[:, qb:qb + 1], in_=P.bank(bk)[:, off + A_V:off + A_V + 1]))
            for qb in range(4):
                bk = 2 + qb // 2
                off = (qb % 2) * dvp
                t_on = DVE.op(lambda e, qb=qb, bk=bk, off=off, ob=ob: e.tensor_scalar(out=ob[:, qb, :], in0=P.bank(bk)[:, off:off + A_V], scalar1=rec[:, qb:qb + 1], scalar2=None, op0=ALU.mult))
            st.acc_free = t_on
            SP.wait(t_on)
            out_tok[oi % 2] = osem[oi % 2].issue(SP, lambda e, g=g, hh=hh, ob=ob: e.dma_start(
                out=o[g * 512:(g + 1) * 512, hh * A_V:(hh + 1) * A_V].rearrange("(qb p) d -> p qb d", p=128), in_=ob[:]))
            oi += 1
            attn_done_prev_head = t_acc
    P.out_toks += [t for t in out_tok if t is not None]
    return P.finish()


def build_B2(NG=S // 512, dbg=0):
    P = Prog()
    xnT = P.dram("xnT", [D, S], BF16, "ExternalInput")
    pos = P.dram("pos", [128, NBLK], I32, "ExternalInput")
    posrow = P.dram("posrow", [1, 512], I32, "ExternalInput")
    posg = P.dram("posg", [1, S // 512], I32, "ExternalInput")
    wq = P.dram("wq", [D, 2 * B_HD], F32, "ExternalInput")
    wkv = P.dram("wkv", [D, 2 * B_HD + B_V], F32, "ExternalInput")
    gn = P.dram("g_norm", [128, D // 128], F32, "ExternalInput")
    qgain = P.dram("qgain", [1, B_HD], F32, "ExternalInput")
    kgain = P.dram("kgain", [1, B_HD], F32, "ExternalInput")
    lam4 = P.dram("lam4", [4, B_HD], F32, "ExternalInput")
    subln = P.dram("subln", [1, B_V], F32, "ExternalInput")
    slope = P.dram("slope", [1, 1], F32, "ExternalInput")
    o = P.dram("o", [S, B_V], F32, "ExternalOutput")
    PE, ACT, DVE, POOL, SP = P.PE, P.ACT, P.DVE, P.POOL, P.SP
    KC = D // 128
    LAM_INIT = 0.8 - 0.6 * math.exp(-0.3 * 1)
    SQ = math.sqrt(B_HD)
    P.make_ident()
    eps_t, eps_tok = P.const_tile(EPS, "eps")
    ds = P.dsem()
    gcol = P.sb("gcol", [128, KC], F32)
    qg_t = P.sb("qg_t", [128, B_HD], F32)
    kg_t = P.sb("kg_t", [128, B_HD], F32)
    lam_t = P.sb("lam_t", [128, 4, B_HD], F32)
    sub_t = P.sb("sub_t", [128, B_V], F32)
    slope_t = P.sb("slope_t", [128, 1], F32)
    pos_i = P.sb("pos_i", [128, NBLK], I32)
    posg_i = P.sb("posg_i", [128, S // 512], I32)
    ds.issue(SP, lambda e: e.dma_start(out=gcol[:], in_=gn))
    ds.issue(SP, lambda e: e.dma_start(out=qg_t[:], in_=qgain.partition_broadcast(128)))
    ds.issue(SP, lambda e: e.dma_start(out=kg_t[:], in_=kgain.partition_broadcast(128)))
    for i in range(4):
        ds.issue(SP, lambda e, i=i: e.dma_start(out=lam_t[:, i, :], in_=lam4[i:i + 1, :].partition_broadcast(128)))
    ds.issue(SP, lambda e: e.dma_start(out=sub_t[:], in_=subln.partition_broadcast(128)))
    ds.issue(SP, lambda e: e.dma_start(out=slope_t[:], in_=slope.partition_broadcast(128)))
    ds.issue(SP, lambda e: e.dma_start(out=posg_i[:], in_=posg.partition_broadcast(128)))
    t_set = ds.issue(SP, lambda e: e.dma_start(out=pos_i[:], in_=pos))
    sbufs = [P.sb("sbias%d" % i, [128, 512], F32) for i in range(4)]
    prow_i = sbufs[0][:].bitcast(I32)
    prow_f = sbufs[1]
    ds2 = P.dsem()
    t_prow = ds2.issue(SP, lambda e: e.dma_start(out=prow_i, in_=posrow.partition_broadcast(128)))
    small = P.sb("small", [128, 16], F32)
    posf = P.sb("posf", [128, NBLK], F32)
    posgf = P.sb("posgf", [128, S // 512], F32)
    T2 = P.sb("T2", [128, 512], F32)
    Tdiag = P.sb("Tdiag", [128, 128], F32)
    tmpd = P.sb("tmpd", [128, 128], F32)
    negM = small[:, 0:1]
    nslope_s = small[:, 1:2]
    neglam = small[:, 2:3]
    DVE.wait(t_set, t_prow)
    DVE.op(lambda e: e.tensor_reduce(out=small[:, 3:4], in_=qg_t[:], axis=AX.X, op=ALU.max, apply_absolute_value=True))
    DVE.op(lambda e: e.tensor_reduce(out=small[:, 4:5], in_=kg_t[:], axis=AX.X, op=ALU.max, apply_absolute_value=True))
    DVE.op(lambda e: e.scalar_tensor_tensor(out=negM, in0=small[:, 3:4], scalar=-SQ, in1=small[:, 4:5], op0=ALU.mult, op1=ALU.mult))
    DVE.op(lambda e: e.tensor_scalar(out=nslope_s, in0=slope_t[:], scalar1=-SQ, scalar2=None, op0=ALU.mult))
    DVE.op(lambda e: e.tensor_tensor(out=tmpd[:], in0=lam_t[:, 0, :], in1=lam_t[:, 1, :], op=ALU.mult))
    DVE.op(lambda e: e.tensor_reduce(out=small[:, 5:6], in_=tmpd[:], axis=AX.X, op=ALU.add))
    DVE.op(lambda e: e.tensor_tensor(out=tmpd[:], in0=lam_t[:, 2, :], in1=lam_t[:, 3, :], op=ALU.mult))
    t_l = DVE.op(lambda e: e.tensor_reduce(out=small[:, 6:7], in_=tmpd[:], axis=AX.X, op=ALU.add))
    ACT.wait(t_l)
    t_e = ACT.op(lambda e: e.activation(out=small[:, 7:9], in_=small[:, 5:7], func=AF.Exp))
    DVE.wait(t_e)
    DVE.op(lambda e: e.tensor_tensor(out=neglam, in0=small[:, 8:9], in1=small[:, 7:8], op=ALU.subtract))
    DVE.op(lambda e: e.tensor_scalar(out=neglam, in0=neglam, scalar1=-LAM_INIT, scalar2=None, op0=ALU.add))
    DVE.op(lambda e: e.tensor_scalar(out=sub_t[:], in0=sub_t[:], scalar1=1.0 - LAM_INIT, scalar2=None, op0=ALU.mult))
    DVE.op(lambda e: e.tensor_copy(out=posf[:], in_=pos_i[:]))
    DVE.op(lambda e: e.tensor_copy(out=posgf[:], in_=posg_i[:]))
    DVE.op(lambda e: e.tensor_copy(out=prow_f[:], in_=prow_i))
    DVE.op(lambda e: e.tensor_scalar(out=T2[:], in0=prow_f[:], scalar1=prow_f[:, 0:1], scalar2=nslope_s, op0=ALU.subtract, op1=ALU.mult))
    DVE.op(lambda e: e.tensor_scalar(out=Tdiag[:], in0=prow_f[:, 0:128], scalar1=posf[:, 0:1], scalar2=None, op0=ALU.subtract))
    DVE.op(lambda e: e.tensor_scalar(out=tmpd[:], in0=Tdiag[:], scalar1=-1.0, scalar2=None, op0=ALU.mult))
    DVE.op(lambda e: e.tensor_tensor(out=Tdiag[:], in0=Tdiag[:], in1=tmpd[:], op=ALU.max))
    DVE.op(lambda e: e.tensor_scalar(out=Tdiag[:], in0=Tdiag[:], scalar1=nslope_s, scalar2=None, op0=ALU.mult))
    t_setup = DVE.op(lambda e: e.tensor_scalar(out=Tdiag[64:128, 0:64], in0=Tdiag[64:128, 0:64], scalar1=NEG_BIG, scalar2=None, op0=ALU.add))
    ACT.wait(t_setup)

    w_b = P.sb("w_b", [128, KC, 512], BF16)
    stage_all = P.sb("stage_all", [128, 1024], F32)
    stage = [stage_all[:, 0:512], stage_all[:, 512:1024]]
    ssems = [P.dsem() for _ in range(2)]
    sstate = {"i": 0, "free": [None, None]}
    t_w = P.load_weight_bf16(w_b, wkv, KC, 512, gcol, t_set, stage, ssems, sstate)
    DVE.wait(t_w)

    KT = [P.sb("KT%d" % c, [128, S], BF16) for c in range(2)]
    dvp = B_V + 2
    vb = P.sb("vb", [128, NBLK, dvp], BF16)
    POOL.op(lambda e: e.memset(vb[:, :, B_V:dvp], 1.0))
    GT = 256
    xg = [P.sb("xg%d" % i, [128, KC, GT], BF16) for i in range(2)]
    xsem = [P.dsem() for _ in range(2)]
    xfree = [None, None]
    ff4 = P.sb("ff4", [128, 4, 2 * B_HD], F32)
    junk4 = stage_all[:, :].rearrange("p (b d) -> p b d", b=4)
    nb4 = P.sb("nb4", [128, 4, 2 * B_HD], BF16)
    ss = P.sb("ss", [128, 32], F32)
    QT = [[P.sb("QT%d_%d" % (c, i), [128, 512], BF16) for i in range(2)] for c in range(2)]
    st = AttnState()
    st.pT = [P.sb("pT%d" % i, [128, 512], BF16) for i in range(4)]
    st.p_free = [None] * 4
    st.s_free = [None] * 4
    st.sb_free = [t_setup] * 4
    st.it = 0
    st.acc_free = None
    st.q_ready = None
    kbias = [P.sb("kbias%d" % i, [128, NBLK], F32) for i in range(2)]
    kb_free = [None, None]
    oc = [P.sb("oc%d" % c, [128, 4, B_V], F32) for c in range(2)]
    osem = P.dsem()
    rec = P.sb("rec", [128, 4], F32)
    S_BANKS = [0, 1, 6, 7]

    def take_slot():
        slot = st.it % len(S_BANKS)
        st.it += 1
        PE.wait(st.s_free[slot])
        return slot, S_BANKS[slot]
    tok = {"p_free": None, "ff_free": None, "nb_free": None, "tr_free": None}
    x_it = [0]

    def load_x(t0):
        s = x_it[0] % 2
        x_it[0] += 1
        SP.wait(xfree[s])
        t = xsem[s].issue(SP, lambda e, s=s, t0=t0: e.dma_start(
            out=xg[s][:], in_=xnT.rearrange("(kc p) t -> p kc t", p=128)[:, :, t0:t0 + GT]))
        return s, t

    def proj_part1(loads, ncol, gain_t, is_k, blk0):
        first = True
        pslot, BK_PROJ = take_slot()
        for li, (s, t) in enumerate(loads):
            PE.wait(t)
            for tb in range(GT // 128):
                b = li * (GT // 128) + tb
                PE.wait(tok["p_free"])
                for kc in range(KC):
                    tp = PE.op(lambda e, kc=kc, s=s, tb=tb: e.matmul(P.bank(BK_PROJ)[:, 0:ncol], lhsT=xg[s][:, kc, tb * 128:(tb + 1) * 128],
                                                                   rhs=w_b[:, kc, 0:ncol], start=(kc == 0), stop=(kc == KC - 1)))
                DVE.wait(tp, tok["ff_free"] if first else None)
                first = False
                DVE.op(lambda e, b=b: e.tensor_copy(out=ff4[:, b, :], in_=P.bank(BK_PROJ)[:, 0:2 * B_HD]))
                if is_k:
                    DVE.op(lambda e, b=b: e.tensor_copy(out=vb[:, blk0 + b, 0:B_V], in_=P.bank(BK_PROJ)[:, 2 * B_HD:2 * B_HD + B_V]))
                tok["p_free"] = (DVE.sem, DVE.count)
            xfree[s] = (PE.sem, PE.count)
        st.s_free[pslot] = tok["p_free"]
        DVE.op(lambda e: e.tensor_tensor(out=junk4, in0=ff4[:], in1=ff4[:], op=ALU.mult))
        t_ss = DVE.op(lambda e: e.tensor_reduce(out=ss[:, 0:8], in_=stage_all[:, :].rearrange("p (b d) -> p b d", b=8), axis=AX.X, op=ALU.add))
        ACT.wait(t_ss, eps_tok)
        ACT.op(lambda e: e.activation(out=ss[:, 8:16], in_=ss[:, 0:8], func=AF.Ln, bias=eps_t[:], scale=1.0 / B_HD))
        t_rs = ACT.op(lambda e: e.activation(out=ss[:, 16:24], in_=ss[:, 8:16], func=AF.Exp, scale=-0.5))
        DVE.wait(t_rs, tok["nb_free"])
        for b in range(4):
            for c in range(2):
                t_nb = DVE.op(lambda e, b=b, c=c: e.scalar_tensor_tensor(
                    out=nb4[:, b, c * B_HD:(c + 1) * B_HD], in0=ff4[:, b, c * B_HD:(c + 1) * B_HD],
                    scalar=ss[:, 16 + 2 * b + c:17 + 2 * b + c], in1=gain_t[:], op0=ALU.mult, op1=ALU.mult))
        tok["ff_free"] = t_nb
        return t_nb

    def proj_part2(t_nb, dstT, tcol0, extra_wait=None):
        tslot, BK_TR = take_slot()
        PE.wait(t_nb)
        for b in range(4):
            for c in range(2):
                t_tr = PE.op(lambda e, b=b, c=c, BK_TR=BK_TR: e.transpose(out=P.bank_bf(BK_TR)[:, (2 * b + c) * 128:(2 * b + c + 1) * 128],
                                                            in_=nb4[:, b, c * B_HD:(c + 1) * B_HD], identity=P.ident[:]))
        tok["nb_free"] = t_tr
        DVE.wait(t_tr, extra_wait)
        trv = P.bank_bf(BK_TR)[:, 0:1024].rearrange("p (b c d) -> p b c d", b=4, c=2)
        for c in range(2):
            t_e = DVE.op(lambda e, c=c: e.tensor_copy(out=dstT[c][:, tcol0:tcol0 + 512].rearrange("p (b d) -> p b d", b=4), in_=trv[:, :, c, :]))
        tok["tr_free"] = t_e
        st.s_free[tslot] = t_e
        return t_e

    PE.wait(t_w, P.ident_tok)
    t_kv = None
    for g in range(NG):
        loads = [load_x(g * 512 + i * GT) for i in range(512 // GT)]
        t_nb = proj_part1(loads, 512, kg_t, True, g * 4)
        t_kv = proj_part2(t_nb, KT, g * 512)
    kv_ready = t_kv
    ACT.wait((PE.sem, PE.count))
    SP.wait((DVE.sem, DVE.count))
    t_wq = P.load_weight_bf16(w_b, wq, KC, 2 * B_HD, gcol, t_set, stage, ssems, sstate)
    PE.wait(t_wq)
    DVE.wait(t_wq)
    q_tok = {}
    q_pend = {}
    x_pre = {}

    def q_prefetch(g):
        if g < NG and g not in x_pre:
            x_pre[g] = [load_x(g * 512 + i * GT) for i in range(512 // GT)]

    def q_part1(g):
        q_prefetch(g)
        q_pend[g] = proj_part1(x_pre[g], 2 * B_HD, qg_t, False, None)

    def q_part2(g):
        qs = g % 2
        q_tok[g] = proj_part2(q_pend[g], [QT[0][qs], QT[1][qs]], 0, q_tok.get(("free", qs)))

    out_tok = None
    q_part1(0)
    q_part2(0)
    for g in range(NG):
        qs = g % 2
        kb = kbias[g % 2]
        DVE.wait(kb_free[g % 2])
        DVE.op(lambda e, kb=kb, g=g: e.tensor_scalar(out=kb[:], in0=posf[:], scalar1=posgf[:, g:g + 1], scalar2=slope_t[:], op0=ALU.subtract, op1=ALU.mult))
        t_kb = DVE.op(lambda e, kb=kb: e.tensor_scalar(out=kb[:], in0=kb[:], scalar1=negM, scalar2=None, op0=ALU.add))
        ACT.wait(t_kb)
        for c in range(2):
            st.q_ready = [q_tok[g], kv_ready]
            parts = [(lambda j, c=c: KT[c][:, j * 128:(j + 1) * 128], QT[c][qs])]
            hooks = {}
            if c == 0 and g + 1 < NG:
                nk = 4 * g + 4
                hooks[0] = [lambda g=g: q_part1(g + 1)]
                hooks.setdefault(min(nk - 1, 10), []).append(lambda g=g: q_part2(g + 1))
            t_acc = emit_attention_group(P, st, g, parts, vb, dvp, 4, 2, S_BANKS, 1.0 / SQ,
                                         lambda j, kb=kb: kb[:, j:j + 1],
                                         alibi=dict(T2=T2, Tdiag=Tdiag, negM=negM, sbuf=sbufs), hooks=hooks)
            DVE.wait(t_acc, out_tok if c == 0 else None)
            for qb in range(4):
                DVE.op(lambda e, qb=qb: e.reciprocal(out=rec[:, qb:qb + 1], in_=P.bank(2 + qb)[:, B_V:B_V + 1]))
            for qb in range(4):
                t_on = DVE.op(lambda e, qb=qb, c=c: e.tensor_scalar(out=oc[c][:, qb, :], in0=P.bank(2 + qb)[:, 0:B_V], scalar1=rec[:, qb:qb + 1], scalar2=None, op0=ALU.mult))
            st.acc_free = t_on
        q_tok[("free", qs)] = t_acc
        kb_free[g % 2] = t_acc
        o0f = oc[0][:].rearrange("p a d -> p (a d)")
        o1f = oc[1][:].rearrange("p a d -> p (a d)")
        DVE.op(lambda e: e.scalar_tensor_tensor(out=o0f, in0=o1f, scalar=neglam, in1=o0f, op0=ALU.mult, op1=ALU.add))
        DVE.op(lambda e: e.tensor_tensor(out=o1f, in0=o0f, in1=o0f, op=ALU.mult))
        t_s4 = DVE.op(lambda e: e.tensor_reduce(out=ss[:, 24:28], in_=oc[1][:], axis=AX.X, op=ALU.add))
        ACT.wait(t_s4, eps_tok)
        ACT.op(lambda e: e.activation(out=ss[:, 24:28], in_=ss[:, 24:28], func=AF.Ln, bias=eps_t[:], scale=1.0 / B_V))
        t_r4 = ACT.op(lambda e: e.activation(out=ss[:, 28:32], in_=ss[:, 24:28], func=AF.Exp, scale=-0.5))
        DVE.wait(t_r4)
        for qb in range(4):
            t_fin = DVE.op(lambda e, qb=qb: e.scalar_tensor_tensor(out=oc[1][:, qb, :], in0=oc[0][:, qb, :], scalar=ss[:, 28 + qb:29 + qb], in1=sub_t[:], op0=ALU.mult, op1=ALU.mult))
        SP.wait(t_fin)
        out_tok = osem.issue(SP, lambda e, g=g: e.dma_start(
            out=o[g * 512:(g + 1) * 512, :].rearrange("(qb p) d -> p qb d", p=128), in_=oc[1][:]))
    P.out_toks.append(out_tok)
    return P.finish()


def build_MIX(with_norm_out):
    P = Prog()
    xres = P.dram("xres", [TPC, D], F32, "ExternalInput")
    oin = P.dram("oin", [TPC, D], F32, "ExternalInput")
    wg = P.dram("wg", [D, D], F32, "ExternalInput")
    wo = P.dram("wo", [D, D], F32, "ExternalInput")
    gn = P.dram("g_norm", [1, D], F32, "ExternalInput")
    xnew = P.dram("xnew", [TPC, D], F32, "ExternalOutput")
    if with_norm_out:
        xnT_out = P.dram("xnT", [D, TPC], BF16, "ExternalOutput")
    PE, ACT, DVE, POOL, SP = P.PE, P.ACT, P.DVE, P.POOL, P.SP
    KC = D // 128
    P.make_ident()
    eps_t, eps_tok = P.const_tile(EPS, "eps")
    gain = P.sb("gain", [128, D], F32)
    ds0 = P.dsem()
    t_g = ds0.issue(SP, lambda e: e.dma_start(out=gain[:], in_=gn.partition_broadcast(128)))
    wgb = P.sb("wgb", [128, KC, D], BF16)
    wob = P.sb("wob", [128, KC, D], BF16)
    t_wg = load_weight_cast(P, wgb, wg, KC, D)
    t_wo = load_weight_cast(P, wob, wo, KC, D)
    NB = TPC // 128
    xt = [P.sb("xt%d" % i, [128, D], F32) for i in range(2)]
    xsem = [P.dsem() for _ in range(2)]
    ot = P.sb("ot", [128, D], F32)
    osem_in = P.dsem()
    junk = P.sb("junk", [128, D], BF16)
    xb = P.sb("xb", [128, D], BF16)
    xnT = P.sb("xnT_s", [128, KC, 128], BF16)
    hb = P.sb("hb", [128, D], BF16)
    hT = P.sb("hT", [128, KC, 128], BF16)
    xo = P.sb("xo", [128, D], F32)
    ss = P.sb("ss", [128, 4], F32)
    rs = P.sb("rs", [128, 4], F32)
    osem = P.dsem()
    osem2 = P.dsem()
    if with_norm_out:
        x2b = P.sb("x2b", [128, D], BF16)
        x2T = P.sb("x2T", [128, KC, 128], BF16)
    T = {"xt_free": [None, None], "ot_free": None, "xb_free": None, "xnT_free": None, "tr_free": None,
         "g_free": None, "hb_free": None, "hT_free": None, "o_free": None, "xo_free": None, "x2T_free": None,
         "x2b_free": None}
    F = {}
    out_toks = {"o": None, "n": None}

    def transposes(src, after):
        PE.wait(after, P.ident_tok, T["tr_free"])
        for kc in range(KC):
            bk = kc // 8
            oo = (kc % 8) * 128
            tt = PE.op(lambda e, kc=kc, bk=bk, oo=oo: e.transpose(out=P.bank_bf(bk)[:, oo:oo + 128], in_=src[:, kc * 128:(kc + 1) * 128], identity=P.ident[:]))
        return tt

    def evac(dst, tt, dst_free):
        DVE.wait(tt, dst_free)
        DVE.op(lambda e: e.tensor_copy(out=dst[:, 0:8, :], in_=P.bank_bf(0)[:, 0:1024]))
        t = DVE.op(lambda e: e.tensor_copy(out=dst[:, 8:16, :], in_=P.bank_bf(1)[:, 0:1024]))
        T["tr_free"] = t
        return t

    def front(b):
        s = b % 2
        SP.wait(T["xt_free"][s])
        t_x = xsem[s].issue(SP, lambda e: e.dma_start(out=xt[s][:], in_=xres[b * 128:(b + 1) * 128, :]))
        SP.wait(T["ot_free"])
        t_o = osem_in.issue(SP, lambda e: e.dma_start(out=ot[:], in_=oin[b * 128:(b + 1) * 128, :]))
        ACT.wait(t_x)
        t_ss = ACT.op(lambda e: e.activation(out=junk[:], in_=xt[s][:], func=AF.Square, accum_out=ss[:, s:s + 1]))
        t_r = emit_rstd(P, ss[:, s:s + 1], D, eps_t, eps_tok, rs[:, s:s + 1], t_ss)
        DVE.wait(t_r, T["xb_free"], t_g)
        t_xb = DVE.op(lambda e: e.scalar_tensor_tensor(out=xb[:], in0=xt[s][:], scalar=rs[:, s:s + 1], in1=gain[:], op0=ALU.mult, op1=ALU.mult))
        tt = transposes(xb, t_xb)
        T["xb_free"] = tt
        t_ev = evac(xnT, tt, T["xnT_free"])
        PE.wait(t_ev, t_wg, T["g_free"])
        for gi in range(4):
            for kc in range(KC):
                tz = PE.op(lambda e, gi=gi, kc=kc: e.matmul(P.bank(2 + gi)[:, :], lhsT=xnT[:, kc, :], rhs=wgb[:, kc, gi * 512:(gi + 1) * 512],
                                                           start=(kc == 0), stop=(kc == KC - 1)))
        T["xnT_free"] = tz
        ACT.wait(tz)
        for gi in range(4):
            t_sg = ACT.op(lambda e, gi=gi: e.activation(out=P.bank(2 + gi)[:, :], in_=P.bank(2 + gi)[:, :], func=AF.Silu), indep=True)
        DVE.wait(t_sg, t_o, T["hb_free"])
        for gi in range(4):
            t_hb = DVE.op(lambda e, gi=gi: e.tensor_tensor(out=hb[:, gi * 512:(gi + 1) * 512], in0=P.bank(2 + gi)[:, :], in1=ot[:, gi * 512:(gi + 1) * 512], op=ALU.mult), indep=True)
        T["g_free"] = t_hb
        T["ot_free"] = t_hb
        F[b] = (t_hb, s)

    def back(b):
        t_hb, s = F.pop(b)
        tt2 = transposes(hb, t_hb)
        T["hb_free"] = tt2
        t_ev2 = evac(hT, tt2, T["hT_free"])
        t_e = None
        for half in range(2):
            PE.wait(t_ev2, t_wo, T["o_free"])
            for gi in range(2):
                c0 = half * 1024 + gi * 512
                for kc in range(KC):
                    tz2 = PE.op(lambda e, gi=gi, kc=kc, c0=c0: e.matmul(P.bank(6 + gi)[:, :], lhsT=hT[:, kc, :], rhs=wob[:, kc, c0:c0 + 512],
                                                                      start=(kc == 0), stop=(kc == KC - 1)))
            if half == 1:
                T["hT_free"] = tz2
            DVE.wait(tz2, T["xo_free"] if half == 0 else None)
            for gi in range(2):
                c0 = half * 1024 + gi * 512
                t_e = DVE.op(lambda e, gi=gi, c0=c0: e.tensor_tensor(out=xo[:, c0:c0 + 512], in0=P.bank(6 + gi)[:, :], in1=xt[s][:, c0:c0 + 512], op=ALU.add), indep=True)
            T["o_free"] = t_e
        T["xt_free"][s] = t_e
        SP.wait(t_e)
        t_out = osem.issue(SP, lambda e: e.dma_start(out=xnew[b * 128:(b + 1) * 128, :], in_=xo[:]))
        out_toks["o"] = t_out
        if with_norm_out:
            ACT.wait(t_e)
            t_ss2 = ACT.op(lambda e: e.activation(out=junk[:], in_=xo[:], func=AF.Square, accum_out=ss[:, 2:3]))
            t_r2 = emit_rstd(P, ss[:, 2:3], D, eps_t, eps_tok, rs[:, 2:3], t_ss2)
            DVE.wait(t_r2, T["x2b_free"])
            t_x2b = DVE.op(lambda e: e.tensor_scalar(out=x2b[:], in0=xo[:], scalar1=rs[:, 2:3], scalar2=None, op0=ALU.mult))
            T["xo_free"] = [t_out, t_x2b]
            tt3 = transposes(x2b, t_x2b)
            T["x2b_free"] = tt3
            t_ev3 = evac(x2T, tt3, T["x2T_free"])
            SP.wait(t_ev3)
            t_o2 = osem2.issue(SP, lambda e: e.dma_start(
                out=xnT_out.rearrange("(kc p) t -> p kc t", p=128)[:, :, b * 128:(b + 1) * 128], in_=x2T[:]))
            T["x2T_free"] = t_o2
            out_toks["n"] = t_o2
        else:
            T["xo_free"] = t_out

    front(0)
    for b in range(NB):
        if b + 1 < NB:
            front(b + 1)
        back(b)
    P.out_toks += [t for t in out_toks.values() if t is not None]
    return P.finish()


_CACHE = {}


def _get(name, fn):
    if name not in _CACHE:
        _CACHE[name] = fn()
    return _CACHE[name]


def _col(v):
    v = np.asarray(v, dtype=np.float32)
    return np.ascontiguousarray(v.reshape(-1, 128).T)


def run(nc, in_maps):
    res = run_bass_kernel_spmd(nc, in_maps, core_ids=list(range(NCORES)))
    return res.results


def _posl(pos):
    return np.ascontiguousarray(np.asarray(pos, dtype=np.int32).reshape(NBLK, 128).T)


def stage_A1(x, inp):
    nc = _get("A1", build_A1)
    w_lat = np.ascontiguousarray(inp["a_w_in"][0][:, :2 * A_LORA + A_ROPE])
    g = np.ascontiguousarray(np.asarray(inp["a_norm"][0], dtype=np.float32)[None, :])
    res = run(nc, [{"x": np.ascontiguousarray(x[c * TPC:(c + 1) * TPC]), "w_lat": w_lat, "g_norm": g} for c in range(NCORES)])
    latT = np.concatenate([r["latT"] for r in res], axis=1)
    kpe = np.concatenate([r["kpe"] for r in res], axis=0)
    return latT, kpe


def stage_A2(latT, kpe, inp):
    nc = _get("A2", build_A2)
    pos = _posl(inp["positions"][0])
    wq = inp["a_w_q_up"][0]
    wkv = inp["a_w_kv_up"][0]
    ims = []
    for c in range(NCORES):
        ims.append({"latT": latT, "kpe": kpe, "pos": pos,
                    "wq": np.ascontiguousarray(wq[:, 2 * c * A_QK:(2 * c + 2) * A_QK]),
                    "wkv": np.ascontiguousarray(wkv[:, 2 * c * 256:(2 * c + 2) * 256]),
                    "gq": _col(inp["a_q_norm"][0]), "gkv": _col(inp["a_kv_norm"][0]),
                    "qgain": np.ascontiguousarray(inp["a_q_gain"][0][None, :]),
                    "kgain": np.ascontiguousarray(inp["a_k_gain"][0][None, :])})
    res = run(nc, ims)
    return np.concatenate([r["o"] for r in res], axis=1)


def stage_MIX(xres, o, w_gate, w_out, g_norm, with_norm_out):
    nc = _get("MIX%d" % int(with_norm_out), lambda: build_MIX(with_norm_out))
    w_gate = np.ascontiguousarray(w_gate)
    w_out = np.ascontiguousarray(w_out)
    g = np.ascontiguousarray(np.asarray(g_norm, dtype=np.float32)[None, :])
    res = run(nc, [{"xres": np.ascontiguousarray(xres[c * TPC:(c + 1) * TPC]),
                    "oin": np.ascontiguousarray(o[c * TPC:(c + 1) * TPC]),
                    "wg": w_gate, "wo": w_out, "g_norm": g} for c in range(NCORES)])
    xnew = np.concatenate([r["xnew"] for r in res], axis=0)
    xnT = np.concatenate([r["xnT"] for r in res], axis=1) if with_norm_out else None
    return xnew, xnT


def stage_B2(xnT, inp):
    nc = _get("B2", build_B2)
    posv = np.asarray(inp["positions"][0], dtype=np.int32)
    pos = _posl(posv)
    w = inp["b_w_in"][0]
    QK = B_H * 2 * B_HD
    ims = []
    for c in range(NCORES):
        wq = np.ascontiguousarray(w[:, c * 256:(c + 1) * 256])
        wkv = np.ascontiguousarray(np.concatenate([w[:, QK + c * 256:QK + (c + 1) * 256],
                                                   w[:, 2 * QK + c * B_V:2 * QK + (c + 1) * B_V]], axis=1))
        ims.append({"xnT": xnT, "pos": pos, "posrow": np.ascontiguousarray(posv[None, 0:512]),
                    "posg": np.ascontiguousarray(posv[None, ::512]), "wq": wq, "wkv": wkv,
                    "g_norm": _col(inp["b_norm"][0]),
                    "qgain": np.ascontiguousarray(inp["b_q_gain"][0][None, :]),
                    "kgain": np.ascontiguousarray(inp["b_k_gain"][0][None, :]),
                    "lam4": np.ascontiguousarray(np.stack([inp["b_lambda_q1"][0], inp["b_lambda_k1"][0],
                                                           inp["b_lambda_q2"][0], inp["b_lambda_k2"][0]])),
                    "subln": np.ascontiguousarray(inp["b_subln"][0][None, :]),
                    "slope": np.full((1, 1), 2.0 ** (-8.0 * (c + 1) / B_H), dtype=np.float32)})
    res = run(nc, ims)
    return np.concatenate([r["o"] for r in res], axis=1)


def kernel(**inputs):
    inp = {k: np.asarray(v) for k, v in inputs.items()}
    x = np.ascontiguousarray(inp["x"][0])
    latT, kpe = stage_A1(x, inp)
    oA = stage_A2(latT, kpe, inp)
    x1, xn1T = stage_MIX(x, oA, inp["a_w_in"][0][:, 2 * A_LORA + A_ROPE:], inp["a_w_out"][0], inp["a_norm"][0], True)
    QK = B_H * 2 * B_HD
    oB = stage_B2(xn1T, inp)
    x2, _ = stage_MIX(x1, oB, inp["b_w_in"][0][:, 2 * QK + B_H * B_V:], inp["b_w_out"][0], inp["b_norm"][0], False)
    return x2[None].astype(np.float32)
```

```python
import math
from contextlib import ExitStack

import numpy as np
import concourse.bass as bass
import concourse.mybir as mybir
from concourse.bass_utils import run_bass_kernel_spmd

F32 = mybir.dt.float32
BF16 = mybir.dt.bfloat16
I32 = mybir.dt.int32
AF = mybir.ActivationFunctionType
ALU = mybir.AluOpType
AX = mybir.AxisListType

NCORES = 8
S = 16384
D = 2048
TPC = S // NCORES
NBLK = S // 128
EPS = 1e-6
CHUNK = 64
A_H, A_NOPE, A_ROPE, A_QK, A_V, A_LORA = 16, 128, 64, 192, 128, 512
B_H, B_HD, B_V = 8, 128, 256
TWO_PI = 2.0 * math.pi
NEG_BIG = -1.0e30


class Eng:
    def __init__(self, name, sem, serialize=False):
        self.name = name
        self.sem = sem
        self.count = 0
        self.waited = {}
        self.thunks = []
        self.serialize = serialize

    def wait(self, *toks):
        for tok in toks:
            if tok is None:
                continue
            if isinstance(tok, (list, tuple)) and (len(tok) == 0 or isinstance(tok[0], (list, tuple)) or tok[0] is None):
                self.wait(*tok)
                continue
            sem, val = tok
            if self.waited.get(sem, 0) >= val:
                continue
            self.waited[sem] = val
            self.thunks.append(lambda e, sem=sem, val=val: e.wait_ge(sem, val))

    def op(self, fn, pub=True, indep=False):
        if self.serialize and not indep and self.count > 0:
            self.wait((self.sem, self.count))
        self.count += 1
        c = self.count
        sem = self.sem
        self.thunks.append(lambda e: fn(e).then_inc(sem, 1))
        return (sem, c)


class DmaSem:
    def __init__(self, sem):
        self.sem = sem
        self.n = 0

    def issue(self, eng, fn):
        self.n += 16
        sem = self.sem
        eng.thunks.append(lambda e: fn(e).then_inc(sem, 16))
        return (sem, self.n)


class Prog:
    def __init__(self):
        self.nc = bass.Bass("TRN2", target_bir_lowering=False)
        self.es = ExitStack()
        self._n = 0
        self.SP = Eng("sync", self.sem("s_sp"))
        self.ACT = Eng("scalar", self.sem("s_act"), serialize=True)
        self.DVE = Eng("vector", self.sem("s_dve"), serialize=True)
        self.POOL = Eng("gpsimd", self.sem("s_pool"), serialize=True)
        self.PE = Eng("tensor", self.sem("s_pe"))
        self.engs = [self.SP, self.ACT, self.DVE, self.POOL, self.PE]
        self.psum = self.es.enter_context(self.nc.psum_tensor("psum", [128, 8, 512], F32))
        self.out_toks = []

    def uid(self, p):
        self._n += 1
        return "%s%d" % (p, self._n)

    def sem(self, name=None):
        return self.es.enter_context(self.nc.semaphore(name or self.uid("sem")))

    def dsem(self, name=None):
        return DmaSem(self.sem(name))

    def sb(self, name, shape, dt):
        return self.es.enter_context(self.nc.sbuf_tensor(name, list(shape), dt))

    def dram(self, name, shape, dt, kind):
        return self.nc.dram_tensor(name, list(shape), dt, kind=kind).ap()

    def bank(self, b):
        return self.psum[:, b, :]

    def bank_bf(self, b):
        return self.psum[:, b, :].bitcast(BF16)

    def finish(self):
        self.SP.wait(*self.out_toks)
        with self.nc.Block() as block:
            for eng in self.engs:
                if not eng.thunks:
                    continue

                def body(e, eng=eng):
                    for th in eng.thunks:
                        th(e)

                getattr(block, eng.name)(body)
        self.es.close()
        return self.nc

    def make_ident(self):
        idf = self.sb("ident_f", [128, 128], F32)
        idb = self.sb("ident_b", [128, 128], BF16)
        t0 = self.POOL.op(lambda e: e.memset(idf[:], 1.0))
        self.POOL.wait(t0)
        t1 = self.POOL.op(lambda e: e.affine_select(out=idf[:], in_=idf[:], pattern=[[-1, 128]],
                                                    compare_op=ALU.is_equal, fill=0.0, base=0,
                                                    channel_multiplier=1))
        self.DVE.wait(t1)
        t2 = self.DVE.op(lambda e: e.tensor_copy(out=idb[:], in_=idf[:]))
        self.ident = idb
        self.ident_tok = t2
        return idb

    def const_tile(self, val, name=None):
        t = self.sb(name or self.uid("c"), [128, 1], F32)
        tok = self.POOL.op(lambda e: e.memset(t[:], float(val)))
        return t, tok

    def load_weight_bf16(self, dst, src, nkc, ncols, gcol, gcol_tok, stage, stage_sems, state):
        W = stage[0].shape[-1]
        t_cv = None
        for kc in range(nkc):
            for c0 in range(0, ncols, W):
                c1 = min(ncols, c0 + W)
                i = state["i"] % len(stage)
                st = stage[i]
                self.SP.wait(state["free"][i])
                t_ld = stage_sems[i].issue(self.SP, lambda e, st=st, kc=kc, c0=c0, c1=c1: e.dma_start(
                    out=st[:, 0:c1 - c0], in_=src[kc * 128:(kc + 1) * 128, c0:c1]))
                self.ACT.wait(t_ld, gcol_tok)
                if gcol is not None:
                    t_cv = self.ACT.op(lambda e, st=st, kc=kc, c0=c0, c1=c1: e.activation(
                        out=dst[:, kc, c0:c1], in_=st[:, 0:c1 - c0], func=AF.Copy, scale=gcol[:, kc:kc + 1]), indep=True)
                else:
                    t_cv = self.ACT.op(lambda e, st=st, kc=kc, c0=c0, c1=c1: e.activation(
                        out=dst[:, kc, c0:c1], in_=st[:, 0:c1 - c0], func=AF.Copy), indep=True)
                state["free"][i] = t_cv
                state["i"] += 1
        return t_cv


def load_weight_cast(P, dst, src, nkc, ncols):
    dsm = P.dsem()
    tok = None
    for kc in range(nkc):
        tok = dsm.issue(P.POOL, lambda e, kc=kc: e.dma_start(out=dst[:, kc, :], in_=src[kc * 128:(kc + 1) * 128, :],
                                                            max_dma_last_dim=4096))
    return tok


def _stage(P, ncols, n=3):
    stage = [P.sb(P.uid("wst"), [128, ncols], F32) for _ in range(n)]
    sems = [P.dsem() for _ in range(n)]
    state = {"i": 0, "free": [None] * n}
    return stage, sems, state


def emit_rstd(P, ss, n, eps_t, eps_tok, out, after):
    P.ACT.wait(after, eps_tok)
    t = P.ACT.op(lambda e: e.activation(out=out[:], in_=ss[:], func=AF.Sqrt, bias=eps_t[:], scale=1.0 / n))
    P.DVE.wait(t)
    return P.DVE.op(lambda e: e.reciprocal(out=out[:], in_=out[:]))


def build_A1():
    P = Prog()
    nc = P.nc
    NL = 2 * A_LORA + A_ROPE
    x = P.dram("x", [TPC, D], F32, "ExternalInput")
    w = P.dram("w_lat", [D, NL], F32, "ExternalInput")
    gn = P.dram("g_norm", [1, D], F32, "ExternalInput")
    latT = P.dram("latT", [2 * A_LORA, TPC], BF16, "ExternalOutput")
    kpe = P.dram("kpe", [TPC, A_ROPE], F32, "ExternalOutput")
    KC = D // 128
    P.make_ident()
    eps_t, eps_tok = P.const_tile(EPS, "eps")
    gain = P.sb("gain", [128, D], F32)
    ds0 = P.dsem()
    t_g = ds0.issue(P.SP, lambda e: e.dma_start(out=gain[:], in_=gn.partition_broadcast(128)))
    wb = P.sb("wb", [128, KC, NL], BF16)
    t_w = load_weight_cast(P, wb, w, KC, NL)

    NB = TPC // 128
    xt = [P.sb("xt%d" % i, [128, D], F32) for i in range(2)]
    xsem = [P.dsem() for _ in range(2)]
    xfree = [None, None]
    junk = P.sb("junk", [128, D], BF16)
    xb = P.sb("xb", [128, D], BF16)
    xnT = P.sb("xnT", [128, KC, 128], BF16)
    ss = P.sb("ss", [128, 4], F32)
    rs = P.sb("rs", [128, 4], F32)
    latb = P.sb("latb", [128, 2 * A_LORA], BF16)
    kpt = P.sb("kpt", [128, A_ROPE], F32)
    latTt = P.sb("latTt", [128, 8, 128], BF16)
    osem1 = P.dsem()
    osem2 = P.dsem()
    t_prev_z = None
    t_prev_lt = None
    t_out1 = None
    t_out2 = None
    t_xb_free = None
    t_evac_lt = None
    t_z_free = None
    for b in range(NB):
        s = b % 2
        P.SP.wait(xfree[s])
        t_x = xsem[s].issue(P.SP, lambda e, s=s, b=b: e.dma_start(out=xt[s][:], in_=x[b * 128:(b + 1) * 128, :]))
        P.ACT.wait(t_x)
        t_ss = P.ACT.op(lambda e, s=s: e.activation(out=junk[:], in_=xt[s][:], func=AF.Square, accum_out=ss[:, 0:1]))
        t_r = emit_rstd(P, ss[:, 0:1], D, eps_t, eps_tok, rs[:, 0:1], t_ss)
        P.DVE.wait(t_r, t_xb_free, t_g)
        t_xb = P.DVE.op(lambda e, s=s: e.scalar_tensor_tensor(out=xb[:], in0=xt[s][:], scalar=rs[:, 0:1], in1=gain[:], op0=ALU.mult, op1=ALU.mult))
        xfree[s] = t_xb
        P.PE.wait(t_xb, P.ident_tok, t_prev_z)
        for kc in range(KC):
            bk = kc // 8
            o = (kc % 8) * 128
            tt = P.PE.op(lambda e, kc=kc, bk=bk, o=o: e.transpose(out=P.bank_bf(bk)[:, o:o + 128], in_=xb[:, kc * 128:(kc + 1) * 128], identity=P.ident[:]),
                         pub=(kc == KC - 1))
        t_xb_free = tt
        P.DVE.wait(tt)
        P.DVE.op(lambda e: e.tensor_copy(out=xnT[:, 0:8, :], in_=P.bank_bf(0)[:, 0:1024]), pub=False)
        t_ev = P.DVE.op(lambda e: e.tensor_copy(out=xnT[:, 8:16, :], in_=P.bank_bf(1)[:, 0:1024]))
        P.PE.wait(t_ev, t_w, t_z_free)
        for gi, (c0, c1) in enumerate([(0, 512), (512, 1024), (1024, NL)]):
            for kc in range(KC):
                tz = P.PE.op(lambda e, gi=gi, c0=c0, c1=c1, kc=kc: e.matmul(
                    P.bank(2 + gi)[:, 0:c1 - c0], lhsT=xnT[:, kc, :], rhs=wb[:, kc, c0:c1],
                    start=(kc == 0), stop=(kc == KC - 1)), pub=(gi == 2 and kc == KC - 1))
        t_prev_z = tz
        P.ACT.wait(tz)
        t_s1 = P.ACT.op(lambda e: e.activation(out=junk[:, 0:512], in_=P.bank(2)[:, :], func=AF.Square, accum_out=ss[:, 1:2]))
        t_s2 = P.ACT.op(lambda e: e.activation(out=junk[:, 512:1024], in_=P.bank(3)[:, :], func=AF.Square, accum_out=ss[:, 2:3]))
        t_r1 = emit_rstd(P, ss[:, 1:3], A_LORA, eps_t, eps_tok, rs[:, 1:3], t_s2)
        P.ACT.wait(t_r1, t_prev_lt)
        P.ACT.op(lambda e: e.activation(out=latb[:, 0:512], in_=P.bank(2)[:, :], func=AF.Copy, scale=rs[:, 1:2]), pub=False)
        t_lb = P.ACT.op(lambda e: e.activation(out=latb[:, 512:1024], in_=P.bank(3)[:, :], func=AF.Copy, scale=rs[:, 2:3]))
        P.DVE.wait(tz, t_out2)
        t_kp = P.DVE.op(lambda e: e.tensor_copy(out=kpt[:], in_=P.bank(4)[:, 0:A_ROPE]))
        t_z_free = [t_lb, t_kp]
        P.SP.wait(t_kp)
        t_out2 = osem2.issue(P.SP, lambda e, b=b: e.dma_start(out=kpe[b * 128:(b + 1) * 128, :], in_=kpt[:]))
        P.PE.wait(t_lb, t_evac_lt)
        for j in range(8):
            tl = P.PE.op(lambda e, j=j: e.transpose(out=P.bank_bf(5)[:, j * 128:(j + 1) * 128], in_=latb[:, j * 128:(j + 1) * 128], identity=P.ident[:]),
                         pub=(j == 7))
        t_prev_lt = tl
        P.DVE.wait(tl, t_out1)
        t_evac_lt = P.DVE.op(lambda e: e.tensor_copy(out=latTt[:].rearrange("p j t -> p (j t)"), in_=P.bank_bf(5)[:, 0:1024]))
        P.SP.wait(t_evac_lt)
        t_out1 = osem1.issue(P.SP, lambda e, b=b: e.dma_start(
            out=latT.rearrange("(j p) t -> p j t", p=128)[:, :, b * 128:(b + 1) * 128], in_=latTt[:]))
    P.out_toks += [t_out1, t_out2]
    return P.finish()


def emit_rope_tables(P, pos_i, pos_tok, cs):
    R2 = A_ROPE // 2
    invf = (np.float32(10000.0) ** (-(np.arange(0, A_ROPE, 2, dtype=np.float32)) / np.float32(A_ROPE))).astype(np.float32)
    posf = P.sb("posf", [128, NBLK], F32)
    CB = 32
    u = P.sb("rt_u", [128, CB, R2], F32)
    tt = P.sb("rt_t", [128, CB, R2], F32)
    ki = P.sb("rt_ki", [128, CB, R2], I32)
    kf = P.sb("rt_kf", [128, CB, R2], F32)
    r = P.sb("rt_r", [128, CB, R2], F32)
    V = P.DVE
    V.wait(pos_tok)
    V.op(lambda e: e.tensor_copy(out=posf[:], in_=pos_i[:]), pub=False)
    C1 = 6.28125
    C2 = float(np.float32(TWO_PI - C1))
    last = None
    for ch in range(NBLK // CB):
        b0 = ch * CB
        for which in range(2):
            for i in range(R2):
                V.op(lambda e, i=i, b0=b0: e.tensor_scalar(out=u[:, :, i], in0=posf[:, b0:b0 + CB], scalar1=float(invf[i]), scalar2=None, op0=ALU.mult), pub=False)
            if which == 0:
                V.op(lambda e: e.tensor_scalar(out=u[:], in0=u[:], scalar1=float(math.pi / 2), scalar2=None, op0=ALU.add), pub=False)
            V.op(lambda e: e.tensor_scalar(out=tt[:], in0=u[:], scalar1=float(1.0 / TWO_PI), scalar2=None, op0=ALU.mult), pub=False)
            V.op(lambda e: e.tensor_copy(out=ki[:], in_=tt[:]), pub=False)
            V.op(lambda e: e.tensor_copy(out=kf[:], in_=ki[:]), pub=False)
            V.op(lambda e: e.scalar_tensor_tensor(out=r[:], in0=kf[:], scalar=-C1, in1=u[:], op0=ALU.mult, op1=ALU.add), pub=False)
            V.op(lambda e: e.scalar_tensor_tensor(out=r[:], in0=kf[:], scalar=-C2, in1=r[:], op0=ALU.mult, op1=ALU.add), pub=False)
            V.op(lambda e: e.tensor_scalar(out=tt[:], in0=r[:], scalar1=float(math.pi), scalar2=-TWO_PI, op0=ALU.is_gt, op1=ALU.mult), pub=False)
            V.op(lambda e: e.tensor_tensor(out=r[:], in0=r[:], in1=tt[:], op=ALU.add), pub=False)
            V.op(lambda e: e.tensor_scalar(out=tt[:], in0=r[:], scalar1=float(-math.pi), scalar2=TWO_PI, op0=ALU.is_lt, op1=ALU.mult), pub=False)
            V.op(lambda e: e.tensor_tensor(out=r[:], in0=r[:], in1=tt[:], op=ALU.add), pub=False)
            tr = V.op(lambda e: e.tensor_scalar(out=r[:], in0=r[:], scalar1=float(-math.pi), scalar2=float(math.pi), op0=ALU.max, op1=ALU.min))
            P.ACT.wait(tr)
            ta = P.ACT.op(lambda e, b0=b0, which=which: e.activation(out=cs[:, b0:b0 + CB, which * R2:(which + 1) * R2], in_=r[:], func=AF.Sin))
            V.wait(ta)
            last = ta
    return last


def emit_rope(P, src, dst, cs, blk, tmp):
    V = P.DVE
    h = A_ROPE // 2
    cos = cs[:, blk, 0:h]
    sin = cs[:, blk, h:2 * h]
    V.op(lambda e: e.tensor_tensor(out=tmp[:, 0:h], in0=src[:, 0:h], in1=cos, op=ALU.mult), pub=False)
    V.op(lambda e: e.tensor_tensor(out=tmp[:, h:2 * h], in0=src[:, h:2 * h], in1=sin, op=ALU.mult), pub=False)
    V.op(lambda e: e.tensor_tensor(out=tmp[:, 2 * h:3 * h], in0=src[:, 0:h], in1=sin, op=ALU.mult), pub=False)
    V.op(lambda e: e.tensor_tensor(out=tmp[:, 3 * h:4 * h], in0=src[:, h:2 * h], in1=cos, op=ALU.mult), pub=False)
    V.op(lambda e: e.tensor_tensor(out=dst[:, 0:h], in0=tmp[:, 0:h], in1=tmp[:, h:2 * h], op=ALU.subtract), pub=False)
    return V.op(lambda e: e.tensor_tensor(out=dst[:, h:2 * h], in0=tmp[:, 2 * h:3 * h], in1=tmp[:, 3 * h:4 * h], op=ALU.add))


def emit_absmax_bcast(P, src_dram, n, out, dsem, tmp):
    t = dsem.issue(P.SP, lambda e: e.dma_start(out=tmp[:, 0:n], in_=src_dram.partition_broadcast(128)))
    P.DVE.wait(t)
    return P.DVE.op(lambda e: e.tensor_reduce(out=out, in_=tmp[:, 0:n], axis=AX.X, op=ALU.max, apply_absolute_value=True))


class AttnState:
    pass


class _Stop(Exception):
    pass


DBG_STEP = [0]
DBG_SKIP = [0]
DBG_VAR = [0]


def _chk(n):
    if DBG_STEP[0] == n:
        if DBG_SKIP[0] > 0:
            DBG_SKIP[0] -= 1
            return
        raise _Stop()


def emit_attention_group(P, st, g, qk_parts, vb, dvp, nacc_banks, acc_bank0, s_banks, exp_scale, bias_fn,
                         alibi=None, mask_pool=True, hooks=None):
    PE, ACT, DVE, POOL = P.PE, P.ACT, P.DVE, P.POOL
    per_bank = 4 // nacc_banks
    nk = 4 * g + 4
    nslots = len(s_banks)

    def acc(qb):
        bk = acc_bank0 + qb // per_bank
        o = (qb % per_bank) * dvp
        return P.bank(bk)[:, o:o + dvp]

    first_in_bank = [True] * nacc_banks
    PE.wait(st.q_ready, st.acc_free)
    pend = []
    t_last = None

    def emit_pv(j, slot, r, t_p):
        nonlocal t_last
        PE.wait(t_p)
        for qb in range(r, 4):
            bk = qb // per_bank
            stt = first_in_bank[bk]
            first_in_bank[bk] = False
            last = (qb == 3)
            tk = PE.op(lambda e, qb=qb, j=j, slot=slot, stt=stt: e.matmul(
                acc(qb), lhsT=st.pT[slot][:, qb * 128:(qb + 1) * 128], rhs=vb[:, j, 0:dvp],
                start=stt, stop=(j == nk - 1), skip_group_check=True), pub=last)
        st.p_free[slot] = tk
        t_last = tk

    for j in range(nk):
        r = max(0, j - 4 * g)
        c0 = r * 128
        slot = st.it % nslots
        st.it += 1
        PE.wait(st.s_free[slot])
        for pi, (ktf, qt) in enumerate(qk_parts):
            ts = PE.op(lambda e, ktf=ktf, qt=qt, j=j, slot=slot, c0=c0, pi=pi: e.matmul(
                P.bank(s_banks[slot])[:, c0:512], lhsT=ktf(j), rhs=qt[:, c0:512],
                start=(pi == 0), stop=(pi == len(qk_parts) - 1)), pub=(pi == len(qk_parts) - 1))
        if len(pend) >= nslots - 1:
            emit_pv(*pend.pop(0))
        if hooks and j in hooks:
            for hk in hooks[j]:
                hk()
        src = P.bank(s_banks[slot])
        if alibi is not None:
            sbt = alibi["sbuf"][slot]
            DVE.wait(ts, st.sb_free[slot])
            if r < 4 and j >= 4 * g:
                DVE.op(lambda e, c0=c0, src=src, sbt=sbt: e.tensor_tensor(out=sbt[:, c0:c0 + 128], in0=src[:, c0:c0 + 128], in1=alibi["Tdiag"][:], op=ALU.add), indep=True)
                if c0 + 128 < 512:
                    td = DVE.op(lambda e, c0=c0, src=src, sbt=sbt: e.tensor_tensor(out=sbt[:, c0 + 128:512], in0=src[:, c0 + 128:512], in1=alibi["T2"][:, c0 + 128:512], op=ALU.add), indep=True)
                else:
                    td = (DVE.sem, DVE.count)
            else:
                td = DVE.op(lambda e, src=src, sbt=sbt: e.tensor_tensor(out=sbt[:], in0=src[:], in1=alibi["T2"][:], op=ALU.add), indep=True)
            st.s_free[slot] = td
            ACT.wait(td, st.p_free[slot])
            if j >= 4 * g:
                ACT.op(lambda e, c0=c0, sbt=sbt, slot=slot: e.activation(out=st.pT[slot][:, c0:c0 + 128], in_=sbt[:, c0:c0 + 128], func=AF.Exp, bias=alibi["negM"][:], scale=exp_scale), indep=True)
                if c0 + 128 < 512:
                    tp = ACT.op(lambda e, c0=c0, sbt=sbt, slot=slot, j=j: e.activation(out=st.pT[slot][:, c0 + 128:512], in_=sbt[:, c0 + 128:512], func=AF.Exp, bias=bias_fn(j), scale=exp_scale), indep=True)
                else:
                    tp = (ACT.sem, ACT.count)
            else:
                tp = ACT.op(lambda e, sbt=sbt, slot=slot, j=j: e.activation(out=st.pT[slot][:], in_=sbt[:], func=AF.Exp, bias=bias_fn(j), scale=exp_scale), indep=True)
            st.sb_free[slot] = tp
        else:
            ACT.wait(ts, st.p_free[slot])
            tp = ACT.op(lambda e, c0=c0, src=src, slot=slot, j=j: e.activation(out=st.pT[slot][:, c0:512], in_=src[:, c0:512], func=AF.Exp, bias=bias_fn(j), scale=exp_scale), indep=True)
            st.s_free[slot] = tp
            if j >= 4 * g:
                POOL.wait(tp)
                tp = POOL.op(lambda e, c0=c0, slot=slot: e.memset(st.pT[slot][64:128, c0:c0 + 64], 0.0), indep=True)
        pend.append((j, slot, r, tp))
    while pend:
        emit_pv(*pend.pop(0))
    return t_last


def build_A2(NG=S // 512, NH=2, dbg=0, G0=0):
    P = Prog()
    latT = P.dram("latT", [2 * A_LORA, S], BF16, "ExternalInput")
    kpe = P.dram("kpe", [S, A_ROPE], F32, "ExternalInput")
    pos = P.dram("pos", [128, NBLK], I32, "ExternalInput")
    wq = P.dram("wq", [A_LORA, 2 * A_QK], F32, "ExternalInput")
    wkv = P.dram("wkv", [A_LORA, 2 * (A_NOPE + A_V)], F32, "ExternalInput")
    gq = P.dram("gq", [128, 4], F32, "ExternalInput")
    gkv = P.dram("gkv", [128, 4], F32, "ExternalInput")
    qgain = P.dram("qgain", [1, A_QK], F32, "ExternalInput")
    kgain = P.dram("kgain", [1, A_QK], F32, "ExternalInput")
    o = P.dram("o", [S, 2 * A_V], F32, "ExternalOutput")
    PE, ACT, DVE, POOL, SP = P.PE, P.ACT, P.DVE, P.POOL, P.SP
    P.make_ident()
    eps_t, eps_tok = P.const_tile(EPS, "eps")
    mhalf, mhalf_tok = P.const_tile(-0.5, "mhalf")
    ds = P.dsem()
    gq_t = P.sb("gq_t", [128, 4], F32)
    gkv_t = P.sb("gkv_t", [128, 4], F32)
    SP_tok = ds.issue(SP, lambda e: e.dma_start(out=gq_t[:], in_=gq))
    SP_tok = ds.issue(SP, lambda e: e.dma_start(out=gkv_t[:], in_=gkv))
    qg_t = P.sb("qg_t", [128, A_QK], F32)
    kg_t = P.sb("kg_t", [128, A_QK], F32)
    ds.issue(SP, lambda e: e.dma_start(out=qg_t[:], in_=qgain.partition_broadcast(128)))
    t_small = ds.issue(SP, lambda e: e.dma_start(out=kg_t[:], in_=kgain.partition_broadcast(128)))
    pos_i = P.sb("pos_i", [128, NBLK], I32)
    t_pos = ds.issue(SP, lambda e: e.dma_start(out=pos_i[:], in_=pos))
    t_small = t_pos
    wq_b = P.sb("wq_b", [128, 4, 2 * A_QK], BF16)
    wkv_b = P.sb("wkv_b", [128, 4, 512], BF16)
    stage, ssems, sstate = _stage(P, 512, n=2)
    t_wq = P.load_weight_bf16(wq_b, wq, 4, 2 * A_QK, gq_t, t_pos, stage, ssems, sstate)
    t_wkv = P.load_weight_bf16(wkv_b, wkv, 4, 512, gkv_t, t_pos, stage, ssems, sstate)
    mq = P.sb("mq", [128, 2], F32)
    negM = P.sb("negM", [128, 1], F32)
    DVE.wait(t_small)
    DVE.op(lambda e: e.tensor_reduce(out=mq[:, 0:1], in_=qg_t[:], axis=AX.X, op=ALU.max, apply_absolute_value=True), pub=False)
    t_mk = DVE.op(lambda e: e.tensor_reduce(out=mq[:, 1:2], in_=kg_t[:], axis=AX.X, op=ALU.max, apply_absolute_value=True))
    DVE.wait(t_mk)
    t_negM = DVE.op(lambda e: e.scalar_tensor_tensor(out=negM[:], in0=mq[:, 0:1], scalar=-math.sqrt(A_QK), in1=mq[:, 1:2], op0=ALU.mult, op1=ALU.mult))
    cs = P.sb("cs", [128, NBLK, A_ROPE], F32)
    t_cs = emit_rope_tables(P, pos_i, t_pos, cs)

    KTn = P.sb("KTn", [128, S], BF16)
    KTr = P.sb("KTr", [128, S], BF16)
    dvp = A_V + 2
    vb = P.sb("vb", [128, NBLK, dvp], BF16)
    t_ones = POOL.op(lambda e: e.memset(vb[:, :, A_V:dvp], 1.0))
    latg = [P.sb("latg%d" % i, [128, 4, 512], BF16) for i in range(2)]
    latsem = [P.dsem() for _ in range(2)]
    latfree = [None, None]
    kpg = [P.sb("kpg%d" % i, [128, 4, A_ROPE], F32) for i in range(2)]
    kpsem = [P.dsem() for _ in range(2)]
    kpfree = [None, None]
    full4 = P.sb("full4", [128, 4, A_QK], F32)
    fn4 = P.sb("fn4", [128, 4, A_QK], F32)
    junk4 = P.sb("junk4", [128, 4, A_QK], F32)
    rt4 = P.sb("rt4", [128, 4, 128], F32)
    nb4 = P.sb("nb4", [128, 4, 256], BF16)
    t_nbz = POOL.op(lambda e: e.memset(nb4[:, :, A_QK:256], 0.0))
    ss = P.sb("ss", [128, 16], F32)
    QTn = [P.sb("QTn%d" % i, [128, 512], BF16) for i in range(2)]
    QTr = [P.sb("QTr%d" % i, [128, 512], BF16) for i in range(2)]
    st = AttnState()
    st.pT = [P.sb("pT%d" % i, [128, 512], BF16) for i in range(4)]
    st.p_free = [None] * 4
    st.s_free = [None] * 4
    st.it = 0
    st.acc_free = None
    st.q_ready = None
    osb = [P.sb("osb%d" % i, [128, 4, A_V], F32) for i in range(2)]
    osem = [P.dsem() for _ in range(2)]
    rec = P.sb("rec", [128, 4], F32)
    BK_P0 = 4
    S_BANKS = [0, 1, 6, 7]

    def take_slot():
        slot = st.it % len(S_BANKS)
        st.it += 1
        PE.wait(st.s_free[slot])
        return slot, S_BANKS[slot]

    TR_OFF = 512
    tok = {"p_free": None, "full_free": None, "nb_free": None, "tr_free": None}
    H2 = A_ROPE // 2

    def proj_part1(latt, wsel, ncol, is_k, kp, gain_t, blk0):
        PE.wait(tok["p_free"])
        for b in range(4):
            bk = BK_P0 + b // 2
            off = (b % 2) * ncol
            for kc in range(4):
                tp = PE.op(lambda e, b=b, bk=bk, off=off, kc=kc: e.matmul(
                    P.bank(bk)[:, off:off + ncol], lhsT=latt[:, kc, b * 128:(b + 1) * 128], rhs=wsel(kc),
                    start=(kc == 0), stop=(kc == 3), skip_group_check=True))
        DVE.wait(tp, tok["full_free"])
        for h2 in range(2):
            src = P.bank(BK_P0 + h2)[:, 0:2 * ncol].rearrange("p (b c) -> p b c", b=2)
            if is_k:
                DVE.op(lambda e, h2=h2, src=src: e.tensor_copy(out=full4[:, 2 * h2:2 * h2 + 2, 0:A_NOPE], in_=src[:, :, 0:A_NOPE]))
                DVE.op(lambda e, h2=h2, src=src: e.tensor_copy(out=vb[:, blk0 + 2 * h2:blk0 + 2 * h2 + 2, 0:A_V], in_=src[:, :, A_NOPE:A_NOPE + A_V]))
            else:
                DVE.op(lambda e, h2=h2, src=src: e.tensor_copy(out=full4[:, 2 * h2:2 * h2 + 2, :], in_=src[:, :, 0:A_QK]))
        tok["p_free"] = (DVE.sem, DVE.count)
        if is_k:
            DVE.op(lambda e: e.tensor_copy(out=full4[:, :, A_NOPE:A_QK], in_=kp[:]))
        DVE.op(lambda e: e.tensor_tensor(out=junk4[:], in0=full4[:], in1=full4[:], op=ALU.mult))
        t_ss = DVE.op(lambda e: e.tensor_reduce(out=ss[:, 0:4], in_=junk4[:], axis=AX.X, op=ALU.add))
        ACT.wait(t_ss, eps_tok)
        ACT.op(lambda e: e.activation(out=ss[:, 4:8], in_=ss[:, 0:4], func=AF.Ln, bias=eps_t[:], scale=1.0 / A_QK))
        t_rs = ACT.op(lambda e: e.activation(out=ss[:, 8:12], in_=ss[:, 4:8], func=AF.Exp, scale=-0.5))
        DVE.wait(t_rs, tok["nb_free"])
        for b in range(4):
            DVE.op(lambda e, b=b: e.scalar_tensor_tensor(out=fn4[:, b, :], in0=full4[:, b, :], scalar=ss[:, 8 + b:9 + b], in1=gain_t[:], op0=ALU.mult, op1=ALU.mult))
        DVE.op(lambda e: e.tensor_copy(out=nb4[:, :, 0:A_NOPE], in_=fn4[:, :, 0:A_NOPE]))
        cos = cs[:, blk0:blk0 + 4, 0:H2]
        sin = cs[:, blk0:blk0 + 4, H2:2 * H2]
        x1 = fn4[:, :, A_NOPE:A_NOPE + H2]
        x2 = fn4[:, :, A_NOPE + H2:A_QK]
        DVE.op(lambda e: e.tensor_tensor(out=rt4[:, :, 0:H2], in0=x1, in1=cos, op=ALU.mult))
        DVE.op(lambda e: e.tensor_tensor(out=rt4[:, :, H2:2 * H2], in0=x2, in1=sin, op=ALU.mult))
        DVE.op(lambda e: e.tensor_tensor(out=rt4[:, :, 2 * H2:3 * H2], in0=x1, in1=sin, op=ALU.mult))
        DVE.op(lambda e: e.tensor_tensor(out=rt4[:, :, 3 * H2:4 * H2], in0=x2, in1=cos, op=ALU.mult))
        DVE.op(lambda e: e.tensor_tensor(out=nb4[:, :, A_NOPE:A_NOPE + H2], in0=rt4[:, :, 0:H2], in1=rt4[:, :, H2:2 * H2], op=ALU.subtract))
        t_nb = DVE.op(lambda e: e.tensor_tensor(out=nb4[:, :, A_NOPE + H2:A_QK], in0=rt4[:, :, 2 * H2:3 * H2], in1=rt4[:, :, 3 * H2:4 * H2], op=ALU.add))
        tok["full_free"] = t_nb
        return t_nb

    def proj_part2(t_nb, dstT_n, dstT_r, tcol0, extra_wait=None):
        tslot, BK_TN = take_slot()
        PE.wait(t_nb)
        for b in range(4):
            PE.op(lambda e, b=b: e.transpose(out=P.bank_bf(BK_TN)[:, b * 128:(b + 1) * 128], in_=nb4[:, b, 0:A_NOPE], identity=P.ident[:]))
        for b in range(4):
            t_tr = PE.op(lambda e, b=b: e.transpose(out=P.bank_bf(BK_TN)[:, TR_OFF + b * 128:TR_OFF + (b + 1) * 128], in_=nb4[:, b, A_NOPE:256], identity=P.ident[:]))
        tok["nb_free"] = t_tr
        DVE.wait(t_tr, extra_wait)
        DVE.op(lambda e: e.tensor_copy(out=dstT_n[:, tcol0:tcol0 + 512], in_=P.bank_bf(BK_TN)[:, 0:512]))
        t_e = DVE.op(lambda e: e.tensor_copy(out=dstT_r[:, tcol0:tcol0 + 512], in_=P.bank_bf(BK_TN)[:, TR_OFF:TR_OFF + 512]))
        tok["tr_free"] = t_e
        st.s_free[tslot] = t_e
        return t_e

    DVE.wait(t_cs, t_negM)
    PE.wait(t_wq, t_wkv, P.ident_tok)
    POOL.wait(t_ones)
    PE.wait(t_nbz)
    if dbg == 1:
        SP.wait((DVE.sem, DVE.count), (ACT.sem, ACT.count), (POOL.sem, POOL.count))
        return P.finish()
    out_tok = [None, None]
    oi = 0
    attn_done_prev_head = None
    lat_it = [0]

    def load_lat(row0, g, with_kpe):
        s = lat_it[0] % 2
        lat_it[0] += 1
        SP.wait(latfree[s], kpfree[s] if with_kpe else None)
        t_l = latsem[s].issue(SP, lambda e, s=s, g=g: e.dma_start(
            out=latg[s][:], in_=latT[row0:row0 + A_LORA, g * 512:(g + 1) * 512].rearrange("(kc p) t -> p kc t", p=128)))
        t_k = None
        if with_kpe:
            t_k = kpsem[s].issue(SP, lambda e, s=s, g=g: e.dma_start(
                out=kpg[s][:], in_=kpe[g * 512:(g + 1) * 512, :].rearrange("(tb p) d -> p tb d", p=128)))
        return s, t_l, t_k

    for hh in range(NH):
        t_kv_last = None
        for g in range(NG):
            s, t_l, t_k = load_lat(A_LORA, g, True)
            PE.wait(t_l)
            DVE.wait(t_k)
            if g == 0:
                DVE.wait(attn_done_prev_head)
            t_nb = proj_part1(latg[s], lambda kc, hh=hh: wkv_b[:, kc, hh * 256:(hh + 1) * 256], 256, True, kpg[s], kg_t, g * 4)
            latfree[s] = (PE.sem, PE.count)
            kpfree[s] = t_nb
            t_kv_last = proj_part2(t_nb, KTn, KTr, g * 512)
        kv_ready = t_kv_last
        if dbg == 2:
            SP.wait(kv_ready)
            return P.finish()
        q_tok = {}
        q_pend = {}
        lat_pre = {}

        def q_prefetch(g):
            if g < NG and g not in lat_pre:
                lat_pre[g] = load_lat(0, g, False)

        def q_part1(g):
            q_prefetch(g)
            s, t_l, _ = lat_pre[g]
            PE.wait(t_l)
            q_pend[g] = proj_part1(latg[s], lambda kc, hh=hh: wq_b[:, kc, hh * A_QK:(hh + 1) * A_QK], A_QK, False, None, qg_t, g * 4)
            latfree[s] = (PE.sem, PE.count)
            q_prefetch(g + 1)

        def q_part2(g):
            qs = g % 2
            q_tok[g] = proj_part2(q_pend[g], QTn[qs], QTr[qs], 0, q_tok.get(("free", qs)))

        q_part1(G0)
        q_part2(G0)
        for g in range(G0, NG):
            qs = g % 2
            st.q_ready = [q_tok[g], kv_ready]
            parts = [(lambda j: KTn[:, j * 128:(j + 1) * 128], QTn[qs]),
                     (lambda j: KTr[:, j * 128:(j + 1) * 128], QTr[qs])]
            hooks = {}
            if g + 1 < NG:
                nk = 4 * g + 4
                hooks[0] = [lambda g=g: q_part1(g + 1)]
                hooks.setdefault(min(nk - 1, 8), []).append(lambda g=g: q_part2(g + 1))
            t_acc = emit_attention_group(P, st, g, parts, vb, dvp, 2, 2, S_BANKS, 1.0 / math.sqrt(A_QK),
                                         lambda j: negM[:], hooks=hooks)
            q_tok[("free", qs)] = t_acc
            ob = osb[oi % 2]
            DVE.wait(t_acc, out_tok[oi % 2])
            for qb in range(4):
                bk = 2 + qb // 2
                off = (qb % 2) * dvp
                DVE.op(lambda e, qb=qb, bk=bk, off=off: e.reciprocal(out=rec[:, qb:qb + 1], in_=P.bank(bk)[:, off + A_V:off + A_V + 1]))
            for qb in range(4):
                bk = 2 + qb // 2
                off = (qb % 2) * dvp
                t_on = DVE.op(lambda e, qb=qb, bk=bk, off=off, ob=ob: e.tensor_scalar(out=ob[:, qb, :], in0=P.bank(bk)[:, off:off + A_V], scalar1=rec[:, qb:qb + 1], scalar2=None, op0=ALU.mult))
            st.acc_free = t_on
            SP.wait(t_on)
            out_tok[oi % 2] = osem[oi % 2].issue(SP, lambda e, g=g, hh=hh, ob=ob: e.dma_start(
                out=o[g * 512:(g + 1) * 512, hh * A_V:(hh + 1) * A_V].rearrange("(qb p) d -> p qb d", p=128), in_=ob[:]))
            oi += 1
            attn_done_prev_head = t_acc
    P.out_toks += [t for t in out_tok if t is not None]
    return P.finish()


def build_B2(NG=S // 512, dbg=0):
    P = Prog()
    xnT = P.dram("xnT", [D, S], BF16, "ExternalInput")
    pos = P.dram("pos", [128, NBLK], I32, "ExternalInput")
    posrow = P.dram("posrow", [1, 512], I32, "ExternalInput")
    posg = P.dram("posg", [1, S // 512], I32, "ExternalInput")
    wq = P.dram("wq", [D, 2 * B_HD], F32, "ExternalInput")
    wkv = P.dram("wkv", [D, 2 * B_HD + B_V], F32, "ExternalInput")
    gn = P.dram("g_norm", [128, D // 128], F32, "ExternalInput")
    qgain = P.dram("qgain", [1, B_HD], F32, "ExternalInput")
    kgain = P.dram("kgain", [1, B_HD], F32, "ExternalInput")
    lam4 = P.dram("lam4", [4, B_HD], F32, "ExternalInput")
    subln = P.dram("subln", [1, B_V], F32, "ExternalInput")
    slope = P.dram("slope", [1, 1], F32, "ExternalInput")
    o = P.dram("o", [S, B_V], F32, "ExternalOutput")
    PE, ACT, DVE, POOL, SP = P.PE, P.ACT, P.DVE, P.POOL, P.SP
    KC = D // 128
    LAM_INIT = 0.8 - 0.6 * math.exp(-0.3 * 1)
    SQ = math.sqrt(B_HD)
    P.make_ident()
    eps_t, eps_tok = P.const_tile(EPS, "eps")
    ds = P.dsem()
    gcol = P.sb("gcol", [128, KC], F32)
    qg_t = P.sb("qg_t", [128, B_HD], F32)
    kg_t = P.sb("kg_t", [128, B_HD], F32)
    lam_t = P.sb("lam_t", [128, 4, B_HD], F32)
    sub_t = P.sb("sub_t", [128, B_V], F32)
    slope_t = P.sb("slope_t", [128, 1], F32)
    pos_i = P.sb("pos_i", [128, NBLK], I32)
    posg_i = P.sb("posg_i", [128, S // 512], I32)
    ds.issue(SP, lambda e: e.dma_start(out=gcol[:], in_=gn))
    ds.issue(SP, lambda e: e.dma_start(out=qg_t[:], in_=qgain.partition_broadcast(128)))
    ds.issue(SP, lambda e: e.dma_start(out=kg_t[:], in_=kgain.partition_broadcast(128)))
    for i in range(4):
        ds.issue(SP, lambda e, i=i: e.dma_start(out=lam_t[:, i, :], in_=lam4[i:i + 1, :].partition_broadcast(128)))
    ds.issue(SP, lambda e: e.dma_start(out=sub_t[:], in_=subln.partition_broadcast(128)))
    ds.issue(SP, lambda e: e.dma_start(out=slope_t[:], in_=slope.partition_broadcast(128)))
    ds.issue(SP, lambda e: e.dma_start(out=posg_i[:], in_=posg.partition_broadcast(128)))
    t_set = ds.issue(SP, lambda e: e.dma_start(out=pos_i[:], in_=pos))
    sbufs = [P.sb("sbias%d" % i, [128, 512], F32) for i in range(4)]
    prow_i = sbufs[0][:].bitcast(I32)
    prow_f = sbufs[1]
    ds2 = P.dsem()
    t_prow = ds2.issue(SP, lambda e: e.dma_start(out=prow_i, in_=posrow.partition_broadcast(128)))
    small = P.sb("small", [128, 16], F32)
    posf = P.sb("posf", [128, NBLK], F32)
    posgf = P.sb("posgf", [128, S // 512], F32)
    T2 = P.sb("T2", [128, 512], F32)
    Tdiag = P.sb("Tdiag", [128, 128], F32)
    tmpd = P.sb("tmpd", [128, 128], F32)
    negM = small[:, 0:1]
    nslope_s = small[:, 1:2]
    neglam = small[:, 2:3]
    DVE.wait(t_set, t_prow)
    DVE.op(lambda e: e.tensor_reduce(out=small[:, 3:4], in_=qg_t[:], axis=AX.X, op=ALU.max, apply_absolute_value=True))
    DVE.op(lambda e: e.tensor_reduce(out=small[:, 4:5], in_=kg_t[:], axis=AX.X, op=ALU.max, apply_absolute_value=True))
    DVE.op(lambda e: e.scalar_tensor_tensor(out=negM, in0=small[:, 3:4], scalar=-SQ, in1=small[:, 4:5], op0=ALU.mult, op1=ALU.mult))
    DVE.op(lambda e: e.tensor_scalar(out=nslope_s, in0=slope_t[:], scalar1=-SQ, scalar2=None, op0=ALU.mult))
    DVE.op(lambda e: e.tensor_tensor(out=tmpd[:], in0=lam_t[:, 0, :], in1=lam_t[:, 1, :], op=ALU.mult))
    DVE.op(lambda e: e.tensor_reduce(out=small[:, 5:6], in_=tmpd[:], axis=AX.X, op=ALU.add))
    DVE.op(lambda e: e.tensor_tensor(out=tmpd[:], in0=lam_t[:, 2, :], in1=lam_t[:, 3, :], op=ALU.mult))
    t_l = DVE.op(lambda e: e.tensor_reduce(out=small[:, 6:7], in_=tmpd[:], axis=AX.X, op=ALU.add))
    ACT.wait(t_l)
    t_e = ACT.op(lambda e: e.activation(out=small[:, 7:9], in_=small[:, 5:7], func=AF.Exp))
    DVE.wait(t_e)
    DVE.op(lambda e: e.tensor_tensor(out=neglam, in0=small[:, 8:9], in1=small[:, 7:8], op=ALU.subtract))
    DVE.op(lambda e: e.tensor_scalar(out=neglam, in0=neglam, scalar1=-LAM_INIT, scalar2=None, op0=ALU.add))
    DVE.op(lambda e: e.tensor_scalar(out=sub_t[:], in0=sub_t[:], scalar1=1.0 - LAM_INIT, scalar2=None, op0=ALU.mult))
    DVE.op(lambda e: e.tensor_copy(out=posf[:], in_=pos_i[:]))
    DVE.op(lambda e: e.tensor_copy(out=posgf[:], in_=posg_i[:]))
    DVE.op(lambda e: e.tensor_copy(out=prow_f[:], in_=prow_i))
    DVE.op(lambda e: e.tensor_scalar(out=T2[:], in0=prow_f[:], scalar1=prow_f[:, 0:1], scalar2=nslope_s, op0=ALU.subtract, op1=ALU.mult))
    DVE.op(lambda e: e.tensor_scalar(out=Tdiag[:], in0=prow_f[:, 0:128], scalar1=posf[:, 0:1], scalar2=None, op0=ALU.subtract))
    DVE.op(lambda e: e.tensor_scalar(out=tmpd[:], in0=Tdiag[:], scalar1=-1.0, scalar2=None, op0=ALU.mult))
    DVE.op(lambda e: e.tensor_tensor(out=Tdiag[:], in0=Tdiag[:], in1=tmpd[:], op=ALU.max))
    DVE.op(lambda e: e.tensor_scalar(out=Tdiag[:], in0=Tdiag[:], scalar1=nslope_s, scalar2=None, op0=ALU.mult))
    t_setup = DVE.op(lambda e: e.tensor_scalar(out=Tdiag[64:128, 0:64], in0=Tdiag[64:128, 0:64], scalar1=NEG_BIG, scalar2=None, op0=ALU.add))
    ACT.wait(t_setup)

    w_b = P.sb("w_b", [128, KC, 512], BF16)
    stage_all = P.sb("stage_all", [128, 1024], F32)
    stage = [stage_all[:, 0:512], stage_all[:, 512:1024]]
    ssems = [P.dsem() for _ in range(2)]
    sstate = {"i": 0, "free": [None, None]}
    t_w = P.load_weight_bf16(w_b, wkv, KC, 512, gcol, t_set, stage, ssems, sstate)
    DVE.wait(t_w)

    KT = [P.sb("KT%d" % c, [128, S], BF16) for c in range(2)]
    dvp = B_V + 2
    vb = P.sb("vb", [128, NBLK, dvp], BF16)
    POOL.op(lambda e: e.memset(vb[:, :, B_V:dvp], 1.0))
    GT = 256
    xg = [P.sb("xg%d" % i, [128, KC, GT], BF16) for i in range(2)]
    xsem = [P.dsem() for _ in range(2)]
    xfree = [None, None]
    ff4 = P.sb("ff4", [128, 4, 2 * B_HD], F32)
    junk4 = stage_all[:, :].rearrange("p (b d) -> p b d", b=4)
    nb4 = P.sb("nb4", [128, 4, 2 * B_HD], BF16)
    ss = P.sb("ss", [128, 32], F32)
    QT = [[P.sb("QT%d_%d" % (c, i), [128, 512], BF16) for i in range(2)] for c in range(2)]
    st = AttnState()
    st.pT = [P.sb("pT%d" % i, [128, 512], BF16) for i in range(4)]
    st.p_free = [None] * 4
    st.s_free = [None] * 4
    st.sb_free = [t_setup] * 4
    st.it = 0
    st.acc_free = None
    st.q_ready = None
    kbias = [P.sb("kbias%d" % i, [128, NBLK], F32) for i in range(2)]
    kb_free = [None, None]
    oc = [P.sb("oc%d" % c, [128, 4, B_V], F32) for c in range(2)]
    osem = P.dsem()
    rec = P.sb("rec", [128, 4], F32)
    S_BANKS = [0, 1, 6, 7]

    def take_slot():
        slot = st.it % len(S_BANKS)
        st.it += 1
        PE.wait(st.s_free[slot])
        return slot, S_BANKS[slot]
    tok = {"p_free": None, "ff_free": None, "nb_free": None, "tr_free": None}
    x_it = [0]

    def load_x(t0):
        s = x_it[0] % 2
        x_it[0] += 1
        SP.wait(xfree[s])
        t = xsem[s].issue(SP, lambda e, s=s, t0=t0: e.dma_start(
            out=xg[s][:], in_=xnT.rearrange("(kc p) t -> p kc t", p=128)[:, :, t0:t0 + GT]))
        return s, t

    def proj_part1(loads, ncol, gain_t, is_k, blk0):
        first = True
        pslot, BK_PROJ = take_slot()
        for li, (s, t) in enumerate(loads):
            PE.wait(t)
            for tb in range(GT // 128):
                b = li * (GT // 128) + tb
                PE.wait(tok["p_free"])
                for kc in range(KC):
                    tp = PE.op(lambda e, kc=kc, s=s, tb=tb: e.matmul(P.bank(BK_PROJ)[:, 0:ncol], lhsT=xg[s][:, kc, tb * 128:(tb + 1) * 128],
                                                                   rhs=w_b[:, kc, 0:ncol], start=(kc == 0), stop=(kc == KC - 1)))
                DVE.wait(tp, tok["ff_free"] if first else None)
                first = False
                DVE.op(lambda e, b=b: e.tensor_copy(out=ff4[:, b, :], in_=P.bank(BK_PROJ)[:, 0:2 * B_HD]))
                if is_k:
                    DVE.op(lambda e, b=b: e.tensor_copy(out=vb[:, blk0 + b, 0:B_V], in_=P.bank(BK_PROJ)[:, 2 * B_HD:2 * B_HD + B_V]))
                tok["p_free"] = (DVE.sem, DVE.count)
            xfree[s] = (PE.sem, PE.count)
        st.s_free[pslot] = tok["p_free"]
        DVE.op(lambda e: e.tensor_tensor(out=junk4, in0=ff4[:], in1=ff4[:], op=ALU.mult))
        t_ss = DVE.op(lambda e: e.tensor_reduce(out=ss[:, 0:8], in_=stage_all[:, :].rearrange("p (b d) -> p b d", b=8), axis=AX.X, op=ALU.add))
        ACT.wait(t_ss, eps_tok)
        ACT.op(lambda e: e.activation(out=ss[:, 8:16], in_=ss[:, 0:8], func=AF.Ln, bias=eps_t[:], scale=1.0 / B_HD))
        t_rs = ACT.op(lambda e: e.activation(out=ss[:, 16:24], in_=ss[:, 8:16], func=AF.Exp, scale=-0.5))
        DVE.wait(t_rs, tok["nb_free"])
        for b in range(4):
            for c in range(2):
                t_nb = DVE.op(lambda e, b=b, c=c: e.scalar_tensor_tensor(
                    out=nb4[:, b, c * B_HD:(c + 1) * B_HD], in0=ff4[:, b, c * B_HD:(c + 1) * B_HD],
                    scalar=ss[:, 16 + 2 * b + c:17 + 2 * b + c], in1=gain_t[:], op0=ALU.mult, op1=ALU.mult))
        tok["ff_free"] = t_nb
        return t_nb

    def proj_part2(t_nb, dstT, tcol0, extra_wait=None):
        tslot, BK_TR = take_slot()
        PE.wait(t_nb)
        for b in range(4):
            for c in range(2):
                t_tr = PE.op(lambda e, b=b, c=c, BK_TR=BK_TR: e.transpose(out=P.bank_bf(BK_TR)[:, (2 * b + c) * 128:(2 * b + c + 1) * 128],
                                                            in_=nb4[:, b, c * B_HD:(c + 1) * B_HD], identity=P.ident[:]))
        tok["nb_free"] = t_tr
        DVE.wait(t_tr, extra_wait)
        trv = P.bank_bf(BK_TR)[:, 0:1024].rearrange("p (b c d) -> p b c d", b=4, c=2)
        for c in range(2):
            t_e = DVE.op(lambda e, c=c: e.tensor_copy(out=dstT[c][:, tcol0:tcol0 + 512].rearrange("p (b d) -> p b d", b=4), in_=trv[:, :, c, :]))
        tok["tr_free"] = t_e
        st.s_free[tslot] = t_e
        return t_e

    PE.wait(t_w, P.ident_tok)
    t_kv = None
    for g in range(NG):
        loads = [load_x(g * 512 + i * GT) for i in range(512 // GT)]
        t_nb = proj_part1(loads, 512, kg_t, True, g * 4)
        t_kv = proj_part2(t_nb, KT, g * 512)
    kv_ready = t_kv
    ACT.wait((PE.sem, PE.count))
    SP.wait((DVE.sem, DVE.count))
    t_wq = P.load_weight_bf16(w_b, wq, KC, 2 * B_HD, gcol, t_set, stage, ssems, sstate)
    PE.wait(t_wq)
    DVE.wait(t_wq)
    q_tok = {}
    q_pend = {}
    x_pre = {}

    def q_prefetch(g):
        if g < NG and g not in x_pre:
            x_pre[g] = [load_x(g * 512 + i * GT) for i in range(512 // GT)]

    def q_part1(g):
        q_prefetch(g)
        q_pend[g] = proj_part1(x_pre[g], 2 * B_HD, qg_t, False, None)

    def q_part2(g):
        qs = g % 2
        q_tok[g] = proj_part2(q_pend[g], [QT[0][qs], QT[1][qs]], 0, q_tok.get(("free", qs)))

    out_tok = None
    q_part1(0)
    q_part2(0)
    for g in range(NG):
        qs = g % 2
        kb = kbias[g % 2]
        DVE.wait(kb_free[g % 2])
        DVE.op(lambda e, kb=kb, g=g: e.tensor_scalar(out=kb[:], in0=posf[:], scalar1=posgf[:, g:g + 1], scalar2=slope_t[:], op0=ALU.subtract, op1=ALU.mult))
        t_kb = DVE.op(lambda e, kb=kb: e.tensor_scalar(out=kb[:], in0=kb[:], scalar1=negM, scalar2=None, op0=ALU.add))
        ACT.wait(t_kb)
        for c in range(2):
            st.q_ready = [q_tok[g], kv_ready]
            parts = [(lambda j, c=c: KT[c][:, j * 128:(j + 1) * 128], QT[c][qs])]
            hooks = {}
            if c == 0 and g + 1 < NG:
                nk = 4 * g + 4
                hooks[0] = [lambda g=g: q_part1(g + 1)]
                hooks.setdefault(min(nk - 1, 10), []).append(lambda g=g: q_part2(g + 1))
            t_acc = emit_attention_group(P, st, g, parts, vb, dvp, 4, 2, S_BANKS, 1.0 / SQ,
                                         lambda j, kb=kb: kb[:, j:j + 1],
                                         alibi=dict(T2=T2, Tdiag=Tdiag, negM=negM, sbuf=sbufs), hooks=hooks)
            DVE.wait(t_acc, out_tok if c == 0 else None)
            for qb in range(4):
                DVE.op(lambda e, qb=qb: e.reciprocal(out=rec[:, qb:qb + 1], in_=P.bank(2 + qb)[:, B_V:B_V + 1]))
            for qb in range(4):
                t_on = DVE.op(lambda e, qb=qb, c=c: e.tensor_scalar(out=oc[c][:, qb, :], in0=P.bank(2 + qb)[:, 0:B_V], scalar1=rec[:, qb:qb + 1], scalar2=None, op0=ALU.mult))
            st.acc_free = t_on
        q_tok[("free", qs)] = t_acc
        kb_free[g % 2] = t_acc
        o0f = oc[0][:].rearrange("p a d -> p (a d)")
        o1f = oc[1][:].rearrange("p a d -> p (a d)")
        DVE.op(lambda e: e.scalar_tensor_tensor(out=o0f, in0=o1f, scalar=neglam, in1=o0f, op0=ALU.mult, op1=ALU.add))
        DVE.op(lambda e: e.tensor_tensor(out=o1f, in0=o0f, in1=o0f, op=ALU.mult))
        t_s4 = DVE.op(lambda e: e.tensor_reduce(out=ss[:, 24:28], in_=oc[1][:], axis=AX.X, op=ALU.add))
        ACT.wait(t_s4, eps_tok)
        ACT.op(lambda e: e.activation(out=ss[:, 24:28], in_=ss[:, 24:28], func=AF.Ln, bias=eps_t[:], scale=1.0 / B_V))
        t_r4 = ACT.op(lambda e: e.activation(out=ss[:, 28:32], in_=ss[:, 24:28], func=AF.Exp, scale=-0.5))
        DVE.wait(t_r4)
        for qb in range(4):
            t_fin = DVE.op(lambda e, qb=qb: e.scalar_tensor_tensor(out=oc[1][:, qb, :], in0=oc[0][:, qb, :], scalar=ss[:, 28 + qb:29 + qb], in1=sub_t[:], op0=ALU.mult, op1=ALU.mult))
        SP.wait(t_fin)
        out_tok = osem.issue(SP, lambda e, g=g: e.dma_start(
            out=o[g * 512:(g + 1) * 512, :].rearrange("(qb p) d -> p qb d", p=128), in_=oc[1][:]))
    P.out_toks.append(out_tok)
    return P.finish()


def build_MIX(with_norm_out):
    P = Prog()
    xres = P.dram("xres", [TPC, D], F32, "ExternalInput")
    oin = P.dram("oin", [TPC, D], F32, "ExternalInput")
    wg = P.dram("wg", [D, D], F32, "ExternalInput")
    wo = P.dram("wo", [D, D], F32, "ExternalInput")
    gn = P.dram("g_norm", [1, D], F32, "ExternalInput")
    xnew = P.dram("xnew", [TPC, D], F32, "ExternalOutput")
    if with_norm_out:
        xnT_out = P.dram("xnT", [D, TPC], BF16, "ExternalOutput")
    PE, ACT, DVE, POOL, SP = P.PE, P.ACT, P.DVE, P.POOL, P.SP
    KC = D // 128
    P.make_ident()
    eps_t, eps_tok = P.const_tile(EPS, "eps")
    gain = P.sb("gain", [128, D], F32)
    ds0 = P.dsem()
    t_g = ds0.issue(SP, lambda e: e.dma_start(out=gain[:], in_=gn.partition_broadcast(128)))
    wgb = P.sb("wgb", [128, KC, D], BF16)
    wob = P.sb("wob", [128, KC, D], BF16)
    t_wg = load_weight_cast(P, wgb, wg, KC, D)
    t_wo = load_weight_cast(P, wob, wo, KC, D)
    NB = TPC // 128
    xt = [P.sb("xt%d" % i, [128, D], F32) for i in range(2)]
    xsem = [P.dsem() for _ in range(2)]
    ot = P.sb("ot", [128, D], F32)
    osem_in = P.dsem()
    junk = P.sb("junk", [128, D], BF16)
    xb = P.sb("xb", [128, D], BF16)
    xnT = P.sb("xnT_s", [128, KC, 128], BF16)
    hb = P.sb("hb", [128, D], BF16)
    hT = P.sb("hT", [128, KC, 128], BF16)
    xo = P.sb("xo", [128, D], F32)
    ss = P.sb("ss", [128, 4], F32)
    rs = P.sb("rs", [128, 4], F32)
    osem = P.dsem()
    osem2 = P.dsem()
    if with_norm_out:
        x2b = P.sb("x2b", [128, D], BF16)
        x2T = P.sb("x2T", [128, KC, 128], BF16)
    T = {"xt_free": [None, None], "ot_free": None, "xb_free": None, "xnT_free": None, "tr_free": None,
         "g_free": None, "hb_free": None, "hT_free": None, "o_free": None, "xo_free": None, "x2T_free": None,
         "x2b_free": None}
    F = {}
    out_toks = {"o": None, "n": None}

    def transposes(src, after):
        PE.wait(after, P.ident_tok, T["tr_free"])
        for kc in range(KC):
            bk = kc // 8
            oo = (kc % 8) * 128
            tt = PE.op(lambda e, kc=kc, bk=bk, oo=oo: e.transpose(out=P.bank_bf(bk)[:, oo:oo + 128], in_=src[:, kc * 128:(kc + 1) * 128], identity=P.ident[:]))
        return tt

    def evac(dst, tt, dst_free):
        DVE.wait(tt, dst_free)
        DVE.op(lambda e: e.tensor_copy(out=dst[:, 0:8, :], in_=P.bank_bf(0)[:, 0:1024]))
        t = DVE.op(lambda e: e.tensor_copy(out=dst[:, 8:16, :], in_=P.bank_bf(1)[:, 0:1024]))
        T["tr_free"] = t
        return t

    def front(b):
        s = b % 2
        SP.wait(T["xt_free"][s])
        t_x = xsem[s].issue(SP, lambda e: e.dma_start(out=xt[s][:], in_=xres[b * 128:(b + 1) * 128, :]))
        SP.wait(T["ot_free"])
        t_o = osem_in.issue(SP, lambda e: e.dma_start(out=ot[:], in_=oin[b * 128:(b + 1) * 128, :]))
        ACT.wait(t_x)
        t_ss = ACT.op(lambda e: e.activation(out=junk[:], in_=xt[s][:], func=AF.Square, accum_out=ss[:, s:s + 1]))
        t_r = emit_rstd(P, ss[:, s:s + 1], D, eps_t, eps_tok, rs[:, s:s + 1], t_ss)
        DVE.wait(t_r, T["xb_free"], t_g)
        t_xb = DVE.op(lambda e: e.scalar_tensor_tensor(out=xb[:], in0=xt[s][:], scalar=rs[:, s:s + 1], in1=gain[:], op0=ALU.mult, op1=ALU.mult))
        tt = transposes(xb, t_xb)
        T["xb_free"] = tt
        t_ev = evac(xnT, tt, T["xnT_free"])
        PE.wait(t_ev, t_wg, T["g_free"])
        for gi in range(4):
            for kc in range(KC):
                tz = PE.op(lambda e, gi=gi, kc=kc: e.matmul(P.bank(2 + gi)[:, :], lhsT=xnT[:, kc, :], rhs=wgb[:, kc, gi * 512:(gi + 1) * 512],
                                                           start=(kc == 0), stop=(kc == KC - 1)))
        T["xnT_free"] = tz
        ACT.wait(tz)
        for gi in range(4):
            t_sg = ACT.op(lambda e, gi=gi: e.activation(out=P.bank(2 + gi)[:, :], in_=P.bank(2 + gi)[:, :], func=AF.Silu), indep=True)
        DVE.wait(t_sg, t_o, T["hb_free"])
        for gi in range(4):
            t_hb = DVE.op(lambda e, gi=gi: e.tensor_tensor(out=hb[:, gi * 512:(gi + 1) * 512], in0=P.bank(2 + gi)[:, :], in1=ot[:, gi * 512:(gi + 1) * 512], op=ALU.mult), indep=True)
        T["g_free"] = t_hb
        T["ot_free"] = t_hb
        F[b] = (t_hb, s)

    def back(b):
        t_hb, s = F.pop(b)
        tt2 = transposes(hb, t_hb)
        T["hb_free"] = tt2
        t_ev2 = evac(hT, tt2, T["hT_free"])
        t_e = None
        for half in range(2):
            PE.wait(t_ev2, t_wo, T["o_free"])
            for gi in range(2):
                c0 = half * 1024 + gi * 512
                for kc in range(KC):
                    tz2 = PE.op(lambda e, gi=gi, kc=kc, c0=c0: e.matmul(P.bank(6 + gi)[:, :], lhsT=hT[:, kc, :], rhs=wob[:, kc, c0:c0 + 512],
                                                                      start=(kc == 0), stop=(kc == KC - 1)))
            if half == 1:
                T["hT_free"] = tz2
            DVE.wait(tz2, T["xo_free"] if half == 0 else None)
            for gi in range(2):
                c0 = half * 1024 + gi * 512
                t_e = DVE.op(lambda e, gi=gi, c0=c0: e.tensor_tensor(out=xo[:, c0:c0 + 512], in0=P.bank(6 + gi)[:, :], in1=xt[s][:, c0:c0 + 512], op=ALU.add), indep=True)
            T["o_free"] = t_e
        T["xt_free"][s] = t_e
        SP.wait(t_e)
        t_out = osem.issue(SP, lambda e: e.dma_start(out=xnew[b * 128:(b + 1) * 128, :], in_=xo[:]))
        out_toks["o"] = t_out
        if with_norm_out:
            ACT.wait(t_e)
            t_ss2 = ACT.op(lambda e: e.activation(out=junk[:], in_=xo[:], func=AF.Square, accum_out=ss[:, 2:3]))
            t_r2 = emit_rstd(P, ss[:, 2:3], D, eps_t, eps_tok, rs[:, 2:3], t_ss2)
            DVE.wait(t_r2, T["x2b_free"])
            t_x2b = DVE.op(lambda e: e.tensor_scalar(out=x2b[:], in0=xo[:], scalar1=rs[:, 2:3], scalar2=None, op0=ALU.mult))
            T["xo_free"] = [t_out, t_x2b]
            tt3 = transposes(x2b, t_x2b)
            T["x2b_free"] = tt3
            t_ev3 = evac(x2T, tt3, T["x2T_free"])
            SP.wait(t_ev3)
            t_o2 = osem2.issue(SP, lambda e: e.dma_start(
                out=xnT_out.rearrange("(kc p) t -> p kc t", p=128)[:, :, b * 128:(b + 1) * 128], in_=x2T[:]))
            T["x2T_free"] = t_o2
            out_toks["n"] = t_o2
        else:
            T["xo_free"] = t_out

    front(0)
    for b in range(NB):
        if b + 1 < NB:
            front(b + 1)
        back(b)
    P.out_toks += [t for t in out_toks.values() if t is not None]
    return P.finish()


_CACHE = {}


def _get(name, fn):
    if name not in _CACHE:
        _CACHE[name] = fn()
    return _CACHE[name]


def _col(v):
    v = np.asarray(v, dtype=np.float32)
    return np.ascontiguousarray(v.reshape(-1, 128).T)


def run(nc, in_maps):
    res = run_bass_kernel_spmd(nc, in_maps, core_ids=list(range(NCORES)))
    return res.results


def _posl(pos):
    return np.ascontiguousarray(np.asarray(pos, dtype=np.int32).reshape(NBLK, 128).T)


def stage_A1(x, inp):
    nc = _get("A1", build_A1)
    w_lat = np.ascontiguousarray(inp["a_w_in"][0][:, :2 * A_LORA + A_ROPE])
    g = np.ascontiguousarray(np.asarray(inp["a_norm"][0], dtype=np.float32)[None, :])
    res = run(nc, [{"x": np.ascontiguousarray(x[c * TPC:(c + 1) * TPC]), "w_lat": w_lat, "g_norm": g} for c in range(NCORES)])
    latT = np.concatenate([r["latT"] for r in res], axis=1)
    kpe = np.concatenate([r["kpe"] for r in res], axis=0)
    return latT, kpe


def stage_A2(latT, kpe, inp):
    nc = _get("A2", build_A2)
    pos = _posl(inp["positions"][0])
    wq = inp["a_w_q_up"][0]
    wkv = inp["a_w_kv_up"][0]
    ims = []
    for c in range(NCORES):
        ims.append({"latT": latT, "kpe": kpe, "pos": pos,
                    "wq": np.ascontiguousarray(wq[:, 2 * c * A_QK:(2 * c + 2) * A_QK]),
                    "wkv": np.ascontiguousarray(wkv[:, 2 * c * 256:(2 * c + 2) * 256]),
                    "gq": _col(inp["a_q_norm"][0]), "gkv": _col(inp["a_kv_norm"][0]),
                    "qgain": np.ascontiguousarray(inp["a_q_gain"][0][None, :]),
                    "kgain": np.ascontiguousarray(inp["a_k_gain"][0][None, :])})
    res = run(nc, ims)
    return np.concatenate([r["o"] for r in res], axis=1)


def stage_MIX(xres, o, w_gate, w_out, g_norm, with_norm_out):
    nc = _get("MIX%d" % int(with_norm_out), lambda: build_MIX(with_norm_out))
    w_gate = np.ascontiguousarray(w_gate)
    w_out = np.ascontiguousarray(w_out)
    g = np.ascontiguousarray(np.asarray(g_norm, dtype=np.float32)[None, :])
    res = run(nc, [{"xres": np.ascontiguousarray(xres[c * TPC:(c + 1) * TPC]),
                    "oin": np.ascontiguousarray(o[c * TPC:(c + 1) * TPC]),
                    "wg": w_gate, "wo": w_out, "g_norm": g} for c in range(NCORES)])
    xnew = np.concatenate([r["xnew"] for r in res], axis=0)
    xnT = np.concatenate([r["xnT"] for r in res], axis=1) if with_norm_out else None
    return xnew, xnT


def stage_B2(xnT, inp):
    nc = _get("B2", build_B2)
    posv = np.asarray(inp["positions"][0], dtype=np.int32)
    pos = _posl(posv)
    w = inp["b_w_in"][0]
    QK = B_H * 2 * B_HD
    ims = []
    for c in range(NCORES):
        wq = np.ascontiguousarray(w[:, c * 256:(c + 1) * 256])
        wkv = np.ascontiguousarray(np.concatenate([w[:, QK + c * 256:QK + (c + 1) * 256],
                                                   w[:, 2 * QK + c * B_V:2 * QK + (c + 1) * B_V]], axis=1))
        ims.append({"xnT": xnT, "pos": pos, "posrow": np.ascontiguousarray(posv[None, 0:512]),
                    "posg": np.ascontiguousarray(posv[None, ::512]), "wq": wq, "wkv": wkv,
                    "g_norm": _col(inp["b_norm"][0]),
                    "qgain": np.ascontiguousarray(inp["b_q_gain"][0][None, :]),
                    "kgain": np.ascontiguousarray(inp["b_k_gain"][0][None, :]),
                    "lam4": np.ascontiguousarray(np.stack([inp["b_lambda_q1"][0], inp["b_lambda_k1"][0],
                                                           inp["b_lambda_q2"][0], inp["b_lambda_k2"][0]])),
                    "subln": np.ascontiguousarray(inp["b_subln"][0][None, :]),
                    "slope": np.full((1, 1), 2.0 ** (-8.0 * (c + 1) / B_H), dtype=np.float32)})
    res = run(nc, ims)
    return np.concatenate([r["o"] for r in res], axis=1)


def kernel(**inputs):
    inp = {k: np.asarray(v) for k, v in inputs.items()}
    x = np.ascontiguousarray(inp["x"][0])
    latT, kpe = stage_A1(x, inp)
    oA = stage_A2(latT, kpe, inp)
    x1, xn1T = stage_MIX(x, oA, inp["a_w_in"][0][:, 2 * A_LORA + A_ROPE:], inp["a_w_out"][0], inp["a_norm"][0], True)
    QK = B_H * 2 * B_HD
    oB = stage_B2(xn1T, inp)
    x2, _ = stage_MIX(x1, oB, inp["b_w_in"][0][:, 2 * QK + B_H * B_V:], inp["b_w_out"][0], inp["b_norm"][0], False)
    return x2[None].astype(np.float32)
```
